# Optimizing a Trainium2 kernel written in Bass

```python
import jax, jax.numpy as jnp
from jax import lax
import numpy as np

D_MODEL = 1024
BATCH = 4
SEQ = 8192
DEPTH = 1

MLA_HEADS = 8
QK_NOPE_DIM = 64
QK_ROPE_DIM = 32
QK_HEAD_DIM = QK_NOPE_DIM + QK_ROPE_DIM
V_HEAD_DIM = 64
Q_LORA_RANK = 384
KV_LORA_RANK = 256
ROPE_THETA = 10000.0
Q_BLOCK = 128
RWKV_HEADS = 8
RWKV_HEAD_DIM = 64
RWKV_DIM = RWKV_HEADS * RWKV_HEAD_DIM
DECAY_LORA = 64
AAA_LORA = 64
GATE_LORA = 128
GN_EPS = RWKV_HEAD_DIM * 1e-5
MLA_COLS = Q_LORA_RANK + KV_LORA_RANK + QK_ROPE_DIM
RWKV_COLS = 3 * RWKV_DIM + DECAY_LORA + AAA_LORA + GATE_LORA
GATE_COLS = 2 * D_MODEL
IN_COLS = MLA_COLS + RWKV_COLS + GATE_COLS
D_FF = 4 * D_MODEL
PLE_DIM = 256
RMS_EPS = 1e-6

kernel_name = "hybrid_mla_rwkv7_gated_block"


def rmsnorm(x, g):
    xf = x.astype(jnp.float32)
    y = xf * lax.rsqrt(jnp.mean(xf * xf, axis=-1, keepdims=True) + RMS_EPS)
    return (y * g.astype(jnp.float32)).astype(x.dtype)


def rope_tables(positions):
    half = QK_ROPE_DIM // 2
    inv_freq = ROPE_THETA ** (-jnp.arange(half, dtype=jnp.float32) / half)
    ang = positions.astype(jnp.float32)[..., None] * inv_freq
    return jnp.cos(ang), jnp.sin(ang)


def apply_rope(x, cos, sin):
    xf = x.astype(jnp.float32)
    x1, x2 = jnp.split(xf, 2, axis=-1)
    out = jnp.concatenate([x1 * cos - x2 * sin, x2 * cos + x1 * sin], axis=-1)
    return out.astype(x.dtype)


def causal_block_attention(q, k, v, scale):
    S = q.shape[1]
    outs = []
    for blk in range(S // Q_BLOCK):
        q0 = blk * Q_BLOCK
        kend = q0 + Q_BLOCK
        qb = q[:, q0:kend]
        s = jnp.einsum('bqhd,bkhd->bhqk', qb, k[:, :kend]).astype(jnp.float32) * scale
        mask = jnp.arange(kend)[None, :] <= (q0 + jnp.arange(Q_BLOCK))[:, None]
        s = jnp.where(mask, s, jnp.float32(-1e30))
        prob = jax.nn.softmax(s, axis=-1).astype(v.dtype)
        outs.append(jnp.einsum('bhqk,bkhd->bqhd', prob, v[:, :kend]))
    return jnp.concatenate(outs, axis=1)


def mla_branch(z, cos, sin, g_q_a, w_uq, g_kv_a, w_ukv, w_o_mla):
    B, S, _ = z.shape
    c_q, c_kv, k_pe = jnp.split(z, [Q_LORA_RANK, Q_LORA_RANK + KV_LORA_RANK], axis=-1)
    q = (rmsnorm(c_q, g_q_a) @ w_uq).reshape(B, S, MLA_HEADS, QK_HEAD_DIM)
    q_nope, q_pe = jnp.split(q, [QK_NOPE_DIM], axis=-1)
    q_pe = apply_rope(q_pe, cos[:, :, None, :], sin[:, :, None, :])
    kv = (rmsnorm(c_kv, g_kv_a) @ w_ukv).reshape(B, S, MLA_HEADS, QK_NOPE_DIM + V_HEAD_DIM)
    k_nope, v = jnp.split(kv, [QK_NOPE_DIM], axis=-1)
    k_pe = apply_rope(k_pe, cos, sin)
    qh = jnp.concatenate([q_nope, q_pe], axis=-1)
    kh = jnp.concatenate([k_nope, jnp.broadcast_to(k_pe[:, :, None, :], (B, S, MLA_HEADS, QK_ROPE_DIM))], axis=-1)
    o = causal_block_attention(qh, kh, v, QK_HEAD_DIM ** -0.5)
    return o.reshape(B, S, MLA_HEADS * V_HEAD_DIM) @ w_o_mla


def token_shift(z, mu):
    prev = jnp.pad(z, ((0, 0), (1, 0), (0, 0)))[:, :-1]
    return z + (prev - z) * mu


def wkv7_scan(r, decay, k, v, a_vec, b_vec):
    B, S, H, N = r.shape

    def step(state, inp):
        r_t, d_t, k_t, v_t, a_t, b_t = inp
        sa = jnp.einsum('bhvk,bhk->bhv', state, a_t)
        state = (state * d_t[:, :, None, :] + sa[..., None] * b_t[:, :, None, :]
                 + v_t[..., None] * k_t[:, :, None, :])
        return state, jnp.einsum('bhvk,bhk->bhv', state, r_t)

    xs = tuple(jnp.moveaxis(t, 1, 0) for t in (r, decay, k, v, a_vec, b_vec))
    _, ys = lax.scan(step, jnp.zeros((B, H, N, N), jnp.float32), xs)
    return jnp.moveaxis(ys, 0, 1)


def rwkv7_branch(z, mu_rwkv, w0, w2, a0, a2, g2, k_k, k_a, r_k, ln_x_w, ln_x_b, w_o_rwkv):
    B, S, _ = z.shape
    f32 = jnp.float32
    z = token_shift(z, mu_rwkv)
    r, k, v, w_lo, a_lo, g_lo = jnp.split(
        z, [RWKV_DIM, 2 * RWKV_DIM, 3 * RWKV_DIM, 3 * RWKV_DIM + DECAY_LORA,
            3 * RWKV_DIM + DECAY_LORA + AAA_LORA], axis=-1)
    w_log = -jax.nn.softplus(-(w0.astype(f32) + (jnp.tanh(w_lo) @ w2).astype(f32))) - 0.5
    decay = jnp.exp(-jnp.exp(w_log))
    a = jax.nn.sigmoid(a0.astype(f32) + (a_lo @ a2).astype(f32))
    g = jax.nn.sigmoid(g_lo) @ g2
    hs = (B, S, RWKV_HEADS, RWKV_HEAD_DIM)
    r = r.astype(f32).reshape(hs)
    k = k.astype(f32)
    v = v.astype(f32).reshape(hs)
    kk = (k * k_k.astype(f32)).reshape(hs)
    kk = kk / jnp.maximum(jnp.linalg.norm(kk, axis=-1, keepdims=True), 1e-12)
    k = (k * (1.0 + (a - 1.0) * k_a.astype(f32))).reshape(hs)
    a = a.reshape(hs)
    decay = decay.reshape(hs)
    y = wkv7_scan(r, decay, k, v, -kk, kk * a)
    mean = jnp.mean(y, axis=-1, keepdims=True)
    var = jnp.mean(jnp.square(y - mean), axis=-1, keepdims=True)
    y = ((y - mean) * lax.rsqrt(var + GN_EPS)).reshape(B, S, RWKV_DIM)
    y = y * ln_x_w.astype(f32) + ln_x_b.astype(f32)
    bonus = jnp.sum(r * k * r_k.astype(f32), axis=-1, keepdims=True) * v
    y = (y + bonus.reshape(B, S, RWKV_DIM)).astype(z.dtype) * g
    return y @ w_o_rwkv


def setup_inputs(seed: int = 0) -> dict:
    key = jax.random.key(seed)
    ks = jax.random.split(key, 40)
    f32 = jnp.float32
    L = DEPTH

    def nrm(k, shape, fan_in):
        return jax.random.normal(k, shape, f32) * (fan_in ** -0.5)

    def gain(k, shape):
        return 1.0 + 0.02 * jax.random.normal(k, shape, f32)

    H, N = RWKV_HEADS, RWKV_HEAD_DIM
    return {
        "x": jax.random.normal(ks[0], (BATCH, SEQ, D_MODEL), f32),
        "p": jax.random.normal(ks[1], (DEPTH, BATCH, SEQ, PLE_DIM), f32),
        "positions": (jax.random.randint(ks[2], (BATCH, 1), 0, 1024, jnp.int32)
                      + jnp.arange(SEQ, dtype=jnp.int32)[None, :]),
        "g_mix": gain(ks[3], (L, D_MODEL)),
        "w_in": nrm(ks[4], (L, D_MODEL, IN_COLS), D_MODEL),
        "g_q_a": gain(ks[5], (L, Q_LORA_RANK)),
        "w_uq": nrm(ks[6], (L, Q_LORA_RANK, MLA_HEADS * QK_HEAD_DIM), Q_LORA_RANK),
        "g_kv_a": gain(ks[7], (L, KV_LORA_RANK)),
        "w_ukv": nrm(ks[8], (L, KV_LORA_RANK, MLA_HEADS * (QK_NOPE_DIM + V_HEAD_DIM)), KV_LORA_RANK),
        "w_o_mla": nrm(ks[9], (L, MLA_HEADS * V_HEAD_DIM, D_MODEL), MLA_HEADS * V_HEAD_DIM),
        "mu_rwkv": jax.random.uniform(ks[10], (L, RWKV_COLS), f32),
        "w0": jax.random.uniform(ks[11], (L, RWKV_DIM), f32, -6.5, -1.5),
        "w2": nrm(ks[12], (L, DECAY_LORA, RWKV_DIM), DECAY_LORA),
        "a0": 0.1 * jax.random.normal(ks[13], (L, RWKV_DIM), f32),
        "a2": nrm(ks[14], (L, AAA_LORA, RWKV_DIM), AAA_LORA),
        "g2": nrm(ks[15], (L, GATE_LORA, RWKV_DIM), GATE_LORA),
        "k_k": 0.85 + 0.02 * jax.random.normal(ks[16], (L, RWKV_DIM), f32),
        "k_a": gain(ks[17], (L, RWKV_DIM)),
        "r_k": 0.1 * jax.random.normal(ks[18], (L, H, N), f32),
        "ln_x_w": gain(ks[19], (L, RWKV_DIM)),
        "ln_x_b": 0.02 * jax.random.normal(ks[20], (L, RWKV_DIM), f32),
        "w_o_rwkv": nrm(ks[21], (L, RWKV_DIM, D_MODEL), RWKV_DIM),
        "w_out": nrm(ks[22], (L, D_MODEL, D_MODEL), D_MODEL),
        "g_ffn": gain(ks[23], (L, D_MODEL)),
        "w_ffn_up": nrm(ks[24], (L, D_MODEL, D_FF), D_MODEL),
        "w_ffn_down": nrm(ks[25], (L, D_FF, D_MODEL), D_FF),
        "g_ple": gain(ks[26], (L, D_MODEL)),
        "w_ple_gate": nrm(ks[27], (L, D_MODEL, D_MODEL), D_MODEL),
        "w_ple_proj": nrm(ks[28], (L, PLE_DIM, D_MODEL), PLE_DIM),
        "g_final": gain(ks[29], (D_MODEL,)),
    }


def reference(x, p, positions, g_mix, w_in, g_q_a, w_uq, g_kv_a, w_ukv, w_o_mla,
              mu_rwkv, w0, w2, a0, a2, g2, k_k, k_a, r_k, ln_x_w, ln_x_b, w_o_rwkv,
              w_out, g_ffn, w_ffn_up, w_ffn_down, g_ple, w_ple_gate, w_ple_proj, g_final):
    cos, sin = rope_tables(positions)
    for i in range(DEPTH):
        h = rmsnorm(x, g_mix[i])
        z = h @ w_in[i]
        z_mla, z_rwkv, z_gate = jnp.split(z, [MLA_COLS, MLA_COLS + RWKV_COLS], axis=-1)
        y_a = mla_branch(z_mla, cos, sin, g_q_a[i], w_uq[i], g_kv_a[i], w_ukv[i], w_o_mla[i])
        y_b = rwkv7_branch(z_rwkv, mu_rwkv[i], w0[i], w2[i], a0[i], a2[i], g2[i], k_k[i], k_a[i],
                           r_k[i], ln_x_w[i], ln_x_b[i], w_o_rwkv[i])
        gate = jax.nn.sigmoid(z_gate)
        gate_a, gate_b = jnp.split(gate, 2, axis=-1)
        x = x + (gate_a * y_a + gate_b * y_b) @ w_out[i]
        h = rmsnorm(x, g_ffn[i])
        x = x + jnp.square(jax.nn.relu(h @ w_ffn_up[i])) @ w_ffn_down[i]
        ple_gate = jax.nn.sigmoid(rmsnorm(x, g_ple[i]) @ w_ple_gate[i])
        x = x + ple_gate * (p[i] @ w_ple_proj[i])
    return rmsnorm(x, g_final)
```

```python
from contextlib import ExitStack
import numpy as np
import ml_dtypes
import concourse.bass as bass
import concourse.mybir as mybir
from concourse.bass_utils import run_bass_kernel_spmd

F32 = mybir.dt.float32
BF16 = mybir.dt.bfloat16
I32 = mybir.dt.int32
AF = mybir.ActivationFunctionType
ALU = mybir.AluOpType
AX = mybir.AxisListType

D = 1024
NH = 8
RMS_EPS = 1e-6
GN_EPS = 64 * 1e-5
SCALE = 96 ** -0.5
EXPH = float(np.exp(-0.5))
TWO_PI = 2.0 * np.pi
C1 = 6.28125
C2 = float(TWO_PI - 6.28125)


class Res:
    __slots__ = ("name", "w", "rd")

    def __init__(self, name=""):
        self.name = name
        self.w = None
        self.rd = []


class Chan:
    def __init__(self, sem, name):
        self.sem = sem
        self.count = 0
        self.name = name


class _Eng:
    def __init__(self, name, sem):
        self.name = name
        self.sem = sem
        self.count = 0
        self.ops = []
        self.waited = {}


class Sched:
    ENGS = ("pe", "act", "dve", "pool", "sp")
    HMAP = {"pe": "tensor", "act": "scalar", "dve": "vector", "pool": "gpsimd", "sp": "sync"}

    def __init__(self, nc, stack, n_chan=90):
        self.nc = nc
        self.e = {}
        for n in self.ENGS:
            self.e[n] = _Eng(n, stack.enter_context(nc.semaphore("s_" + n)))
        self.chans = [Chan(stack.enter_context(nc.semaphore("c%d" % i)), "c%d" % i) for i in range(n_chan)]
        self.chan_i = 4
        self.nops = 0
        self.misc_i = 0

    def misc(self, q="sp"):
        base = 0 if q == "sp" else 2
        c = self.chans[base + self.misc_i % 2]
        self.misc_i += 1
        return c

    def chan(self):
        c = self.chans[self.chan_i]
        self.chan_i += 1
        return c

    def _need(self, eng, reads, writes):
        E = self.e[eng]
        need = {}

        def add(t):
            if t is None:
                return
            key, val = t
            if key is E and eng == "pe":
                return
            if need.get(key, 0) < val:
                need[key] = val

        for r in reads:
            add(r.w)
        for w in writes:
            add(w.w)
            for t in w.rd:
                add(t)
        for key, val in need.items():
            if E.waited.get(key, 0) < val:
                E.waited[key] = val
                E.ops.append(("wait", key.sem, val))

    def op(self, eng, fn, reads=(), writes=(), inc=True):
        E = self.e[eng]
        self._need(eng, reads, writes)
        if inc:
            E.count += 1
            t = (E, E.count)
        else:
            t = (E, E.count + 1)
        E.ops.append(("op", fn, inc))
        for r in reads:
            r.rd.append(t)
            if len(r.rd) > 64:
                r.rd = _compress(r.rd)
        for w in writes:
            w.w = t
            w.rd = []
        self.nops += 1
        return t

    def dma(self, chan, pairs, reads=(), writes=(), q="sp"):
        E = self.e[q]
        if chan.count > 0 and E.waited.get(chan, 0) < chan.count:
            E.waited[chan] = chan.count
            E.ops.append(("wait", chan.sem, chan.count))
        self._need(q, reads, writes)
        for (o, i) in pairs:
            chan.count += 16
            E.ops.append(("dma", o, i, chan.sem))
        t = (chan, chan.count)
        for r in reads:
            r.rd.append(t)
        for w in writes:
            w.w = t
            w.rd = []
        return t

    def barrier(self):
        for n in self.ENGS:
            E = self.e[n]
            for m in self.ENGS:
                O = self.e[m]
                if O is E or O.count == 0:
                    continue
                if E.waited.get(O, 0) < O.count:
                    E.waited[O] = O.count
                    E.ops.append(("wait", O.sem, O.count))
            for c in self.chans:
                if c.count and E.waited.get(c, 0) < c.count:
                    E.waited[c] = c.count
                    E.ops.append(("wait", c.sem, c.count))

    def emit(self):
        nc = self.nc
        with nc.Block() as block:
            for n in self.ENGS:
                E = self.e[n]

                def body(h, E=E):
                    for o in E.ops:
                        if o[0] == "wait":
                            h.wait_ge(o[1], o[2])
                        elif o[0] == "op":
                            ins = o[1](h)
                            if o[2]:
                                ins.then_inc(E.sem, 1)
                        else:
                            h.dma_start(out=o[1], in_=o[2]).then_inc(o[3], 16)

                getattr(block, self.HMAP[n])(body)


def _compress(tickets):
    best = {}
    for key, val in tickets:
        if best.get(key, 0) < val:
            best[key] = val
    return list(best.items())


class Ring:
    def __init__(self, items):
        self.items = items
        self.i = 0

    def next(self):
        it = self.items[self.i % len(self.items)]
        self.i += 1
        return it


VEC_LAYOUT = [("g_mix", 8), ("g_q_a", 3), ("g_kv_a", 2), ("mu", 14), ("w0", 4), ("a0", 4), ("k_k", 4),
              ("k_a", 4), ("r_k", 4), ("ln_w", 4), ("ln_b", 4), ("g_ffn", 8), ("g_ple", 8), ("g_final", 8)]
VEC_OFF = {}
_o = 0
for _n, _c in VEC_LAYOUT:
    VEC_OFF[_n] = _o
    _o += _c
NVEC = _o
OM_OFF = NVEC
OMKA_OFF = NVEC + 14
NVEC_TOT = NVEC + 18


class _Stop(Exception):
    pass


class B:
    def __init__(self, TP, TO, debug=False, upto=5, p2_stop=0):
        self.p2_stop = p2_stop
        self.debug = debug
        self.upto = upto
        self.TP, self.TO = TP, TO
        self.TT = TP + TO
        self.nc = bass.Bass("TRN2", target_bir_lowering=False)
        self.st = ExitStack()
        self.S = None

    def din(self, name, shape, dt=F32):
        return self.nc.dram_tensor(name, list(shape), dt, kind="ExternalInput").ap()

    def dscr(self, name, shape, dt=BF16):
        kind = "ExternalOutput" if self.debug else "Internal"
        return self.nc.dram_tensor(name, list(shape), dt, kind=kind).ap()

    def sb(self, st, name, shape, dt=F32):
        self._uid = getattr(self, "_uid", 0) + 1
        return st.enter_context(self.nc.sbuf_tensor("s%d_%s" % (self._uid, name), list(shape), dt))

    def ring(self, st, name, shape, dt, n):
        return Ring([(self.sb(st, "%s%d" % (name, i), shape, dt), Res("%s%d" % (name, i))) for i in range(n)])

    def mm(self, out, lhsT, rhs, reads, writes, start=True, stop=True, inc=None):
        self.S.op("pe", lambda e: e.matmul(out, lhsT, rhs, start=start, stop=stop), reads, writes, inc=(stop if inc is None else inc))

    def tr(self, out, in_, ident, reads, writes, inc=True):
        self.S.op("pe", lambda e: e.transpose(out, in_, ident), reads, writes, inc=inc)

    def act(self, out, in_, func, reads, writes, bias=0.0, scale=1.0):
        if func == AF.Copy and not (isinstance(bias, float) and isinstance(scale, float)):
            func = AF.Identity
        self.S.op("act", lambda e: e.activation(out=out, in_=in_, func=func, bias=bias, scale=scale), reads, writes)

    def tt(self, eng, out, in0, in1, op, reads, writes):
        self.S.op(eng, lambda e: e.tensor_tensor(out=out, in0=in0, in1=in1, op=op), reads, writes)

    def ts(self, eng, out, in0, s1, op0, reads, writes, s2=None, op1=None):
        if op1 is None:
            self.S.op(eng, lambda e: e.tensor_scalar(out=out, in0=in0, scalar1=s1, scalar2=None, op0=op0), reads, writes)
        else:
            self.S.op(eng, lambda e: e.tensor_scalar(out=out, in0=in0, scalar1=s1, scalar2=s2, op0=op0, op1=op1), reads, writes)

    def stt(self, out, in0, scalar, in1, op0, op1, reads, writes):
        self.S.op("dve", lambda e: e.scalar_tensor_tensor(out=out, in0=in0, scalar=scalar, in1=in1, op0=op0, op1=op1), reads, writes)

    def cp(self, eng, out, in_, reads, writes):
        if eng == "act":
            self.act(out, in_, AF.Copy, reads, writes)
        else:
            self.S.op(eng, lambda e: e.tensor_copy(out=out, in_=in_), reads, writes)

    def ckpt(self, n):
        if self.p2_stop == n:
            self.S.barrier()
            self.S.emit()
            raise _Stop()

    def memset(self, eng, ap, val, writes):
        self.S.op(eng, lambda e: e.memset(ap, val), (), writes)

    def recip(self, out, in_, reads, writes):
        self.S.op("dve", lambda e: e.reciprocal(out=out, in_=in_), reads, writes)

    def build(self):
        try:
            self._build()
        except _Stop:
            pass
        return self.nc

    def _build(self):
        nc, TP, TO, TT = self.nc, self.TP, self.TO, self.TT
        NT, NTP, NTO = TT // 512, TP // 512, TO // 512
        NB = TT // 128
        xT = self.din("xT", [D, TT])
        pT = self.din("pT", [256, TO])
        pos = self.din("pos", [1, TT], I32)
        maskrow = self.din("maskrow", [1, TT], BF16)
        vecs_d = self.din("vecs", [128, NVEC])
        rc_d = self.din("ropec", [128, 2])
        w_in = self.din("w_in", [D, 4512])
        w_kpe = self.din("w_kpe", [D, 192])
        wq_d = self.din("wq", [384, 768])
        wqs_d = self.din("wq_sw", [384, 768])
        wkk_d = self.din("wukv_k", [256, 512])
        wkv_d = self.din("wukv_v", [256, 512])
        womla_d = self.din("w_o_mla", [512, D])
        w2_d = self.din("w2", [64, 512])
        a2_d = self.din("a2", [64, 512])
        g2_d = self.din("g2", [128, 512])
        worw_d = self.din("w_o_rwkv", [512, D])
        wout_d = self.din("w_out", [D, D])
        wup_d = self.din("w_up", [D, 4096])
        wdn_d = self.din("w_down", [4096, D])
        wpg_d = self.din("w_pg", [D, D])
        wpp_d = self.din("w_pp", [256, D])
        cm_ident_d = self.din("c_ident", [128, 128])
        cm_mask4_d = self.din("c_mask4", [128, 512])
        cm_maskL_d = self.din("c_maskL", [128, 128])
        cm_tri_d = self.din("c_tri", [128, 128])
        cm_bd_d = self.din("c_bd", [128, 128])
        cm_scan_d = self.din("c_scan", [128, 512])
        outT = nc.dram_tensor("outT", [D, TO], F32, kind="ExternalOutput").ap()
        QT = self.dscr("QT", [NH, 97, TO])
        KnT = self.dscr("KnT", [512, TT])
        KpeT = self.dscr("KpeT", [33, TT])
        Vs = self.dscr("Vs", [128, NB, 520])
        GT = self.dscr("GT", [2048, TO])
        ARt = self.dscr("ARt", [512, NB, 256])
        Bt = self.dscr("Bt", [512, TT])
        Kt = self.dscr("Kt", [512, TT])
        Vt = self.dscr("Vt", [512, TT])
        PCt = self.dscr("PCt", [512, TT // 64], F32)
        GrT = self.dscr("GrT", [512, TO])
        BoT = self.dscr("BoT", [512, TO])
        OaT = self.dscr("OaT", [512, TO])
        YbT = self.dscr("YbT", [512, TO])
        wupb = self.dscr("wupb", [D, 4096])
        wdnb = self.dscr("wdnb", [4096, D])
        R_wconv = Res("wconv")
        R_QT, R_KnT, R_KpeT, R_Vs, R_GT = Res("QT"), Res("KnT"), Res("KpeT"), Res("Vs"), Res("GT")
        R_rw, R_OaT, R_YbT = Res("rwscr"), Res("OaT"), Res("YbT")

        with self.st as st0:
            S = self.S = Sched(nc, st0)
            vecs = self.sb(st0, "vecs", [128, NVEC_TOT]); Rvec = Res("vecs")
            ropec = self.sb(st0, "ropec", [128, 2]); Rrc = Res()
            ident = self.sb(st0, "ident", [128, 128]); Rid = Res()
            identb = self.sb(st0, "identb", [128, 128], BF16); Ridb = Res()
            mask4 = self.sb(st0, "mask4", [128, 512]); Rm4 = Res()
            maskL = self.sb(st0, "maskL", [128, 128]); RmL = Res()
            trim = self.sb(st0, "trim", [128, 128], BF16); Rtri = Res()
            bdb = self.sb(st0, "bdb", [128, 128], BF16); Rbd = Res()
            onesb = self.sb(st0, "onesb", [128, 128], BF16); Rones = Res()
            onesf = self.sb(st0, "onesf", [128, 128]); Ronesf = Res()
            scanm = self.sb(st0, "scanm", [128, 512]); Rscan = Res()
            S.dma(S.misc(), [(vecs[:, 0:NVEC], vecs_d[:, :])], writes=[Rvec])
            S.dma(S.misc(), [(ropec[:], rc_d[:, :])], writes=[Rrc])
            S.dma(S.misc(), [(ident[:], cm_ident_d[:, :])], writes=[Rid])
            S.dma(S.misc("pool"), [(identb[:], cm_ident_d[:, :])], writes=[Ridb], q="pool")
            S.dma(S.misc(), [(mask4[:], cm_mask4_d[:, :])], writes=[Rm4])
            S.dma(S.misc(), [(maskL[:], cm_maskL_d[:, :])], writes=[RmL])
            S.dma(S.misc("pool"), [(trim[:], cm_tri_d[:, :])], writes=[Rtri], q="pool")
            S.dma(S.misc("pool"), [(bdb[:], cm_bd_d[:, :])], writes=[Rbd], q="pool")
            S.dma(S.misc(), [(scanm[:], cm_scan_d[:, :])], writes=[Rscan])
            self.memset("pool", onesb[:], 1.0, [Rones])
            self.memset("pool", onesf[:], 1.0, [Ronesf])
            self.ts("dve", vecs[:, OM_OFF:OM_OFF + 14], vecs[:, VEC_OFF["mu"]:VEC_OFF["mu"] + 14], -1.0, ALU.mult,
                    [Rvec], [Rvec], s2=1.0, op1=ALU.add)
            self.ts("dve", vecs[:, OMKA_OFF:OMKA_OFF + 4], vecs[:, VEC_OFF["k_a"]:VEC_OFF["k_a"] + 4], -1.0, ALU.mult,
                    [Rvec], [Rvec], s2=1.0, op1=ALU.add)
            S.dma(S.misc(), [(KpeT[32:33, :], maskrow[0:1, :])], writes=[R_KpeT])

            def vcol(name, j, p0=0, p1=128):
                o = VEC_OFF[name] + j
                return vecs[p0:p1, o:o + 1]

            banks = [(st0.enter_context(nc.psum_tensor("bank%d" % i, [128, 512], F32)), Res("bank%d" % i)) for i in range(7)]
            bankb = (st0.enter_context(nc.psum_tensor("bankb", [128, 1024], BF16)), Res("bankb"))
            consts = [Rvec, Rrc, Rid, Ridb, Rm4, RmL, Rtri, Rbd, Rones, Ronesf, Rscan]

            xT3 = xT.rearrange("(c p) t -> p c t", p=128)
            w_in3 = w_in.rearrange("(c p) n -> p c n", p=128)
            w_kpe3 = w_kpe.rearrange("(c p) n -> p c n", p=128)
            pi = [0]

            def pbank():
                b_ = banks[pi[0] % 7]
                pi[0] += 1
                return b_

            def make_common(st, ncol):
                cm = {}
                cm["win"] = self.sb(st, "win", [128, 8, ncol], BF16)
                cm["Rwin"] = Res()
                cm["xr"] = self.ring(st, "x1_", [128, 8, 512], F32, 1)
                cm["xch"] = S.chan()
                cm["sqr"] = self.ring(st, "sq1_", [128, 512], BF16, 4)
                cm["hr"] = self.ring(st, "h1_", [128, 8, 512], BF16, 2)
                cm["rstdr"] = self.ring(st, "rstd1_", [128, 512], F32, 2)
                cm["sqt"] = self.ring(st, "sqt1_", [128, 512], F32, 2)
                return cm

            def rms_stats(cm, src3, nchunk, scale, Rsrc):
                ps, Rps = pbank()
                for c in range(nchunk):
                    sq, Rsq = cm["sqr"].next()
                    self.act(sq[:], src3[:, c, :], AF.Square, [Rsrc], [Rsq])
                    self.mm(ps[:, :], onesb[:, :], sq[:], [Rones, Rsq], [Rps], start=(c == 0), stop=(c == nchunk - 1), inc=True)
                t1, Rt1 = cm["sqt"].next()
                self.act(t1[:], ps[:, :], AF.Sqrt, [Rps], [Rt1], bias=RMS_EPS, scale=scale)
                rs, Rrs = cm["rstdr"].next()
                self.recip(rs[:], t1[:], [Rt1], [Rrs])
                return rs, Rrs

            def load_h(cm, t):
                c0 = t * 512
                xt, Rxt = cm["xr"].next()
                S.dma(cm["xch"], [(xt[:], xT3[:, :, c0:c0 + 512])], writes=[Rxt])
                rs, Rrs = rms_stats(cm, xt, 8, 1.0 / D, Rxt)
                h, Rh = cm["hr"].next()
                for c in range(8):
                    self.stt(h[:, c, :], xt[:, c, :], vcol("g_mix", c), rs[:], ALU.mult, ALU.mult, [Rxt, Rvec, Rrs], [Rh])
                return h, Rh

            def zmm(cm, h, Rh, col0, M):
                ps, Rps = pbank()
                for c in range(8):
                    self.mm(ps[0:M, :], cm["win"][:, c, col0:col0 + M], h[:, c, :], [cm["Rwin"], Rh], [Rps], start=(c == 0), stop=(c == 7))
                return ps, Rps

            with ExitStack() as st:
                if self.upto < 1:
                    raise _Stop()
                NCOL = 384 + 256 + 192 + 2048
                OFF_CQ, OFF_CKV, OFF_KPE, OFF_G = 0, 384, 640, 832
                cm = make_common(st, NCOL)
                win, Rwin = cm["win"], cm["Rwin"]
                for c in range(8):
                    S.dma(S.misc("pool"), [(win[:, c, 0:640], w_in3[:, c, 0:640]),
                                     (win[:, c, 640:832], w_kpe3[:, c, :]),
                                     (win[:, c, 832:NCOL], w_in3[:, c, 2464:4512])], writes=[Rwin], q="pool")
                wq = self.sb(st, "wq", [128, 3, 768], BF16)
                wqs = self.sb(st, "wqs", [128, 3, 768], BF16)
                wkk = self.sb(st, "wkk", [128, 2, 512], BF16)
                wkv = self.sb(st, "wkv", [128, 2, 512], BF16)
                Rw1 = Res("w1")
                S.dma(S.misc("pool"), [(wq[:], wq_d.rearrange("(c p) n -> p c n", p=128)),
                                 (wqs[:], wqs_d.rearrange("(c p) n -> p c n", p=128)),
                                 (wkk[:], wkk_d.rearrange("(c p) n -> p c n", p=128)),
                                 (wkv[:], wkv_d.rearrange("(c p) n -> p c n", p=128))],
                      writes=[Rw1], q="pool")
                tmpr = self.ring(st, "tmp1_", [128, 512], F32, 4)
                cq = self.sb(st, "cq", [128, 3, 512]); Rcq = Res()
                cqn = self.sb(st, "cqn", [128, 3, 512], BF16); Rcqn = Res()
                ckv = self.sb(st, "ckv", [128, 2, 512]); Rckv = Res()
                ckvn = self.sb(st, "ckvn", [128, 2, 512], BF16); Rckvn = Res()
                qst = self.ring(st, "qst", [128, 512], BF16, 3)
                qch = [S.chan() for _ in range(3)]
                for (t_, r_) in qst.items:
                    self.memset("pool", t_[64:97, :], 1.0, [r_])
                knst = self.ring(st, "knst", [128, 512], BF16, 2)
                knch = [S.chan() for _ in range(2)]
                vst = self.ring(st, "vst", [128, 4, 520], BF16, 2)
                vch = [S.chan() for _ in range(2)]
                for (t_, r_) in vst.items:
                    self.memset("pool", t_[:], 1.0, [r_])
                kpst = self.ring(st, "kpst", [128, 512], BF16, 2)
                kpch = [S.chan() for _ in range(2)]
                gst = self.ring(st, "gst", [128, 4, 512], BF16, 2)
                gch = [S.chan() for _ in range(2)]
                posi = self.sb(st, "posi", [128, 512], I32); Rposi = Res()
                posch = S.chan()
                rp = [self.sb(st, "rp%d" % i, [128, 512]) for i in range(6)]
                Rrp = [Res() for _ in range(6)]
                ki = self.sb(st, "ki", [128, 512], I32); Rki = Res()
                sl = slice(64, 96)
                for t in range(NT):
                    own = t >= NTP
                    to = t - NTP
                    c0 = t * 512
                    h, Rh = load_h(cm, t)
                    S.dma(posch, [(posi[64:96, :], pos[0:1, c0:c0 + 512].partition_broadcast(32))], writes=[Rposi])
                    ang, sinT, cosT, sinQ, cosQ, rr = rp
                    Rang, RsinT, RcosT, RsinQ, RcosQ, Rrr = Rrp
                    self.cp("dve", ang[sl, :], posi[sl, :], [Rposi], [Rang])
                    self.ts("dve", ang[sl, :], ang[sl, :], ropec[sl, 0:1], ALU.mult, [Rang, Rrc], [Rang])
                    self.ts("dve", rr[sl, :], ang[sl, :], float(1.0 / TWO_PI), ALU.mult, [Rang], [Rrr])
                    self.cp("dve", ki[sl, :], rr[sl, :], [Rrr], [Rki])
                    self.cp("dve", rr[sl, :], ki[sl, :], [Rki], [Rrr])
                    self.stt(ang[sl, :], rr[sl, :], -C1, ang[sl, :], ALU.mult, ALU.add, [Rrr, Rang], [Rang])
                    self.stt(ang[sl, :], rr[sl, :], -C2, ang[sl, :], ALU.mult, ALU.add, [Rrr, Rang], [Rang])
                    self.ts("dve", ang[sl, :], ang[sl, :], float(np.pi), ALU.min, [Rang], [Rang], s2=float(-np.pi), op1=ALU.max)
                    self.act(sinT[sl, :], ang[sl, :], AF.Sin, [Rang, Rrc], [RsinT], scale=ropec[sl, 1:2])
                    self.act(rr[sl, :], ang[sl, :], AF.Abs, [Rang], [Rrr])
                    self.ts("dve", rr[sl, :], rr[sl, :], -1.0, ALU.mult, [Rrr], [Rrr], s2=float(np.pi / 2), op1=ALU.add)
                    self.act(cosT[sl, :], rr[sl, :], AF.Sin, [Rrr], [RcosT])
                    if own:
                        self.ts("pool", sinQ[sl, :], sinT[sl, :], SCALE, ALU.mult, [RsinT], [RsinQ])
                        self.ts("pool", cosQ[sl, :], cosT[sl, :], SCALE, ALU.mult, [RcosT], [RcosQ])
                    psA, RpsA = zmm(cm, h, Rh, OFF_KPE, 96)
                    psB, RpsB = zmm(cm, h, Rh, OFF_KPE + 96, 96)
                    ta, Rta = tmpr.next()
                    tb, Rtb = tmpr.next()
                    self.tt("dve", ta[sl, :], psA[sl, :], cosT[sl, :], ALU.mult, [RpsA, RcosT], [Rta])
                    self.tt("dve", tb[sl, :], psB[sl, :], sinT[sl, :], ALU.mult, [RpsB, RsinT], [Rtb])
                    kp, Rkp = kpst.next()
                    self.tt("pool", kp[sl, :], ta[sl, :], tb[sl, :], ALU.add, [Rta, Rtb], [Rkp])
                    S.dma(kpch[t % 2], [(KpeT[0:32, c0:c0 + 512], kp[sl, :])], reads=[Rkp], writes=[R_KpeT])
                    for m in range(2):
                        ps, Rps = zmm(cm, h, Rh, OFF_CKV + m * 128, 128)
                        self.cp("act", ckv[:, m, :], ps[:, :], [Rps], [Rckv])
                    rs2, Rrs2 = rms_stats(cm, ckv, 2, 1.0 / 256, Rckv)
                    for m in range(2):
                        self.stt(ckvn[:, m, :], ckv[:, m, :], vcol("g_kv_a", m), rs2[:], ALU.mult, ALU.mult, [Rckv, Rvec, Rrs2], [Rckvn])
                    for m in range(4):
                        ps, Rps = pbank()
                        for c in range(2):
                            self.mm(ps[:, :], wkk[:, c, m * 128:(m + 1) * 128], ckvn[:, c, :], [Rw1, Rckvn], [Rps], start=(c == 0), stop=(c == 1))
                        kn, Rkn = knst.next()
                        kk_ = (knst.i - 1) % 2
                        self.cp("act", kn[:], ps[:, :], [Rps], [Rkn])
                        S.dma(knch[kk_], [(KnT[m * 128:(m + 1) * 128, c0:c0 + 512], kn[:])], reads=[Rkn], writes=[R_KnT])
                    vt_, Rvt = vst.next()
                    vk = (vst.i - 1) % 2
                    for s_ in range(4):
                        ps, Rps = pbank()
                        for c in range(2):
                            self.mm(ps[:, :], ckvn[:, c, s_ * 128:(s_ + 1) * 128], wkv[:, c, :], [Rckvn, Rw1], [Rps], start=(c == 0), stop=(c == 1))
                        v4 = vt_[:, s_, :].rearrange("p (h d) -> p h d", d=65)
                        self.cp("act", v4[:, :, 0:64], ps[:, :].rearrange("p (h d) -> p h d", d=64), [Rps], [Rvt])
                    S.dma(vch[vk], [(Vs[:, t * 4:(t + 1) * 4, :], vt_[:])], reads=[Rvt], writes=[R_Vs])
                    if own:
                        o0 = to * 512
                        for m in range(3):
                            ps, Rps = zmm(cm, h, Rh, OFF_CQ + m * 128, 128)
                            self.cp("act", cq[:, m, :], ps[:, :], [Rps], [Rcq])
                        rs3, Rrs3 = rms_stats(cm, cq, 3, 1.0 / 384, Rcq)
                        for m in range(3):
                            self.stt(cqn[:, m, :], cq[:, m, :], vcol("g_q_a", m), rs3[:], ALU.mult, ALU.mult, [Rcq, Rvec, Rrs3], [Rcqn])
                        for hd in range(NH):
                            psA, RpsA = pbank()
                            psB, RpsB = pbank()
                            for c in range(3):
                                self.mm(psA[0:96, :], wq[:, c, hd * 96:(hd + 1) * 96], cqn[:, c, :], [Rw1, Rcqn], [RpsA], start=(c == 0), stop=(c == 2))
                            for c in range(3):
                                self.mm(psB[0:96, :], wqs[:, c, hd * 96:(hd + 1) * 96], cqn[:, c, :], [Rw1, Rcqn], [RpsB], start=(c == 0), stop=(c == 2))
                            q_, Rq_ = qst.next()
                            qk = (qst.i - 1) % 3
                            self.act(q_[0:64, :], psA[0:64, :], AF.Copy, [RpsA], [Rq_], scale=SCALE)
                            ta, Rta = tmpr.next()
                            tb, Rtb = tmpr.next()
                            self.tt("dve", ta[sl, :], psA[sl, :], cosQ[sl, :], ALU.mult, [RpsA, RcosQ], [Rta])
                            self.tt("dve", tb[sl, :], psB[sl, :], sinQ[sl, :], ALU.mult, [RpsB, RsinQ], [Rtb])
                            self.tt("pool", q_[sl, :], ta[sl, :], tb[sl, :], ALU.add, [Rta, Rtb], [Rq_])
                            S.dma(qch[qk], [(QT[hd, :, o0:o0 + 512], q_[0:97, :])], reads=[Rq_], writes=[R_QT])
                        for gq in range(4):
                            g_, Rg_ = gst.next()
                            gk = (gst.i - 1) % 2
                            for j in range(4):
                                ps, Rps = zmm(cm, h, Rh, OFF_G + (gq * 4 + j) * 128, 128)
                                self.act(g_[:, j, :], ps[:, :], AF.Sigmoid, [Rps], [Rg_])
                            S.dma(gch[gk], [(GT[gq * 512:(gq + 1) * 512, o0:o0 + 512].rearrange("(j p) t -> p j t", p=128), g_[:])],
                                  reads=[Rg_], writes=[R_GT])
                S.barrier()
                S.emit()
                for n_ in S.ENGS:
                    S.e[n_].ops = []

            with ExitStack() as st:
                if self.upto < 2:
                    raise _Stop()
                cm = make_common(st, 1792)
                win, Rwin = cm["win"], cm["Rwin"]
                for c in range(8):
                    S.dma(S.misc("pool"), [(win[:, c, :], w_in3[:, c, 672:2464])], writes=[Rwin], q="pool")
                w2s = self.sb(st, "w2s", [128, 512], BF16)
                a2s = self.sb(st, "a2s", [128, 512], BF16)
                g2s = self.sb(st, "g2s", [128, 512], BF16)
                Rw1 = Res("w1b")
                S.dma(S.misc("pool"), [(w2s[0:64, :], w2_d[:, :]), (a2s[64:128, :], a2_d[:, :]), (g2s[:], g2_d[:, :])], writes=[Rw1], q="pool")
                tmpr = self.ring(st, "tmp1b_", [128, 512], F32, 3)
                tmpb = self.ring(st, "tmpb1_", [128, 512], BF16, 3)
                zcw = self.ring(st, "zcw", [128, 513], F32, 3)
                carry = self.sb(st, "carry", [128, 16]); Rcar = Res()
                self.memset("pool", carry[:], 0.0, [Rcar])
                zsr = self.ring(st, "zsr", [128, 512], F32, 2)
                zsk = self.ring(st, "zsk", [128, 512], F32, 2)
                zsv = self.ring(st, "zsv", [128, 512], F32, 2)
                zs12 = self.sb(st, "zs12", [128, 512]); Rzs12 = Res()
                zs13 = self.sb(st, "zs13", [128, 512]); Rzs13 = Res()
                names = ["sig", "av", "Lc", "Lx", "kk", "nr", "tk", "EP", "EN"]
                nb = {n_: (self.sb(st, "rb_" + n_, [128, 512]), Res(n_)) for n_ in names}
                twb = self.sb(st, "twb", [128, 512], BF16); Rtwb = Res()
                gsb = self.sb(st, "gsb", [128, 512], BF16); Rgsb = Res()
                rwst = self.ring(st, "rwst", [128, 512], BF16, 8)
                rwch = [S.chan() for _ in range(8)]
                rwi = [0]
                pcst = self.ring(st, "pcst", [128, 8], F32, 4)
                pcch = [S.chan() for _ in range(4)]

                def rw_store(dst_ap, src_fn, eng_fn):
                    k = rwi[0] % 8
                    rwi[0] += 1
                    t_, r_ = rwst.items[k]
                    eng_fn(t_, r_)
                    S.dma(rwch[k], [(dst_ap, src_fn(t_))], reads=[r_], writes=[R_rw])

                MU, OM = VEC_OFF["mu"], OM_OFF

                def shift(cm, h, Rh, m, dst, Rdst):
                    ps, Rps = zmm(cm, h, Rh, m * 128, 128)
                    zc, Rzc = zcw.next()
                    self.act(zc[:, 1:513], ps[:, :], AF.Copy, [Rps, Rvec], [Rzc], scale=vecs[:, MU + m:MU + m + 1])
                    self.cp("pool", zc[:, 0:1], carry[:, m:m + 1], [Rcar], [Rzc])
                    self.stt(dst[:], ps[:, :], vecs[:, OM + m:OM + m + 1], zc[:, 0:512], ALU.mult, ALU.add, [Rps, Rvec, Rzc], [Rdst])
                    self.cp("pool", carry[:, m:m + 1], zc[:, 512:513], [Rzc], [Rcar])

                for t in range(NT):
                    own = t >= NTP
                    o0 = (t - NTP) * 512
                    c0 = t * 512
                    h, Rh = load_h(cm, t)
                    shift(cm, h, Rh, 12, zs12, Rzs12)
                    shift(cm, h, Rh, 13, zs13, Rzs13)
                    self.act(twb[0:64, :], zs12[0:64, :], AF.Tanh, [Rzs12], [Rtwb])
                    self.cp("pool", twb[64:128, :], zs12[64:128, :], [Rzs12], [Rtwb])
                    self.act(gsb[:], zs13[:], AF.Sigmoid, [Rzs13], [Rgsb])
                    for m in range(4):
                        r_m, Rr = zsr.next(); k_m, Rk = zsk.next(); v_m, Rv = zsv.next()
                        shift(cm, h, Rh, m, r_m, Rr)
                        shift(cm, h, Rh, 4 + m, k_m, Rk)
                        shift(cm, h, Rh, 8 + m, v_m, Rv)
                        ms = slice(m * 128, (m + 1) * 128)
                        (sig, Rsig), (av, Rav), (Lc, RLc), (Lx, RLx), (kk, Rkk), (nr, Rnr), (tk, Rtk), (EP, REP), (EN, REN) = [nb[n_] for n_ in names]
                        ps, Rps = pbank()
                        self.mm(ps[:, :], w2s[0:64, ms], twb[0:64, :], [Rw1, Rtwb], [Rps])
                        self.act(sig[:], ps[:, :], AF.Sigmoid, [Rps, Rvec], [Rsig], bias=vcol("w0", m))
                        ps, Rps = pbank()
                        self.mm(ps[:, :], a2s[64:128, ms], twb[64:128, :], [Rw1, Rtwb], [Rps])
                        self.act(av[:], ps[:, :], AF.Sigmoid, [Rps, Rvec], [Rav], bias=vcol("a0", m))
                        S.op("dve", lambda e, o=Lc, d=sig: e.tensor_tensor_scan(out=o[:], data0=scanm[:], data1=d[:], initial=0.0,
                                                                              op0=ALU.mult, op1=ALU.add), [Rscan, Rsig], [RLc])
                        self.act(kk[:], k_m[:], AF.Copy, [Rk, Rvec], [Rkk], scale=vcol("k_k", m))
                        kk2, Rkk2 = tmpb.next()
                        self.act(kk2[:], k_m[:], AF.Square, [Rk, Rvec], [Rkk2], scale=vcol("k_k", m))
                        ps, Rps = pbank()
                        self.mm(ps[:, :], bdb[:, :], kk2[:], [Rbd, Rkk2], [Rps])
                        self.act(nr[:], ps[:, :], AF.Sqrt, [Rps], [Rnr])
                        self.ts("dve", nr[:], nr[:], 1e-12, ALU.max, [Rnr], [Rnr])
                        self.recip(nr[:], nr[:], [Rnr], [Rnr])
                        self.tt("dve", kk[:], kk[:], nr[:], ALU.mult, [Rkk, Rnr], [Rkk])
                        self.ts("dve", tk[:], av[:], vcol("k_a", m), ALU.mult, [Rav, Rvec], [Rtk],
                                s2=vecs[:, OMKA_OFF + m:OMKA_OFF + m + 1], op1=ALU.add)
                        self.tt("dve", tk[:], tk[:], k_m[:], ALU.mult, [Rtk, Rk], [Rtk])
                        self.act(EP[:], Lc[:], AF.Exp, [RLc], [REP], scale=-EXPH)
                        self.act(EN[:], Lc[:], AF.Exp, [RLc], [REN], scale=EXPH)
                        if own:
                            rk, Rrk = tmpb.next()
                            self.stt(rk[:], r_m[:], vcol("r_k", m), tk[:], ALU.mult, ALU.mult, [Rr, Rvec, Rtk], [Rrk])
                            psb, Rpsb = pbank()
                            self.mm(psb[:, :], bdb[:, :], rk[:], [Rbd, Rrk], [Rpsb])
                            rw_store(BoT[ms, o0:o0 + 512], lambda t_: t_[:],
                                     lambda t_, r_: self.tt("dve", t_[:], psb[:, :], v_m[:], ALU.mult, [Rpsb, Rv], [r_]))
                            psg, Rpsg = pbank()
                            self.mm(psg[:, :], g2s[:, ms], gsb[:], [Rw1, Rgsb], [Rpsg])
                            rw_store(GrT[ms, o0:o0 + 512], lambda t_: t_[:],
                                     lambda t_, r_: self.cp("act", t_[:], psg[:, :], [Rpsg], [r_]))
                        ARv = ARt[ms, t * 4:(t + 1) * 4, :]

                        def a_tilde(t_, r_):
                            self.stt(t_[:, 1:512], kk[:, 1:512], -1.0, EP[:, 0:511], ALU.mult, ALU.mult, [Rkk, REP], [r_])
                            self.ts("pool", t_[:].rearrange("p (c t) -> p c t", t=64)[:, :, 0:1],
                                    kk[:].rearrange("p (c t) -> p c t", t=64)[:, :, 0:1], -1.0, ALU.mult, [Rkk], [r_])

                        rw_store(ARv[:, :, 0:128], lambda t_: t_[:].rearrange("p (b t) -> p b t", t=128), a_tilde)
                        rw_store(ARv[:, :, 128:256], lambda t_: t_[:].rearrange("p (b t) -> p b t", t=128),
                                 lambda t_, r_: self.tt("pool", t_[:], r_m[:], EP[:], ALU.mult, [Rr, REP], [r_]))
                        self.tt("dve", Lx[:], kk[:], av[:], ALU.mult, [Rkk, Rav], [RLx])
                        rw_store(Bt[ms, c0:c0 + 512], lambda t_: t_[:],
                                 lambda t_, r_: self.tt("dve", t_[:], Lx[:], EN[:], ALU.mult, [RLx, REN], [r_]))
                        rw_store(Kt[ms, c0:c0 + 512], lambda t_: t_[:],
                                 lambda t_, r_: self.tt("pool", t_[:], tk[:], EN[:], ALU.mult, [Rtk, REN], [r_]))
                        rw_store(Vt[ms, c0:c0 + 512], lambda t_: t_[:],
                                 lambda t_, r_: self.cp("act", t_[:], v_m[:], [Rv], [r_]))
                        pc, Rpc = pcst.next()
                        pk = (pcst.i - 1) % 4
                        self.cp("pool", pc[:], EP[:].rearrange("p (c t) -> p c t", t=64)[:, :, 63], [REP], [Rpc])
                        S.dma(pcch[pk], [(PCt[ms, t * 8:(t + 1) * 8], pc[:])], reads=[Rpc], writes=[R_rw])
                S.barrier()
                S.emit()
                for n_ in S.ENGS:
                    S.e[n_].ops = []

            with ExitStack() as st:
                if self.upto < 3:
                    raise _Stop()
                for i in range(8):
                    S.dma(S.misc("pool"), [(wupb[i * 128:(i + 1) * 128, :], wup_d[i * 128:(i + 1) * 128, :])], writes=[R_wconv], q="pool")
                for i in range(8):
                    S.dma(S.misc("pool"), [(wdnb[i * 512:(i + 1) * 512, :], wdn_d[i * 512:(i + 1) * 512, :])], writes=[R_wconv], q="pool")
                arl = self.ring(st, "arl", [128, 4, 256], BF16, 2)
                btl = self.ring(st, "btl", [128, 512], BF16, 2)
                ktl = self.ring(st, "ktl", [128, 512], BF16, 2)
                vtl = self.ring(st, "vtl", [128, 512], BF16, 2)
                pcl = self.ring(st, "pcl", [128, 8], F32, 2)
                bol = self.ring(st, "bol", [128, 512], BF16, 2)
                grl = self.ring(st, "grl", [128, 512], BF16, 2)
                ldch = [S.chan(), S.chan()]
                MT = self.ring(st, "MT", [128, 2, 512], BF16, 8)
                Lr = self.ring(st, "Lr", [128, 2, 128], BF16, 12)
                ASr = self.ring(st, "ASr", [128, 2, 256], BF16, 12)
                TM = self.ring(st, "TM", [128, 4, 128], BF16, 8)
                Xb = self.ring(st, "Xb", [128, 2, 128], BF16, 8)
                TXr = self.ring(st, "TXr", [128, 2, 128], BF16, 8)
                TXf = self.ring(st, "TXf", [128, 2, 128], F32, 8)
                Bf = self.ring(st, "Bf", [128, 128], F32, 8)
                RqT = self.ring(st, "RqT", [128, 128], F32, 8)
                Y0 = self.ring(st, "Y0", [128, 2, 64], F32, 8)
                GTr = self.ring(st, "GTr", [128, 64], F32, 16)
                Fr = self.ring(st, "Fr", [128, 64], F32, 16)
                Hs = self.ring(st, "Hs", [128, 64], F32, 3)
                Yr = self.ring(st, "Yr", [128, 2, 64], F32, 4)
                gn = self.ring(st, "gn", [128, 2, 64], F32, 4)
                gs_ = self.ring(st, "gs_", [128, 2], F32, 6)
                yT = self.ring(st, "yT", [128, 512], F32, 2)
                yst = self.ring(st, "yst", [128, 512], BF16, 2)
                ych = [S.chan(), S.chan()]
                pi2 = [0]

                def pb2():
                    b_ = banks[pi2[0] % 7]
                    pi2[0] += 1
                    return b_

                e2 = lambda ap: ap.rearrange("p (e s) -> p e s", e=2)
                idb3 = identb[:].unsqueeze(1).broadcast_to([128, 2, 128])
                NBK = 4
                for m in range(4):
                    ms = slice(m * 128, (m + 1) * 128)
                    H, RH = Hs.next()
                    self.memset("pool", H[:], 0.0, [RH])
                    for t in range(NT):
                        own = t >= NTP
                        o0 = (t - NTP) * 512
                        c0 = t * 512
                        ar, Rar = arl.next(); bt, Rbt = btl.next(); kt, Rkt = ktl.next(); vt, Rvt = vtl.next(); pc, Rpc = pcl.next()
                        lk = (arl.i - 1) % 2
                        pairs = [(ar[:], ARt[ms, t * 4:(t + 1) * 4, :]), (bt[:], Bt[ms, c0:c0 + 512]), (kt[:], Kt[ms, c0:c0 + 512]),
                                 (vt[:], Vt[ms, c0:c0 + 512]), (pc[:], PCt[ms, t * 8:(t + 1) * 8])]
                        wr = [Rar, Rbt, Rkt, Rvt, Rpc]
                        if own:
                            bo, Rbo = bol.next(); gr, Rgr = grl.next()
                            pairs += [(bo[:], BoT[ms, o0:o0 + 512]), (gr[:], GrT[ms, o0:o0 + 512])]
                            wr += [Rbo, Rgr]
                            yt_, Ryt = yT.next()
                        S.dma(ldch[lk], pairs, reads=[R_rw], writes=wr)
                        X = [dict() for _ in range(NBK)]
                        for b in range(NBK):
                            c_ = X[b]
                            bs = slice(b * 128, (b + 1) * 128)
                            c_["bs"] = bs
                            mt, Rmt = MT.next()
                            L0, RL0 = Lr.next()
                            for e in range(2):
                                hs = slice(e * 64, (e + 1) * 64)
                                ps, Rps = pb2()
                                self.mm(ps[:, 0:256], bt[hs, bs], ar[hs, b, :], [Rbt, Rar], [Rps])
                                self.mm(ps[:, 256:512], kt[hs, bs], ar[hs, b, :], [Rkt, Rar], [Rps])
                                self.tt("dve", mt[:, e, :], ps[:, :], mask4[:], ALU.mult, [Rps, Rm4], [Rmt])
                                psL, RpsL = pb2()
                                self.mm(psL[:, 0:128], ar[hs, b, 0:128], bt[hs, bs], [Rar, Rbt], [RpsL])
                                self.tt("dve", L0[:, e, :], psL[:, 0:128], maskL[:], ALU.mult, [RpsL, RmL], [RL0])
                            c_["mt"], c_["Rmt"], c_["L"], c_["RL"] = mt, Rmt, L0, RL0
                        for b in range(NBK):
                            c_ = X[b]
                            bs = c_["bs"]
                            pst, Rpst = bankb
                            o_ = (b % 2) * 512
                            self.tr(pst[:, o_ + 0:o_ + 128], ar[:, b, 0:128], identb[:], [Rar, Ridb], [Rpst])
                            self.tr(pst[:, o_ + 128:o_ + 256], bt[:, bs], identb[:], [Rbt, Ridb], [Rpst])
                            self.tr(pst[:, o_ + 256:o_ + 384], kt[:, bs], identb[:], [Rkt, Ridb], [Rpst])
                            self.tr(pst[:, o_ + 384:o_ + 512], vt[:, bs], identb[:], [Rvt, Ridb], [Rpst])
                            tm, Rtm = TM.next()
                            self.cp("act", tm[:], pst[:, o_:o_ + 512].rearrange("p (q f) -> p q f", q=4), [Rpst], [Rtm])
                            bf_, Rbf = Bf.next()
                            self.cp("pool", bf_[:], tm[:, 1, :], [Rtm], [Rbf])
                            c_["tm"], c_["Rtm"], c_["bf"], c_["Rbf"] = tm, Rtm, bf_, Rbf
                        for b in range(NBK):
                            c_ = X[b]
                            mt, Rmt = c_["mt"], c_["Rmt"]
                            AS, RAS = ASr.next()
                            self.cp("pool", AS[:, :, 0:128], mt[:, :, 0:128], [Rmt], [RAS])
                            self.tt("pool", AS[:, :, 128:256], mt[:, :, 0:128], idb3, ALU.add, [Rmt, Ridb], [RAS])
                            c_["AS"], c_["RAS"] = AS, RAS
                        for b in range(NBK):
                            c_ = X[b]
                            AS, RAS, Lk, RLk = c_["AS"], c_["RAS"], c_["L"], c_["RL"]
                            psa, Rpsa = pb2()
                            psl, Rpsl = pb2()
                            for e in range(2):
                                self.mm(psa[:, e * 128:(e + 1) * 128], Lk[:, e, :], AS[:, e, 0:128], [RLk, RAS], [Rpsa])
                                self.mm(psl[:, e * 128:(e + 1) * 128], AS[:, e, 0:128], Lk[:, e, :], [RAS, RLk], [Rpsl])
                            ASn, RASn = ASr.next()
                            self.cp("dve", ASn[:, :, 0:128], e2(psa[:, 0:256]), [Rpsa], [RASn])
                            self.cp("pool", ASn[:, :, 128:256], AS[:, :, 128:256], [RAS], [RASn])
                            Ln, RLn = Lr.next()
                            self.cp("act", Ln[:], e2(psl[:, 0:256]), [Rpsl], [RLn])
                            c_["AS"], c_["RAS"], c_["L"], c_["RL"] = ASn, RASn, Ln, RLn
                        for it in range(5):
                            last = it == 4
                            w_ = 128 if last else 0
                            for b in range(NBK):
                                c_ = X[b]
                                AS, RAS, Lk, RLk = c_["AS"], c_["RAS"], c_["L"], c_["RL"]
                                psm, Rpsm = pb2()
                                for e in range(2):
                                    self.mm(psm[:, e * 256 + w_:(e + 1) * 256], Lk[:, e, :], AS[:, e, w_:256], [RLk, RAS], [Rpsm])
                                ASn, RASn = ASr.next()
                                pm3 = e2(psm[:, 0:512])
                                self.tt("dve", ASn[:, :, 128:256], pm3[:, :, 128:256], AS[:, :, 128:256], ALU.add, [Rpsm, RAS], [RASn])
                                if not last:
                                    self.cp("act", ASn[:, :, 0:128], pm3[:, :, 0:128], [Rpsm], [RASn])
                                    psl, Rpsl = pb2()
                                    for e in range(2):
                                        self.mm(psl[:, e * 128:(e + 1) * 128], AS[:, e, 0:128], Lk[:, e, :], [RAS, RLk], [Rpsl])
                                    Ln, RLn = Lr.next()
                                    self.cp("act", Ln[:], e2(psl[:, 0:256]), [Rpsl], [RLn])
                                    c_["L"], c_["RL"] = Ln, RLn
                                c_["AS"], c_["RAS"] = ASn, RASn
                        for b in range(NBK):
                            c_ = X[b]
                            mt, Rmt, tm, Rtm = c_["mt"], c_["Rmt"], c_["tm"], c_["Rtm"]
                            xb, Rxb = Xb.next()
                            psv, Rpsv = pb2()
                            for e in range(2):
                                self.mm(psv[:, e * 64:(e + 1) * 64], mt[:, e, 256:384], tm[:, 3, e * 64:(e + 1) * 64], [Rmt, Rtm], [Rpsv])
                            self.cp("act", xb[:, :, 64:128], psv[:, 0:128].rearrange("p (e v) -> p e v", e=2), [Rpsv], [Rxb])
                            self.cp("pool", xb[:, :, 0:64], tm[:, 0, :].rearrange("p (e v) -> p e v", e=2), [Rtm], [Rxb])
                            c_["xb"], c_["Rxb"] = xb, Rxb
                        for b in range(NBK):
                            c_ = X[b]
                            AS, RAS, xb, Rxb = c_["AS"], c_["RAS"], c_["xb"], c_["Rxb"]
                            pstx, Rpstx = pb2()
                            for e in range(2):
                                self.mm(pstx[:, e * 128:(e + 1) * 128], AS[:, e, 128:256], xb[:, e, :], [RAS, Rxb], [Rpstx])
                            tx, Rtx = TXr.next()
                            txf, Rtxf = TXf.next()
                            self.cp("act", tx[:], e2(pstx[:, 0:256]), [Rpstx], [Rtx])
                            self.cp("act", txf[:], e2(pstx[:, 0:256]), [Rpstx], [Rtxf])
                            c_["tx"], c_["Rtx"], c_["txf"], c_["Rtxf"] = tx, Rtx, txf, Rtxf
                        if own:
                            for b in range(NBK):
                                c_ = X[b]
                                mt, Rmt, tm, Rtm, tx, Rtx = c_["mt"], c_["Rmt"], c_["tm"], c_["Rtm"], c_["tx"], c_["Rtx"]
                                psr, Rpsr = pb2()
                                for e in range(2):
                                    self.mm(psr[e * 64:(e + 1) * 64, 0:128], tx[:, e, 0:64], mt[:, e, 128:256], [Rtx, Rmt], [Rpsr])
                                rq, Rrq = RqT.next()
                                self.tt("dve", rq[:], psr[:, 0:128], ar[:, b, 128:256], ALU.add, [Rpsr, Rar], [Rrq])
                                psy, Rpsy = pb2()
                                for e in range(2):
                                    self.mm(psy[:, e * 64:(e + 1) * 64], mt[:, e, 128:256], tx[:, e, 64:128], [Rmt, Rtx], [Rpsy], start=True, stop=False)
                                    self.mm(psy[:, e * 64:(e + 1) * 64], mt[:, e, 384:512], tm[:, 3, e * 64:(e + 1) * 64], [Rmt, Rtm], [Rpsy], start=False, stop=True)
                                y0, Ry0 = Y0.next()
                                self.cp("act", y0[:], psy[:, 0:128].rearrange("p (e v) -> p e v", e=2), [Rpsy], [Ry0])
                                c_["rq"], c_["Rrq"], c_["y0"], c_["Ry0"] = rq, Rrq, y0, Ry0
                        for b in range(NBK):
                            c_ = X[b]
                            tm, Rtm, tx, Rtx, txf, Rtxf, bf_, Rbf = c_["tm"], c_["Rtm"], c_["tx"], c_["Rtx"], c_["txf"], c_["Rtxf"], c_["bf"], c_["Rbf"]
                            c_["gt"], c_["ff"] = [], []
                            for c in range(2):
                                cs = slice(c * 64, (c + 1) * 64)
                                psg, Rpsg = pb2()
                                for e in range(2):
                                    self.mm(psg[e * 64:(e + 1) * 64, 0:64], txf[cs, e, 0:64], bf_[cs, e * 64:(e + 1) * 64], [Rtxf, Rbf], [Rpsg])
                                gt, Rgt = GTr.next()
                                self.tt("dve", gt[0:64, :], psg[0:64, 0:64], ident[0:64, 0:64], ALU.add, [Rpsg, Rid], [Rgt])
                                self.tt("dve", gt[64:128, :], psg[64:128, 0:64], ident[64:128, 64:128], ALU.add, [Rpsg, Rid], [Rgt])
                                psf, Rpsf = pb2()
                                for e in range(2):
                                    es = slice(e * 64, (e + 1) * 64)
                                    self.mm(psf[es, 0:64], tm[cs, 1, es], tx[cs, e, 64:128], [Rtm, Rtx], [Rpsf], start=True, stop=False)
                                    self.mm(psf[es, 0:64], tm[cs, 2, es], tm[cs, 3, es], [Rtm], [Rpsf], start=False, stop=True)
                                ff, Rff = Fr.next()
                                pcc = pc[:, b * 2 + c:b * 2 + c + 1]
                                self.act(ff[:], psf[:, 0:64], AF.Copy, [Rpsf, Rpc], [Rff], scale=pcc)
                                c_["gt"].append((gt, Rgt))
                                c_["ff"].append((ff, Rff))
                        for b in range(NBK):
                            c_ = X[b]
                            bs = c_["bs"]
                            if own:
                                yy, Ryy = Yr.next()
                                rq, Rrq, y0, Ry0 = c_["rq"], c_["Rrq"], c_["y0"], c_["Ry0"]
                            for c in range(2):
                                cs = slice(c * 64, (c + 1) * 64)
                                gt, Rgt = c_["gt"][c]
                                ff, Rff = c_["ff"][c]
                                if own:
                                    for e in range(2):
                                        es = slice(e * 64, (e + 1) * 64)
                                        psq, Rpsq = pb2()
                                        self.mm(psq[cs, 0:64], rq[es, cs], H[es, :], [Rrq, RH], [Rpsq])
                                        self.tt("dve", yy[cs, e, :], psq[cs, 0:64], y0[cs, e, :], ALU.add, [Rpsq, Ry0], [Ryy])
                                Hn, RHn = Hs.next()
                                for e in range(2):
                                    es = slice(e * 64, (e + 1) * 64)
                                    psh, Rpsh = pb2()
                                    self.mm(psh[es, 0:64], gt[es, :], H[es, :], [Rgt, RH], [Rpsh])
                                    self.stt(Hn[es, :], psh[es, 0:64], pc[es, b * 2 + c:b * 2 + c + 1], ff[es, :], ALU.mult, ALU.add, [Rpsh, Rpc, Rff], [RHn])
                                H, RH = Hn, RHn
                            if own:
                                s1, Rs1 = gs_.next()
                                S.op("dve", lambda e, o=s1, i=yy: e.tensor_reduce(out=o[:], in_=i[:], axis=AX.X, op=ALU.add), [Ryy], [Rs1])
                                self.ts("dve", s1[:], s1[:], -1.0 / 64, ALU.mult, [Rs1], [Rs1])
                                yc, Ryc = gn.next()
                                self.tt("dve", yc[:], yy[:], s1[:].unsqueeze(2).broadcast_to([128, 2, 64]), ALU.add, [Ryy, Rs1], [Ryc])
                                y2, Ry2 = gn.next()
                                self.tt("pool", y2[:], yc[:], yc[:], ALU.mult, [Ryc], [Ry2])
                                s2, Rs2 = gs_.next()
                                S.op("dve", lambda e, o=s2, i=y2: e.tensor_reduce(out=o[:], in_=i[:], axis=AX.X, op=ALU.add), [Ry2], [Rs2])
                                self.act(s2[:], s2[:], AF.Sqrt, [Rs2], [Rs2], bias=GN_EPS, scale=1.0 / 64)
                                self.recip(s2[:], s2[:], [Rs2], [Rs2])
                                self.tt("dve", yc[:], yc[:], s2[:].unsqueeze(2).broadcast_to([128, 2, 64]), ALU.mult, [Ryc, Rs2], [Ryc])
                                pstt, Rpstt = pb2()
                                self.tr(pstt[:, 0:128], yc[:].rearrange("p e v -> p (e v)"), ident[:], [Ryc, Rid], [Rpstt])
                                self.act(yt_[:, bs], pstt[:, 0:128], AF.Identity, [Rpstt, Rvec], [Ryt], bias=vcol("ln_b", m), scale=vcol("ln_w", m))
                        if own:
                            self.tt("pool", yt_[:], yt_[:], bo[:], ALU.add, [Ryt, Rbo], [Ryt])
                            ys, Rys = yst.next()
                            yk = (yst.i - 1) % 2
                            self.tt("pool", ys[:], yt_[:], gr[:], ALU.mult, [Ryt, Rgr], [Rys])
                            S.dma(ych[yk], [(YbT[ms, o0:o0 + 512], ys[:])], reads=[Rys], writes=[R_YbT])
                S.barrier()
                S.emit()
                for n_ in S.ENGS:
                    S.e[n_].ops = []

            with ExitStack() as st:
                if self.upto < 4:
                    raise _Stop()
                Vall = self.sb(st, "Vall", [128, NB, 520], BF16); RVall = Res()
                nv = max(1, NB // 16)
                for i in range(0, NB, 16):
                    j = min(NB, i + 16)
                    S.dma(S.misc(), [(Vall[:, i:j, :], Vs[:, i:j, :])], reads=[R_Vs], writes=[RVall])
                Kh = self.ring(st, "Kh", [128, TT], BF16, 2)
                Qh = self.ring(st, "Qh", [128, TO], BF16, 2)
                kqch = [S.chan(), S.chan()]
                PT = self.ring(st, "PT", [128, 512], BF16, 6)
                osb = self.ring(st, "osb", [128, 512], F32, 2)
                rl = self.ring(st, "rl", [128, 512], F32, 2)
                ost = self.ring(st, "ost", [128, 512], BF16, 2)
                och = [S.chan(), S.chan()]
                sbanks = Ring(banks[0:4])
                obanks = Ring(banks[4:6])
                bbanks = Ring(banks[6:7])
                V5 = Vall[:].rearrange("p b (h d) -> p b h d", d=65)
                LOOK = 2
                heads = []

                def load_head(hd):
                    K_, RK_ = Kh.next(); Q_, RQ_ = Qh.next()
                    hk = (Kh.i - 1) % 2
                    S.dma(kqch[hk], [(K_[0:64, :], KnT[hd * 64:(hd + 1) * 64, :]), (K_[64:97, :], KpeT[:, :]), (Q_[0:97, :], QT[hd, :, :])],
                          reads=[R_KnT, R_KpeT, R_QT], writes=[RK_, RQ_])
                    return (K_, RK_, Q_, RQ_)

                pend = []

                def do_pv(item):
                    (hd, qt, kb, nkb, cst, pt, Rpt, po, Rpo) = item
                    self.mm(po[0:65, cst:512], V5[:, kb, hd, :], pt[:, cst:512], [RVall, Rpt], [Rpo], start=(kb == 0), stop=(kb == nkb - 1), inc=True)
                    if kb == nkb - 1:
                        q0 = qt * 512
                        o_, Ro_ = osb.next()
                        self.cp("act", o_[0:65, :], po[0:65, :], [Rpo], [Ro_])
                        r_, Rr_ = rl.next()
                        self.recip(r_[64:65, :], o_[64:65, :], [Ro_], [Rr_])
                        pb_, Rpb_ = bbanks.next()
                        self.mm(pb_[0:64, :], onesf[64:65, 0:64], r_[64:65, :], [Ronesf, Rr_], [Rpb_])
                        os_, Ros_ = ost.next()
                        ok = (ost.i - 1) % 2
                        self.tt("dve", os_[0:64, :], pb_[0:64, :], o_[0:64, :], ALU.mult, [Rpb_, Ro_], [Ros_])
                        S.dma(och[ok], [(OaT[hd * 64:(hd + 1) * 64, q0:q0 + 512], os_[0:64, :])], reads=[Ros_], writes=[R_OaT])

                nxt = load_head(0)
                for hd in range(NH):
                    K_, RK_, Q_, RQ_ = nxt
                    if hd + 1 < NH:
                        nxt = load_head(hd + 1)
                    for qt in range(NTO):
                        q0 = qt * 512
                        nkb = (TP + q0 + 512) // 128
                        po, Rpo = obanks.next()
                        for kb in range(nkb):
                            jd = kb - (nkb - 4)
                            cst = 0 if jd < 0 else jd * 128
                            ps, Rps = sbanks.next()
                            self.mm(ps[:, cst:512], K_[0:97, kb * 128:(kb + 1) * 128], Q_[0:97, q0 + cst:q0 + 512], [RK_, RQ_], [Rps])
                            pt, Rpt = PT.next()
                            self.act(pt[:, cst:512], ps[:, cst:512], AF.Exp, [Rps], [Rpt])
                            if jd >= 0:
                                self.tt("pool", pt[:, cst:cst + 128], pt[:, cst:cst + 128], trim[:], ALU.mult, [Rpt, Rtri], [Rpt])
                            pend.append((hd, qt, kb, nkb, cst, pt, Rpt, po, Rpo))
                            if len(pend) > LOOK:
                                do_pv(pend.pop(0))
                while pend:
                    do_pv(pend.pop(0))
                S.barrier()
                S.emit()
                for n_ in S.ENGS:
                    S.e[n_].ops = []

            with ExitStack() as st:
                if self.upto < 5:
                    raise _Stop()
                womla = self.sb(st, "womla", [128, 4, D], BF16)
                worw = self.sb(st, "worw", [128, 4, D], BF16)
                wout = self.sb(st, "wout", [128, 8, D], BF16)
                wpg = self.sb(st, "wpg", [128, 8, D], BF16)
                wpp = self.sb(st, "wpp", [128, 2, D], BF16)
                Rw4 = Res("w4")
                S.dma(S.misc("pool"), [(womla[:], womla_d.rearrange("(c p) n -> p c n", p=128)),
                                 (worw[:], worw_d.rearrange("(c p) n -> p c n", p=128)),
                                 (wpp[:], wpp_d.rearrange("(c p) n -> p c n", p=128))], writes=[Rw4], q="pool")
                wout3 = wout_d.rearrange("(c p) n -> p c n", p=128)
                wpg3 = wpg_d.rearrange("(c p) n -> p c n", p=128)
                for c in range(8):
                    S.dma(S.misc("pool"), [(wout[:, c, :], wout3[:, c, :]), (wpg[:, c, :], wpg3[:, c, :])], writes=[Rw4], q="pool")
                wupr = self.ring(st, "wupr", [128, 8, 256], BF16, 3)
                wupc = [S.chan() for _ in range(3)]
                wdnr = self.ring(st, "wdnr", [128, 2, 512], BF16, 4)
                wdnc = [S.chan() for _ in range(4)]
                wup3 = wupb.rearrange("(c p) n -> p c n", p=128)
                wdn3 = wdnb.rearrange("(c p) n -> p c n", p=128)
                x4 = self.ring(st, "x4_", [128, 8, 512], F32, 1)
                xc4 = S.chan()
                inb = self.ring(st, "inb", [128, 4, 512], BF16, 2)
                inc_ = [S.chan(), S.chan()]
                gtl = self.ring(st, "gtl", [128, 8, 512], BF16, 2)
                gtc = [S.chan(), S.chan()]
                pin = self.sb(st, "pin", [128, 2, 512], BF16); Rpin = Res()
                pinc = S.chan()
                mix = self.sb(st, "mix", [128, 8, 512], BF16); Rmix = Res()
                mtmp = self.ring(st, "mtmp", [128, 512], F32, 3)
                sq4 = self.ring(st, "sq4", [128, 512], BF16, 4)
                h4 = self.sb(st, "h4", [128, 8, 512], BF16); Rh4 = Res()
                hid = self.sb(st, "hid", [128, 16, 512], BF16)
                Rhid = [Res() for _ in range(16)]
                rel = self.ring(st, "rel", [128, 512], F32, 3)
                rs4 = self.ring(st, "rs4", [128, 512], F32, 2)
                t4 = self.ring(st, "t4", [128, 512], F32, 2)
                gsg = self.ring(st, "gsg", [128, 512], F32, 2)
                osb4 = self.ring(st, "osb4", [128, 512], F32, 2)
                och4 = [S.chan(), S.chan()]
                pi4 = [0]

                def pb4():
                    b_ = banks[pi4[0] % 7]
                    pi4[0] += 1
                    return b_

                def rms4(xt, Rxt):
                    ps, Rps = pb4()
                    for c in range(8):
                        sq, Rsq = sq4.next()
                        self.act(sq[:], xt[:, c, :], AF.Square, [Rxt], [Rsq])
                        self.mm(ps[:, :], onesb[:, :], sq[:], [Rones, Rsq], [Rps], start=(c == 0), stop=(c == 7), inc=True)
                    t1, Rt1 = t4.next()
                    self.act(t1[:], ps[:, :], AF.Sqrt, [Rps], [Rt1], bias=RMS_EPS, scale=1.0 / D)
                    rs, Rrs = rs4.next()
                    self.recip(rs[:], t1[:], [Rt1], [Rrs])
                    return rs, Rrs

                for to in range(NTO):
                    o0 = to * 512
                    xt, Rxt = x4.next()
                    S.dma(xc4, [(xt[:], xT3[:, :, TP + o0:TP + o0 + 512])], writes=[Rxt])
                    oa, Roa = inb.next()
                    S.dma(inc_[0], [(oa[:], OaT[:, o0:o0 + 512].rearrange("(c p) t -> p c t", p=128))], reads=[R_OaT], writes=[Roa])
                    yb, Ryb = inb.next()
                    S.dma(inc_[1], [(yb[:], YbT[:, o0:o0 + 512].rearrange("(c p) t -> p c t", p=128))], reads=[R_YbT], writes=[Ryb])
                    ga, Rga = gtl.next()
                    S.dma(gtc[0], [(ga[:], GT[0:1024, o0:o0 + 512].rearrange("(c p) t -> p c t", p=128))], reads=[R_GT], writes=[Rga])
                    gb, Rgb = gtl.next()
                    S.dma(gtc[1], [(gb[:], GT[1024:2048, o0:o0 + 512].rearrange("(c p) t -> p c t", p=128))], reads=[R_GT], writes=[Rgb])
                    S.dma(pinc, [(pin[:], pT[:, o0:o0 + 512].rearrange("(c p) t -> p c t", p=128))], writes=[Rpin], q="pool")
                    for mt_ in range(8):
                        msl = slice(mt_ * 128, (mt_ + 1) * 128)
                        psa, Rpsa = pb4()
                        for c in range(4):
                            self.mm(psa[:, :], womla[:, c, msl], oa[:, c, :], [Rw4, Roa], [Rpsa], start=(c == 0), stop=(c == 3))
                        psb, Rpsb = pb4()
                        for c in range(4):
                            self.mm(psb[:, :], worw[:, c, msl], yb[:, c, :], [Rw4, Ryb], [Rpsb], start=(c == 0), stop=(c == 3))
                        m1, Rm1 = mtmp.next()
                        self.tt("dve", m1[:], psa[:, :], ga[:, mt_, :], ALU.mult, [Rpsa, Rga], [Rm1])
                        m2, Rm2 = mtmp.next()
                        self.tt("dve", m2[:], psb[:, :], gb[:, mt_, :], ALU.mult, [Rpsb, Rgb], [Rm2])
                        self.tt("pool", mix[:, mt_, :], m1[:], m2[:], ALU.add, [Rm1, Rm2], [Rmix])
                    for mt_ in range(8):
                        msl = slice(mt_ * 128, (mt_ + 1) * 128)
                        ps, Rps = pb4()
                        for c in range(8):
                            self.mm(ps[:, :], wout[:, c, msl], mix[:, c, :], [Rw4, Rmix], [Rps], start=(c == 0), stop=(c == 7))
                        self.tt("dve", xt[:, mt_, :], ps[:, :], xt[:, mt_, :], ALU.add, [Rps, Rxt], [Rxt])
                    rs, Rrs = rms4(xt, Rxt)
                    for c in range(8):
                        self.stt(h4[:, c, :], xt[:, c, :], vcol("g_ffn", c), rs[:], ALU.mult, ALU.mult, [Rxt, Rvec, Rrs], [Rh4])
                    for hh in range(2):
                        for uc in range(8):
                            wu, Rwu = wupr.next()
                            uk = (wupr.i - 1) % 3
                            col = hh * 2048 + uc * 256
                            S.dma(wupc[uk], [(wu[:], wup3[:, :, col:col + 256])], reads=[R_wconv], writes=[Rwu])
                            for j in range(2):
                                mi = uc * 2 + j
                                ps, Rps = pb4()
                                for c in range(8):
                                    self.mm(ps[:, :], wu[:, c, j * 128:(j + 1) * 128], h4[:, c, :], [Rwu, Rh4], [Rps], start=(c == 0), stop=(c == 7))
                                rl_, Rrl = rel.next()
                                self.act(rl_[:], ps[:, :], AF.Relu, [Rps], [Rrl])
                                self.tt("pool", hid[:, mi, :], rl_[:], rl_[:], ALU.mult, [Rrl], [Rhid[mi]])
                        for oh in range(2):
                            pss = [pb4() for _ in range(4)]
                            for kc in range(8):
                                wd, Rwd = wdnr.next()
                                dk = (wdnr.i - 1) % 4
                                kr = hh * 16 + kc * 2
                                S.dma(wdnc[dk], [(wd[:], wdn3[:, kr:kr + 2, oh * 512:(oh + 1) * 512])], reads=[R_wconv], writes=[Rwd])
                                for j in range(2):
                                    kci = kc * 2 + j
                                    for mq in range(4):
                                        ps, Rps = pss[mq]
                                        self.mm(ps[:, :], wd[:, j, mq * 128:(mq + 1) * 128], hid[:, kci, :], [Rwd, Rhid[kci]], [Rps],
                                                start=(kci == 0), stop=(kci == 15), inc=(kci == 15 or (j == 1 and mq == 3)))
                            for mq in range(4):
                                mt_ = oh * 4 + mq
                                ps, Rps = pss[mq]
                                self.tt("dve", xt[:, mt_, :], ps[:, :], xt[:, mt_, :], ALU.add, [Rps, Rxt], [Rxt])
                    rs, Rrs = rms4(xt, Rxt)
                    for c in range(8):
                        self.stt(h4[:, c, :], xt[:, c, :], vcol("g_ple", c), rs[:], ALU.mult, ALU.mult, [Rxt, Rvec, Rrs], [Rh4])
                    for mt_ in range(8):
                        msl = slice(mt_ * 128, (mt_ + 1) * 128)
                        ps, Rps = pb4()
                        for c in range(8):
                            self.mm(ps[:, :], wpg[:, c, msl], h4[:, c, :], [Rw4, Rh4], [Rps], start=(c == 0), stop=(c == 7))
                        gg, Rgg = gsg.next()
                        self.act(gg[:], ps[:, :], AF.Sigmoid, [Rps], [Rgg])
                        ps2, Rps2 = pb4()
                        for c in range(2):
                            self.mm(ps2[:, :], wpp[:, c, msl], pin[:, c, :], [Rw4, Rpin], [Rps2], start=(c == 0), stop=(c == 1))
                        self.tt("dve", gg[:], ps2[:, :], gg[:], ALU.mult, [Rps2, Rgg], [Rgg])
                        self.tt("pool", xt[:, mt_, :], xt[:, mt_, :], gg[:], ALU.add, [Rxt, Rgg], [Rxt])
                    rs, Rrs = rms4(xt, Rxt)
                    for c in range(8):
                        ob, Rob = osb4.next()
                        okk = (osb4.i - 1) % 2
                        self.stt(ob[:], xt[:, c, :], vcol("g_final", c), rs[:], ALU.mult, ALU.mult, [Rxt, Rvec, Rrs], [Rob])
                        S.dma(och4[okk], [(outT[c * 128:(c + 1) * 128, o0:o0 + 512], ob[:])], reads=[Rob])
                S.barrier()
                S.emit()
        return nc


def const_inputs():
    s = np.arange(128)[:, None]
    t = np.arange(128)[None, :]
    same = (s // 64) == (t // 64)
    m_lt = ((s < t) & same).astype(np.float32)
    m_le = ((s <= t) & same).astype(np.float32)
    mask4 = np.concatenate([m_lt, m_le, m_lt, m_le], axis=1)
    maskL = ((t < s) & same).astype(np.float32)
    tri = (s <= t).astype(np.float32)
    bd = same.astype(np.float32)
    scan = np.ones((128, 512), np.float32)
    scan[:, ::64] = 0.0
    inv_freq = (np.float32(10000.0) ** (-np.arange(16, dtype=np.float32) / np.float32(16))).astype(np.float32)
    ropec = np.zeros((128, 2), np.float32)
    ropec[64:80, 0] = inv_freq
    ropec[80:96, 0] = inv_freq
    ropec[64:80, 1] = -1.0
    ropec[80:96, 1] = 1.0
    return dict(c_ident=np.eye(128, dtype=np.float32), c_mask4=mask4, c_maskL=maskL, c_tri=tri, c_bd=bd, c_scan=scan, ropec=ropec)


def shared_inputs(inp):
    f = lambda a: np.ascontiguousarray(np.asarray(a, dtype=np.float32))
    w_in = f(inp["w_in"][0])
    z64 = np.zeros((D, 64), np.float32)
    kpe = w_in[:, 640:672]
    w_kpe = np.concatenate([z64, kpe, z64, kpe[:, 16:32], kpe[:, 0:16]], axis=1)
    w_uq = f(inp["w_uq"][0]).reshape(384, NH, 96)
    wq_sw = np.zeros_like(w_uq)
    wq_sw[:, :, 64:80] = w_uq[:, :, 80:96]
    wq_sw[:, :, 80:96] = w_uq[:, :, 64:80]
    w_ukv = f(inp["w_ukv"][0]).reshape(256, NH, 128)
    vec = {"g_mix": inp["g_mix"][0], "g_q_a": inp["g_q_a"][0], "g_kv_a": inp["g_kv_a"][0], "mu": inp["mu_rwkv"][0],
           "w0": inp["w0"][0], "a0": inp["a0"][0], "k_k": inp["k_k"][0], "k_a": inp["k_a"][0],
           "r_k": np.asarray(inp["r_k"][0]).reshape(-1), "ln_w": inp["ln_x_w"][0], "ln_b": inp["ln_x_b"][0],
           "g_ffn": inp["g_ffn"][0], "g_ple": inp["g_ple"][0], "g_final": inp["g_final"]}
    vecs = np.zeros((128, NVEC), np.float32)
    for name, n in VEC_LAYOUT:
        v = f(vec[name]).reshape(n, 128)
        vecs[:, VEC_OFF[name]:VEC_OFF[name] + n] = v.T
    d = dict(vecs=vecs, w_in=w_in, w_kpe=np.ascontiguousarray(w_kpe), wq=np.ascontiguousarray(w_uq.reshape(384, 768)),
             wq_sw=np.ascontiguousarray(wq_sw.reshape(384, 768)),
             wukv_k=np.ascontiguousarray(w_ukv[:, :, 0:64].reshape(256, 512)),
             wukv_v=np.ascontiguousarray(w_ukv[:, :, 64:128].reshape(256, 512)),
             w_o_mla=f(inp["w_o_mla"][0]), w2=f(inp["w2"][0]), a2=f(inp["a2"][0]), g2=f(inp["g2"][0]),
             w_o_rwkv=f(inp["w_o_rwkv"][0]), w_out=f(inp["w_out"][0]), w_up=f(inp["w_ffn_up"][0]), w_down=f(inp["w_ffn_down"][0]),
             w_pg=f(inp["w_ple_gate"][0]), w_pp=f(inp["w_ple_proj"][0]))
    d.update(const_inputs())
    return d


def core_inputs(x_b, p_b, pos_b, half, TP, TO):
    TT = TP + TO
    xT = np.zeros((D, TT), np.float32)
    posr = np.zeros((1, TT), np.int32)
    mrow = np.zeros((1, TT), np.float32)
    o0 = half * TP
    if half == 1:
        xT[:, 0:TP] = x_b[0:TP].T
        posr[0, 0:TP] = pos_b[0:TP]
    else:
        mrow[0, 0:TP] = -30000.0
    xT[:, TP:] = x_b[o0:o0 + TO].T
    posr[0, TP:] = pos_b[o0:o0 + TO]
    pT = np.ascontiguousarray(p_b[o0:o0 + TO].T.astype(np.float32))
    return dict(xT=xT, pT=pT, pos=posr, maskrow=mrow.astype(ml_dtypes.bfloat16))


_NC_CACHE = {}


def get_nc(TP, TO):
    if (TP, TO) not in _NC_CACHE:
        _NC_CACHE[(TP, TO)] = B(TP, TO).build()
    return _NC_CACHE[(TP, TO)]


def kernel(**inputs):
    x = np.asarray(inputs["x"], dtype=np.float32)
    p = np.asarray(inputs["p"], dtype=np.float32)[0]
    pos = np.asarray(inputs["positions"]).astype(np.int32)
    Bn, Sq, _ = x.shape
    TP = TO = Sq // 2
    nc = get_nc(TP, TO)
    sh = shared_inputs(inputs)
    in_maps = []
    for c in range(8):
        b, half = c // 2, c % 2
        m = dict(sh)
        m.update(core_inputs(x[b], p[b], pos[b], half, TP, TO))
        in_maps.append(m)
    res = run_bass_kernel_spmd(nc, in_maps, core_ids=list(range(8)))
    out = np.zeros((Bn, Sq, D), np.float32)
    for c in range(8):
        b, half = c // 2, c % 2
        out[b, half * TO:(half + 1) * TO, :] = res.results[c]["outT"].T
    return out
```

```python
from contextlib import ExitStack
import numpy as np
import ml_dtypes
import concourse.bass as bass
import concourse.mybir as mybir
from concourse.bass_utils import run_bass_kernel_spmd

F32 = mybir.dt.float32
BF16 = mybir.dt.bfloat16
I32 = mybir.dt.int32
AF = mybir.ActivationFunctionType
ALU = mybir.AluOpType
AX = mybir.AxisListType

D = 1024
NH = 8
RMS_EPS = 1e-6
GN_EPS = 64 * 1e-5
SCALE = 96 ** -0.5
EXPH = float(np.exp(-0.5))
TWO_PI = 2.0 * np.pi
C1 = 6.28125
C2 = float(TWO_PI - 6.28125)


class Res:
    __slots__ = ("name", "w", "rd")

    def __init__(self, name=""):
        self.name = name
        self.w = None
        self.rd = []


class Chan:
    def __init__(self, sem, name):
        self.sem = sem
        self.count = 0
        self.name = name


class _Eng:
    def __init__(self, name, sem):
        self.name = name
        self.sem = sem
        self.count = 0
        self.ops = []
        self.waited = {}


class Sched:
    ENGS = ("pe", "act", "dve", "pool", "sp")
    HMAP = {"pe": "tensor", "act": "scalar", "dve": "vector", "pool": "gpsimd", "sp": "sync"}

    def __init__(self, nc, stack, n_chan=90):
        self.nc = nc
        self.e = {}
        for n in self.ENGS:
            self.e[n] = _Eng(n, stack.enter_context(nc.semaphore("s_" + n)))
        self.chans = [Chan(stack.enter_context(nc.semaphore("c%d" % i)), "c%d" % i) for i in range(n_chan)]
        self.chan_i = 4
        self.nops = 0
        self.misc_i = 0

    def misc(self, q="sp"):
        base = 0 if q == "sp" else 2
        c = self.chans[base + self.misc_i % 2]
        self.misc_i += 1
        return c

    def chan(self):
        c = self.chans[self.chan_i]
        self.chan_i += 1
        return c

    def _need(self, eng, reads, writes):
        E = self.e[eng]
        need = {}

        def add(t):
            if t is None:
                return
            key, val = t
            if key is E and eng == "pe":
                return
            if need.get(key, 0) < val:
                need[key] = val

        for r in reads:
            add(r.w)
        for w in writes:
            add(w.w)
            for t in w.rd:
                add(t)
        for key, val in need.items():
            if E.waited.get(key, 0) < val:
                E.waited[key] = val
                E.ops.append(("wait", key.sem, val))

    def op(self, eng, fn, reads=(), writes=(), inc=True):
        E = self.e[eng]
        self._need(eng, reads, writes)
        if inc:
            E.count += 1
            t = (E, E.count)
        else:
            t = (E, E.count + 1)
        E.ops.append(("op", fn, inc))
        for r in reads:
            r.rd.append(t)
            if len(r.rd) > 64:
                r.rd = _compress(r.rd)
        for w in writes:
            w.w = t
            w.rd = []
        self.nops += 1
        return t

    def dma(self, chan, pairs, reads=(), writes=(), q="sp"):
        E = self.e[q]
        if chan.count > 0 and E.waited.get(chan, 0) < chan.count:
            E.waited[chan] = chan.count
            E.ops.append(("wait", chan.sem, chan.count))
        self._need(q, reads, writes)
        for (o, i) in pairs:
            chan.count += 16
            E.ops.append(("dma", o, i, chan.sem))
        t = (chan, chan.count)
        for r in reads:
            r.rd.append(t)
        for w in writes:
            w.w = t
            w.rd = []
        return t

    def barrier(self):
        for n in self.ENGS:
            E = self.e[n]
            for m in self.ENGS:
                O = self.e[m]
                if O is E or O.count == 0:
                    continue
                if E.waited.get(O, 0) < O.count:
                    E.waited[O] = O.count
                    E.ops.append(("wait", O.sem, O.count))
            for c in self.chans:
                if c.count and E.waited.get(c, 0) < c.count:
                    E.waited[c] = c.count
                    E.ops.append(("wait", c.sem, c.count))

    def emit(self):
        nc = self.nc
        with nc.Block() as block:
            for n in self.ENGS:
                E = self.e[n]

                def body(h, E=E):
                    for o in E.ops:
                        if o[0] == "wait":
                            h.wait_ge(o[1], o[2])
                        elif o[0] == "op":
                            ins = o[1](h)
                            if o[2]:
                                ins.then_inc(E.sem, 1)
                        else:
                            h.dma_start(out=o[1], in_=o[2]).then_inc(o[3], 16)

                getattr(block, self.HMAP[n])(body)


def _compress(tickets):
    best = {}
    for key, val in tickets:
        if best.get(key, 0) < val:
            best[key] = val
    return list(best.items())


class Ring:
    def __init__(self, items):
        self.items = items
        self.i = 0

    def next(self):
        it = self.items[self.i % len(self.items)]
        self.i += 1
        return it


VEC_LAYOUT = [("g_mix", 8), ("g_q_a", 3), ("g_kv_a", 2), ("mu", 14), ("w0", 4), ("a0", 4), ("k_k", 4),
              ("k_a", 4), ("r_k", 4), ("ln_w", 4), ("ln_b", 4), ("g_ffn", 8), ("g_ple", 8), ("g_final", 8)]
VEC_OFF = {}
_o = 0
for _n, _c in VEC_LAYOUT:
    VEC_OFF[_n] = _o
    _o += _c
NVEC = _o
OM_OFF = NVEC
OMKA_OFF = NVEC + 14
NVEC_TOT = NVEC + 18


class _Stop(Exception):
    pass


class B:
    def __init__(self, TP, TO, debug=False, upto=5, p2_stop=0):
        self.p2_stop = p2_stop
        self.debug = debug
        self.upto = upto
        self.TP, self.TO = TP, TO
        self.TT = TP + TO
        self.nc = bass.Bass("TRN2", target_bir_lowering=False)
        self.st = ExitStack()
        self.S = None

    def din(self, name, shape, dt=F32):
        return self.nc.dram_tensor(name, list(shape), dt, kind="ExternalInput").ap()

    def dscr(self, name, shape, dt=BF16):
        kind = "ExternalOutput" if self.debug else "Internal"
        return self.nc.dram_tensor(name, list(shape), dt, kind=kind).ap()

    def sb(self, st, name, shape, dt=F32):
        self._uid = getattr(self, "_uid", 0) + 1
        return st.enter_context(self.nc.sbuf_tensor("s%d_%s" % (self._uid, name), list(shape), dt))

    def ring(self, st, name, shape, dt, n):
        return Ring([(self.sb(st, "%s%d" % (name, i), shape, dt), Res("%s%d" % (name, i))) for i in range(n)])

    def mm(self, out, lhsT, rhs, reads, writes, start=True, stop=True, inc=None):
        self.S.op("pe", lambda e: e.matmul(out, lhsT, rhs, start=start, stop=stop), reads, writes, inc=(stop if inc is None else inc))

    def tr(self, out, in_, ident, reads, writes, inc=True):
        self.S.op("pe", lambda e: e.transpose(out, in_, ident), reads, writes, inc=inc)

    def act(self, out, in_, func, reads, writes, bias=0.0, scale=1.0):
        if func == AF.Copy and not (isinstance(bias, float) and isinstance(scale, float)):
            func = AF.Identity
        self.S.op("act", lambda e: e.activation(out=out, in_=in_, func=func, bias=bias, scale=scale), reads, writes)

    def tt(self, eng, out, in0, in1, op, reads, writes):
        self.S.op(eng, lambda e: e.tensor_tensor(out=out, in0=in0, in1=in1, op=op), reads, writes)

    def ts(self, eng, out, in0, s1, op0, reads, writes, s2=None, op1=None):
        if op1 is None:
            self.S.op(eng, lambda e: e.tensor_scalar(out=out, in0=in0, scalar1=s1, scalar2=None, op0=op0), reads, writes)
        else:
            self.S.op(eng, lambda e: e.tensor_scalar(out=out, in0=in0, scalar1=s1, scalar2=s2, op0=op0, op1=op1), reads, writes)

    def stt(self, out, in0, scalar, in1, op0, op1, reads, writes):
        self.S.op("dve", lambda e: e.scalar_tensor_tensor(out=out, in0=in0, scalar=scalar, in1=in1, op0=op0, op1=op1), reads, writes)

    def cp(self, eng, out, in_, reads, writes):
        if eng == "act":
            self.act(out, in_, AF.Copy, reads, writes)
        else:
            self.S.op(eng, lambda e: e.tensor_copy(out=out, in_=in_), reads, writes)

    def ckpt(self, n):
        if self.p2_stop == n:
            self.S.barrier()
            self.S.emit()
            raise _Stop()

    def memset(self, eng, ap, val, writes):
        self.S.op(eng, lambda e: e.memset(ap, val), (), writes)

    def recip(self, out, in_, reads, writes):
        self.S.op("dve", lambda e: e.reciprocal(out=out, in_=in_), reads, writes)

    def build(self):
        try:
            self._build()
        except _Stop:
            pass
        return self.nc

    def _build(self):
        nc, TP, TO, TT = self.nc, self.TP, self.TO, self.TT
        NT, NTP, NTO = TT // 512, TP // 512, TO // 512
        NB = TT // 128
        xT = self.din("xT", [D, TT])
        pT = self.din("pT", [256, TO])
        pos = self.din("pos", [1, TT], I32)
        maskrow = self.din("maskrow", [1, TT], BF16)
        vecs_d = self.din("vecs", [128, NVEC])
        rc_d = self.din("ropec", [128, 2])
        w_in = self.din("w_in", [D, 4512])
        w_kpe = self.din("w_kpe", [D, 192])
        wq_d = self.din("wq", [384, 768])
        wqs_d = self.din("wq_sw", [384, 768])
        wkk_d = self.din("wukv_k", [256, 512])
        wkv_d = self.din("wukv_v", [256, 512])
        womla_d = self.din("w_o_mla", [512, D])
        w2_d = self.din("w2", [64, 512])
        a2_d = self.din("a2", [64, 512])
        g2_d = self.din("g2", [128, 512])
        worw_d = self.din("w_o_rwkv", [512, D])
        wout_d = self.din("w_out", [D, D])
        wup_d = self.din("w_up", [D, 4096])
        wdn_d = self.din("w_down", [4096, D])
        wpg_d = self.din("w_pg", [D, D])
        wpp_d = self.din("w_pp", [256, D])
        cm_ident_d = self.din("c_ident", [128, 128])
        cm_mask4_d = self.din("c_mask4", [128, 512])
        cm_maskL_d = self.din("c_maskL", [128, 128])
        cm_tri_d = self.din("c_tri", [128, 128])
        cm_bd_d = self.din("c_bd", [128, 128])
        cm_scan_d = self.din("c_scan", [128, 512])
        outT = nc.dram_tensor("outT", [D, TO], F32, kind="ExternalOutput").ap()
        QT = self.dscr("QT", [NH, 97, TO])
        KnT = self.dscr("KnT", [512, TT])
        KpeT = self.dscr("KpeT", [33, TT])
        Vs = self.dscr("Vs", [128, NB, 520])
        GT = self.dscr("GT", [2048, TO])
        ARt = self.dscr("ARt", [512, NB, 256])
        Bt = self.dscr("Bt", [512, TT])
        Kt = self.dscr("Kt", [512, TT])
        Vt = self.dscr("Vt", [512, TT])
        PCt = self.dscr("PCt", [512, TT // 64], F32)
        GrT = self.dscr("GrT", [512, TO])
        BoT = self.dscr("BoT", [512, TO])
        OaT = self.dscr("OaT", [512, TO])
        YbT = self.dscr("YbT", [512, TO])
        wupb = self.dscr("wupb", [D, 4096])
        wdnb = self.dscr("wdnb", [4096, D])
        R_wconv = Res("wconv")
        R_QT, R_KnT, R_KpeT, R_Vs, R_GT = Res("QT"), Res("KnT"), Res("KpeT"), Res("Vs"), Res("GT")
        R_rw, R_OaT, R_YbT = Res("rwscr"), Res("OaT"), Res("YbT")

        with self.st as st0:
            S = self.S = Sched(nc, st0)
            vecs = self.sb(st0, "vecs", [128, NVEC_TOT]); Rvec = Res("vecs")
            ropec = self.sb(st0, "ropec", [128, 2]); Rrc = Res()
            ident = self.sb(st0, "ident", [128, 128]); Rid = Res()
            identb = self.sb(st0, "identb", [128, 128], BF16); Ridb = Res()
            mask4 = self.sb(st0, "mask4", [128, 512]); Rm4 = Res()
            maskL = self.sb(st0, "maskL", [128, 128]); RmL = Res()
            trim = self.sb(st0, "trim", [128, 128], BF16); Rtri = Res()
            bdb = self.sb(st0, "bdb", [128, 128], BF16); Rbd = Res()
            onesb = self.sb(st0, "onesb", [128, 128], BF16); Rones = Res()
            onesf = self.sb(st0, "onesf", [128, 128]); Ronesf = Res()
            scanm = self.sb(st0, "scanm", [128, 512]); Rscan = Res()
            S.dma(S.misc(), [(vecs[:, 0:NVEC], vecs_d[:, :])], writes=[Rvec])
            S.dma(S.misc(), [(ropec[:], rc_d[:, :])], writes=[Rrc])
            S.dma(S.misc(), [(ident[:], cm_ident_d[:, :])], writes=[Rid])
            S.dma(S.misc("pool"), [(identb[:], cm_ident_d[:, :])], writes=[Ridb], q="pool")
            S.dma(S.misc(), [(mask4[:], cm_mask4_d[:, :])], writes=[Rm4])
            S.dma(S.misc(), [(maskL[:], cm_maskL_d[:, :])], writes=[RmL])
            S.dma(S.misc("pool"), [(trim[:], cm_tri_d[:, :])], writes=[Rtri], q="pool")
            S.dma(S.misc("pool"), [(bdb[:], cm_bd_d[:, :])], writes=[Rbd], q="pool")
            S.dma(S.misc(), [(scanm[:], cm_scan_d[:, :])], writes=[Rscan])
            self.memset("pool", onesb[:], 1.0, [Rones])
            self.memset("pool", onesf[:], 1.0, [Ronesf])
            self.ts("dve", vecs[:, OM_OFF:OM_OFF + 14], vecs[:, VEC_OFF["mu"]:VEC_OFF["mu"] + 14], -1.0, ALU.mult,
                    [Rvec], [Rvec], s2=1.0, op1=ALU.add)
            self.ts("dve", vecs[:, OMKA_OFF:OMKA_OFF + 4], vecs[:, VEC_OFF["k_a"]:VEC_OFF["k_a"] + 4], -1.0, ALU.mult,
                    [Rvec], [Rvec], s2=1.0, op1=ALU.add)
            S.dma(S.misc(), [(KpeT[32:33, :], maskrow[0:1, :])], writes=[R_KpeT])

            def vcol(name, j, p0=0, p1=128):
                o = VEC_OFF[name] + j
                return vecs[p0:p1, o:o + 1]

            banks = [(st0.enter_context(nc.psum_tensor("bank%d" % i, [128, 512], F32)), Res("bank%d" % i)) for i in range(7)]
            bankb = (st0.enter_context(nc.psum_tensor("bankb", [128, 1024], BF16)), Res("bankb"))
            consts = [Rvec, Rrc, Rid, Ridb, Rm4, RmL, Rtri, Rbd, Rones, Ronesf, Rscan]

            xT3 = xT.rearrange("(c p) t -> p c t", p=128)
            w_in3 = w_in.rearrange("(c p) n -> p c n", p=128)
            w_kpe3 = w_kpe.rearrange("(c p) n -> p c n", p=128)
            pi = [0]

            def pbank():
                b_ = banks[pi[0] % 7]
                pi[0] += 1
                return b_

            def make_common(st, ncol):
                cm = {}
                cm["win"] = self.sb(st, "win", [128, 8, ncol], BF16)
                cm["Rwin"] = Res()
                cm["xr"] = self.ring(st, "x1_", [128, 8, 512], F32, 1)
                cm["xch"] = S.chan()
                cm["sqr"] = self.ring(st, "sq1_", [128, 512], BF16, 4)
                cm["hr"] = self.ring(st, "h1_", [128, 8, 512], BF16, 2)
                cm["rstdr"] = self.ring(st, "rstd1_", [128, 512], F32, 2)
                cm["sqt"] = self.ring(st, "sqt1_", [128, 512], F32, 2)
                return cm

            def rms_stats(cm, src3, nchunk, scale, Rsrc):
                ps, Rps = pbank()
                for c in range(nchunk):
                    sq, Rsq = cm["sqr"].next()
                    self.act(sq[:], src3[:, c, :], AF.Square, [Rsrc], [Rsq])
                    self.mm(ps[:, :], onesb[:, :], sq[:], [Rones, Rsq], [Rps], start=(c == 0), stop=(c == nchunk - 1), inc=True)
                t1, Rt1 = cm["sqt"].next()
                self.act(t1[:], ps[:, :], AF.Sqrt, [Rps], [Rt1], bias=RMS_EPS, scale=scale)
                rs, Rrs = cm["rstdr"].next()
                self.recip(rs[:], t1[:], [Rt1], [Rrs])
                return rs, Rrs

            def load_h(cm, t):
                c0 = t * 512
                xt, Rxt = cm["xr"].next()
                S.dma(cm["xch"], [(xt[:], xT3[:, :, c0:c0 + 512])], writes=[Rxt])
                rs, Rrs = rms_stats(cm, xt, 8, 1.0 / D, Rxt)
                h, Rh = cm["hr"].next()
                for c in range(8):
                    self.stt(h[:, c, :], xt[:, c, :], vcol("g_mix", c), rs[:], ALU.mult, ALU.mult, [Rxt, Rvec, Rrs], [Rh])
                return h, Rh

            def zmm(cm, h, Rh, col0, M):
                ps, Rps = pbank()
                for c in range(8):
                    self.mm(ps[0:M, :], cm["win"][:, c, col0:col0 + M], h[:, c, :], [cm["Rwin"], Rh], [Rps], start=(c == 0), stop=(c == 7))
                return ps, Rps

            with ExitStack() as st:
                if self.upto < 1:
                    raise _Stop()
                NCOL = 384 + 256 + 192 + 2048
                OFF_CQ, OFF_CKV, OFF_KPE, OFF_G = 0, 384, 640, 832
                cm = make_common(st, NCOL)
                win, Rwin = cm["win"], cm["Rwin"]
                for c in range(8):
                    S.dma(S.misc("pool"), [(win[:, c, 0:640], w_in3[:, c, 0:640]),
                                     (win[:, c, 640:832], w_kpe3[:, c, :]),
                                     (win[:, c, 832:NCOL], w_in3[:, c, 2464:4512])], writes=[Rwin], q="pool")
                wq = self.sb(st, "wq", [128, 3, 768], BF16)
                wqs = self.sb(st, "wqs", [128, 3, 768], BF16)
                wkk = self.sb(st, "wkk", [128, 2, 512], BF16)
                wkv = self.sb(st, "wkv", [128, 2, 512], BF16)
                Rw1 = Res("w1")
                S.dma(S.misc("pool"), [(wq[:], wq_d.rearrange("(c p) n -> p c n", p=128)),
                                 (wqs[:], wqs_d.rearrange("(c p) n -> p c n", p=128)),
                                 (wkk[:], wkk_d.rearrange("(c p) n -> p c n", p=128)),
                                 (wkv[:], wkv_d.rearrange("(c p) n -> p c n", p=128))],
                      writes=[Rw1], q="pool")
                tmpr = self.ring(st, "tmp1_", [128, 512], F32, 4)
                cq = self.sb(st, "cq", [128, 3, 512]); Rcq = Res()
                cqn = self.sb(st, "cqn", [128, 3, 512], BF16); Rcqn = Res()
                ckv = self.sb(st, "ckv", [128, 2, 512]); Rckv = Res()
                ckvn = self.sb(st, "ckvn", [128, 2, 512], BF16); Rckvn = Res()
                qst = self.ring(st, "qst", [128, 512], BF16, 3)
                qch = [S.chan() for _ in range(3)]
                for (t_, r_) in qst.items:
                    self.memset("pool", t_[64:97, :], 1.0, [r_])
                knst = self.ring(st, "knst", [128, 512], BF16, 2)
                knch = [S.chan() for _ in range(2)]
                vst = self.ring(st, "vst", [128, 4, 520], BF16, 2)
                vch = [S.chan() for _ in range(2)]
                for (t_, r_) in vst.items:
                    self.memset("pool", t_[:], 1.0, [r_])
                kpst = self.ring(st, "kpst", [128, 512], BF16, 2)
                kpch = [S.chan() for _ in range(2)]
                gst = self.ring(st, "gst", [128, 4, 512], BF16, 2)
                gch = [S.chan() for _ in range(2)]
                posi = self.sb(st, "posi", [128, 512], I32); Rposi = Res()
                posch = S.chan()
                rp = [self.sb(st, "rp%d" % i, [128, 512]) for i in range(6)]
                Rrp = [Res() for _ in range(6)]
                ki = self.sb(st, "ki", [128, 512], I32); Rki = Res()
                sl = slice(64, 96)
                for t in range(NT):
                    own = t >= NTP
                    to = t - NTP
                    c0 = t * 512
                    if t == 0:
                        hnext = load_h(cm, 0)
                    h, Rh = hnext
                    S.dma(posch, [(posi[64:96, :], pos[0:1, c0:c0 + 512].partition_broadcast(32))], writes=[Rposi])
                    ang, sinT, cosT, sinQ, cosQ, rr = rp
                    Rang, RsinT, RcosT, RsinQ, RcosQ, Rrr = Rrp
                    self.cp("dve", ang[sl, :], posi[sl, :], [Rposi], [Rang])
                    self.ts("dve", ang[sl, :], ang[sl, :], ropec[sl, 0:1], ALU.mult, [Rang, Rrc], [Rang])
                    self.ts("dve", rr[sl, :], ang[sl, :], float(1.0 / TWO_PI), ALU.mult, [Rang], [Rrr])
                    self.cp("dve", ki[sl, :], rr[sl, :], [Rrr], [Rki])
                    self.cp("dve", rr[sl, :], ki[sl, :], [Rki], [Rrr])
                    self.stt(ang[sl, :], rr[sl, :], -C1, ang[sl, :], ALU.mult, ALU.add, [Rrr, Rang], [Rang])
                    self.stt(ang[sl, :], rr[sl, :], -C2, ang[sl, :], ALU.mult, ALU.add, [Rrr, Rang], [Rang])
                    self.ts("dve", ang[sl, :], ang[sl, :], float(np.pi), ALU.min, [Rang], [Rang], s2=float(-np.pi), op1=ALU.max)
                    self.act(sinT[sl, :], ang[sl, :], AF.Sin, [Rang, Rrc], [RsinT], scale=ropec[sl, 1:2])
                    self.act(rr[sl, :], ang[sl, :], AF.Abs, [Rang], [Rrr])
                    self.ts("dve", rr[sl, :], rr[sl, :], -1.0, ALU.mult, [Rrr], [Rrr], s2=float(np.pi / 2), op1=ALU.add)
                    self.act(cosT[sl, :], rr[sl, :], AF.Sin, [Rrr], [RcosT])
                    if own:
                        self.ts("pool", sinQ[sl, :], sinT[sl, :], SCALE, ALU.mult, [RsinT], [RsinQ])
                        self.ts("pool", cosQ[sl, :], cosT[sl, :], SCALE, ALU.mult, [RcosT], [RcosQ])
                    psA, RpsA = zmm(cm, h, Rh, OFF_KPE, 96)
                    psB, RpsB = zmm(cm, h, Rh, OFF_KPE + 96, 96)
                    ta, Rta = tmpr.next()
                    tb, Rtb = tmpr.next()
                    self.tt("dve", ta[sl, :], psA[sl, :], cosT[sl, :], ALU.mult, [RpsA, RcosT], [Rta])
                    self.tt("dve", tb[sl, :], psB[sl, :], sinT[sl, :], ALU.mult, [RpsB, RsinT], [Rtb])
                    kp, Rkp = kpst.next()
                    self.tt("pool", kp[sl, :], ta[sl, :], tb[sl, :], ALU.add, [Rta, Rtb], [Rkp])
                    S.dma(kpch[t % 2], [(KpeT[0:32, c0:c0 + 512], kp[sl, :])], reads=[Rkp], writes=[R_KpeT])
                    for m in range(2):
                        ps, Rps = zmm(cm, h, Rh, OFF_CKV + m * 128, 128)
                        self.cp("act", ckv[:, m, :], ps[:, :], [Rps], [Rckv])
                    if t + 1 < NT:
                        hnext = load_h(cm, t + 1)
                    rs2, Rrs2 = rms_stats(cm, ckv, 2, 1.0 / 256, Rckv)
                    for m in range(2):
                        self.stt(ckvn[:, m, :], ckv[:, m, :], vcol("g_kv_a", m), rs2[:], ALU.mult, ALU.mult, [Rckv, Rvec, Rrs2], [Rckvn])
                    for m in range(4):
                        ps, Rps = pbank()
                        for c in range(2):
                            self.mm(ps[:, :], wkk[:, c, m * 128:(m + 1) * 128], ckvn[:, c, :], [Rw1, Rckvn], [Rps], start=(c == 0), stop=(c == 1))
                        kn, Rkn = knst.next()
                        kk_ = (knst.i - 1) % 2
                        self.cp("act", kn[:], ps[:, :], [Rps], [Rkn])
                        S.dma(knch[kk_], [(KnT[m * 128:(m + 1) * 128, c0:c0 + 512], kn[:])], reads=[Rkn], writes=[R_KnT])
                    vt_, Rvt = vst.next()
                    vk = (vst.i - 1) % 2
                    for s_ in range(4):
                        ps, Rps = pbank()
                        for c in range(2):
                            self.mm(ps[:, :], ckvn[:, c, s_ * 128:(s_ + 1) * 128], wkv[:, c, :], [Rckvn, Rw1], [Rps], start=(c == 0), stop=(c == 1))
                        v4 = vt_[:, s_, :].rearrange("p (h d) -> p h d", d=65)
                        self.cp("act", v4[:, :, 0:64], ps[:, :].rearrange("p (h d) -> p h d", d=64), [Rps], [Rvt])
                    S.dma(vch[vk], [(Vs[:, t * 4:(t + 1) * 4, :], vt_[:])], reads=[Rvt], writes=[R_Vs])
                    if own:
                        o0 = to * 512
                        for m in range(3):
                            ps, Rps = zmm(cm, h, Rh, OFF_CQ + m * 128, 128)
                            self.cp("act", cq[:, m, :], ps[:, :], [Rps], [Rcq])
                        rs3, Rrs3 = rms_stats(cm, cq, 3, 1.0 / 384, Rcq)
                        for m in range(3):
                            self.stt(cqn[:, m, :], cq[:, m, :], vcol("g_q_a", m), rs3[:], ALU.mult, ALU.mult, [Rcq, Rvec, Rrs3], [Rcqn])
                        for hd in range(NH):
                            psA, RpsA = pbank()
                            psB, RpsB = pbank()
                            for c in range(3):
                                self.mm(psA[0:96, :], wq[:, c, hd * 96:(hd + 1) * 96], cqn[:, c, :], [Rw1, Rcqn], [RpsA], start=(c == 0), stop=(c == 2))
                            for c in range(3):
                                self.mm(psB[0:96, :], wqs[:, c, hd * 96:(hd + 1) * 96], cqn[:, c, :], [Rw1, Rcqn], [RpsB], start=(c == 0), stop=(c == 2))
                            q_, Rq_ = qst.next()
                            qk = (qst.i - 1) % 3
                            self.act(q_[0:64, :], psA[0:64, :], AF.Copy, [RpsA], [Rq_], scale=SCALE)
                            ta, Rta = tmpr.next()
                            tb, Rtb = tmpr.next()
                            self.tt("dve", ta[sl, :], psA[sl, :], cosQ[sl, :], ALU.mult, [RpsA, RcosQ], [Rta])
                            self.tt("dve", tb[sl, :], psB[sl, :], sinQ[sl, :], ALU.mult, [RpsB, RsinQ], [Rtb])
                            self.tt("pool", q_[sl, :], ta[sl, :], tb[sl, :], ALU.add, [Rta, Rtb], [Rq_])
                            S.dma(qch[qk], [(QT[hd, :, o0:o0 + 512], q_[0:97, :])], reads=[Rq_], writes=[R_QT])
                        for gq in range(4):
                            g_, Rg_ = gst.next()
                            gk = (gst.i - 1) % 2
                            for j in range(4):
                                ps, Rps = zmm(cm, h, Rh, OFF_G + (gq * 4 + j) * 128, 128)
                                self.act(g_[:, j, :], ps[:, :], AF.Sigmoid, [Rps], [Rg_])
                            S.dma(gch[gk], [(GT[gq * 512:(gq + 1) * 512, o0:o0 + 512].rearrange("(j p) t -> p j t", p=128), g_[:])],
                                  reads=[Rg_], writes=[R_GT])
                S.barrier()
                S.emit()
                for n_ in S.ENGS:
                    S.e[n_].ops = []

            with ExitStack() as st:
                if self.upto < 2:
                    raise _Stop()
                cm = make_common(st, 1792)
                win, Rwin = cm["win"], cm["Rwin"]
                for c in range(8):
                    S.dma(S.misc("pool"), [(win[:, c, :], w_in3[:, c, 672:2464])], writes=[Rwin], q="pool")
                w2s = self.sb(st, "w2s", [128, 512], BF16)
                a2s = self.sb(st, "a2s", [128, 512], BF16)
                g2s = self.sb(st, "g2s", [128, 512], BF16)
                Rw1 = Res("w1b")
                S.dma(S.misc("pool"), [(w2s[0:64, :], w2_d[:, :]), (a2s[64:128, :], a2_d[:, :]), (g2s[:], g2_d[:, :])], writes=[Rw1], q="pool")
                tmpr = self.ring(st, "tmp1b_", [128, 512], F32, 3)
                tmpb = self.ring(st, "tmpb1_", [128, 512], BF16, 4)
                zcw = self.ring(st, "zcw", [128, 513], F32, 4)
                carry = self.sb(st, "carry", [128, 16]); Rcar = Res()
                self.memset("pool", carry[:], 0.0, [Rcar])
                zsr = self.ring(st, "zsr", [128, 512], F32, 2)
                zsk = self.ring(st, "zsk", [128, 512], F32, 2)
                zsv = self.ring(st, "zsv", [128, 512], F32, 2)
                zs12 = self.sb(st, "zs12", [128, 512]); Rzs12 = Res()
                zs13 = self.sb(st, "zs13", [128, 512]); Rzs13 = Res()
                names = ["sig", "av", "Lc", "Lx", "kk", "nr", "tk", "EP", "EN"]
                nb2 = [{n_: (self.sb(st, "rb%d_" % k_ + n_, [128, 512]), Res(n_)) for n_ in names} for k_ in range(2)]
                twb = self.sb(st, "twb", [128, 512], BF16); Rtwb = Res()
                gsb = self.sb(st, "gsb", [128, 512], BF16); Rgsb = Res()
                rwst = self.ring(st, "rwst", [128, 512], BF16, 12)
                rwch = [S.chan() for _ in range(12)]
                rwi = [0]
                pcst = self.ring(st, "pcst", [128, 8], F32, 4)
                pcch = [S.chan() for _ in range(4)]

                def rw_store(dst_ap, src_fn, eng_fn):
                    k = rwi[0] % 12
                    rwi[0] += 1
                    t_, r_ = rwst.items[k]
                    eng_fn(t_, r_)
                    S.dma(rwch[k], [(dst_ap, src_fn(t_))], reads=[r_], writes=[R_rw])

                MU, OM = VEC_OFF["mu"], OM_OFF

                def shift(cm, h, Rh, m, dst, Rdst):
                    ps, Rps = zmm(cm, h, Rh, m * 128, 128)
                    zc, Rzc = zcw.next()
                    self.act(zc[:, 1:513], ps[:, :], AF.Copy, [Rps, Rvec], [Rzc], scale=vecs[:, MU + m:MU + m + 1])
                    self.cp("pool", zc[:, 0:1], carry[:, m:m + 1], [Rcar], [Rzc])
                    self.stt(dst[:], ps[:, :], vecs[:, OM + m:OM + m + 1], zc[:, 0:512], ALU.mult, ALU.add, [Rps, Rvec, Rzc], [Rdst])
                    self.cp("pool", carry[:, m:m + 1], zc[:, 512:513], [Rzc], [Rcar])

                for t in range(NT):
                    own = t >= NTP
                    o0 = (t - NTP) * 512
                    c0 = t * 512
                    if t == 0:
                        hnext = load_h(cm, 0)
                    h, Rh = hnext
                    shift(cm, h, Rh, 12, zs12, Rzs12)
                    shift(cm, h, Rh, 13, zs13, Rzs13)
                    if t + 1 < NT:
                        hnext = load_h(cm, t + 1)
                    self.act(twb[0:64, :], zs12[0:64, :], AF.Tanh, [Rzs12], [Rtwb])
                    self.cp("pool", twb[64:128, :], zs12[64:128, :], [Rzs12], [Rtwb])
                    self.act(gsb[:], zs13[:], AF.Sigmoid, [Rzs13], [Rgsb])
                    for m in range(4):
                        r_m, Rr = zsr.next(); k_m, Rk = zsk.next(); v_m, Rv = zsv.next()
                        shift(cm, h, Rh, m, r_m, Rr)
                        shift(cm, h, Rh, 4 + m, k_m, Rk)
                        shift(cm, h, Rh, 8 + m, v_m, Rv)
                        ms = slice(m * 128, (m + 1) * 128)
                        (sig, Rsig), (av, Rav), (Lc, RLc), (Lx, RLx), (kk, Rkk), (nr, Rnr), (tk, Rtk), (EP, REP), (EN, REN) = [nb2[m % 2][n_] for n_ in names]
                        ps, Rps = pbank()
                        self.mm(ps[:, :], w2s[0:64, ms], twb[0:64, :], [Rw1, Rtwb], [Rps])
                        self.act(sig[:], ps[:, :], AF.Sigmoid, [Rps, Rvec], [Rsig], bias=vcol("w0", m))
                        ps, Rps = pbank()
                        self.mm(ps[:, :], a2s[64:128, ms], twb[64:128, :], [Rw1, Rtwb], [Rps])
                        self.act(av[:], ps[:, :], AF.Sigmoid, [Rps, Rvec], [Rav], bias=vcol("a0", m))
                        S.op("dve", lambda e, o=Lc, d=sig: e.tensor_tensor_scan(out=o[:], data0=scanm[:], data1=d[:], initial=0.0,
                                                                              op0=ALU.mult, op1=ALU.add), [Rscan, Rsig], [RLc])
                        self.act(kk[:], k_m[:], AF.Copy, [Rk, Rvec], [Rkk], scale=vcol("k_k", m))
                        kk2, Rkk2 = tmpb.next()
                        self.act(kk2[:], k_m[:], AF.Square, [Rk, Rvec], [Rkk2], scale=vcol("k_k", m))
                        ps, Rps = pbank()
                        self.mm(ps[:, :], bdb[:, :], kk2[:], [Rbd, Rkk2], [Rps])
                        self.act(nr[:], ps[:, :], AF.Sqrt, [Rps], [Rnr])
                        self.ts("dve", nr[:], nr[:], 1e-12, ALU.max, [Rnr], [Rnr])
                        self.recip(nr[:], nr[:], [Rnr], [Rnr])
                        self.tt("dve", kk[:], kk[:], nr[:], ALU.mult, [Rkk, Rnr], [Rkk])
                        self.ts("dve", tk[:], av[:], vcol("k_a", m), ALU.mult, [Rav, Rvec], [Rtk],
                                s2=vecs[:, OMKA_OFF + m:OMKA_OFF + m + 1], op1=ALU.add)
                        self.tt("dve", tk[:], tk[:], k_m[:], ALU.mult, [Rtk, Rk], [Rtk])
                        self.act(EP[:], Lc[:], AF.Exp, [RLc], [REP], scale=-EXPH)
                        self.act(EN[:], Lc[:], AF.Exp, [RLc], [REN], scale=EXPH)
                        if own:
                            rk, Rrk = tmpb.next()
                            self.stt(rk[:], r_m[:], vcol("r_k", m), tk[:], ALU.mult, ALU.mult, [Rr, Rvec, Rtk], [Rrk])
                            psb, Rpsb = pbank()
                            self.mm(psb[:, :], bdb[:, :], rk[:], [Rbd, Rrk], [Rpsb])
                            rw_store(BoT[ms, o0:o0 + 512], lambda t_: t_[:],
                                     lambda t_, r_: self.tt("dve", t_[:], psb[:, :], v_m[:], ALU.mult, [Rpsb, Rv], [r_]))
                            psg, Rpsg = pbank()
                            self.mm(psg[:, :], g2s[:, ms], gsb[:], [Rw1, Rgsb], [Rpsg])
                            rw_store(GrT[ms, o0:o0 + 512], lambda t_: t_[:],
                                     lambda t_, r_: self.cp("act", t_[:], psg[:, :], [Rpsg], [r_]))
                        ARv = ARt[ms, t * 4:(t + 1) * 4, :]

                        def a_tilde(t_, r_):
                            self.stt(t_[:, 1:512], kk[:, 1:512], -1.0, EP[:, 0:511], ALU.mult, ALU.mult, [Rkk, REP], [r_])
                            self.ts("pool", t_[:].rearrange("p (c t) -> p c t", t=64)[:, :, 0:1],
                                    kk[:].rearrange("p (c t) -> p c t", t=64)[:, :, 0:1], -1.0, ALU.mult, [Rkk], [r_])

                        rw_store(ARv[:, :, 0:128], lambda t_: t_[:].rearrange("p (b t) -> p b t", t=128), a_tilde)
                        rw_store(ARv[:, :, 128:256], lambda t_: t_[:].rearrange("p (b t) -> p b t", t=128),
                                 lambda t_, r_: self.tt("pool", t_[:], r_m[:], EP[:], ALU.mult, [Rr, REP], [r_]))
                        self.tt("dve", Lx[:], kk[:], av[:], ALU.mult, [Rkk, Rav], [RLx])
                        rw_store(Bt[ms, c0:c0 + 512], lambda t_: t_[:],
                                 lambda t_, r_: self.tt("dve", t_[:], Lx[:], EN[:], ALU.mult, [RLx, REN], [r_]))
                        rw_store(Kt[ms, c0:c0 + 512], lambda t_: t_[:],
                                 lambda t_, r_: self.tt("pool", t_[:], tk[:], EN[:], ALU.mult, [Rtk, REN], [r_]))
                        rw_store(Vt[ms, c0:c0 + 512], lambda t_: t_[:],
                                 lambda t_, r_: self.cp("act", t_[:], v_m[:], [Rv], [r_]))
                        pc, Rpc = pcst.next()
                        pk = (pcst.i - 1) % 4
                        self.cp("pool", pc[:], EP[:].rearrange("p (c t) -> p c t", t=64)[:, :, 63], [REP], [Rpc])
                        S.dma(pcch[pk], [(PCt[ms, t * 8:(t + 1) * 8], pc[:])], reads=[Rpc], writes=[R_rw])
                S.barrier()
                S.emit()
                for n_ in S.ENGS:
                    S.e[n_].ops = []

            with ExitStack() as st:
                if self.upto < 3:
                    raise _Stop()
                for i in range(8):
                    S.dma(S.misc("pool"), [(wupb[i * 128:(i + 1) * 128, :], wup_d[i * 128:(i + 1) * 128, :])], writes=[R_wconv], q="pool")
                for i in range(8):
                    S.dma(S.misc("pool"), [(wdnb[i * 512:(i + 1) * 512, :], wdn_d[i * 512:(i + 1) * 512, :])], writes=[R_wconv], q="pool")
                arl = self.ring(st, "arl", [128, 4, 256], BF16, 2)
                btl = self.ring(st, "btl", [128, 512], BF16, 2)
                ktl = self.ring(st, "ktl", [128, 512], BF16, 2)
                vtl = self.ring(st, "vtl", [128, 512], BF16, 2)
                pcl = self.ring(st, "pcl", [128, 8], F32, 2)
                bol = self.ring(st, "bol", [128, 512], BF16, 2)
                grl = self.ring(st, "grl", [128, 512], BF16, 2)
                ldch = [S.chan(), S.chan()]
                MT = self.ring(st, "MT", [128, 2, 512], BF16, 8)
                Lr = self.ring(st, "Lr", [128, 2, 128], BF16, 12)
                ASr = self.ring(st, "ASr", [128, 2, 256], BF16, 12)
                TM = self.ring(st, "TM", [128, 4, 128], BF16, 8)
                Xb = self.ring(st, "Xb", [128, 2, 128], BF16, 8)
                TXr = self.ring(st, "TXr", [128, 2, 128], BF16, 8)
                TXf = self.ring(st, "TXf", [128, 2, 128], F32, 8)
                Bf = self.ring(st, "Bf", [128, 128], F32, 8)
                RqT = self.ring(st, "RqT", [128, 128], F32, 8)
                Y0 = self.ring(st, "Y0", [128, 2, 64], F32, 8)
                GTr = self.ring(st, "GTr", [128, 64], F32, 16)
                Fr = self.ring(st, "Fr", [128, 64], F32, 16)
                Hs = self.ring(st, "Hs", [128, 64], F32, 3)
                Yr = self.ring(st, "Yr", [128, 2, 64], F32, 4)
                gn = self.ring(st, "gn", [128, 2, 64], F32, 4)
                gs_ = self.ring(st, "gs_", [128, 2], F32, 6)
                yT = self.ring(st, "yT", [128, 512], F32, 2)
                yst = self.ring(st, "yst", [128, 512], BF16, 2)
                ych = [S.chan(), S.chan()]
                pi2 = [0]

                def pb2():
                    b_ = banks[pi2[0] % 7]
                    pi2[0] += 1
                    return b_

                e2 = lambda ap: ap.rearrange("p (e s) -> p e s", e=2)
                idb3 = identb[:].unsqueeze(1).broadcast_to([128, 2, 128])
                NBK = 4
                for m in range(4):
                    ms = slice(m * 128, (m + 1) * 128)
                    H, RH = Hs.next()
                    self.memset("pool", H[:], 0.0, [RH])
                    for t in range(NT):
                        own = t >= NTP
                        o0 = (t - NTP) * 512
                        c0 = t * 512
                        ar, Rar = arl.next(); bt, Rbt = btl.next(); kt, Rkt = ktl.next(); vt, Rvt = vtl.next(); pc, Rpc = pcl.next()
                        lk = (arl.i - 1) % 2
                        pairs = [(ar[:], ARt[ms, t * 4:(t + 1) * 4, :]), (bt[:], Bt[ms, c0:c0 + 512]), (kt[:], Kt[ms, c0:c0 + 512]),
                                 (vt[:], Vt[ms, c0:c0 + 512]), (pc[:], PCt[ms, t * 8:(t + 1) * 8])]
                        wr = [Rar, Rbt, Rkt, Rvt, Rpc]
                        if own:
                            bo, Rbo = bol.next(); gr, Rgr = grl.next()
                            pairs += [(bo[:], BoT[ms, o0:o0 + 512]), (gr[:], GrT[ms, o0:o0 + 512])]
                            wr += [Rbo, Rgr]
                            yt_, Ryt = yT.next()
                        S.dma(ldch[lk], pairs, reads=[R_rw], writes=wr)
                        X = [dict() for _ in range(NBK)]
                        for b in range(NBK):
                            c_ = X[b]
                            bs = slice(b * 128, (b + 1) * 128)
                            c_["bs"] = bs
                            mt, Rmt = MT.next()
                            L0, RL0 = Lr.next()
                            for e in range(2):
                                hs = slice(e * 64, (e + 1) * 64)
                                ps, Rps = pb2()
                                self.mm(ps[:, 0:256], bt[hs, bs], ar[hs, b, :], [Rbt, Rar], [Rps])
                                self.mm(ps[:, 256:512], kt[hs, bs], ar[hs, b, :], [Rkt, Rar], [Rps])
                                self.tt("dve", mt[:, e, :], ps[:, :], mask4[:], ALU.mult, [Rps, Rm4], [Rmt])
                                psL, RpsL = pb2()
                                self.mm(psL[:, 0:128], ar[hs, b, 0:128], bt[hs, bs], [Rar, Rbt], [RpsL])
                                self.tt("dve", L0[:, e, :], psL[:, 0:128], maskL[:], ALU.mult, [RpsL, RmL], [RL0])
                            c_["mt"], c_["Rmt"], c_["L"], c_["RL"] = mt, Rmt, L0, RL0
                        for b in range(NBK):
                            c_ = X[b]
                            bs = c_["bs"]
                            pst, Rpst = bankb
                            o_ = (b % 2) * 512
                            self.tr(pst[:, o_ + 0:o_ + 128], ar[:, b, 0:128], identb[:], [Rar, Ridb], [Rpst])
                            self.tr(pst[:, o_ + 128:o_ + 256], bt[:, bs], identb[:], [Rbt, Ridb], [Rpst])
                            self.tr(pst[:, o_ + 256:o_ + 384], kt[:, bs], identb[:], [Rkt, Ridb], [Rpst])
                            self.tr(pst[:, o_ + 384:o_ + 512], vt[:, bs], identb[:], [Rvt, Ridb], [Rpst])
                            tm, Rtm = TM.next()
                            self.cp("act", tm[:], pst[:, o_:o_ + 512].rearrange("p (q f) -> p q f", q=4), [Rpst], [Rtm])
                            bf_, Rbf = Bf.next()
                            self.cp("pool", bf_[:], tm[:, 1, :], [Rtm], [Rbf])
                            c_["tm"], c_["Rtm"], c_["bf"], c_["Rbf"] = tm, Rtm, bf_, Rbf
                        for b in range(NBK):
                            c_ = X[b]
                            mt, Rmt = c_["mt"], c_["Rmt"]
                            AS, RAS = ASr.next()
                            self.cp("pool", AS[:, :, 0:128], mt[:, :, 0:128], [Rmt], [RAS])
                            self.tt("pool", AS[:, :, 128:256], mt[:, :, 0:128], idb3, ALU.add, [Rmt, Ridb], [RAS])
                            c_["AS"], c_["RAS"] = AS, RAS
                        for b in range(NBK):
                            c_ = X[b]
                            AS, RAS, Lk, RLk = c_["AS"], c_["RAS"], c_["L"], c_["RL"]
                            psa, Rpsa = pb2()
                            psl, Rpsl = pb2()
                            for e in range(2):
                                self.mm(psa[:, e * 128:(e + 1) * 128], Lk[:, e, :], AS[:, e, 0:128], [RLk, RAS], [Rpsa])
                                self.mm(psl[:, e * 128:(e + 1) * 128], AS[:, e, 0:128], Lk[:, e, :], [RAS, RLk], [Rpsl])
                            ASn, RASn = ASr.next()
                            self.cp("dve", ASn[:, :, 0:128], e2(psa[:, 0:256]), [Rpsa], [RASn])
                            self.cp("pool", ASn[:, :, 128:256], AS[:, :, 128:256], [RAS], [RASn])
                            Ln, RLn = Lr.next()
                            self.cp("act", Ln[:], e2(psl[:, 0:256]), [Rpsl], [RLn])
                            c_["AS"], c_["RAS"], c_["L"], c_["RL"] = ASn, RASn, Ln, RLn
                        for it in range(5):
                            last = it == 4
                            w_ = 128 if last else 0
                            for b in range(NBK):
                                c_ = X[b]
                                AS, RAS, Lk, RLk = c_["AS"], c_["RAS"], c_["L"], c_["RL"]
                                psm, Rpsm = pb2()
                                for e in range(2):
                                    self.mm(psm[:, e * 256 + w_:(e + 1) * 256], Lk[:, e, :], AS[:, e, w_:256], [RLk, RAS], [Rpsm])
                                ASn, RASn = ASr.next()
                                pm3 = e2(psm[:, 0:512])
                                self.tt("dve", ASn[:, :, 128:256], pm3[:, :, 128:256], AS[:, :, 128:256], ALU.add, [Rpsm, RAS], [RASn])
                                if not last:
                                    self.cp("act", ASn[:, :, 0:128], pm3[:, :, 0:128], [Rpsm], [RASn])
                                    psl, Rpsl = pb2()
                                    for e in range(2):
                                        self.mm(psl[:, e * 128:(e + 1) * 128], AS[:, e, 0:128], Lk[:, e, :], [RAS, RLk], [Rpsl])
                                    Ln, RLn = Lr.next()
                                    self.cp("act", Ln[:], e2(psl[:, 0:256]), [Rpsl], [RLn])
                                    c_["L"], c_["RL"] = Ln, RLn
                                c_["AS"], c_["RAS"] = ASn, RASn
                        for b in range(NBK):
                            c_ = X[b]
                            mt, Rmt, tm, Rtm = c_["mt"], c_["Rmt"], c_["tm"], c_["Rtm"]
                            xb, Rxb = Xb.next()
                            psv, Rpsv = pb2()
                            for e in range(2):
                                self.mm(psv[:, e * 64:(e + 1) * 64], mt[:, e, 256:384], tm[:, 3, e * 64:(e + 1) * 64], [Rmt, Rtm], [Rpsv])
                            self.cp("act", xb[:, :, 64:128], psv[:, 0:128].rearrange("p (e v) -> p e v", e=2), [Rpsv], [Rxb])
                            self.cp("pool", xb[:, :, 0:64], tm[:, 0, :].rearrange("p (e v) -> p e v", e=2), [Rtm], [Rxb])
                            c_["xb"], c_["Rxb"] = xb, Rxb
                        for b in range(NBK):
                            c_ = X[b]
                            AS, RAS, xb, Rxb = c_["AS"], c_["RAS"], c_["xb"], c_["Rxb"]
                            pstx, Rpstx = pb2()
                            for e in range(2):
                                self.mm(pstx[:, e * 128:(e + 1) * 128], AS[:, e, 128:256], xb[:, e, :], [RAS, Rxb], [Rpstx])
                            tx, Rtx = TXr.next()
                            txf, Rtxf = TXf.next()
                            self.cp("act", tx[:], e2(pstx[:, 0:256]), [Rpstx], [Rtx])
                            self.cp("act", txf[:], e2(pstx[:, 0:256]), [Rpstx], [Rtxf])
                            c_["tx"], c_["Rtx"], c_["txf"], c_["Rtxf"] = tx, Rtx, txf, Rtxf
                        if own:
                            for b in range(NBK):
                                c_ = X[b]
                                mt, Rmt, tm, Rtm, tx, Rtx = c_["mt"], c_["Rmt"], c_["tm"], c_["Rtm"], c_["tx"], c_["Rtx"]
                                psr, Rpsr = pb2()
                                for e in range(2):
                                    self.mm(psr[e * 64:(e + 1) * 64, 0:128], tx[:, e, 0:64], mt[:, e, 128:256], [Rtx, Rmt], [Rpsr])
                                rq, Rrq = RqT.next()
                                self.tt("dve", rq[:], psr[:, 0:128], ar[:, b, 128:256], ALU.add, [Rpsr, Rar], [Rrq])
                                psy, Rpsy = pb2()
                                for e in range(2):
                                    self.mm(psy[:, e * 64:(e + 1) * 64], mt[:, e, 128:256], tx[:, e, 64:128], [Rmt, Rtx], [Rpsy], start=True, stop=False)
                                    self.mm(psy[:, e * 64:(e + 1) * 64], mt[:, e, 384:512], tm[:, 3, e * 64:(e + 1) * 64], [Rmt, Rtm], [Rpsy], start=False, stop=True)
                                y0, Ry0 = Y0.next()
                                self.cp("act", y0[:], psy[:, 0:128].rearrange("p (e v) -> p e v", e=2), [Rpsy], [Ry0])
                                c_["rq"], c_["Rrq"], c_["y0"], c_["Ry0"] = rq, Rrq, y0, Ry0
                        for b in range(NBK):
                            c_ = X[b]
                            tm, Rtm, tx, Rtx, txf, Rtxf, bf_, Rbf = c_["tm"], c_["Rtm"], c_["tx"], c_["Rtx"], c_["txf"], c_["Rtxf"], c_["bf"], c_["Rbf"]
                            c_["gt"], c_["ff"] = [], []
                            for c in range(2):
                                cs = slice(c * 64, (c + 1) * 64)
                                psg, Rpsg = pb2()
                                for e in range(2):
                                    self.mm(psg[e * 64:(e + 1) * 64, 0:64], txf[cs, e, 0:64], bf_[cs, e * 64:(e + 1) * 64], [Rtxf, Rbf], [Rpsg])
                                gt, Rgt = GTr.next()
                                self.tt("dve", gt[0:64, :], psg[0:64, 0:64], ident[0:64, 0:64], ALU.add, [Rpsg, Rid], [Rgt])
                                self.tt("dve", gt[64:128, :], psg[64:128, 0:64], ident[64:128, 64:128], ALU.add, [Rpsg, Rid], [Rgt])
                                psf, Rpsf = pb2()
                                for e in range(2):
                                    es = slice(e * 64, (e + 1) * 64)
                                    self.mm(psf[es, 0:64], tm[cs, 1, es], tx[cs, e, 64:128], [Rtm, Rtx], [Rpsf], start=True, stop=False)
                                    self.mm(psf[es, 0:64], tm[cs, 2, es], tm[cs, 3, es], [Rtm], [Rpsf], start=False, stop=True)
                                ff, Rff = Fr.next()
                                pcc = pc[:, b * 2 + c:b * 2 + c + 1]
                                self.act(ff[:], psf[:, 0:64], AF.Copy, [Rpsf, Rpc], [Rff], scale=pcc)
                                c_["gt"].append((gt, Rgt))
                                c_["ff"].append((ff, Rff))
                        for b in range(NBK):
                            c_ = X[b]
                            bs = c_["bs"]
                            if own:
                                yy, Ryy = Yr.next()
                                rq, Rrq, y0, Ry0 = c_["rq"], c_["Rrq"], c_["y0"], c_["Ry0"]
                            for c in range(2):
                                cs = slice(c * 64, (c + 1) * 64)
                                gt, Rgt = c_["gt"][c]
                                ff, Rff = c_["ff"][c]
                                if own:
                                    for e in range(2):
                                        es = slice(e * 64, (e + 1) * 64)
                                        psq, Rpsq = pb2()
                                        self.mm(psq[cs, 0:64], rq[es, cs], H[es, :], [Rrq, RH], [Rpsq])
                                        self.tt("dve", yy[cs, e, :], psq[cs, 0:64], y0[cs, e, :], ALU.add, [Rpsq, Ry0], [Ryy])
                                Hn, RHn = Hs.next()
                                for e in range(2):
                                    es = slice(e * 64, (e + 1) * 64)
                                    psh, Rpsh = pb2()
                                    self.mm(psh[es, 0:64], gt[es, :], H[es, :], [Rgt, RH], [Rpsh])
                                    self.stt(Hn[es, :], psh[es, 0:64], pc[es, b * 2 + c:b * 2 + c + 1], ff[es, :], ALU.mult, ALU.add, [Rpsh, Rpc, Rff], [RHn])
                                H, RH = Hn, RHn
                            if own:
                                s1, Rs1 = gs_.next()
                                S.op("dve", lambda e, o=s1, i=yy: e.tensor_reduce(out=o[:], in_=i[:], axis=AX.X, op=ALU.add), [Ryy], [Rs1])
                                self.ts("dve", s1[:], s1[:], -1.0 / 64, ALU.mult, [Rs1], [Rs1])
                                yc, Ryc = gn.next()
                                self.tt("dve", yc[:], yy[:], s1[:].unsqueeze(2).broadcast_to([128, 2, 64]), ALU.add, [Ryy, Rs1], [Ryc])
                                y2, Ry2 = gn.next()
                                self.tt("pool", y2[:], yc[:], yc[:], ALU.mult, [Ryc], [Ry2])
                                s2, Rs2 = gs_.next()
                                S.op("dve", lambda e, o=s2, i=y2: e.tensor_reduce(out=o[:], in_=i[:], axis=AX.X, op=ALU.add), [Ry2], [Rs2])
                                self.act(s2[:], s2[:], AF.Sqrt, [Rs2], [Rs2], bias=GN_EPS, scale=1.0 / 64)
                                self.recip(s2[:], s2[:], [Rs2], [Rs2])
                                self.tt("dve", yc[:], yc[:], s2[:].unsqueeze(2).broadcast_to([128, 2, 64]), ALU.mult, [Ryc, Rs2], [Ryc])
                                pstt, Rpstt = pb2()
                                self.tr(pstt[:, 0:128], yc[:].rearrange("p e v -> p (e v)"), ident[:], [Ryc, Rid], [Rpstt])
                                self.act(yt_[:, bs], pstt[:, 0:128], AF.Identity, [Rpstt, Rvec], [Ryt], bias=vcol("ln_b", m), scale=vcol("ln_w", m))
                        if own:
                            self.tt("pool", yt_[:], yt_[:], bo[:], ALU.add, [Ryt, Rbo], [Ryt])
                            ys, Rys = yst.next()
                            yk = (yst.i - 1) % 2
                            self.tt("pool", ys[:], yt_[:], gr[:], ALU.mult, [Ryt, Rgr], [Rys])
                            S.dma(ych[yk], [(YbT[ms, o0:o0 + 512], ys[:])], reads=[Rys], writes=[R_YbT])
                S.barrier()
                S.emit()
                for n_ in S.ENGS:
                    S.e[n_].ops = []

            with ExitStack() as st:
                if self.upto < 4:
                    raise _Stop()
                Vall = self.sb(st, "Vall", [128, NB, 520], BF16); RVall = Res()
                nv = max(1, NB // 16)
                for i in range(0, NB, 16):
                    j = min(NB, i + 16)
                    S.dma(S.misc(), [(Vall[:, i:j, :], Vs[:, i:j, :])], reads=[R_Vs], writes=[RVall])
                Kh = self.ring(st, "Kh", [128, TT], BF16, 2)
                Qh = self.ring(st, "Qh", [128, TO], BF16, 2)
                kqch = [S.chan(), S.chan()]
                PT = self.ring(st, "PT", [128, 512], BF16, 6)
                osb = self.ring(st, "osb", [128, 512], F32, 2)
                rl = self.ring(st, "rl", [128, 512], F32, 2)
                ost = self.ring(st, "ost", [128, 512], BF16, 2)
                och = [S.chan(), S.chan()]
                sbanks = Ring(banks[0:4])
                obanks = Ring(banks[4:6])
                bbanks = Ring(banks[6:7])
                V5 = Vall[:].rearrange("p b (h d) -> p b h d", d=65)
                LOOK = 2
                heads = []

                def load_head(hd):
                    K_, RK_ = Kh.next(); Q_, RQ_ = Qh.next()
                    hk = (Kh.i - 1) % 2
                    S.dma(kqch[hk], [(K_[0:64, :], KnT[hd * 64:(hd + 1) * 64, :]), (K_[64:97, :], KpeT[:, :]), (Q_[0:97, :], QT[hd, :, :])],
                          reads=[R_KnT, R_KpeT, R_QT], writes=[RK_, RQ_])
                    return (K_, RK_, Q_, RQ_)

                pend = []

                def do_pv(item):
                    (hd, qt, kb, nkb, cst, pt, Rpt, po, Rpo) = item
                    self.mm(po[0:65, cst:512], V5[:, kb, hd, :], pt[:, cst:512], [RVall, Rpt], [Rpo], start=(kb == 0), stop=(kb == nkb - 1), inc=True)
                    if kb == nkb - 1:
                        q0 = qt * 512
                        o_, Ro_ = osb.next()
                        self.cp("act", o_[0:65, :], po[0:65, :], [Rpo], [Ro_])
                        r_, Rr_ = rl.next()
                        self.recip(r_[64:65, :], o_[64:65, :], [Ro_], [Rr_])
                        pb_, Rpb_ = bbanks.next()
                        self.mm(pb_[0:64, :], onesf[64:65, 0:64], r_[64:65, :], [Ronesf, Rr_], [Rpb_])
                        os_, Ros_ = ost.next()
                        ok = (ost.i - 1) % 2
                        self.tt("dve", os_[0:64, :], pb_[0:64, :], o_[0:64, :], ALU.mult, [Rpb_, Ro_], [Ros_])
                        S.dma(och[ok], [(OaT[hd * 64:(hd + 1) * 64, q0:q0 + 512], os_[0:64, :])], reads=[Ros_], writes=[R_OaT])

                nxt = load_head(0)
                for hd in range(NH):
                    K_, RK_, Q_, RQ_ = nxt
                    if hd + 1 < NH:
                        nxt = load_head(hd + 1)
                    for qt in range(NTO):
                        q0 = qt * 512
                        nkb = (TP + q0 + 512) // 128
                        po, Rpo = obanks.next()
                        for kb in range(nkb):
                            jd = kb - (nkb - 4)
                            cst = 0 if jd < 0 else jd * 128
                            ps, Rps = sbanks.next()
                            self.mm(ps[:, cst:512], K_[0:97, kb * 128:(kb + 1) * 128], Q_[0:97, q0 + cst:q0 + 512], [RK_, RQ_], [Rps])
                            pt, Rpt = PT.next()
                            self.act(pt[:, cst:512], ps[:, cst:512], AF.Exp, [Rps], [Rpt])
                            if jd >= 0:
                                self.tt("pool", pt[:, cst:cst + 128], pt[:, cst:cst + 128], trim[:], ALU.mult, [Rpt, Rtri], [Rpt])
                            pend.append((hd, qt, kb, nkb, cst, pt, Rpt, po, Rpo))
                            if len(pend) > LOOK:
                                do_pv(pend.pop(0))
                while pend:
                    do_pv(pend.pop(0))
                S.barrier()
                S.emit()
                for n_ in S.ENGS:
                    S.e[n_].ops = []

            with ExitStack() as st:
                if self.upto < 5:
                    raise _Stop()
                womla = self.sb(st, "womla", [128, 4, D], BF16)
                worw = self.sb(st, "worw", [128, 4, D], BF16)
                wout = self.sb(st, "wout", [128, 8, D], BF16)
                wpg = self.sb(st, "wpg", [128, 8, D], BF16)
                wpp = self.sb(st, "wpp", [128, 2, D], BF16)
                Rw4 = Res("w4")
                S.dma(S.misc("pool"), [(womla[:], womla_d.rearrange("(c p) n -> p c n", p=128)),
                                 (worw[:], worw_d.rearrange("(c p) n -> p c n", p=128)),
                                 (wpp[:], wpp_d.rearrange("(c p) n -> p c n", p=128))], writes=[Rw4], q="pool")
                wout3 = wout_d.rearrange("(c p) n -> p c n", p=128)
                wpg3 = wpg_d.rearrange("(c p) n -> p c n", p=128)
                for c in range(8):
                    S.dma(S.misc("pool"), [(wout[:, c, :], wout3[:, c, :]), (wpg[:, c, :], wpg3[:, c, :])], writes=[Rw4], q="pool")
                wupr = self.ring(st, "wupr", [128, 8, 256], BF16, 3)
                wupc = [S.chan() for _ in range(3)]
                wdnr = self.ring(st, "wdnr", [128, 2, 512], BF16, 4)
                wdnc = [S.chan() for _ in range(4)]
                wup3 = wupb.rearrange("(c p) n -> p c n", p=128)
                wdn3 = wdnb.rearrange("(c p) n -> p c n", p=128)
                x4 = self.ring(st, "x4_", [128, 8, 512], F32, 1)
                xc4 = S.chan()
                inb = self.ring(st, "inb", [128, 4, 512], BF16, 2)
                inc_ = [S.chan(), S.chan()]
                gtl = self.ring(st, "gtl", [128, 8, 512], BF16, 2)
                gtc = [S.chan(), S.chan()]
                pin = self.sb(st, "pin", [128, 2, 512], BF16); Rpin = Res()
                pinc = S.chan()
                mix = self.sb(st, "mix", [128, 8, 512], BF16); Rmix = Res()
                mtmp = self.ring(st, "mtmp", [128, 512], F32, 3)
                sq4 = self.ring(st, "sq4", [128, 512], BF16, 4)
                h4 = self.sb(st, "h4", [128, 8, 512], BF16); Rh4 = Res()
                hid = self.sb(st, "hid", [128, 16, 512], BF16)
                Rhid = [Res() for _ in range(16)]
                rel = self.ring(st, "rel", [128, 512], F32, 3)
                rs4 = self.ring(st, "rs4", [128, 512], F32, 2)
                t4 = self.ring(st, "t4", [128, 512], F32, 2)
                gsg = self.ring(st, "gsg", [128, 512], F32, 2)
                osb4 = self.ring(st, "osb4", [128, 512], F32, 2)
                och4 = [S.chan(), S.chan()]
                pi4 = [0]

                def pb4():
                    b_ = banks[pi4[0] % 7]
                    pi4[0] += 1
                    return b_

                def rms4(xt, Rxt):
                    ps, Rps = pb4()
                    for c in range(8):
                        sq, Rsq = sq4.next()
                        self.act(sq[:], xt[:, c, :], AF.Square, [Rxt], [Rsq])
                        self.mm(ps[:, :], onesb[:, :], sq[:], [Rones, Rsq], [Rps], start=(c == 0), stop=(c == 7), inc=True)
                    t1, Rt1 = t4.next()
                    self.act(t1[:], ps[:, :], AF.Sqrt, [Rps], [Rt1], bias=RMS_EPS, scale=1.0 / D)
                    rs, Rrs = rs4.next()
                    self.recip(rs[:], t1[:], [Rt1], [Rrs])
                    return rs, Rrs

                for to in range(NTO):
                    o0 = to * 512
                    xt, Rxt = x4.next()
                    S.dma(xc4, [(xt[:], xT3[:, :, TP + o0:TP + o0 + 512])], writes=[Rxt])
                    oa, Roa = inb.next()
                    S.dma(inc_[0], [(oa[:], OaT[:, o0:o0 + 512].rearrange("(c p) t -> p c t", p=128))], reads=[R_OaT], writes=[Roa])
                    yb, Ryb = inb.next()
                    S.dma(inc_[1], [(yb[:], YbT[:, o0:o0 + 512].rearrange("(c p) t -> p c t", p=128))], reads=[R_YbT], writes=[Ryb])
                    ga, Rga = gtl.next()
                    S.dma(gtc[0], [(ga[:], GT[0:1024, o0:o0 + 512].rearrange("(c p) t -> p c t", p=128))], reads=[R_GT], writes=[Rga])
                    gb, Rgb = gtl.next()
                    S.dma(gtc[1], [(gb[:], GT[1024:2048, o0:o0 + 512].rearrange("(c p) t -> p c t", p=128))], reads=[R_GT], writes=[Rgb])
                    S.dma(pinc, [(pin[:], pT[:, o0:o0 + 512].rearrange("(c p) t -> p c t", p=128))], writes=[Rpin], q="pool")
                    for mt_ in range(8):
                        msl = slice(mt_ * 128, (mt_ + 1) * 128)
                        psa, Rpsa = pb4()
                        for c in range(4):
                            self.mm(psa[:, :], womla[:, c, msl], oa[:, c, :], [Rw4, Roa], [Rpsa], start=(c == 0), stop=(c == 3))
                        psb, Rpsb = pb4()
                        for c in range(4):
                            self.mm(psb[:, :], worw[:, c, msl], yb[:, c, :], [Rw4, Ryb], [Rpsb], start=(c == 0), stop=(c == 3))
                        m1, Rm1 = mtmp.next()
                        self.tt("dve", m1[:], psa[:, :], ga[:, mt_, :], ALU.mult, [Rpsa, Rga], [Rm1])
                        m2, Rm2 = mtmp.next()
                        self.tt("dve", m2[:], psb[:, :], gb[:, mt_, :], ALU.mult, [Rpsb, Rgb], [Rm2])
                        self.tt("pool", mix[:, mt_, :], m1[:], m2[:], ALU.add, [Rm1, Rm2], [Rmix])
                    for mt_ in range(8):
                        msl = slice(mt_ * 128, (mt_ + 1) * 128)
                        ps, Rps = pb4()
                        for c in range(8):
                            self.mm(ps[:, :], wout[:, c, msl], mix[:, c, :], [Rw4, Rmix], [Rps], start=(c == 0), stop=(c == 7))
                        self.tt("dve", xt[:, mt_, :], ps[:, :], xt[:, mt_, :], ALU.add, [Rps, Rxt], [Rxt])
                    rs, Rrs = rms4(xt, Rxt)
                    for c in range(8):
                        self.stt(h4[:, c, :], xt[:, c, :], vcol("g_ffn", c), rs[:], ALU.mult, ALU.mult, [Rxt, Rvec, Rrs], [Rh4])
                    for hh in range(2):
                        for uc in range(8):
                            wu, Rwu = wupr.next()
                            uk = (wupr.i - 1) % 3
                            col = hh * 2048 + uc * 256
                            S.dma(wupc[uk], [(wu[:], wup3[:, :, col:col + 256])], reads=[R_wconv], writes=[Rwu])
                            for j in range(2):
                                mi = uc * 2 + j
                                ps, Rps = pb4()
                                for c in range(8):
                                    self.mm(ps[:, :], wu[:, c, j * 128:(j + 1) * 128], h4[:, c, :], [Rwu, Rh4], [Rps], start=(c == 0), stop=(c == 7))
                                rl_, Rrl = rel.next()
                                self.act(rl_[:], ps[:, :], AF.Relu, [Rps], [Rrl])
                                self.tt("pool", hid[:, mi, :], rl_[:], rl_[:], ALU.mult, [Rrl], [Rhid[mi]])
                        for oh in range(2):
                            pss = [pb4() for _ in range(4)]
                            for kc in range(8):
                                wd, Rwd = wdnr.next()
                                dk = (wdnr.i - 1) % 4
                                kr = hh * 16 + kc * 2
                                S.dma(wdnc[dk], [(wd[:], wdn3[:, kr:kr + 2, oh * 512:(oh + 1) * 512])], reads=[R_wconv], writes=[Rwd])
                                for j in range(2):
                                    kci = kc * 2 + j
                                    for mq in range(4):
                                        ps, Rps = pss[mq]
                                        self.mm(ps[:, :], wd[:, j, mq * 128:(mq + 1) * 128], hid[:, kci, :], [Rwd, Rhid[kci]], [Rps],
                                                start=(kci == 0), stop=(kci == 15), inc=(kci == 15 or (j == 1 and mq == 3)))
                            for mq in range(4):
                                mt_ = oh * 4 + mq
                                ps, Rps = pss[mq]
                                self.tt("dve", xt[:, mt_, :], ps[:, :], xt[:, mt_, :], ALU.add, [Rps, Rxt], [Rxt])
                    rs, Rrs = rms4(xt, Rxt)
                    for c in range(8):
                        self.stt(h4[:, c, :], xt[:, c, :], vcol("g_ple", c), rs[:], ALU.mult, ALU.mult, [Rxt, Rvec, Rrs], [Rh4])
                    for mt_ in range(8):
                        msl = slice(mt_ * 128, (mt_ + 1) * 128)
                        ps, Rps = pb4()
                        for c in range(8):
                            self.mm(ps[:, :], wpg[:, c, msl], h4[:, c, :], [Rw4, Rh4], [Rps], start=(c == 0), stop=(c == 7))
                        gg, Rgg = gsg.next()
                        self.act(gg[:], ps[:, :], AF.Sigmoid, [Rps], [Rgg])
                        ps2, Rps2 = pb4()
                        for c in range(2):
                            self.mm(ps2[:, :], wpp[:, c, msl], pin[:, c, :], [Rw4, Rpin], [Rps2], start=(c == 0), stop=(c == 1))
                        self.tt("dve", gg[:], ps2[:, :], gg[:], ALU.mult, [Rps2, Rgg], [Rgg])
                        self.tt("pool", xt[:, mt_, :], xt[:, mt_, :], gg[:], ALU.add, [Rxt, Rgg], [Rxt])
                    rs, Rrs = rms4(xt, Rxt)
                    for c in range(8):
                        ob, Rob = osb4.next()
                        okk = (osb4.i - 1) % 2
                        self.stt(ob[:], xt[:, c, :], vcol("g_final", c), rs[:], ALU.mult, ALU.mult, [Rxt, Rvec, Rrs], [Rob])
                        S.dma(och4[okk], [(outT[c * 128:(c + 1) * 128, o0:o0 + 512], ob[:])], reads=[Rob])
                S.barrier()
                S.emit()
        return nc


def const_inputs():
    s = np.arange(128)[:, None]
    t = np.arange(128)[None, :]
    same = (s // 64) == (t // 64)
    m_lt = ((s < t) & same).astype(np.float32)
    m_le = ((s <= t) & same).astype(np.float32)
    mask4 = np.concatenate([m_lt, m_le, m_lt, m_le], axis=1)
    maskL = ((t < s) & same).astype(np.float32)
    tri = (s <= t).astype(np.float32)
    bd = same.astype(np.float32)
    scan = np.ones((128, 512), np.float32)
    scan[:, ::64] = 0.0
    inv_freq = (np.float32(10000.0) ** (-np.arange(16, dtype=np.float32) / np.float32(16))).astype(np.float32)
    ropec = np.zeros((128, 2), np.float32)
    ropec[64:80, 0] = inv_freq
    ropec[80:96, 0] = inv_freq
    ropec[64:80, 1] = -1.0
    ropec[80:96, 1] = 1.0
    return dict(c_ident=np.eye(128, dtype=np.float32), c_mask4=mask4, c_maskL=maskL, c_tri=tri, c_bd=bd, c_scan=scan, ropec=ropec)


def shared_inputs(inp):
    f = lambda a: np.ascontiguousarray(np.asarray(a, dtype=np.float32))
    w_in = f(inp["w_in"][0])
    z64 = np.zeros((D, 64), np.float32)
    kpe = w_in[:, 640:672]
    w_kpe = np.concatenate([z64, kpe, z64, kpe[:, 16:32], kpe[:, 0:16]], axis=1)
    w_uq = f(inp["w_uq"][0]).reshape(384, NH, 96)
    wq_sw = np.zeros_like(w_uq)
    wq_sw[:, :, 64:80] = w_uq[:, :, 80:96]
    wq_sw[:, :, 80:96] = w_uq[:, :, 64:80]
    w_ukv = f(inp["w_ukv"][0]).reshape(256, NH, 128)
    vec = {"g_mix": inp["g_mix"][0], "g_q_a": inp["g_q_a"][0], "g_kv_a": inp["g_kv_a"][0], "mu": inp["mu_rwkv"][0],
           "w0": inp["w0"][0], "a0": inp["a0"][0], "k_k": inp["k_k"][0], "k_a": inp["k_a"][0],
           "r_k": np.asarray(inp["r_k"][0]).reshape(-1), "ln_w": inp["ln_x_w"][0], "ln_b": inp["ln_x_b"][0],
           "g_ffn": inp["g_ffn"][0], "g_ple": inp["g_ple"][0], "g_final": inp["g_final"]}
    vecs = np.zeros((128, NVEC), np.float32)
    for name, n in VEC_LAYOUT:
        v = f(vec[name]).reshape(n, 128)
        vecs[:, VEC_OFF[name]:VEC_OFF[name] + n] = v.T
    d = dict(vecs=vecs, w_in=w_in, w_kpe=np.ascontiguousarray(w_kpe), wq=np.ascontiguousarray(w_uq.reshape(384, 768)),
             wq_sw=np.ascontiguousarray(wq_sw.reshape(384, 768)),
             wukv_k=np.ascontiguousarray(w_ukv[:, :, 0:64].reshape(256, 512)),
             wukv_v=np.ascontiguousarray(w_ukv[:, :, 64:128].reshape(256, 512)),
             w_o_mla=f(inp["w_o_mla"][0]), w2=f(inp["w2"][0]), a2=f(inp["a2"][0]), g2=f(inp["g2"][0]),
             w_o_rwkv=f(inp["w_o_rwkv"][0]), w_out=f(inp["w_out"][0]), w_up=f(inp["w_ffn_up"][0]), w_down=f(inp["w_ffn_down"][0]),
             w_pg=f(inp["w_ple_gate"][0]), w_pp=f(inp["w_ple_proj"][0]))
    d.update(const_inputs())
    return d


def core_inputs(x_b, p_b, pos_b, half, TP, TO):
    TT = TP + TO
    xT = np.zeros((D, TT), np.float32)
    posr = np.zeros((1, TT), np.int32)
    mrow = np.zeros((1, TT), np.float32)
    o0 = half * TP
    if half == 1:
        xT[:, 0:TP] = x_b[0:TP].T
        posr[0, 0:TP] = pos_b[0:TP]
    else:
        mrow[0, 0:TP] = -30000.0
    xT[:, TP:] = x_b[o0:o0 + TO].T
    posr[0, TP:] = pos_b[o0:o0 + TO]
    pT = np.ascontiguousarray(p_b[o0:o0 + TO].T.astype(np.float32))
    return dict(xT=xT, pT=pT, pos=posr, maskrow=mrow.astype(ml_dtypes.bfloat16))


_NC_CACHE = {}


def get_nc(TP, TO):
    if (TP, TO) not in _NC_CACHE:
        _NC_CACHE[(TP, TO)] = B(TP, TO).build()
    return _NC_CACHE[(TP, TO)]


def kernel(**inputs):
    x = np.asarray(inputs["x"], dtype=np.float32)
    p = np.asarray(inputs["p"], dtype=np.float32)[0]
    pos = np.asarray(inputs["positions"]).astype(np.int32)
    Bn, Sq, _ = x.shape
    TP = TO = Sq // 2
    nc = get_nc(TP, TO)
    sh = shared_inputs(inputs)
    in_maps = []
    for c in range(8):
        b, half = c // 2, c % 2
        m = dict(sh)
        m.update(core_inputs(x[b], p[b], pos[b], half, TP, TO))
        in_maps.append(m)
    res = run_bass_kernel_spmd(nc, in_maps, core_ids=list(range(8)))
    out = np.zeros((Bn, Sq, D), np.float32)
    for c in range(8):
        b, half = c // 2, c % 2
        out[b, half * TO:(half + 1) * TO, :] = res.results[c]["outT"].T
    return out
```

```python
from contextlib import ExitStack
import numpy as np
import ml_dtypes
import concourse.bass as bass
import concourse.mybir as mybir
from concourse.bass_utils import run_bass_kernel_spmd

F32 = mybir.dt.float32
BF16 = mybir.dt.bfloat16
I32 = mybir.dt.int32
AF = mybir.ActivationFunctionType
ALU = mybir.AluOpType
AX = mybir.AxisListType

D = 1024
NH = 8
RMS_EPS = 1e-6
GN_EPS = 64 * 1e-5
SCALE = 96 ** -0.5
EXPH = float(np.exp(-0.5))
TWO_PI = 2.0 * np.pi
C1 = 6.28125
C2 = float(TWO_PI - 6.28125)


class Res:
    __slots__ = ("name", "w", "rd")

    def __init__(self, name=""):
        self.name = name
        self.w = None
        self.rd = []


class Chan:
    def __init__(self, sem, name):
        self.sem = sem
        self.count = 0
        self.name = name


class _Eng:
    def __init__(self, name, sem):
        self.name = name
        self.sem = sem
        self.count = 0
        self.ops = []
        self.waited = {}


class Sched:
    ENGS = ("pe", "act", "dve", "pool", "sp")
    HMAP = {"pe": "tensor", "act": "scalar", "dve": "vector", "pool": "gpsimd", "sp": "sync"}

    def __init__(self, nc, stack, n_chan=90):
        self.nc = nc
        self.e = {}
        for n in self.ENGS:
            self.e[n] = _Eng(n, stack.enter_context(nc.semaphore("s_" + n)))
        self.chans = [Chan(stack.enter_context(nc.semaphore("c%d" % i)), "c%d" % i) for i in range(n_chan)]
        self.chan_i = 4
        self.nops = 0
        self.misc_i = 0

    def misc(self, q="sp"):
        base = 0 if q == "sp" else 2
        c = self.chans[base + self.misc_i % 2]
        self.misc_i += 1
        return c

    def chan(self):
        c = self.chans[self.chan_i]
        self.chan_i += 1
        return c

    def _need(self, eng, reads, writes):
        E = self.e[eng]
        need = {}

        def add(t):
            if t is None:
                return
            key, val = t
            if key is E and eng == "pe":
                return
            if need.get(key, 0) < val:
                need[key] = val

        for r in reads:
            add(r.w)
        for w in writes:
            add(w.w)
            for t in w.rd:
                add(t)
        for key, val in need.items():
            if E.waited.get(key, 0) < val:
                E.waited[key] = val
                E.ops.append(("wait", key.sem, val))

    def op(self, eng, fn, reads=(), writes=(), inc=True):
        E = self.e[eng]
        self._need(eng, reads, writes)
        if inc:
            E.count += 1
            t = (E, E.count)
        else:
            t = (E, E.count + 1)
        E.ops.append(("op", fn, inc))
        for r in reads:
            r.rd.append(t)
            if len(r.rd) > 64:
                r.rd = _compress(r.rd)
        for w in writes:
            w.w = t
            w.rd = []
        self.nops += 1
        return t

    def dma(self, chan, pairs, reads=(), writes=(), q="sp"):
        E = self.e[q]
        if chan.count > 0 and E.waited.get(chan, 0) < chan.count:
            E.waited[chan] = chan.count
            E.ops.append(("wait", chan.sem, chan.count))
        self._need(q, reads, writes)
        for (o, i) in pairs:
            chan.count += 16
            E.ops.append(("dma", o, i, chan.sem))
        t = (chan, chan.count)
        for r in reads:
            r.rd.append(t)
        for w in writes:
            w.w = t
            w.rd = []
        return t

    def barrier(self):
        for n in self.ENGS:
            E = self.e[n]
            for m in self.ENGS:
                O = self.e[m]
                if O is E or O.count == 0:
                    continue
                if E.waited.get(O, 0) < O.count:
                    E.waited[O] = O.count
                    E.ops.append(("wait", O.sem, O.count))
            for c in self.chans:
                if c.count and E.waited.get(c, 0) < c.count:
                    E.waited[c] = c.count
                    E.ops.append(("wait", c.sem, c.count))

    def emit(self):
        nc = self.nc
        with nc.Block() as block:
            for n in self.ENGS:
                E = self.e[n]

                def body(h, E=E):
                    for o in E.ops:
                        if o[0] == "wait":
                            h.wait_ge(o[1], o[2])
                        elif o[0] == "op":
                            ins = o[1](h)
                            if o[2]:
                                ins.then_inc(E.sem, 1)
                        else:
                            h.dma_start(out=o[1], in_=o[2]).then_inc(o[3], 16)

                getattr(block, self.HMAP[n])(body)


def _compress(tickets):
    best = {}
    for key, val in tickets:
        if best.get(key, 0) < val:
            best[key] = val
    return list(best.items())


class Ring:
    def __init__(self, items):
        self.items = items
        self.i = 0

    def next(self):
        it = self.items[self.i % len(self.items)]
        self.i += 1
        return it


VEC_LAYOUT = [("g_mix", 8), ("g_q_a", 3), ("g_kv_a", 2), ("mu", 14), ("w0", 4), ("a0", 4), ("k_k", 4),
              ("k_a", 4), ("r_k", 4), ("ln_w", 4), ("ln_b", 4), ("g_ffn", 8), ("g_ple", 8), ("g_final", 8)]
VEC_OFF = {}
_o = 0
for _n, _c in VEC_LAYOUT:
    VEC_OFF[_n] = _o
    _o += _c
NVEC = _o
OM_OFF = NVEC
OMKA_OFF = NVEC + 14
NVEC_TOT = NVEC + 18


class _Stop(Exception):
    pass


class B:
    def __init__(self, TP, TO, debug=False, upto=5, p2_stop=0):
        self.p2_stop = p2_stop
        self.debug = debug
        self.upto = upto
        self.TP, self.TO = TP, TO
        self.TT = TP + TO
        self.nc = bass.Bass("TRN2", target_bir_lowering=False)
        self.st = ExitStack()
        self.S = None

    def din(self, name, shape, dt=F32):
        return self.nc.dram_tensor(name, list(shape), dt, kind="ExternalInput").ap()

    def dscr(self, name, shape, dt=BF16):
        kind = "ExternalOutput" if self.debug else "Internal"
        return self.nc.dram_tensor(name, list(shape), dt, kind=kind).ap()

    def sb(self, st, name, shape, dt=F32):
        self._uid = getattr(self, "_uid", 0) + 1
        return st.enter_context(self.nc.sbuf_tensor("s%d_%s" % (self._uid, name), list(shape), dt))

    def ring(self, st, name, shape, dt, n):
        return Ring([(self.sb(st, "%s%d" % (name, i), shape, dt), Res("%s%d" % (name, i))) for i in range(n)])

    def mm(self, out, lhsT, rhs, reads, writes, start=True, stop=True, inc=None):
        self.S.op("pe", lambda e: e.matmul(out, lhsT, rhs, start=start, stop=stop), reads, writes, inc=(stop if inc is None else inc))

    def tr(self, out, in_, ident, reads, writes, inc=True):
        self.S.op("pe", lambda e: e.transpose(out, in_, ident), reads, writes, inc=inc)

    def act(self, out, in_, func, reads, writes, bias=0.0, scale=1.0):
        if func == AF.Copy and not (isinstance(bias, float) and isinstance(scale, float)):
            func = AF.Identity
        self.S.op("act", lambda e: e.activation(out=out, in_=in_, func=func, bias=bias, scale=scale), reads, writes)

    def tt(self, eng, out, in0, in1, op, reads, writes):
        self.S.op(eng, lambda e: e.tensor_tensor(out=out, in0=in0, in1=in1, op=op), reads, writes)

    def ts(self, eng, out, in0, s1, op0, reads, writes, s2=None, op1=None):
        if op1 is None:
            self.S.op(eng, lambda e: e.tensor_scalar(out=out, in0=in0, scalar1=s1, scalar2=None, op0=op0), reads, writes)
        else:
            self.S.op(eng, lambda e: e.tensor_scalar(out=out, in0=in0, scalar1=s1, scalar2=s2, op0=op0, op1=op1), reads, writes)

    def stt(self, out, in0, scalar, in1, op0, op1, reads, writes):
        self.S.op("dve", lambda e: e.scalar_tensor_tensor(out=out, in0=in0, scalar=scalar, in1=in1, op0=op0, op1=op1), reads, writes)

    def cp(self, eng, out, in_, reads, writes):
        if eng == "act":
            self.act(out, in_, AF.Copy, reads, writes)
        else:
            self.S.op(eng, lambda e: e.tensor_copy(out=out, in_=in_), reads, writes)

    def ckpt(self, n):
        if self.p2_stop == n:
            self.S.barrier()
            self.S.emit()
            raise _Stop()

    def memset(self, eng, ap, val, writes):
        self.S.op(eng, lambda e: e.memset(ap, val), (), writes)

    def recip(self, out, in_, reads, writes):
        self.S.op("dve", lambda e: e.reciprocal(out=out, in_=in_), reads, writes)

    def build(self):
        try:
            self._build()
        except _Stop:
            pass
        return self.nc

    def _build(self):
        nc, TP, TO, TT = self.nc, self.TP, self.TO, self.TT
        NT, NTP, NTO = TT // 512, TP // 512, TO // 512
        NB = TT // 128
        xT = self.din("xT", [D, TT])
        pT = self.din("pT", [256, TO])
        pos = self.din("pos", [1, TT], I32)
        maskrow = self.din("maskrow", [1, TT], BF16)
        vecs_d = self.din("vecs", [128, NVEC])
        rc_d = self.din("ropec", [128, 2])
        w_in = self.din("w_in", [D, 4512])
        w_kpe = self.din("w_kpe", [D, 192])
        wq_d = self.din("wq", [384, 768])
        wqs_d = self.din("wq_sw", [384, 768])
        wkk_d = self.din("wukv_k", [256, 512])
        wkv_d = self.din("wukv_v", [256, 512])
        womla_d = self.din("w_o_mla", [512, D])
        w2_d = self.din("w2", [64, 512])
        a2_d = self.din("a2", [64, 512])
        g2_d = self.din("g2", [128, 512])
        worw_d = self.din("w_o_rwkv", [512, D])
        wout_d = self.din("w_out", [D, D])
        wup_d = self.din("w_up", [D, 4096])
        wdn_d = self.din("w_down", [4096, D])
        wpg_d = self.din("w_pg", [D, D])
        wpp_d = self.din("w_pp", [256, D])
        cm_ident_d = self.din("c_ident", [128, 128])
        cm_mask4_d = self.din("c_mask4", [128, 512])
        cm_maskL_d = self.din("c_maskL", [128, 128])
        cm_tri_d = self.din("c_tri", [128, 128])
        cm_bd_d = self.din("c_bd", [128, 128])
        cm_scan_d = self.din("c_scan", [128, 512])
        outT = nc.dram_tensor("outT", [D, TO], F32, kind="ExternalOutput").ap()
        QT = self.dscr("QT", [NH, 97, TO])
        KnT = self.dscr("KnT", [512, TT])
        KpeT = self.dscr("KpeT", [33, TT])
        Vs = self.dscr("Vs", [128, NB, 520])
        GT = self.dscr("GT", [2048, TO])
        ARt = self.dscr("ARt", [512, NB, 256])
        Bt = self.dscr("Bt", [512, TT])
        Kt = self.dscr("Kt", [512, TT])
        Vt = self.dscr("Vt", [512, TT])
        PCt = self.dscr("PCt", [512, TT // 64], F32)
        GrT = self.dscr("GrT", [512, TO])
        BoT = self.dscr("BoT", [512, TO])
        OaT = self.dscr("OaT", [512, TO])
        YbT = self.dscr("YbT", [512, TO])
        wupb = self.dscr("wupb", [D, 4096])
        wdnb = self.dscr("wdnb", [4096, D])
        R_wconv = Res("wconv")
        R_QT, R_KnT, R_KpeT, R_Vs, R_GT = Res("QT"), Res("KnT"), Res("KpeT"), Res("Vs"), Res("GT")
        R_rw, R_OaT, R_YbT = Res("rwscr"), Res("OaT"), Res("YbT")

        with self.st as st0:
            S = self.S = Sched(nc, st0)
            vecs = self.sb(st0, "vecs", [128, NVEC_TOT]); Rvec = Res("vecs")
            ropec = self.sb(st0, "ropec", [128, 2]); Rrc = Res()
            ident = self.sb(st0, "ident", [128, 128]); Rid = Res()
            identb = self.sb(st0, "identb", [128, 128], BF16); Ridb = Res()
            mask4 = self.sb(st0, "mask4", [128, 512]); Rm4 = Res()
            maskL = self.sb(st0, "maskL", [128, 128]); RmL = Res()
            trim = self.sb(st0, "trim", [128, 128], BF16); Rtri = Res()
            bdb = self.sb(st0, "bdb", [128, 128], BF16); Rbd = Res()
            onesb = self.sb(st0, "onesb", [128, 128], BF16); Rones = Res()
            onesf = self.sb(st0, "onesf", [128, 128]); Ronesf = Res()
            scanm = self.sb(st0, "scanm", [128, 512]); Rscan = Res()
            S.dma(S.misc(), [(vecs[:, 0:NVEC], vecs_d[:, :])], writes=[Rvec])
            S.dma(S.misc(), [(ropec[:], rc_d[:, :])], writes=[Rrc])
            S.dma(S.misc(), [(ident[:], cm_ident_d[:, :])], writes=[Rid])
            S.dma(S.misc("pool"), [(identb[:], cm_ident_d[:, :])], writes=[Ridb], q="pool")
            S.dma(S.misc(), [(mask4[:], cm_mask4_d[:, :])], writes=[Rm4])
            S.dma(S.misc(), [(maskL[:], cm_maskL_d[:, :])], writes=[RmL])
            S.dma(S.misc("pool"), [(trim[:], cm_tri_d[:, :])], writes=[Rtri], q="pool")
            S.dma(S.misc("pool"), [(bdb[:], cm_bd_d[:, :])], writes=[Rbd], q="pool")
            S.dma(S.misc(), [(scanm[:], cm_scan_d[:, :])], writes=[Rscan])
            self.memset("pool", onesb[:], 1.0, [Rones])
            self.memset("pool", onesf[:], 1.0, [Ronesf])
            self.ts("dve", vecs[:, OM_OFF:OM_OFF + 14], vecs[:, VEC_OFF["mu"]:VEC_OFF["mu"] + 14], -1.0, ALU.mult,
                    [Rvec], [Rvec], s2=1.0, op1=ALU.add)
            self.ts("dve", vecs[:, OMKA_OFF:OMKA_OFF + 4], vecs[:, VEC_OFF["k_a"]:VEC_OFF["k_a"] + 4], -1.0, ALU.mult,
                    [Rvec], [Rvec], s2=1.0, op1=ALU.add)
            S.dma(S.misc(), [(KpeT[32:33, :], maskrow[0:1, :])], writes=[R_KpeT])

            def vcol(name, j, p0=0, p1=128):
                o = VEC_OFF[name] + j
                return vecs[p0:p1, o:o + 1]

            banks = [(st0.enter_context(nc.psum_tensor("bank%d" % i, [128, 512], F32)), Res("bank%d" % i)) for i in range(7)]
            bankb = (st0.enter_context(nc.psum_tensor("bankb", [128, 1024], BF16)), Res("bankb"))
            consts = [Rvec, Rrc, Rid, Ridb, Rm4, RmL, Rtri, Rbd, Rones, Ronesf, Rscan]

            xT3 = xT.rearrange("(c p) t -> p c t", p=128)
            w_in3 = w_in.rearrange("(c p) n -> p c n", p=128)
            w_kpe3 = w_kpe.rearrange("(c p) n -> p c n", p=128)
            pi = [0]

            def pbank():
                b_ = banks[pi[0] % 7]
                pi[0] += 1
                return b_

            def make_common(st, ncol):
                cm = {}
                cm["win"] = self.sb(st, "win", [128, 8, ncol], BF16)
                cm["Rwin"] = Res()
                cm["xr"] = self.ring(st, "x1_", [128, 8, 512], F32, 1)
                cm["xch"] = S.chan()
                cm["sqr"] = self.ring(st, "sq1_", [128, 512], BF16, 4)
                cm["hr"] = self.ring(st, "h1_", [128, 8, 512], BF16, 2)
                cm["rstdr"] = self.ring(st, "rstd1_", [128, 512], F32, 2)
                cm["sqt"] = self.ring(st, "sqt1_", [128, 512], F32, 2)
                return cm

            def rms_stats(cm, src3, nchunk, scale, Rsrc):
                ps, Rps = pbank()
                for c in range(nchunk):
                    sq, Rsq = cm["sqr"].next()
                    self.act(sq[:], src3[:, c, :], AF.Square, [Rsrc], [Rsq])
                    self.mm(ps[:, :], onesb[:, :], sq[:], [Rones, Rsq], [Rps], start=(c == 0), stop=(c == nchunk - 1), inc=True)
                t1, Rt1 = cm["sqt"].next()
                self.act(t1[:], ps[:, :], AF.Sqrt, [Rps], [Rt1], bias=RMS_EPS, scale=scale)
                rs, Rrs = cm["rstdr"].next()
                self.recip(rs[:], t1[:], [Rt1], [Rrs])
                return rs, Rrs

            def load_h(cm, t):
                c0 = t * 512
                xt, Rxt = cm["xr"].next()
                S.dma(cm["xch"], [(xt[:], xT3[:, :, c0:c0 + 512])], writes=[Rxt])
                rs, Rrs = rms_stats(cm, xt, 8, 1.0 / D, Rxt)
                h, Rh = cm["hr"].next()
                for c in range(8):
                    self.stt(h[:, c, :], xt[:, c, :], vcol("g_mix", c), rs[:], ALU.mult, ALU.mult, [Rxt, Rvec, Rrs], [Rh])
                return h, Rh

            def zmm(cm, h, Rh, col0, M):
                ps, Rps = pbank()
                for c in range(8):
                    self.mm(ps[0:M, :], cm["win"][:, c, col0:col0 + M], h[:, c, :], [cm["Rwin"], Rh], [Rps], start=(c == 0), stop=(c == 7))
                return ps, Rps

            with ExitStack() as st:
                if self.upto < 1:
                    raise _Stop()
                NCOL = 384 + 256 + 192 + 2048
                OFF_CQ, OFF_CKV, OFF_KPE, OFF_G = 0, 384, 640, 832
                cm = make_common(st, NCOL)
                win, Rwin = cm["win"], cm["Rwin"]
                for c in range(8):
                    S.dma(S.misc("pool"), [(win[:, c, 0:640], w_in3[:, c, 0:640]),
                                     (win[:, c, 640:832], w_kpe3[:, c, :]),
                                     (win[:, c, 832:NCOL], w_in3[:, c, 2464:4512])], writes=[Rwin], q="pool")
                wq = self.sb(st, "wq", [128, 3, 768], BF16)
                wqs = self.sb(st, "wqs", [128, 3, 768], BF16)
                wkk = self.sb(st, "wkk", [128, 2, 512], BF16)
                wkv = self.sb(st, "wkv", [128, 2, 512], BF16)
                Rw1 = Res("w1")
                S.dma(S.misc("pool"), [(wq[:], wq_d.rearrange("(c p) n -> p c n", p=128)),
                                 (wqs[:], wqs_d.rearrange("(c p) n -> p c n", p=128)),
                                 (wkk[:], wkk_d.rearrange("(c p) n -> p c n", p=128)),
                                 (wkv[:], wkv_d.rearrange("(c p) n -> p c n", p=128))],
                      writes=[Rw1], q="pool")
                tmpr = self.ring(st, "tmp1_", [128, 512], F32, 4)
                cq = self.sb(st, "cq", [128, 3, 512]); Rcq = Res()
                cqn = self.sb(st, "cqn", [128, 3, 512], BF16); Rcqn = Res()
                ckv = self.sb(st, "ckv", [128, 2, 512]); Rckv = Res()
                ckvn = self.sb(st, "ckvn", [128, 2, 512], BF16); Rckvn = Res()
                qst = self.ring(st, "qst", [128, 512], BF16, 3)
                qch = [S.chan() for _ in range(3)]
                for (t_, r_) in qst.items:
                    self.memset("pool", t_[64:97, :], 1.0, [r_])
                knst = self.ring(st, "knst", [128, 512], BF16, 2)
                knch = [S.chan() for _ in range(2)]
                vst = self.ring(st, "vst", [128, 4, 520], BF16, 2)
                vch = [S.chan() for _ in range(2)]
                for (t_, r_) in vst.items:
                    self.memset("pool", t_[:], 1.0, [r_])
                kpst = self.ring(st, "kpst", [128, 512], BF16, 2)
                kpch = [S.chan() for _ in range(2)]
                gst = self.ring(st, "gst", [128, 4, 512], BF16, 2)
                gch = [S.chan() for _ in range(2)]
                posi = self.sb(st, "posi", [128, 512], I32); Rposi = Res()
                posch = S.chan()
                rp = [self.sb(st, "rp%d" % i, [128, 512]) for i in range(6)]
                Rrp = [Res() for _ in range(6)]
                ki = self.sb(st, "ki", [128, 512], I32); Rki = Res()
                sl = slice(64, 96)
                for t in range(NT):
                    own = t >= NTP
                    to = t - NTP
                    c0 = t * 512
                    if t == 0:
                        hnext = load_h(cm, 0)
                    h, Rh = hnext
                    S.dma(posch, [(posi[64:96, :], pos[0:1, c0:c0 + 512].partition_broadcast(32))], writes=[Rposi])
                    ang, sinT, cosT, sinQ, cosQ, rr = rp
                    Rang, RsinT, RcosT, RsinQ, RcosQ, Rrr = Rrp
                    self.cp("dve", ang[sl, :], posi[sl, :], [Rposi], [Rang])
                    self.ts("dve", ang[sl, :], ang[sl, :], ropec[sl, 0:1], ALU.mult, [Rang, Rrc], [Rang])
                    self.ts("dve", rr[sl, :], ang[sl, :], float(1.0 / TWO_PI), ALU.mult, [Rang], [Rrr])
                    self.cp("dve", ki[sl, :], rr[sl, :], [Rrr], [Rki])
                    self.cp("dve", rr[sl, :], ki[sl, :], [Rki], [Rrr])
                    self.stt(ang[sl, :], rr[sl, :], -C1, ang[sl, :], ALU.mult, ALU.add, [Rrr, Rang], [Rang])
                    self.stt(ang[sl, :], rr[sl, :], -C2, ang[sl, :], ALU.mult, ALU.add, [Rrr, Rang], [Rang])
                    self.ts("dve", ang[sl, :], ang[sl, :], float(np.pi), ALU.min, [Rang], [Rang], s2=float(-np.pi), op1=ALU.max)
                    self.act(sinT[sl, :], ang[sl, :], AF.Sin, [Rang, Rrc], [RsinT], scale=ropec[sl, 1:2])
                    self.act(rr[sl, :], ang[sl, :], AF.Abs, [Rang], [Rrr])
                    self.ts("dve", rr[sl, :], rr[sl, :], -1.0, ALU.mult, [Rrr], [Rrr], s2=float(np.pi / 2), op1=ALU.add)
                    self.act(cosT[sl, :], rr[sl, :], AF.Sin, [Rrr], [RcosT])
                    if own:
                        self.ts("pool", sinQ[sl, :], sinT[sl, :], SCALE, ALU.mult, [RsinT], [RsinQ])
                        self.ts("pool", cosQ[sl, :], cosT[sl, :], SCALE, ALU.mult, [RcosT], [RcosQ])
                    psA, RpsA = zmm(cm, h, Rh, OFF_KPE, 96)
                    psB, RpsB = zmm(cm, h, Rh, OFF_KPE + 96, 96)
                    ta, Rta = tmpr.next()
                    tb, Rtb = tmpr.next()
                    self.tt("dve", ta[sl, :], psA[sl, :], cosT[sl, :], ALU.mult, [RpsA, RcosT], [Rta])
                    self.tt("dve", tb[sl, :], psB[sl, :], sinT[sl, :], ALU.mult, [RpsB, RsinT], [Rtb])
                    kp, Rkp = kpst.next()
                    self.tt("pool", kp[sl, :], ta[sl, :], tb[sl, :], ALU.add, [Rta, Rtb], [Rkp])
                    S.dma(kpch[t % 2], [(KpeT[0:32, c0:c0 + 512], kp[sl, :])], reads=[Rkp], writes=[R_KpeT])
                    for m in range(2):
                        ps, Rps = zmm(cm, h, Rh, OFF_CKV + m * 128, 128)
                        self.cp("act", ckv[:, m, :], ps[:, :], [Rps], [Rckv])
                    if t + 1 < NT:
                        hnext = load_h(cm, t + 1)
                    rs2, Rrs2 = rms_stats(cm, ckv, 2, 1.0 / 256, Rckv)
                    for m in range(2):
                        self.stt(ckvn[:, m, :], ckv[:, m, :], vcol("g_kv_a", m), rs2[:], ALU.mult, ALU.mult, [Rckv, Rvec, Rrs2], [Rckvn])
                    for m in range(4):
                        ps, Rps = pbank()
                        for c in range(2):
                            self.mm(ps[:, :], wkk[:, c, m * 128:(m + 1) * 128], ckvn[:, c, :], [Rw1, Rckvn], [Rps], start=(c == 0), stop=(c == 1))
                        kn, Rkn = knst.next()
                        kk_ = (knst.i - 1) % 2
                        self.cp("act", kn[:], ps[:, :], [Rps], [Rkn])
                        S.dma(knch[kk_], [(KnT[m * 128:(m + 1) * 128, c0:c0 + 512], kn[:])], reads=[Rkn], writes=[R_KnT])
                    vt_, Rvt = vst.next()
                    vk = (vst.i - 1) % 2
                    for s_ in range(4):
                        ps, Rps = pbank()
                        for c in range(2):
                            self.mm(ps[:, :], ckvn[:, c, s_ * 128:(s_ + 1) * 128], wkv[:, c, :], [Rckvn, Rw1], [Rps], start=(c == 0), stop=(c == 1))
                        v4 = vt_[:, s_, :].rearrange("p (h d) -> p h d", d=65)
                        self.cp("act", v4[:, :, 0:64], ps[:, :].rearrange("p (h d) -> p h d", d=64), [Rps], [Rvt])
                    S.dma(vch[vk], [(Vs[:, t * 4:(t + 1) * 4, :], vt_[:])], reads=[Rvt], writes=[R_Vs])
                    if own:
                        o0 = to * 512
                        for m in range(3):
                            ps, Rps = zmm(cm, h, Rh, OFF_CQ + m * 128, 128)
                            self.cp("act", cq[:, m, :], ps[:, :], [Rps], [Rcq])
                        rs3, Rrs3 = rms_stats(cm, cq, 3, 1.0 / 384, Rcq)
                        for m in range(3):
                            self.stt(cqn[:, m, :], cq[:, m, :], vcol("g_q_a", m), rs3[:], ALU.mult, ALU.mult, [Rcq, Rvec, Rrs3], [Rcqn])
                        for hd in range(NH):
                            psA, RpsA = pbank()
                            psB, RpsB = pbank()
                            for c in range(3):
                                self.mm(psA[0:96, :], wq[:, c, hd * 96:(hd + 1) * 96], cqn[:, c, :], [Rw1, Rcqn], [RpsA], start=(c == 0), stop=(c == 2))
                            for c in range(3):
                                self.mm(psB[0:96, :], wqs[:, c, hd * 96:(hd + 1) * 96], cqn[:, c, :], [Rw1, Rcqn], [RpsB], start=(c == 0), stop=(c == 2))
                            q_, Rq_ = qst.next()
                            qk = (qst.i - 1) % 3
                            self.act(q_[0:64, :], psA[0:64, :], AF.Copy, [RpsA], [Rq_], scale=SCALE)
                            ta, Rta = tmpr.next()
                            tb, Rtb = tmpr.next()
                            self.tt("dve", ta[sl, :], psA[sl, :], cosQ[sl, :], ALU.mult, [RpsA, RcosQ], [Rta])
                            self.tt("dve", tb[sl, :], psB[sl, :], sinQ[sl, :], ALU.mult, [RpsB, RsinQ], [Rtb])
                            self.tt("pool", q_[sl, :], ta[sl, :], tb[sl, :], ALU.add, [Rta, Rtb], [Rq_])
                            S.dma(qch[qk], [(QT[hd, :, o0:o0 + 512], q_[0:97, :])], reads=[Rq_], writes=[R_QT])
                        for gq in range(4):
                            g_, Rg_ = gst.next()
                            gk = (gst.i - 1) % 2
                            for j in range(4):
                                ps, Rps = zmm(cm, h, Rh, OFF_G + (gq * 4 + j) * 128, 128)
                                self.act(g_[:, j, :], ps[:, :], AF.Sigmoid, [Rps], [Rg_])
                            S.dma(gch[gk], [(GT[gq * 512:(gq + 1) * 512, o0:o0 + 512].rearrange("(j p) t -> p j t", p=128), g_[:])],
                                  reads=[Rg_], writes=[R_GT])
                S.barrier()
                S.emit()
                for n_ in S.ENGS:
                    S.e[n_].ops = []

            with ExitStack() as st:
                if self.upto < 2:
                    raise _Stop()
                cm = make_common(st, 1792)
                win, Rwin = cm["win"], cm["Rwin"]
                for c in range(8):
                    S.dma(S.misc("pool"), [(win[:, c, :], w_in3[:, c, 672:2464])], writes=[Rwin], q="pool")
                w2s = self.sb(st, "w2s", [128, 512], BF16)
                a2s = self.sb(st, "a2s", [128, 512], BF16)
                g2s = self.sb(st, "g2s", [128, 512], BF16)
                Rw1 = Res("w1b")
                S.dma(S.misc("pool"), [(w2s[0:64, :], w2_d[:, :]), (a2s[64:128, :], a2_d[:, :]), (g2s[:], g2_d[:, :])], writes=[Rw1], q="pool")
                tmpr = self.ring(st, "tmp1b_", [128, 512], F32, 3)
                tmpb = self.ring(st, "tmpb1_", [128, 512], BF16, 4)
                zcw = self.ring(st, "zcw", [128, 513], F32, 4)
                carry = self.sb(st, "carry", [128, 16]); Rcar = Res()
                self.memset("pool", carry[:], 0.0, [Rcar])
                zsr = self.ring(st, "zsr", [128, 512], F32, 2)
                zsk = self.ring(st, "zsk", [128, 512], F32, 2)
                zsv = self.ring(st, "zsv", [128, 512], F32, 2)
                zs12 = self.sb(st, "zs12", [128, 512]); Rzs12 = Res()
                zs13 = self.sb(st, "zs13", [128, 512]); Rzs13 = Res()
                names = ["sig", "av", "Lc", "Lx", "kk", "nr", "tk", "EP", "EN"]
                nb2 = [{n_: (self.sb(st, "rb%d_" % k_ + n_, [128, 512]), Res(n_)) for n_ in names} for k_ in range(2)]
                twb = self.sb(st, "twb", [128, 512], BF16); Rtwb = Res()
                gsb = self.sb(st, "gsb", [128, 512], BF16); Rgsb = Res()
                rwst = self.ring(st, "rwst", [128, 512], BF16, 12)
                rwch = [S.chan() for _ in range(12)]
                rwi = [0]
                pcst = self.ring(st, "pcst", [128, 8], F32, 4)
                pcch = [S.chan() for _ in range(4)]

                def rw_store(dst_ap, src_fn, eng_fn):
                    k = rwi[0] % 12
                    rwi[0] += 1
                    t_, r_ = rwst.items[k]
                    eng_fn(t_, r_)
                    S.dma(rwch[k], [(dst_ap, src_fn(t_))], reads=[r_], writes=[R_rw])

                MU, OM = VEC_OFF["mu"], OM_OFF

                def shift(cm, h, Rh, m, dst, Rdst):
                    ps, Rps = zmm(cm, h, Rh, m * 128, 128)
                    zc, Rzc = zcw.next()
                    self.act(zc[:, 1:513], ps[:, :], AF.Copy, [Rps, Rvec], [Rzc], scale=vecs[:, MU + m:MU + m + 1])
                    self.cp("pool", zc[:, 0:1], carry[:, m:m + 1], [Rcar], [Rzc])
                    self.stt(dst[:], ps[:, :], vecs[:, OM + m:OM + m + 1], zc[:, 0:512], ALU.mult, ALU.add, [Rps, Rvec, Rzc], [Rdst])
                    self.cp("pool", carry[:, m:m + 1], zc[:, 512:513], [Rzc], [Rcar])

                for t in range(NT):
                    own = t >= NTP
                    o0 = (t - NTP) * 512
                    c0 = t * 512
                    if t == 0:
                        hnext = load_h(cm, 0)
                    h, Rh = hnext
                    shift(cm, h, Rh, 12, zs12, Rzs12)
                    shift(cm, h, Rh, 13, zs13, Rzs13)
                    if t + 1 < NT:
                        hnext = load_h(cm, t + 1)
                    self.act(twb[0:64, :], zs12[0:64, :], AF.Tanh, [Rzs12], [Rtwb])
                    self.cp("pool", twb[64:128, :], zs12[64:128, :], [Rzs12], [Rtwb])
                    self.act(gsb[:], zs13[:], AF.Sigmoid, [Rzs13], [Rgsb])
                    def st1(c_):
                        m = c_["m"]
                        c_["r"] = zsr.next(); c_["k"] = zsk.next(); c_["v"] = zsv.next()
                        shift(cm, h, Rh, m, *c_["r"])
                        shift(cm, h, Rh, 4 + m, *c_["k"])
                        shift(cm, h, Rh, 8 + m, *c_["v"])
                        c_["ms"] = slice(m * 128, (m + 1) * 128)
                        c_["nb"] = nb2[m % 2]

                    def st2(c_):
                        m, ms, nb = c_["m"], c_["ms"], c_["nb"]
                        (sig, Rsig), (av, Rav) = nb["sig"], nb["av"]
                        ps, Rps = pbank()
                        self.mm(ps[:, :], w2s[0:64, ms], twb[0:64, :], [Rw1, Rtwb], [Rps])
                        self.act(sig[:], ps[:, :], AF.Sigmoid, [Rps, Rvec], [Rsig], bias=vcol("w0", m))
                        ps, Rps = pbank()
                        self.mm(ps[:, :], a2s[64:128, ms], twb[64:128, :], [Rw1, Rtwb], [Rps])
                        self.act(av[:], ps[:, :], AF.Sigmoid, [Rps, Rvec], [Rav], bias=vcol("a0", m))

                    def st3(c_):
                        m, nb = c_["m"], c_["nb"]
                        (sig, Rsig), (Lc, RLc), (kk, Rkk) = nb["sig"], nb["Lc"], nb["kk"]
                        k_m, Rk = c_["k"]
                        S.op("dve", lambda e, o=Lc, d=sig: e.tensor_tensor_scan(out=o[:], data0=scanm[:], data1=d[:], initial=0.0,
                                                                              op0=ALU.mult, op1=ALU.add), [Rscan, Rsig], [RLc])
                        self.act(kk[:], k_m[:], AF.Copy, [Rk, Rvec], [Rkk], scale=vcol("k_k", m))
                        kk2, Rkk2 = tmpb.next()
                        self.act(kk2[:], k_m[:], AF.Square, [Rk, Rvec], [Rkk2], scale=vcol("k_k", m))
                        ps, Rps = pbank()
                        self.mm(ps[:, :], bdb[:, :], kk2[:], [Rbd, Rkk2], [Rps])
                        c_["psn"] = (ps, Rps)

                    def st4(c_):
                        m, nb = c_["m"], c_["nb"]
                        (av, Rav), (kk, Rkk), (nr, Rnr), (tk, Rtk) = nb["av"], nb["kk"], nb["nr"], nb["tk"]
                        k_m, Rk = c_["k"]
                        ps, Rps = c_["psn"]
                        self.act(nr[:], ps[:, :], AF.Sqrt, [Rps], [Rnr])
                        self.ts("dve", nr[:], nr[:], 1e-12, ALU.max, [Rnr], [Rnr])
                        self.recip(nr[:], nr[:], [Rnr], [Rnr])
                        self.tt("dve", kk[:], kk[:], nr[:], ALU.mult, [Rkk, Rnr], [Rkk])
                        self.ts("dve", tk[:], av[:], vcol("k_a", m), ALU.mult, [Rav, Rvec], [Rtk],
                                s2=vecs[:, OMKA_OFF + m:OMKA_OFF + m + 1], op1=ALU.add)
                        self.tt("dve", tk[:], tk[:], k_m[:], ALU.mult, [Rtk, Rk], [Rtk])

                    def st5(c_):
                        nb = c_["nb"]
                        (Lc, RLc), (EP, REP), (EN, REN) = nb["Lc"], nb["EP"], nb["EN"]
                        self.act(EP[:], Lc[:], AF.Exp, [RLc], [REP], scale=-EXPH)
                        self.act(EN[:], Lc[:], AF.Exp, [RLc], [REN], scale=EXPH)

                    def st6(c_):
                        m, ms, nb = c_["m"], c_["ms"], c_["nb"]
                        (tk, Rtk) = nb["tk"]
                        r_m, Rr = c_["r"]; v_m, Rv = c_["v"]
                        rk, Rrk = tmpb.next()
                        self.stt(rk[:], r_m[:], vcol("r_k", m), tk[:], ALU.mult, ALU.mult, [Rr, Rvec, Rtk], [Rrk])
                        psb, Rpsb = pbank()
                        self.mm(psb[:, :], bdb[:, :], rk[:], [Rbd, Rrk], [Rpsb])
                        rw_store(BoT[ms, o0:o0 + 512], lambda t_: t_[:],
                                 lambda t_, r_: self.tt("dve", t_[:], psb[:, :], v_m[:], ALU.mult, [Rpsb, Rv], [r_]))
                        psg, Rpsg = pbank()
                        self.mm(psg[:, :], g2s[:, ms], gsb[:], [Rw1, Rgsb], [Rpsg])
                        rw_store(GrT[ms, o0:o0 + 512], lambda t_: t_[:],
                                 lambda t_, r_: self.cp("act", t_[:], psg[:, :], [Rpsg], [r_]))

                    def st7(c_):
                        m, ms, nb = c_["m"], c_["ms"], c_["nb"]
                        (av, Rav), (Lx, RLx), (kk, Rkk), (tk, Rtk), (EP, REP), (EN, REN) = nb["av"], nb["Lx"], nb["kk"], nb["tk"], nb["EP"], nb["EN"]
                        r_m, Rr = c_["r"]; v_m, Rv = c_["v"]
                        ARv = ARt[ms, t * 4:(t + 1) * 4, :]

                        def a_tilde(t_, r_):
                            self.stt(t_[:, 1:512], kk[:, 1:512], -1.0, EP[:, 0:511], ALU.mult, ALU.mult, [Rkk, REP], [r_])
                            self.ts("pool", t_[:].rearrange("p (c t) -> p c t", t=64)[:, :, 0:1],
                                    kk[:].rearrange("p (c t) -> p c t", t=64)[:, :, 0:1], -1.0, ALU.mult, [Rkk], [r_])

                        rw_store(ARv[:, :, 0:128], lambda t_: t_[:].rearrange("p (b t) -> p b t", t=128), a_tilde)
                        rw_store(ARv[:, :, 128:256], lambda t_: t_[:].rearrange("p (b t) -> p b t", t=128),
                                 lambda t_, r_: self.tt("pool", t_[:], r_m[:], EP[:], ALU.mult, [Rr, REP], [r_]))
                        self.tt("dve", Lx[:], kk[:], av[:], ALU.mult, [Rkk, Rav], [RLx])
                        rw_store(Bt[ms, c0:c0 + 512], lambda t_: t_[:],
                                 lambda t_, r_: self.tt("dve", t_[:], Lx[:], EN[:], ALU.mult, [RLx, REN], [r_]))
                        rw_store(Kt[ms, c0:c0 + 512], lambda t_: t_[:],
                                 lambda t_, r_: self.tt("pool", t_[:], tk[:], EN[:], ALU.mult, [Rtk, REN], [r_]))
                        rw_store(Vt[ms, c0:c0 + 512], lambda t_: t_[:],
                                 lambda t_, r_: self.cp("act", t_[:], v_m[:], [Rv], [r_]))
                        pc, Rpc = pcst.next()
                        pk = (pcst.i - 1) % 4
                        self.cp("pool", pc[:], EP[:].rearrange("p (c t) -> p c t", t=64)[:, :, 63], [REP], [Rpc])
                        S.dma(pcch[pk], [(PCt[ms, t * 8:(t + 1) * 8], pc[:])], reads=[Rpc], writes=[R_rw])

                    for pr in range(2):
                        cs_ = [{"m": pr * 2}, {"m": pr * 2 + 1}]
                        for stg in (st1, st2, st3, st4, st5):
                            for c_ in cs_:
                                stg(c_)
                        if own:
                            for c_ in cs_:
                                st6(c_)
                        for c_ in cs_:
                            st7(c_)
                S.barrier()
                S.emit()
                for n_ in S.ENGS:
                    S.e[n_].ops = []

            with ExitStack() as st:
                if self.upto < 3:
                    raise _Stop()
                for i in range(8):
                    S.dma(S.misc("pool"), [(wupb[i * 128:(i + 1) * 128, :], wup_d[i * 128:(i + 1) * 128, :])], writes=[R_wconv], q="pool")
                for i in range(8):
                    S.dma(S.misc("pool"), [(wdnb[i * 512:(i + 1) * 512, :], wdn_d[i * 512:(i + 1) * 512, :])], writes=[R_wconv], q="pool")
                arl = self.ring(st, "arl", [128, 4, 256], BF16, 2)
                btl = self.ring(st, "btl", [128, 512], BF16, 2)
                ktl = self.ring(st, "ktl", [128, 512], BF16, 2)
                vtl = self.ring(st, "vtl", [128, 512], BF16, 2)
                pcl = self.ring(st, "pcl", [128, 8], F32, 2)
                bol = self.ring(st, "bol", [128, 512], BF16, 2)
                grl = self.ring(st, "grl", [128, 512], BF16, 2)
                ldch = [S.chan(), S.chan()]
                MT = self.ring(st, "MT", [128, 2, 512], BF16, 8)
                Lr = self.ring(st, "Lr", [128, 2, 128], BF16, 12)
                ASr = self.ring(st, "ASr", [128, 2, 256], BF16, 12)
                TM = self.ring(st, "TM", [128, 4, 128], BF16, 8)
                Xb = self.ring(st, "Xb", [128, 2, 128], BF16, 8)
                TXr = self.ring(st, "TXr", [128, 2, 128], BF16, 8)
                RqT = self.ring(st, "RqT", [128, 128], BF16, 8)
                Y0 = self.ring(st, "Y0", [128, 2, 64], F32, 8)
                GTr = self.ring(st, "GTr", [128, 64], BF16, 16)
                Fr = self.ring(st, "Fr", [128, 64], F32, 16)
                Hs = self.ring(st, "Hs", [128, 64], BF16, 3)
                Yr = self.ring(st, "Yr", [128, 2, 64], F32, 4)
                gn = self.ring(st, "gn", [128, 2, 64], F32, 4)
                gs_ = self.ring(st, "gs_", [128, 2], F32, 6)
                yT = self.ring(st, "yT", [128, 512], F32, 2)
                yst = self.ring(st, "yst", [128, 512], BF16, 2)
                ych = [S.chan(), S.chan()]
                pi2 = [0]

                def pb2():
                    b_ = banks[pi2[0] % 7]
                    pi2[0] += 1
                    return b_

                e2 = lambda ap: ap.rearrange("p (e s) -> p e s", e=2)
                idb3 = identb[:].unsqueeze(1).broadcast_to([128, 2, 128])
                NBK = 4
                for m in range(4):
                    ms = slice(m * 128, (m + 1) * 128)
                    H, RH = Hs.next()
                    self.memset("pool", H[:], 0.0, [RH])
                    for t in range(NT):
                        own = t >= NTP
                        o0 = (t - NTP) * 512
                        c0 = t * 512
                        ar, Rar = arl.next(); bt, Rbt = btl.next(); kt, Rkt = ktl.next(); vt, Rvt = vtl.next(); pc, Rpc = pcl.next()
                        lk = (arl.i - 1) % 2
                        pairs = [(ar[:], ARt[ms, t * 4:(t + 1) * 4, :]), (bt[:], Bt[ms, c0:c0 + 512]), (kt[:], Kt[ms, c0:c0 + 512]),
                                 (vt[:], Vt[ms, c0:c0 + 512]), (pc[:], PCt[ms, t * 8:(t + 1) * 8])]
                        wr = [Rar, Rbt, Rkt, Rvt, Rpc]
                        if own:
                            bo, Rbo = bol.next(); gr, Rgr = grl.next()
                            pairs += [(bo[:], BoT[ms, o0:o0 + 512]), (gr[:], GrT[ms, o0:o0 + 512])]
                            wr += [Rbo, Rgr]
                            yt_, Ryt = yT.next()
                        S.dma(ldch[lk], pairs, reads=[R_rw], writes=wr)
                        X = [dict() for _ in range(NBK)]
                        for b in range(NBK):
                            c_ = X[b]
                            bs = slice(b * 128, (b + 1) * 128)
                            c_["bs"] = bs
                            mt, Rmt = MT.next()
                            L0, RL0 = Lr.next()
                            for e in range(2):
                                hs = slice(e * 64, (e + 1) * 64)
                                ps, Rps = pb2()
                                self.mm(ps[:, 0:256], bt[hs, bs], ar[hs, b, :], [Rbt, Rar], [Rps])
                                self.mm(ps[:, 256:512], kt[hs, bs], ar[hs, b, :], [Rkt, Rar], [Rps])
                                self.tt("dve", mt[:, e, :], ps[:, :], mask4[:], ALU.mult, [Rps, Rm4], [Rmt])
                                psL, RpsL = pb2()
                                self.mm(psL[:, 0:128], ar[hs, b, 0:128], bt[hs, bs], [Rar, Rbt], [RpsL])
                                self.tt("dve", L0[:, e, :], psL[:, 0:128], maskL[:], ALU.mult, [RpsL, RmL], [RL0])
                            c_["mt"], c_["Rmt"], c_["L"], c_["RL"] = mt, Rmt, L0, RL0
                        for b in range(NBK):
                            c_ = X[b]
                            bs = c_["bs"]
                            pst, Rpst = bankb
                            o_ = (b % 2) * 512
                            self.tr(pst[:, o_ + 0:o_ + 128], ar[:, b, 0:128], identb[:], [Rar, Ridb], [Rpst])
                            self.tr(pst[:, o_ + 128:o_ + 256], bt[:, bs], identb[:], [Rbt, Ridb], [Rpst])
                            self.tr(pst[:, o_ + 256:o_ + 384], kt[:, bs], identb[:], [Rkt, Ridb], [Rpst])
                            self.tr(pst[:, o_ + 384:o_ + 512], vt[:, bs], identb[:], [Rvt, Ridb], [Rpst])
                            tm, Rtm = TM.next()
                            self.cp("act", tm[:], pst[:, o_:o_ + 512].rearrange("p (q f) -> p q f", q=4), [Rpst], [Rtm])
                            c_["tm"], c_["Rtm"] = tm, Rtm
                        for b in range(NBK):
                            c_ = X[b]
                            mt, Rmt = c_["mt"], c_["Rmt"]
                            AS, RAS = ASr.next()
                            self.cp("pool", AS[:, :, 0:128], mt[:, :, 0:128], [Rmt], [RAS])
                            self.tt("pool", AS[:, :, 128:256], mt[:, :, 0:128], idb3, ALU.add, [Rmt, Ridb], [RAS])
                            c_["AS"], c_["RAS"] = AS, RAS
                        for b in range(NBK):
                            c_ = X[b]
                            AS, RAS, Lk, RLk = c_["AS"], c_["RAS"], c_["L"], c_["RL"]
                            psa, Rpsa = pb2()
                            psl, Rpsl = pb2()
                            for e in range(2):
                                self.mm(psa[:, e * 128:(e + 1) * 128], Lk[:, e, :], AS[:, e, 0:128], [RLk, RAS], [Rpsa])
                                self.mm(psl[:, e * 128:(e + 1) * 128], AS[:, e, 0:128], Lk[:, e, :], [RAS, RLk], [Rpsl])
                            ASn, RASn = ASr.next()
                            self.cp("dve", ASn[:, :, 0:128], e2(psa[:, 0:256]), [Rpsa], [RASn])
                            self.cp("pool", ASn[:, :, 128:256], AS[:, :, 128:256], [RAS], [RASn])
                            Ln, RLn = Lr.next()
                            self.cp("act", Ln[:], e2(psl[:, 0:256]), [Rpsl], [RLn])
                            c_["AS"], c_["RAS"], c_["L"], c_["RL"] = ASn, RASn, Ln, RLn
                        for it in range(5):
                            last = it == 4
                            w_ = 128 if last else 0
                            for b in range(NBK):
                                c_ = X[b]
                                AS, RAS, Lk, RLk = c_["AS"], c_["RAS"], c_["L"], c_["RL"]
                                psm, Rpsm = pb2()
                                for e in range(2):
                                    self.mm(psm[:, e * 256 + w_:(e + 1) * 256], Lk[:, e, :], AS[:, e, w_:256], [RLk, RAS], [Rpsm])
                                ASn, RASn = ASr.next()
                                pm3 = e2(psm[:, 0:512])
                                self.tt("dve", ASn[:, :, 128:256], pm3[:, :, 128:256], AS[:, :, 128:256], ALU.add, [Rpsm, RAS], [RASn])
                                if not last:
                                    self.cp("act", ASn[:, :, 0:128], pm3[:, :, 0:128], [Rpsm], [RASn])
                                    psl, Rpsl = pb2()
                                    for e in range(2):
                                        self.mm(psl[:, e * 128:(e + 1) * 128], AS[:, e, 0:128], Lk[:, e, :], [RAS, RLk], [Rpsl])
                                    Ln, RLn = Lr.next()
                                    self.cp("act", Ln[:], e2(psl[:, 0:256]), [Rpsl], [RLn])
                                    c_["L"], c_["RL"] = Ln, RLn
                                c_["AS"], c_["RAS"] = ASn, RASn
                        for b in range(NBK):
                            c_ = X[b]
                            mt, Rmt, tm, Rtm = c_["mt"], c_["Rmt"], c_["tm"], c_["Rtm"]
                            xb, Rxb = Xb.next()
                            psv, Rpsv = pb2()
                            for e in range(2):
                                self.mm(psv[:, e * 64:(e + 1) * 64], mt[:, e, 256:384], tm[:, 3, e * 64:(e + 1) * 64], [Rmt, Rtm], [Rpsv])
                            self.cp("act", xb[:, :, 64:128], psv[:, 0:128].rearrange("p (e v) -> p e v", e=2), [Rpsv], [Rxb])
                            self.cp("pool", xb[:, :, 0:64], tm[:, 0, :].rearrange("p (e v) -> p e v", e=2), [Rtm], [Rxb])
                            c_["xb"], c_["Rxb"] = xb, Rxb
                        for b in range(NBK):
                            c_ = X[b]
                            AS, RAS, xb, Rxb = c_["AS"], c_["RAS"], c_["xb"], c_["Rxb"]
                            pstx, Rpstx = pb2()
                            for e in range(2):
                                self.mm(pstx[:, e * 128:(e + 1) * 128], AS[:, e, 128:256], xb[:, e, :], [RAS, Rxb], [Rpstx])
                            tx, Rtx = TXr.next()
                            self.cp("act", tx[:], e2(pstx[:, 0:256]), [Rpstx], [Rtx])
                            c_["tx"], c_["Rtx"] = tx, Rtx
                        if own:
                            for b in range(NBK):
                                c_ = X[b]
                                mt, Rmt, tm, Rtm, tx, Rtx = c_["mt"], c_["Rmt"], c_["tm"], c_["Rtm"], c_["tx"], c_["Rtx"]
                                psr, Rpsr = pb2()
                                for e in range(2):
                                    self.mm(psr[e * 64:(e + 1) * 64, 0:128], tx[:, e, 0:64], mt[:, e, 128:256], [Rtx, Rmt], [Rpsr])
                                rq, Rrq = RqT.next()
                                self.tt("dve", rq[:], psr[:, 0:128], ar[:, b, 128:256], ALU.add, [Rpsr, Rar], [Rrq])
                                psy, Rpsy = pb2()
                                for e in range(2):
                                    self.mm(psy[:, e * 64:(e + 1) * 64], mt[:, e, 128:256], tx[:, e, 64:128], [Rmt, Rtx], [Rpsy], start=True, stop=False)
                                    self.mm(psy[:, e * 64:(e + 1) * 64], mt[:, e, 384:512], tm[:, 3, e * 64:(e + 1) * 64], [Rmt, Rtm], [Rpsy], start=False, stop=True)
                                y0, Ry0 = Y0.next()
                                self.cp("act", y0[:], psy[:, 0:128].rearrange("p (e v) -> p e v", e=2), [Rpsy], [Ry0])
                                c_["rq"], c_["Rrq"], c_["y0"], c_["Ry0"] = rq, Rrq, y0, Ry0
                        for b in range(NBK):
                            c_ = X[b]
                            tm, Rtm, tx, Rtx = c_["tm"], c_["Rtm"], c_["tx"], c_["Rtx"]
                            c_["gt"], c_["ff"] = [], []
                            for c in range(2):
                                cs = slice(c * 64, (c + 1) * 64)
                                psg, Rpsg = pb2()
                                for e in range(2):
                                    self.mm(psg[e * 64:(e + 1) * 64, 0:64], tx[cs, e, 0:64], tm[cs, 1, e * 64:(e + 1) * 64], [Rtx, Rtm], [Rpsg])
                                gt, Rgt = GTr.next()
                                self.tt("dve", gt[0:64, :], psg[0:64, 0:64], ident[0:64, 0:64], ALU.add, [Rpsg, Rid], [Rgt])
                                self.tt("dve", gt[64:128, :], psg[64:128, 0:64], ident[64:128, 64:128], ALU.add, [Rpsg, Rid], [Rgt])
                                psf, Rpsf = pb2()
                                for e in range(2):
                                    es = slice(e * 64, (e + 1) * 64)
                                    self.mm(psf[es, 0:64], tm[cs, 1, es], tx[cs, e, 64:128], [Rtm, Rtx], [Rpsf], start=True, stop=False)
                                    self.mm(psf[es, 0:64], tm[cs, 2, es], tm[cs, 3, es], [Rtm], [Rpsf], start=False, stop=True)
                                ff, Rff = Fr.next()
                                pcc = pc[:, b * 2 + c:b * 2 + c + 1]
                                self.act(ff[:], psf[:, 0:64], AF.Copy, [Rpsf, Rpc], [Rff], scale=pcc)
                                c_["gt"].append((gt, Rgt))
                                c_["ff"].append((ff, Rff))
                        for b in range(NBK):
                            c_ = X[b]
                            bs = c_["bs"]
                            if own:
                                yy, Ryy = Yr.next()
                                rq, Rrq, y0, Ry0 = c_["rq"], c_["Rrq"], c_["y0"], c_["Ry0"]
                            for c in range(2):
                                cs = slice(c * 64, (c + 1) * 64)
                                gt, Rgt = c_["gt"][c]
                                ff, Rff = c_["ff"][c]
                                if own:
                                    for e in range(2):
                                        es = slice(e * 64, (e + 1) * 64)
                                        psq, Rpsq = pb2()
                                        self.mm(psq[cs, 0:64], rq[es, cs], H[es, :], [Rrq, RH], [Rpsq])
                                        self.tt("dve", yy[cs, e, :], psq[cs, 0:64], y0[cs, e, :], ALU.add, [Rpsq, Ry0], [Ryy])
                                Hn, RHn = Hs.next()
                                for e in range(2):
                                    es = slice(e * 64, (e + 1) * 64)
                                    psh, Rpsh = pb2()
                                    self.mm(psh[es, 0:64], gt[es, :], H[es, :], [Rgt, RH], [Rpsh])
                                    self.stt(Hn[es, :], psh[es, 0:64], pc[es, b * 2 + c:b * 2 + c + 1], ff[es, :], ALU.mult, ALU.add, [Rpsh, Rpc, Rff], [RHn])
                                H, RH = Hn, RHn
                            if own:
                                s1, Rs1 = gs_.next()
                                S.op("dve", lambda e, o=s1, i=yy: e.tensor_reduce(out=o[:], in_=i[:], axis=AX.X, op=ALU.add), [Ryy], [Rs1])
                                self.ts("dve", s1[:], s1[:], -1.0 / 64, ALU.mult, [Rs1], [Rs1])
                                yc, Ryc = gn.next()
                                self.tt("dve", yc[:], yy[:], s1[:].unsqueeze(2).broadcast_to([128, 2, 64]), ALU.add, [Ryy, Rs1], [Ryc])
                                y2, Ry2 = gn.next()
                                self.tt("pool", y2[:], yc[:], yc[:], ALU.mult, [Ryc], [Ry2])
                                s2, Rs2 = gs_.next()
                                S.op("dve", lambda e, o=s2, i=y2: e.tensor_reduce(out=o[:], in_=i[:], axis=AX.X, op=ALU.add), [Ry2], [Rs2])
                                self.act(s2[:], s2[:], AF.Sqrt, [Rs2], [Rs2], bias=GN_EPS, scale=1.0 / 64)
                                self.recip(s2[:], s2[:], [Rs2], [Rs2])
                                self.tt("dve", yc[:], yc[:], s2[:].unsqueeze(2).broadcast_to([128, 2, 64]), ALU.mult, [Ryc, Rs2], [Ryc])
                                pstt, Rpstt = pb2()
                                self.tr(pstt[:, 0:128], yc[:].rearrange("p e v -> p (e v)"), ident[:], [Ryc, Rid], [Rpstt])
                                self.act(yt_[:, bs], pstt[:, 0:128], AF.Identity, [Rpstt, Rvec], [Ryt], bias=vcol("ln_b", m), scale=vcol("ln_w", m))
                        if own:
                            self.tt("pool", yt_[:], yt_[:], bo[:], ALU.add, [Ryt, Rbo], [Ryt])
                            ys, Rys = yst.next()
                            yk = (yst.i - 1) % 2
                            self.tt("pool", ys[:], yt_[:], gr[:], ALU.mult, [Ryt, Rgr], [Rys])
                            S.dma(ych[yk], [(YbT[ms, o0:o0 + 512], ys[:])], reads=[Rys], writes=[R_YbT])
                S.barrier()
                S.emit()
                for n_ in S.ENGS:
                    S.e[n_].ops = []

            with ExitStack() as st:
                if self.upto < 4:
                    raise _Stop()
                Vall = self.sb(st, "Vall", [128, NB, 520], BF16); RVall = Res()
                nv = max(1, NB // 16)
                for i in range(0, NB, 16):
                    j = min(NB, i + 16)
                    S.dma(S.misc(), [(Vall[:, i:j, :], Vs[:, i:j, :])], reads=[R_Vs], writes=[RVall])
                Kh = self.ring(st, "Kh", [128, TT], BF16, 2)
                Qh = self.ring(st, "Qh", [128, TO], BF16, 2)
                kqch = [S.chan(), S.chan()]
                PT = self.ring(st, "PT", [128, 512], BF16, 6)
                osb = self.ring(st, "osb", [128, 512], F32, 2)
                rl = self.ring(st, "rl", [128, 512], F32, 2)
                ost = self.ring(st, "ost", [128, 512], BF16, 2)
                och = [S.chan(), S.chan()]
                sbanks = Ring(banks[0:4])
                obanks = Ring(banks[4:6])
                bbanks = Ring(banks[6:7])
                V5 = Vall[:].rearrange("p b (h d) -> p b h d", d=65)
                LOOK = 2
                heads = []

                def load_head(hd):
                    K_, RK_ = Kh.next(); Q_, RQ_ = Qh.next()
                    hk = (Kh.i - 1) % 2
                    S.dma(kqch[hk], [(K_[0:64, :], KnT[hd * 64:(hd + 1) * 64, :]), (K_[64:97, :], KpeT[:, :]), (Q_[0:97, :], QT[hd, :, :])],
                          reads=[R_KnT, R_KpeT, R_QT], writes=[RK_, RQ_])
                    return (K_, RK_, Q_, RQ_)

                pend = []

                def do_pv(item):
                    (hd, qt, kb, nkb, cst, pt, Rpt, po, Rpo) = item
                    self.mm(po[0:65, cst:512], V5[:, kb, hd, :], pt[:, cst:512], [RVall, Rpt], [Rpo], start=(kb == 0), stop=(kb == nkb - 1), inc=True)
                    if kb == nkb - 1:
                        q0 = qt * 512
                        o_, Ro_ = osb.next()
                        self.cp("act", o_[0:65, :], po[0:65, :], [Rpo], [Ro_])
                        r_, Rr_ = rl.next()
                        self.recip(r_[64:65, :], o_[64:65, :], [Ro_], [Rr_])
                        pb_, Rpb_ = bbanks.next()
                        self.mm(pb_[0:64, :], onesf[64:65, 0:64], r_[64:65, :], [Ronesf, Rr_], [Rpb_])
                        os_, Ros_ = ost.next()
                        ok = (ost.i - 1) % 2
                        self.tt("dve", os_[0:64, :], pb_[0:64, :], o_[0:64, :], ALU.mult, [Rpb_, Ro_], [Ros_])
                        S.dma(och[ok], [(OaT[hd * 64:(hd + 1) * 64, q0:q0 + 512], os_[0:64, :])], reads=[Ros_], writes=[R_OaT])

                nxt = load_head(0)
                for hd in range(NH):
                    K_, RK_, Q_, RQ_ = nxt
                    if hd + 1 < NH:
                        nxt = load_head(hd + 1)
                    for qt in range(NTO):
                        q0 = qt * 512
                        nkb = (TP + q0 + 512) // 128
                        po, Rpo = obanks.next()
                        for kb in range(nkb):
                            jd = kb - (nkb - 4)
                            cst = 0 if jd < 0 else jd * 128
                            ps, Rps = sbanks.next()
                            self.mm(ps[:, cst:512], K_[0:97, kb * 128:(kb + 1) * 128], Q_[0:97, q0 + cst:q0 + 512], [RK_, RQ_], [Rps])
                            pt, Rpt = PT.next()
                            self.act(pt[:, cst:512], ps[:, cst:512], AF.Exp, [Rps], [Rpt])
                            if jd >= 0:
                                self.tt("pool", pt[:, cst:cst + 128], pt[:, cst:cst + 128], trim[:], ALU.mult, [Rpt, Rtri], [Rpt])
                            pend.append((hd, qt, kb, nkb, cst, pt, Rpt, po, Rpo))
                            if len(pend) > LOOK:
                                do_pv(pend.pop(0))
                while pend:
                    do_pv(pend.pop(0))
                S.barrier()
                S.emit()
                for n_ in S.ENGS:
                    S.e[n_].ops = []

            with ExitStack() as st:
                if self.upto < 5:
                    raise _Stop()
                womla = self.sb(st, "womla", [128, 4, D], BF16)
                worw = self.sb(st, "worw", [128, 4, D], BF16)
                wout = self.sb(st, "wout", [128, 8, D], BF16)
                wpg = self.sb(st, "wpg", [128, 8, D], BF16)
                wpp = self.sb(st, "wpp", [128, 2, D], BF16)
                Rw4 = Res("w4")
                S.dma(S.misc("pool"), [(womla[:], womla_d.rearrange("(c p) n -> p c n", p=128)),
                                 (worw[:], worw_d.rearrange("(c p) n -> p c n", p=128)),
                                 (wpp[:], wpp_d.rearrange("(c p) n -> p c n", p=128))], writes=[Rw4], q="pool")
                wout3 = wout_d.rearrange("(c p) n -> p c n", p=128)
                wpg3 = wpg_d.rearrange("(c p) n -> p c n", p=128)
                for c in range(8):
                    S.dma(S.misc("pool"), [(wout[:, c, :], wout3[:, c, :]), (wpg[:, c, :], wpg3[:, c, :])], writes=[Rw4], q="pool")
                wupr = self.ring(st, "wupr", [128, 8, 256], BF16, 3)
                wupc = [S.chan() for _ in range(3)]
                wdnr = self.ring(st, "wdnr", [128, 2, 512], BF16, 4)
                wdnc = [S.chan() for _ in range(4)]
                wup3 = wupb.rearrange("(c p) n -> p c n", p=128)
                wdn3 = wdnb.rearrange("(c p) n -> p c n", p=128)
                x4 = self.ring(st, "x4_", [128, 8, 512], F32, 1)
                xc4 = S.chan()
                inb = self.ring(st, "inb", [128, 4, 512], BF16, 2)
                inc_ = [S.chan(), S.chan()]
                gtl = self.ring(st, "gtl", [128, 8, 512], BF16, 2)
                gtc = [S.chan(), S.chan()]
                pin = self.sb(st, "pin", [128, 2, 512], BF16); Rpin = Res()
                pinc = S.chan()
                mix = self.sb(st, "mix", [128, 8, 512], BF16); Rmix = Res()
                mtmp = self.ring(st, "mtmp", [128, 512], F32, 3)
                sq4 = self.ring(st, "sq4", [128, 512], BF16, 4)
                h4 = self.sb(st, "h4", [128, 8, 512], BF16); Rh4 = Res()
                hid = self.sb(st, "hid", [128, 16, 512], BF16)
                Rhid = [Res() for _ in range(16)]
                rel = self.ring(st, "rel", [128, 512], F32, 3)
                rs4 = self.ring(st, "rs4", [128, 512], F32, 2)
                t4 = self.ring(st, "t4", [128, 512], F32, 2)
                gsg = self.ring(st, "gsg", [128, 512], F32, 2)
                osb4 = self.ring(st, "osb4", [128, 512], F32, 2)
                och4 = [S.chan(), S.chan()]
                pi4 = [0]

                def pb4():
                    b_ = banks[pi4[0] % 7]
                    pi4[0] += 1
                    return b_

                def rms4(xt, Rxt):
                    ps, Rps = pb4()
                    for c in range(8):
                        sq, Rsq = sq4.next()
                        self.act(sq[:], xt[:, c, :], AF.Square, [Rxt], [Rsq])
                        self.mm(ps[:, :], onesb[:, :], sq[:], [Rones, Rsq], [Rps], start=(c == 0), stop=(c == 7), inc=True)
                    t1, Rt1 = t4.next()
                    self.act(t1[:], ps[:, :], AF.Sqrt, [Rps], [Rt1], bias=RMS_EPS, scale=1.0 / D)
                    rs, Rrs = rs4.next()
                    self.recip(rs[:], t1[:], [Rt1], [Rrs])
                    return rs, Rrs

                for to in range(NTO):
                    o0 = to * 512
                    xt, Rxt = x4.next()
                    S.dma(xc4, [(xt[:], xT3[:, :, TP + o0:TP + o0 + 512])], writes=[Rxt])
                    oa, Roa = inb.next()
                    S.dma(inc_[0], [(oa[:], OaT[:, o0:o0 + 512].rearrange("(c p) t -> p c t", p=128))], reads=[R_OaT], writes=[Roa])
                    yb, Ryb = inb.next()
                    S.dma(inc_[1], [(yb[:], YbT[:, o0:o0 + 512].rearrange("(c p) t -> p c t", p=128))], reads=[R_YbT], writes=[Ryb])
                    ga, Rga = gtl.next()
                    S.dma(gtc[0], [(ga[:], GT[0:1024, o0:o0 + 512].rearrange("(c p) t -> p c t", p=128))], reads=[R_GT], writes=[Rga])
                    gb, Rgb = gtl.next()
                    S.dma(gtc[1], [(gb[:], GT[1024:2048, o0:o0 + 512].rearrange("(c p) t -> p c t", p=128))], reads=[R_GT], writes=[Rgb])
                    S.dma(pinc, [(pin[:], pT[:, o0:o0 + 512].rearrange("(c p) t -> p c t", p=128))], writes=[Rpin], q="pool")
                    for mt_ in range(8):
                        msl = slice(mt_ * 128, (mt_ + 1) * 128)
                        psa, Rpsa = pb4()
                        for c in range(4):
                            self.mm(psa[:, :], womla[:, c, msl], oa[:, c, :], [Rw4, Roa], [Rpsa], start=(c == 0), stop=(c == 3))
                        psb, Rpsb = pb4()
                        for c in range(4):
                            self.mm(psb[:, :], worw[:, c, msl], yb[:, c, :], [Rw4, Ryb], [Rpsb], start=(c == 0), stop=(c == 3))
                        m1, Rm1 = mtmp.next()
                        self.tt("dve", m1[:], psa[:, :], ga[:, mt_, :], ALU.mult, [Rpsa, Rga], [Rm1])
                        m2, Rm2 = mtmp.next()
                        self.tt("dve", m2[:], psb[:, :], gb[:, mt_, :], ALU.mult, [Rpsb, Rgb], [Rm2])
                        self.tt("pool", mix[:, mt_, :], m1[:], m2[:], ALU.add, [Rm1, Rm2], [Rmix])
                    for mt_ in range(8):
                        msl = slice(mt_ * 128, (mt_ + 1) * 128)
                        ps, Rps = pb4()
                        for c in range(8):
                            self.mm(ps[:, :], wout[:, c, msl], mix[:, c, :], [Rw4, Rmix], [Rps], start=(c == 0), stop=(c == 7))
                        self.tt("dve", xt[:, mt_, :], ps[:, :], xt[:, mt_, :], ALU.add, [Rps, Rxt], [Rxt])
                    rs, Rrs = rms4(xt, Rxt)
                    for c in range(8):
                        self.stt(h4[:, c, :], xt[:, c, :], vcol("g_ffn", c), rs[:], ALU.mult, ALU.mult, [Rxt, Rvec, Rrs], [Rh4])
                    for hh in range(2):
                        for uc in range(8):
                            wu, Rwu = wupr.next()
                            uk = (wupr.i - 1) % 3
                            col = hh * 2048 + uc * 256
                            S.dma(wupc[uk], [(wu[:], wup3[:, :, col:col + 256])], reads=[R_wconv], writes=[Rwu])
                            for j in range(2):
                                mi = uc * 2 + j
                                ps, Rps = pb4()
                                for c in range(8):
                                    self.mm(ps[:, :], wu[:, c, j * 128:(j + 1) * 128], h4[:, c, :], [Rwu, Rh4], [Rps], start=(c == 0), stop=(c == 7))
                                rl_, Rrl = rel.next()
                                self.act(rl_[:], ps[:, :], AF.Relu, [Rps], [Rrl])
                                self.tt("pool", hid[:, mi, :], rl_[:], rl_[:], ALU.mult, [Rrl], [Rhid[mi]])
                        for oh in range(2):
                            pss = [pb4() for _ in range(4)]
                            for kc in range(8):
                                wd, Rwd = wdnr.next()
                                dk = (wdnr.i - 1) % 4
                                kr = hh * 16 + kc * 2
                                S.dma(wdnc[dk], [(wd[:], wdn3[:, kr:kr + 2, oh * 512:(oh + 1) * 512])], reads=[R_wconv], writes=[Rwd])
                                for j in range(2):
                                    kci = kc * 2 + j
                                    for mq in range(4):
                                        ps, Rps = pss[mq]
                                        self.mm(ps[:, :], wd[:, j, mq * 128:(mq + 1) * 128], hid[:, kci, :], [Rwd, Rhid[kci]], [Rps],
                                                start=(kci == 0), stop=(kci == 15), inc=(kci == 15 or (j == 1 and mq == 3)))
                            for mq in range(4):
                                mt_ = oh * 4 + mq
                                ps, Rps = pss[mq]
                                self.tt("dve", xt[:, mt_, :], ps[:, :], xt[:, mt_, :], ALU.add, [Rps, Rxt], [Rxt])
                    rs, Rrs = rms4(xt, Rxt)
                    for c in range(8):
                        self.stt(h4[:, c, :], xt[:, c, :], vcol("g_ple", c), rs[:], ALU.mult, ALU.mult, [Rxt, Rvec, Rrs], [Rh4])
                    for mt_ in range(8):
                        msl = slice(mt_ * 128, (mt_ + 1) * 128)
                        ps, Rps = pb4()
                        for c in range(8):
                            self.mm(ps[:, :], wpg[:, c, msl], h4[:, c, :], [Rw4, Rh4], [Rps], start=(c == 0), stop=(c == 7))
                        gg, Rgg = gsg.next()
                        self.act(gg[:], ps[:, :], AF.Sigmoid, [Rps], [Rgg])
                        ps2, Rps2 = pb4()
                        for c in range(2):
                            self.mm(ps2[:, :], wpp[:, c, msl], pin[:, c, :], [Rw4, Rpin], [Rps2], start=(c == 0), stop=(c == 1))
                        self.tt("dve", gg[:], ps2[:, :], gg[:], ALU.mult, [Rps2, Rgg], [Rgg])
                        self.tt("pool", xt[:, mt_, :], xt[:, mt_, :], gg[:], ALU.add, [Rxt, Rgg], [Rxt])
                    rs, Rrs = rms4(xt, Rxt)
                    for c in range(8):
                        ob, Rob = osb4.next()
                        okk = (osb4.i - 1) % 2
                        self.stt(ob[:], xt[:, c, :], vcol("g_final", c), rs[:], ALU.mult, ALU.mult, [Rxt, Rvec, Rrs], [Rob])
                        S.dma(och4[okk], [(outT[c * 128:(c + 1) * 128, o0:o0 + 512], ob[:])], reads=[Rob])
                S.barrier()
                S.emit()
        return nc


def const_inputs():
    s = np.arange(128)[:, None]
    t = np.arange(128)[None, :]
    same = (s // 64) == (t // 64)
    m_lt = ((s < t) & same).astype(np.float32)
    m_le = ((s <= t) & same).astype(np.float32)
    mask4 = np.concatenate([m_lt, m_le, m_lt, m_le], axis=1)
    maskL = ((t < s) & same).astype(np.float32)
    tri = (s <= t).astype(np.float32)
    bd = same.astype(np.float32)
    scan = np.ones((128, 512), np.float32)
    scan[:, ::64] = 0.0
    inv_freq = (np.float32(10000.0) ** (-np.arange(16, dtype=np.float32) / np.float32(16))).astype(np.float32)
    ropec = np.zeros((128, 2), np.float32)
    ropec[64:80, 0] = inv_freq
    ropec[80:96, 0] = inv_freq
    ropec[64:80, 1] = -1.0
    ropec[80:96, 1] = 1.0
    return dict(c_ident=np.eye(128, dtype=np.float32), c_mask4=mask4, c_maskL=maskL, c_tri=tri, c_bd=bd, c_scan=scan, ropec=ropec)


def shared_inputs(inp):
    f = lambda a: np.ascontiguousarray(np.asarray(a, dtype=np.float32))
    w_in = f(inp["w_in"][0])
    z64 = np.zeros((D, 64), np.float32)
    kpe = w_in[:, 640:672]
    w_kpe = np.concatenate([z64, kpe, z64, kpe[:, 16:32], kpe[:, 0:16]], axis=1)
    w_uq = f(inp["w_uq"][0]).reshape(384, NH, 96)
    wq_sw = np.zeros_like(w_uq)
    wq_sw[:, :, 64:80] = w_uq[:, :, 80:96]
    wq_sw[:, :, 80:96] = w_uq[:, :, 64:80]
    w_ukv = f(inp["w_ukv"][0]).reshape(256, NH, 128)
    vec = {"g_mix": inp["g_mix"][0], "g_q_a": inp["g_q_a"][0], "g_kv_a": inp["g_kv_a"][0], "mu": inp["mu_rwkv"][0],
           "w0": inp["w0"][0], "a0": inp["a0"][0], "k_k": inp["k_k"][0], "k_a": inp["k_a"][0],
           "r_k": np.asarray(inp["r_k"][0]).reshape(-1), "ln_w": inp["ln_x_w"][0], "ln_b": inp["ln_x_b"][0],
           "g_ffn": inp["g_ffn"][0], "g_ple": inp["g_ple"][0], "g_final": inp["g_final"]}
    vecs = np.zeros((128, NVEC), np.float32)
    for name, n in VEC_LAYOUT:
        v = f(vec[name]).reshape(n, 128)
        vecs[:, VEC_OFF[name]:VEC_OFF[name] + n] = v.T
    d = dict(vecs=vecs, w_in=w_in, w_kpe=np.ascontiguousarray(w_kpe), wq=np.ascontiguousarray(w_uq.reshape(384, 768)),
             wq_sw=np.ascontiguousarray(wq_sw.reshape(384, 768)),
             wukv_k=np.ascontiguousarray(w_ukv[:, :, 0:64].reshape(256, 512)),
             wukv_v=np.ascontiguousarray(w_ukv[:, :, 64:128].reshape(256, 512)),
             w_o_mla=f(inp["w_o_mla"][0]), w2=f(inp["w2"][0]), a2=f(inp["a2"][0]), g2=f(inp["g2"][0]),
             w_o_rwkv=f(inp["w_o_rwkv"][0]), w_out=f(inp["w_out"][0]), w_up=f(inp["w_ffn_up"][0]), w_down=f(inp["w_ffn_down"][0]),
             w_pg=f(inp["w_ple_gate"][0]), w_pp=f(inp["w_ple_proj"][0]))
    d.update(const_inputs())
    return d


def core_inputs(x_b, p_b, pos_b, half, TP, TO):
    TT = TP + TO
    xT = np.zeros((D, TT), np.float32)
    posr = np.zeros((1, TT), np.int32)
    mrow = np.zeros((1, TT), np.float32)
    o0 = half * TP
    if half == 1:
        xT[:, 0:TP] = x_b[0:TP].T
        posr[0, 0:TP] = pos_b[0:TP]
    else:
        mrow[0, 0:TP] = -30000.0
    xT[:, TP:] = x_b[o0:o0 + TO].T
    posr[0, TP:] = pos_b[o0:o0 + TO]
    pT = np.ascontiguousarray(p_b[o0:o0 + TO].T.astype(np.float32))
    return dict(xT=xT, pT=pT, pos=posr, maskrow=mrow.astype(ml_dtypes.bfloat16))


_NC_CACHE = {}


def get_nc(TP, TO):
    if (TP, TO) not in _NC_CACHE:
        _NC_CACHE[(TP, TO)] = B(TP, TO).build()
    return _NC_CACHE[(TP, TO)]


def kernel(**inputs):
    x = np.asarray(inputs["x"], dtype=np.float32)
    p = np.asarray(inputs["p"], dtype=np.float32)[0]
    pos = np.asarray(inputs["positions"]).astype(np.int32)
    Bn, Sq, _ = x.shape
    TP = TO = Sq // 2
    nc = get_nc(TP, TO)
    sh = shared_inputs(inputs)
    in_maps = []
    for c in range(8):
        b, half = c // 2, c % 2
        m = dict(sh)
        m.update(core_inputs(x[b], p[b], pos[b], half, TP, TO))
        in_maps.append(m)
    res = run_bass_kernel_spmd(nc, in_maps, core_ids=list(range(8)))
    out = np.zeros((Bn, Sq, D), np.float32)
    for c in range(8):
        b, half = c // 2, c % 2
        out[b, half * TO:(half + 1) * TO, :] = res.results[c]["outT"].T
    return out
```

```python
from contextlib import ExitStack
import numpy as np
import ml_dtypes
import concourse.bass as bass
import concourse.mybir as mybir
from concourse.bass_utils import run_bass_kernel_spmd

F32 = mybir.dt.float32
BF16 = mybir.dt.bfloat16
I32 = mybir.dt.int32
AF = mybir.ActivationFunctionType
ALU = mybir.AluOpType
AX = mybir.AxisListType

D = 1024
NH = 8
RMS_EPS = 1e-6
GN_EPS = 64 * 1e-5
SCALE = 96 ** -0.5
EXPH = float(np.exp(-0.5))
TWO_PI = 2.0 * np.pi
C1 = 6.28125
C2 = float(TWO_PI - 6.28125)


class Res:
    __slots__ = ("name", "w", "rd")

    def __init__(self, name=""):
        self.name = name
        self.w = None
        self.rd = []


class Chan:
    def __init__(self, sem, name):
        self.sem = sem
        self.count = 0
        self.name = name


class _Eng:
    def __init__(self, name, sem):
        self.name = name
        self.sem = sem
        self.count = 0
        self.ops = []
        self.waited = {}


class Sched:
    ENGS = ("pe", "act", "dve", "pool", "sp")
    HMAP = {"pe": "tensor", "act": "scalar", "dve": "vector", "pool": "gpsimd", "sp": "sync"}

    def __init__(self, nc, stack, n_chan=90):
        self.nc = nc
        self.e = {}
        for n in self.ENGS:
            self.e[n] = _Eng(n, stack.enter_context(nc.semaphore("s_" + n)))
        self.chans = [Chan(stack.enter_context(nc.semaphore("c%d" % i)), "c%d" % i) for i in range(n_chan)]
        self.chan_i = 4
        self.nops = 0
        self.misc_i = 0

    def misc(self, q="sp"):
        base = 0 if q == "sp" else 2
        c = self.chans[base + self.misc_i % 2]
        self.misc_i += 1
        return c

    def chan(self):
        c = self.chans[self.chan_i]
        self.chan_i += 1
        return c

    def _need(self, eng, reads, writes):
        E = self.e[eng]
        need = {}

        def add(t):
            if t is None:
                return
            key, val = t
            if key is E and eng == "pe":
                return
            if need.get(key, 0) < val:
                need[key] = val

        for r in reads:
            add(r.w)
        for w in writes:
            add(w.w)
            for t in w.rd:
                add(t)
        for key, val in need.items():
            if E.waited.get(key, 0) < val:
                E.waited[key] = val
                E.ops.append(("wait", key.sem, val))

    def op(self, eng, fn, reads=(), writes=(), inc=True):
        E = self.e[eng]
        self._need(eng, reads, writes)
        if inc:
            E.count += 1
            t = (E, E.count)
        else:
            t = (E, E.count + 1)
        E.ops.append(("op", fn, inc))
        for r in reads:
            r.rd.append(t)
            if len(r.rd) > 64:
                r.rd = _compress(r.rd)
        for w in writes:
            w.w = t
            w.rd = []
        self.nops += 1
        return t

    def dma(self, chan, pairs, reads=(), writes=(), q="sp"):
        E = self.e[q]
        if chan.count > 0 and E.waited.get(chan, 0) < chan.count:
            E.waited[chan] = chan.count
            E.ops.append(("wait", chan.sem, chan.count))
        self._need(q, reads, writes)
        for (o, i) in pairs:
            chan.count += 16
            E.ops.append(("dma", o, i, chan.sem))
        t = (chan, chan.count)
        for r in reads:
            r.rd.append(t)
        for w in writes:
            w.w = t
            w.rd = []
        return t

    def barrier(self):
        for n in self.ENGS:
            E = self.e[n]
            for m in self.ENGS:
                O = self.e[m]
                if O is E or O.count == 0:
                    continue
                if E.waited.get(O, 0) < O.count:
                    E.waited[O] = O.count
                    E.ops.append(("wait", O.sem, O.count))
            for c in self.chans:
                if c.count and E.waited.get(c, 0) < c.count:
                    E.waited[c] = c.count
                    E.ops.append(("wait", c.sem, c.count))

    def emit(self):
        nc = self.nc
        with nc.Block() as block:
            for n in self.ENGS:
                E = self.e[n]

                def body(h, E=E):
                    for o in E.ops:
                        if o[0] == "wait":
                            h.wait_ge(o[1], o[2])
                        elif o[0] == "op":
                            ins = o[1](h)
                            if o[2]:
                                ins.then_inc(E.sem, 1)
                        else:
                            h.dma_start(out=o[1], in_=o[2]).then_inc(o[3], 16)

                getattr(block, self.HMAP[n])(body)


def _compress(tickets):
    best = {}
    for key, val in tickets:
        if best.get(key, 0) < val:
            best[key] = val
    return list(best.items())


class Ring:
    def __init__(self, items):
        self.items = items
        self.i = 0

    def next(self):
        it = self.items[self.i % len(self.items)]
        self.i += 1
        return it


VEC_LAYOUT = [("g_mix", 8), ("g_q_a", 3), ("g_kv_a", 2), ("mu", 14), ("w0", 4), ("a0", 4), ("k_k", 4),
              ("k_a", 4), ("r_k", 4), ("ln_w", 4), ("ln_b", 4), ("g_ffn", 8), ("g_ple", 8), ("g_final", 8)]
VEC_OFF = {}
_o = 0
for _n, _c in VEC_LAYOUT:
    VEC_OFF[_n] = _o
    _o += _c
NVEC = _o
OM_OFF = NVEC
OMKA_OFF = NVEC + 14
NVEC_TOT = NVEC + 18


class _Stop(Exception):
    pass


class B:
    def __init__(self, TP, TO, debug=False, upto=5, p2_stop=0):
        self.p2_stop = p2_stop
        self.debug = debug
        self.upto = upto
        self.TP, self.TO = TP, TO
        self.TT = TP + TO
        self.nc = bass.Bass("TRN2", target_bir_lowering=False)
        self.st = ExitStack()
        self.S = None

    def din(self, name, shape, dt=F32):
        return self.nc.dram_tensor(name, list(shape), dt, kind="ExternalInput").ap()

    def dscr(self, name, shape, dt=BF16):
        kind = "ExternalOutput" if self.debug else "Internal"
        return self.nc.dram_tensor(name, list(shape), dt, kind=kind).ap()

    def sb(self, st, name, shape, dt=F32):
        self._uid = getattr(self, "_uid", 0) + 1
        return st.enter_context(self.nc.sbuf_tensor("s%d_%s" % (self._uid, name), list(shape), dt))

    def ring(self, st, name, shape, dt, n):
        return Ring([(self.sb(st, "%s%d" % (name, i), shape, dt), Res("%s%d" % (name, i))) for i in range(n)])

    def mm(self, out, lhsT, rhs, reads, writes, start=True, stop=True, inc=None):
        self.S.op("pe", lambda e: e.matmul(out, lhsT, rhs, start=start, stop=stop), reads, writes, inc=(stop if inc is None else inc))

    def tr(self, out, in_, ident, reads, writes, inc=True):
        self.S.op("pe", lambda e: e.transpose(out, in_, ident), reads, writes, inc=inc)

    def act(self, out, in_, func, reads, writes, bias=0.0, scale=1.0):
        if func == AF.Copy and not (isinstance(bias, float) and isinstance(scale, float)):
            func = AF.Identity
        self.S.op("act", lambda e: e.activation(out=out, in_=in_, func=func, bias=bias, scale=scale), reads, writes)

    def tt(self, eng, out, in0, in1, op, reads, writes):
        self.S.op(eng, lambda e: e.tensor_tensor(out=out, in0=in0, in1=in1, op=op), reads, writes)

    def ts(self, eng, out, in0, s1, op0, reads, writes, s2=None, op1=None):
        if op1 is None:
            self.S.op(eng, lambda e: e.tensor_scalar(out=out, in0=in0, scalar1=s1, scalar2=None, op0=op0), reads, writes)
        else:
            self.S.op(eng, lambda e: e.tensor_scalar(out=out, in0=in0, scalar1=s1, scalar2=s2, op0=op0, op1=op1), reads, writes)

    def stt(self, out, in0, scalar, in1, op0, op1, reads, writes):
        self.S.op("dve", lambda e: e.scalar_tensor_tensor(out=out, in0=in0, scalar=scalar, in1=in1, op0=op0, op1=op1), reads, writes)

    def cp(self, eng, out, in_, reads, writes):
        if eng == "act":
            self.act(out, in_, AF.Copy, reads, writes)
        else:
            self.S.op(eng, lambda e: e.tensor_copy(out=out, in_=in_), reads, writes)

    def ckpt(self, n):
        if self.p2_stop == n:
            self.S.barrier()
            self.S.emit()
            raise _Stop()

    def memset(self, eng, ap, val, writes):
        self.S.op(eng, lambda e: e.memset(ap, val), (), writes)

    def recip(self, out, in_, reads, writes):
        self.S.op("dve", lambda e: e.reciprocal(out=out, in_=in_), reads, writes)

    def build(self):
        try:
            self._build()
        except _Stop:
            pass
        return self.nc

    def _build(self):
        nc, TP, TO, TT = self.nc, self.TP, self.TO, self.TT
        NT, NTP, NTO = TT // 512, TP // 512, TO // 512
        NB = TT // 128
        xT = self.din("xT", [D, TT])
        pT = self.din("pT", [256, TO])
        pos = self.din("pos", [1, TT], I32)
        maskrow = self.din("maskrow", [1, TT], BF16)
        vecs_d = self.din("vecs", [128, NVEC])
        rc_d = self.din("ropec", [128, 2])
        w_in = self.din("w_in", [D, 4512])
        w_kpe = self.din("w_kpe", [D, 192])
        wq_d = self.din("wq", [384, 768])
        wqs_d = self.din("wq_sw", [384, 768])
        wkk_d = self.din("wukv_k", [256, 512])
        wkv_d = self.din("wukv_v", [256, 512])
        womla_d = self.din("w_o_mla", [512, D])
        w2_d = self.din("w2", [64, 512])
        a2_d = self.din("a2", [64, 512])
        g2_d = self.din("g2", [128, 512])
        worw_d = self.din("w_o_rwkv", [512, D])
        wout_d = self.din("w_out", [D, D])
        wup_d = self.din("w_up", [D, 4096])
        wdn_d = self.din("w_down", [4096, D])
        wpg_d = self.din("w_pg", [D, D])
        wpp_d = self.din("w_pp", [256, D])
        cm_ident_d = self.din("c_ident", [128, 128])
        cm_mask4_d = self.din("c_mask4", [128, 512])
        cm_maskL_d = self.din("c_maskL", [128, 128])
        cm_tri_d = self.din("c_tri", [128, 128])
        cm_bd_d = self.din("c_bd", [128, 128])
        cm_scan_d = self.din("c_scan", [128, 512])
        outT = nc.dram_tensor("outT", [D, TO], F32, kind="ExternalOutput").ap()
        QT = self.dscr("QT", [NH, 97, TO])
        KnT = self.dscr("KnT", [512, TT])
        KpeT = self.dscr("KpeT", [33, TT])
        Vs = self.dscr("Vs", [128, NB, 520])
        GT = self.dscr("GT", [2048, TO])
        ARt = self.dscr("ARt", [512, NB, 256])
        Bt = self.dscr("Bt", [512, TT])
        Kt = self.dscr("Kt", [512, TT])
        Vt = self.dscr("Vt", [512, TT])
        PCt = self.dscr("PCt", [512, TT // 64], F32)
        GrT = self.dscr("GrT", [512, TO])
        BoT = self.dscr("BoT", [512, TO])
        OaT = self.dscr("OaT", [512, TO])
        YbT = self.dscr("YbT", [512, TO])
        wupb = self.dscr("wupb", [D, 4096])
        wdnb = self.dscr("wdnb", [4096, D])
        R_wconv = Res("wconv")
        R_QT, R_KnT, R_KpeT, R_Vs, R_GT = Res("QT"), Res("KnT"), Res("KpeT"), Res("Vs"), Res("GT")
        R_rw, R_OaT, R_YbT = Res("rwscr"), Res("OaT"), Res("YbT")

        with self.st as st0:
            S = self.S = Sched(nc, st0)
            vecs = self.sb(st0, "vecs", [128, NVEC_TOT]); Rvec = Res("vecs")
            ropec = self.sb(st0, "ropec", [128, 2]); Rrc = Res()
            ident = self.sb(st0, "ident", [128, 128]); Rid = Res()
            identb = self.sb(st0, "identb", [128, 128], BF16); Ridb = Res()
            mask4 = self.sb(st0, "mask4", [128, 512]); Rm4 = Res()
            maskL = self.sb(st0, "maskL", [128, 128]); RmL = Res()
            trim = self.sb(st0, "trim", [128, 128], BF16); Rtri = Res()
            bdb = self.sb(st0, "bdb", [128, 128], BF16); Rbd = Res()
            onesb = self.sb(st0, "onesb", [128, 128], BF16); Rones = Res()
            onesf = self.sb(st0, "onesf", [128, 128]); Ronesf = Res()
            scanm = self.sb(st0, "scanm", [128, 512]); Rscan = Res()
            S.dma(S.misc(), [(vecs[:, 0:NVEC], vecs_d[:, :])], writes=[Rvec])
            S.dma(S.misc(), [(ropec[:], rc_d[:, :])], writes=[Rrc])
            S.dma(S.misc(), [(ident[:], cm_ident_d[:, :])], writes=[Rid])
            S.dma(S.misc("pool"), [(identb[:], cm_ident_d[:, :])], writes=[Ridb], q="pool")
            S.dma(S.misc(), [(mask4[:], cm_mask4_d[:, :])], writes=[Rm4])
            S.dma(S.misc(), [(maskL[:], cm_maskL_d[:, :])], writes=[RmL])
            S.dma(S.misc("pool"), [(trim[:], cm_tri_d[:, :])], writes=[Rtri], q="pool")
            S.dma(S.misc("pool"), [(bdb[:], cm_bd_d[:, :])], writes=[Rbd], q="pool")
            S.dma(S.misc(), [(scanm[:], cm_scan_d[:, :])], writes=[Rscan])
            epsc = self.sb(st0, "epsc", [128, 2]); Reps = Res()
            self.memset("pool", epsc[:, 0:1], RMS_EPS, [Reps])
            self.memset("pool", epsc[:, 1:2], 1e-24, [Reps])
            self.memset("pool", onesb[:], 1.0, [Rones])
            self.memset("pool", onesf[:], 1.0, [Ronesf])
            self.ts("dve", vecs[:, OM_OFF:OM_OFF + 14], vecs[:, VEC_OFF["mu"]:VEC_OFF["mu"] + 14], -1.0, ALU.mult,
                    [Rvec], [Rvec], s2=1.0, op1=ALU.add)
            self.ts("dve", vecs[:, OMKA_OFF:OMKA_OFF + 4], vecs[:, VEC_OFF["k_a"]:VEC_OFF["k_a"] + 4], -1.0, ALU.mult,
                    [Rvec], [Rvec], s2=1.0, op1=ALU.add)
            S.dma(S.misc(), [(KpeT[32:33, :], maskrow[0:1, :])], writes=[R_KpeT])

            def vcol(name, j, p0=0, p1=128):
                o = VEC_OFF[name] + j
                return vecs[p0:p1, o:o + 1]

            banks = [(st0.enter_context(nc.psum_tensor("bank%d" % i, [128, 512], F32)), Res("bank%d" % i)) for i in range(7)]
            bankb = (st0.enter_context(nc.psum_tensor("bankb", [128, 1024], BF16)), Res("bankb"))
            consts = [Rvec, Rrc, Rid, Ridb, Rm4, RmL, Rtri, Rbd, Rones, Ronesf, Rscan]

            xT3 = xT.rearrange("(c p) t -> p c t", p=128)
            w_in3 = w_in.rearrange("(c p) n -> p c n", p=128)
            w_kpe3 = w_kpe.rearrange("(c p) n -> p c n", p=128)
            pi = [0]

            def pbank():
                b_ = banks[pi[0] % 7]
                pi[0] += 1
                return b_

            def make_common(st, ncol):
                cm = {}
                cm["win"] = self.sb(st, "win", [128, 8, ncol], BF16)
                cm["Rwin"] = Res()
                cm["xr"] = self.ring(st, "x1_", [128, 8, 512], F32, 1)
                cm["xch"] = S.chan()
                cm["sqr"] = self.ring(st, "sq1_", [128, 512], BF16, 4)
                cm["hr"] = self.ring(st, "h1_", [128, 8, 512], BF16, 2)
                cm["rstdr"] = self.ring(st, "rstd1_", [128, 512], F32, 2)
                cm["sqt"] = self.ring(st, "sqt1_", [128, 512], F32, 2)
                return cm

            def rms_stats(cm, src3, nchunk, scale, Rsrc):
                ps, Rps = pbank()
                for c in range(nchunk):
                    sq, Rsq = cm["sqr"].next()
                    self.act(sq[:], src3[:, c, :], AF.Square, [Rsrc], [Rsq])
                    self.mm(ps[:, :], onesb[:, :], sq[:], [Rones, Rsq], [Rps], start=(c == 0), stop=(c == nchunk - 1), inc=True)
                t1, Rt1 = cm["sqt"].next()
                self.act(t1[:], ps[:, :], AF.Ln, [Rps, Reps], [Rt1], bias=epsc[:, 0:1], scale=scale)
                rs, Rrs = cm["rstdr"].next()
                self.act(rs[:], t1[:], AF.Exp, [Rt1], [Rrs], scale=-0.5)
                return rs, Rrs

            def load_h(cm, t):
                c0 = t * 512
                xt, Rxt = cm["xr"].next()
                S.dma(cm["xch"], [(xt[:], xT3[:, :, c0:c0 + 512])], writes=[Rxt])
                rs, Rrs = rms_stats(cm, xt, 8, 1.0 / D, Rxt)
                h, Rh = cm["hr"].next()
                for c in range(8):
                    self.stt(h[:, c, :], xt[:, c, :], vcol("g_mix", c), rs[:], ALU.mult, ALU.mult, [Rxt, Rvec, Rrs], [Rh])
                return h, Rh

            def zmm(cm, h, Rh, col0, M):
                ps, Rps = pbank()
                for c in range(8):
                    self.mm(ps[0:M, :], cm["win"][:, c, col0:col0 + M], h[:, c, :], [cm["Rwin"], Rh], [Rps], start=(c == 0), stop=(c == 7))
                return ps, Rps

            with ExitStack() as st:
                if self.upto < 1:
                    raise _Stop()
                NCOL = 384 + 256 + 192 + 2048
                OFF_CQ, OFF_CKV, OFF_KPE, OFF_G = 0, 384, 640, 832
                cm = make_common(st, NCOL)
                win, Rwin = cm["win"], cm["Rwin"]
                for c in range(8):
                    S.dma(S.misc("pool"), [(win[:, c, 0:640], w_in3[:, c, 0:640]),
                                     (win[:, c, 640:832], w_kpe3[:, c, :]),
                                     (win[:, c, 832:NCOL], w_in3[:, c, 2464:4512])], writes=[Rwin], q="pool")
                wq = self.sb(st, "wq", [128, 3, 768], BF16)
                wqs = self.sb(st, "wqs", [128, 3, 768], BF16)
                wkk = self.sb(st, "wkk", [128, 2, 512], BF16)
                wkv = self.sb(st, "wkv", [128, 2, 512], BF16)
                Rw1 = Res("w1")
                S.dma(S.misc("pool"), [(wq[:], wq_d.rearrange("(c p) n -> p c n", p=128)),
                                 (wqs[:], wqs_d.rearrange("(c p) n -> p c n", p=128)),
                                 (wkk[:], wkk_d.rearrange("(c p) n -> p c n", p=128)),
                                 (wkv[:], wkv_d.rearrange("(c p) n -> p c n", p=128))],
                      writes=[Rw1], q="pool")
                tmpr = self.ring(st, "tmp1_", [128, 512], F32, 4)
                cq = self.sb(st, "cq", [128, 3, 512]); Rcq = Res()
                cqn = self.sb(st, "cqn", [128, 3, 512], BF16); Rcqn = Res()
                ckv = self.sb(st, "ckv", [128, 2, 512]); Rckv = Res()
                ckvn = self.sb(st, "ckvn", [128, 2, 512], BF16); Rckvn = Res()
                qst = self.ring(st, "qst", [128, 512], BF16, 3)
                qch = [S.chan() for _ in range(3)]
                for (t_, r_) in qst.items:
                    self.memset("pool", t_[64:97, :], 1.0, [r_])
                knst = self.ring(st, "knst", [128, 512], BF16, 2)
                knch = [S.chan() for _ in range(2)]
                vst = self.ring(st, "vst", [128, 4, 520], BF16, 2)
                vch = [S.chan() for _ in range(2)]
                for (t_, r_) in vst.items:
                    self.memset("pool", t_[:], 1.0, [r_])
                kpst = self.ring(st, "kpst", [128, 512], BF16, 2)
                kpch = [S.chan() for _ in range(2)]
                gst = self.ring(st, "gst", [128, 4, 512], BF16, 2)
                gch = [S.chan() for _ in range(2)]
                posi = self.sb(st, "posi", [128, 512], I32); Rposi = Res()
                posch = S.chan()
                rp = [self.sb(st, "rp%d" % i, [128, 512]) for i in range(6)]
                Rrp = [Res() for _ in range(6)]
                ki = self.sb(st, "ki", [128, 512], I32); Rki = Res()
                sl = slice(64, 96)
                for t in range(NT):
                    own = t >= NTP
                    to = t - NTP
                    c0 = t * 512
                    if t == 0:
                        hnext = load_h(cm, 0)
                    h, Rh = hnext
                    S.dma(posch, [(posi[64:96, :], pos[0:1, c0:c0 + 512].partition_broadcast(32))], writes=[Rposi])
                    ang, sinT, cosT, sinQ, cosQ, rr = rp
                    Rang, RsinT, RcosT, RsinQ, RcosQ, Rrr = Rrp
                    self.cp("dve", ang[sl, :], posi[sl, :], [Rposi], [Rang])
                    self.ts("dve", ang[sl, :], ang[sl, :], ropec[sl, 0:1], ALU.mult, [Rang, Rrc], [Rang])
                    self.ts("dve", rr[sl, :], ang[sl, :], float(1.0 / TWO_PI), ALU.mult, [Rang], [Rrr])
                    self.cp("dve", ki[sl, :], rr[sl, :], [Rrr], [Rki])
                    self.cp("dve", rr[sl, :], ki[sl, :], [Rki], [Rrr])
                    self.stt(ang[sl, :], rr[sl, :], -C1, ang[sl, :], ALU.mult, ALU.add, [Rrr, Rang], [Rang])
                    self.stt(ang[sl, :], rr[sl, :], -C2, ang[sl, :], ALU.mult, ALU.add, [Rrr, Rang], [Rang])
                    self.ts("dve", ang[sl, :], ang[sl, :], float(np.pi), ALU.min, [Rang], [Rang], s2=float(-np.pi), op1=ALU.max)
                    self.act(sinT[sl, :], ang[sl, :], AF.Sin, [Rang, Rrc], [RsinT], scale=ropec[sl, 1:2])
                    self.act(rr[sl, :], ang[sl, :], AF.Abs, [Rang], [Rrr])
                    self.ts("dve", rr[sl, :], rr[sl, :], -1.0, ALU.mult, [Rrr], [Rrr], s2=float(np.pi / 2), op1=ALU.add)
                    self.act(cosT[sl, :], rr[sl, :], AF.Sin, [Rrr], [RcosT])
                    if own:
                        self.act(sinQ[sl, :], sinT[sl, :], AF.Copy, [RsinT], [RsinQ], scale=SCALE)
                        self.act(cosQ[sl, :], cosT[sl, :], AF.Copy, [RcosT], [RcosQ], scale=SCALE)
                    psA, RpsA = zmm(cm, h, Rh, OFF_KPE, 96)
                    psB, RpsB = zmm(cm, h, Rh, OFF_KPE + 96, 96)
                    ta, Rta = tmpr.next()
                    tb, Rtb = tmpr.next()
                    self.tt("dve", ta[sl, :], psA[sl, :], cosT[sl, :], ALU.mult, [RpsA, RcosT], [Rta])
                    self.tt("dve", tb[sl, :], psB[sl, :], sinT[sl, :], ALU.mult, [RpsB, RsinT], [Rtb])
                    kp, Rkp = kpst.next()
                    self.tt("pool", kp[sl, :], ta[sl, :], tb[sl, :], ALU.add, [Rta, Rtb], [Rkp])
                    S.dma(kpch[t % 2], [(KpeT[0:32, c0:c0 + 512], kp[sl, :])], reads=[Rkp], writes=[R_KpeT])
                    for m in range(2):
                        ps, Rps = zmm(cm, h, Rh, OFF_CKV + m * 128, 128)
                        self.cp("act", ckv[:, m, :], ps[:, :], [Rps], [Rckv])
                    if t + 1 < NT:
                        hnext = load_h(cm, t + 1)
                    rs2, Rrs2 = rms_stats(cm, ckv, 2, 1.0 / 256, Rckv)
                    for m in range(2):
                        self.stt(ckvn[:, m, :], ckv[:, m, :], vcol("g_kv_a", m), rs2[:], ALU.mult, ALU.mult, [Rckv, Rvec, Rrs2], [Rckvn])
                    for m in range(4):
                        ps, Rps = pbank()
                        for c in range(2):
                            self.mm(ps[:, :], wkk[:, c, m * 128:(m + 1) * 128], ckvn[:, c, :], [Rw1, Rckvn], [Rps], start=(c == 0), stop=(c == 1))
                        kn, Rkn = knst.next()
                        kk_ = (knst.i - 1) % 2
                        self.cp("act", kn[:], ps[:, :], [Rps], [Rkn])
                        S.dma(knch[kk_], [(KnT[m * 128:(m + 1) * 128, c0:c0 + 512], kn[:])], reads=[Rkn], writes=[R_KnT])
                    vt_, Rvt = vst.next()
                    vk = (vst.i - 1) % 2
                    for s_ in range(4):
                        ps, Rps = pbank()
                        for c in range(2):
                            self.mm(ps[:, :], ckvn[:, c, s_ * 128:(s_ + 1) * 128], wkv[:, c, :], [Rckvn, Rw1], [Rps], start=(c == 0), stop=(c == 1))
                        v4 = vt_[:, s_, :].rearrange("p (h d) -> p h d", d=65)
                        self.cp("act", v4[:, :, 0:64], ps[:, :].rearrange("p (h d) -> p h d", d=64), [Rps], [Rvt])
                    S.dma(vch[vk], [(Vs[:, t * 4:(t + 1) * 4, :], vt_[:])], reads=[Rvt], writes=[R_Vs])
                    if own:
                        o0 = to * 512
                        for m in range(3):
                            ps, Rps = zmm(cm, h, Rh, OFF_CQ + m * 128, 128)
                            self.cp("act", cq[:, m, :], ps[:, :], [Rps], [Rcq])
                        rs3, Rrs3 = rms_stats(cm, cq, 3, 1.0 / 384, Rcq)
                        for m in range(3):
                            self.stt(cqn[:, m, :], cq[:, m, :], vcol("g_q_a", m), rs3[:], ALU.mult, ALU.mult, [Rcq, Rvec, Rrs3], [Rcqn])
                        for hd in range(NH):
                            psA, RpsA = pbank()
                            psB, RpsB = pbank()
                            for c in range(3):
                                self.mm(psA[0:96, :], wq[:, c, hd * 96:(hd + 1) * 96], cqn[:, c, :], [Rw1, Rcqn], [RpsA], start=(c == 0), stop=(c == 2))
                            for c in range(3):
                                self.mm(psB[0:96, :], wqs[:, c, hd * 96:(hd + 1) * 96], cqn[:, c, :], [Rw1, Rcqn], [RpsB], start=(c == 0), stop=(c == 2))
                            q_, Rq_ = qst.next()
                            qk = (qst.i - 1) % 3
                            self.act(q_[0:64, :], psA[0:64, :], AF.Copy, [RpsA], [Rq_], scale=SCALE)
                            ta, Rta = tmpr.next()
                            tb, Rtb = tmpr.next()
                            self.tt("dve", ta[sl, :], psA[sl, :], cosQ[sl, :], ALU.mult, [RpsA, RcosQ], [Rta])
                            self.tt("dve", tb[sl, :], psB[sl, :], sinQ[sl, :], ALU.mult, [RpsB, RsinQ], [Rtb])
                            self.tt("pool", q_[sl, :], ta[sl, :], tb[sl, :], ALU.add, [Rta, Rtb], [Rq_])
                            S.dma(qch[qk], [(QT[hd, :, o0:o0 + 512], q_[0:97, :])], reads=[Rq_], writes=[R_QT])
                        for gq in range(4):
                            g_, Rg_ = gst.next()
                            gk = (gst.i - 1) % 2
                            for j in range(4):
                                ps, Rps = zmm(cm, h, Rh, OFF_G + (gq * 4 + j) * 128, 128)
                                self.act(g_[:, j, :], ps[:, :], AF.Sigmoid, [Rps], [Rg_])
                            S.dma(gch[gk], [(GT[gq * 512:(gq + 1) * 512, o0:o0 + 512].rearrange("(j p) t -> p j t", p=128), g_[:])],
                                  reads=[Rg_], writes=[R_GT])
                S.barrier()
                S.emit()
                for n_ in S.ENGS:
                    S.e[n_].ops = []

            with ExitStack() as st:
                if self.upto < 2:
                    raise _Stop()
                cm = make_common(st, 1792)
                win, Rwin = cm["win"], cm["Rwin"]
                for c in range(8):
                    S.dma(S.misc("pool"), [(win[:, c, :], w_in3[:, c, 672:2464])], writes=[Rwin], q="pool")
                w2s = self.sb(st, "w2s", [128, 512], BF16)
                a2s = self.sb(st, "a2s", [128, 512], BF16)
                g2s = self.sb(st, "g2s", [128, 512], BF16)
                Rw1 = Res("w1b")
                S.dma(S.misc("pool"), [(w2s[0:64, :], w2_d[:, :]), (a2s[64:128, :], a2_d[:, :]), (g2s[:], g2_d[:, :])], writes=[Rw1], q="pool")
                tmpr = self.ring(st, "tmp1b_", [128, 512], F32, 3)
                tmpb = self.ring(st, "tmpb1_", [128, 512], BF16, 4)
                zcw = self.ring(st, "zcw", [128, 513], F32, 4)
                carry = self.sb(st, "carry", [128, 16]); Rcar = Res()
                self.memset("pool", carry[:], 0.0, [Rcar])
                zsr = self.ring(st, "zsr", [128, 512], F32, 2)
                zsk = self.ring(st, "zsk", [128, 512], F32, 2)
                zsv = self.ring(st, "zsv", [128, 512], F32, 2)
                zs12 = self.sb(st, "zs12", [128, 512]); Rzs12 = Res()
                zs13 = self.sb(st, "zs13", [128, 512]); Rzs13 = Res()
                names = ["sig", "av", "Lc", "Lx", "kk", "nr", "tk", "EP", "EN"]
                nb2 = [{n_: (self.sb(st, "rb%d_" % k_ + n_, [128, 512]), Res(n_)) for n_ in names} for k_ in range(2)]
                twb = self.sb(st, "twb", [128, 512], BF16); Rtwb = Res()
                gsb = self.sb(st, "gsb", [128, 512], BF16); Rgsb = Res()
                rwst = self.ring(st, "rwst", [128, 512], BF16, 12)
                rwch = [S.chan() for _ in range(12)]
                rwi = [0]
                pcst = self.ring(st, "pcst", [128, 8], F32, 4)
                pcch = [S.chan() for _ in range(4)]

                def rw_store(dst_ap, src_fn, eng_fn):
                    k = rwi[0] % 12
                    rwi[0] += 1
                    t_, r_ = rwst.items[k]
                    eng_fn(t_, r_)
                    S.dma(rwch[k], [(dst_ap, src_fn(t_))], reads=[r_], writes=[R_rw])

                MU, OM = VEC_OFF["mu"], OM_OFF

                def shift(cm, h, Rh, m, dst, Rdst):
                    ps, Rps = zmm(cm, h, Rh, m * 128, 128)
                    zc, Rzc = zcw.next()
                    self.act(zc[:, 1:513], ps[:, :], AF.Copy, [Rps, Rvec], [Rzc], scale=vecs[:, MU + m:MU + m + 1])
                    self.cp("pool", zc[:, 0:1], carry[:, m:m + 1], [Rcar], [Rzc])
                    self.stt(dst[:], ps[:, :], vecs[:, OM + m:OM + m + 1], zc[:, 0:512], ALU.mult, ALU.add, [Rps, Rvec, Rzc], [Rdst])
                    self.cp("pool", carry[:, m:m + 1], zc[:, 512:513], [Rzc], [Rcar])

                for t in range(NT):
                    own = t >= NTP
                    o0 = (t - NTP) * 512
                    c0 = t * 512
                    if t == 0:
                        hnext = load_h(cm, 0)
                    h, Rh = hnext
                    shift(cm, h, Rh, 12, zs12, Rzs12)
                    shift(cm, h, Rh, 13, zs13, Rzs13)
                    if t + 1 < NT:
                        hnext = load_h(cm, t + 1)
                    self.act(twb[0:64, :], zs12[0:64, :], AF.Tanh, [Rzs12], [Rtwb])
                    self.cp("pool", twb[64:128, :], zs12[64:128, :], [Rzs12], [Rtwb])
                    self.act(gsb[:], zs13[:], AF.Sigmoid, [Rzs13], [Rgsb])
                    def st1(c_):
                        m = c_["m"]
                        c_["r"] = zsr.next(); c_["k"] = zsk.next(); c_["v"] = zsv.next()
                        shift(cm, h, Rh, m, *c_["r"])
                        shift(cm, h, Rh, 4 + m, *c_["k"])
                        shift(cm, h, Rh, 8 + m, *c_["v"])
                        c_["ms"] = slice(m * 128, (m + 1) * 128)
                        c_["nb"] = nb2[m % 2]

                    def st2(c_):
                        m, ms, nb = c_["m"], c_["ms"], c_["nb"]
                        (sig, Rsig), (av, Rav) = nb["sig"], nb["av"]
                        ps, Rps = pbank()
                        self.mm(ps[:, :], w2s[0:64, ms], twb[0:64, :], [Rw1, Rtwb], [Rps])
                        self.act(sig[:], ps[:, :], AF.Sigmoid, [Rps, Rvec], [Rsig], bias=vcol("w0", m))
                        ps, Rps = pbank()
                        self.mm(ps[:, :], a2s[64:128, ms], twb[64:128, :], [Rw1, Rtwb], [Rps])
                        self.act(av[:], ps[:, :], AF.Sigmoid, [Rps, Rvec], [Rav], bias=vcol("a0", m))

                    def st3(c_):
                        m, nb = c_["m"], c_["nb"]
                        (sig, Rsig), (Lc, RLc), (kk, Rkk) = nb["sig"], nb["Lc"], nb["kk"]
                        k_m, Rk = c_["k"]
                        S.op("dve", lambda e, o=Lc, d=sig: e.tensor_tensor_scan(out=o[:], data0=scanm[:], data1=d[:], initial=0.0,
                                                                              op0=ALU.mult, op1=ALU.add), [Rscan, Rsig], [RLc])
                        self.act(kk[:], k_m[:], AF.Copy, [Rk, Rvec], [Rkk], scale=vcol("k_k", m))
                        kk2, Rkk2 = tmpb.next()
                        self.act(kk2[:], k_m[:], AF.Square, [Rk, Rvec], [Rkk2], scale=vcol("k_k", m))
                        ps, Rps = pbank()
                        self.mm(ps[:, :], bdb[:, :], kk2[:], [Rbd, Rkk2], [Rps])
                        c_["psn"] = (ps, Rps)

                    def st4(c_):
                        m, nb = c_["m"], c_["nb"]
                        (av, Rav), (kk, Rkk), (nr, Rnr), (tk, Rtk) = nb["av"], nb["kk"], nb["nr"], nb["tk"]
                        k_m, Rk = c_["k"]
                        ps, Rps = c_["psn"]
                        self.ts("dve", nr[:], ps[:, :], 1e-24, ALU.max, [Rps], [Rnr])
                        self.act(nr[:], nr[:], AF.Ln, [Rnr], [Rnr])
                        self.act(nr[:], nr[:], AF.Exp, [Rnr], [Rnr], scale=-0.5)
                        self.tt("dve", kk[:], kk[:], nr[:], ALU.mult, [Rkk, Rnr], [Rkk])
                        self.ts("dve", tk[:], av[:], vcol("k_a", m), ALU.mult, [Rav, Rvec], [Rtk],
                                s2=vecs[:, OMKA_OFF + m:OMKA_OFF + m + 1], op1=ALU.add)
                        self.tt("dve", tk[:], tk[:], k_m[:], ALU.mult, [Rtk, Rk], [Rtk])

                    def st5(c_):
                        nb = c_["nb"]
                        (Lc, RLc), (EP, REP), (EN, REN) = nb["Lc"], nb["EP"], nb["EN"]
                        self.act(EP[:], Lc[:], AF.Exp, [RLc], [REP], scale=-EXPH)
                        self.act(EN[:], Lc[:], AF.Exp, [RLc], [REN], scale=EXPH)

                    def st6(c_):
                        m, ms, nb = c_["m"], c_["ms"], c_["nb"]
                        (tk, Rtk) = nb["tk"]
                        r_m, Rr = c_["r"]; v_m, Rv = c_["v"]
                        rk, Rrk = tmpb.next()
                        self.stt(rk[:], r_m[:], vcol("r_k", m), tk[:], ALU.mult, ALU.mult, [Rr, Rvec, Rtk], [Rrk])
                        psb, Rpsb = pbank()
                        self.mm(psb[:, :], bdb[:, :], rk[:], [Rbd, Rrk], [Rpsb])
                        rw_store(BoT[ms, o0:o0 + 512], lambda t_: t_[:],
                                 lambda t_, r_: self.tt("dve", t_[:], psb[:, :], v_m[:], ALU.mult, [Rpsb, Rv], [r_]))
                        psg, Rpsg = pbank()
                        self.mm(psg[:, :], g2s[:, ms], gsb[:], [Rw1, Rgsb], [Rpsg])
                        rw_store(GrT[ms, o0:o0 + 512], lambda t_: t_[:],
                                 lambda t_, r_: self.cp("act", t_[:], psg[:, :], [Rpsg], [r_]))

                    def st7(c_):
                        m, ms, nb = c_["m"], c_["ms"], c_["nb"]
                        (av, Rav), (Lx, RLx), (kk, Rkk), (tk, Rtk), (EP, REP), (EN, REN) = nb["av"], nb["Lx"], nb["kk"], nb["tk"], nb["EP"], nb["EN"]
                        r_m, Rr = c_["r"]; v_m, Rv = c_["v"]
                        ARv = ARt[ms, t * 4:(t + 1) * 4, :]

                        def a_tilde(t_, r_):
                            self.stt(t_[:, 1:512], kk[:, 1:512], -1.0, EP[:, 0:511], ALU.mult, ALU.mult, [Rkk, REP], [r_])
                            self.ts("pool", t_[:].rearrange("p (c t) -> p c t", t=64)[:, :, 0:1],
                                    kk[:].rearrange("p (c t) -> p c t", t=64)[:, :, 0:1], -1.0, ALU.mult, [Rkk], [r_])

                        rw_store(ARv[:, :, 0:128], lambda t_: t_[:].rearrange("p (b t) -> p b t", t=128), a_tilde)
                        rw_store(ARv[:, :, 128:256], lambda t_: t_[:].rearrange("p (b t) -> p b t", t=128),
                                 lambda t_, r_: self.tt("pool", t_[:], r_m[:], EP[:], ALU.mult, [Rr, REP], [r_]))
                        self.tt("dve", Lx[:], kk[:], av[:], ALU.mult, [Rkk, Rav], [RLx])
                        rw_store(Bt[ms, c0:c0 + 512], lambda t_: t_[:],
                                 lambda t_, r_: self.tt("dve", t_[:], Lx[:], EN[:], ALU.mult, [RLx, REN], [r_]))
                        rw_store(Kt[ms, c0:c0 + 512], lambda t_: t_[:],
                                 lambda t_, r_: self.tt("pool", t_[:], tk[:], EN[:], ALU.mult, [Rtk, REN], [r_]))
                        rw_store(Vt[ms, c0:c0 + 512], lambda t_: t_[:],
                                 lambda t_, r_: self.cp("act", t_[:], v_m[:], [Rv], [r_]))
                        pc, Rpc = pcst.next()
                        pk = (pcst.i - 1) % 4
                        self.cp("pool", pc[:], EP[:].rearrange("p (c t) -> p c t", t=64)[:, :, 63], [REP], [Rpc])
                        S.dma(pcch[pk], [(PCt[ms, t * 8:(t + 1) * 8], pc[:])], reads=[Rpc], writes=[R_rw])

                    for pr in range(2):
                        cs_ = [{"m": pr * 2}, {"m": pr * 2 + 1}]
                        for stg in (st1, st2, st3, st4, st5):
                            for c_ in cs_:
                                stg(c_)
                        if own:
                            for c_ in cs_:
                                st6(c_)
                        for c_ in cs_:
                            st7(c_)
                S.barrier()
                S.emit()
                for n_ in S.ENGS:
                    S.e[n_].ops = []

            with ExitStack() as st:
                if self.upto < 3:
                    raise _Stop()
                for i in range(8):
                    S.dma(S.misc("pool"), [(wupb[i * 128:(i + 1) * 128, :], wup_d[i * 128:(i + 1) * 128, :])], writes=[R_wconv], q="pool")
                for i in range(8):
                    S.dma(S.misc("pool"), [(wdnb[i * 512:(i + 1) * 512, :], wdn_d[i * 512:(i + 1) * 512, :])], writes=[R_wconv], q="pool")
                arl = self.ring(st, "arl", [128, 4, 256], BF16, 2)
                btl = self.ring(st, "btl", [128, 512], BF16, 2)
                ktl = self.ring(st, "ktl", [128, 512], BF16, 2)
                vtl = self.ring(st, "vtl", [128, 512], BF16, 2)
                pcl = self.ring(st, "pcl", [128, 8], F32, 2)
                bol = self.ring(st, "bol", [128, 512], BF16, 2)
                grl = self.ring(st, "grl", [128, 512], BF16, 2)
                ldch = [S.chan(), S.chan()]
                MT = self.ring(st, "MT", [128, 2, 512], BF16, 8)
                Lr = self.ring(st, "Lr", [128, 2, 128], BF16, 12)
                ASr = self.ring(st, "ASr", [128, 2, 256], BF16, 12)
                TM = self.ring(st, "TM", [128, 4, 128], BF16, 8)
                Xb = self.ring(st, "Xb", [128, 2, 128], BF16, 8)
                TXr = self.ring(st, "TXr", [128, 2, 128], BF16, 8)
                RqT = self.ring(st, "RqT", [128, 128], BF16, 8)
                Y0 = self.ring(st, "Y0", [128, 2, 64], F32, 8)
                GTr = self.ring(st, "GTr", [128, 64], BF16, 16)
                Fr = self.ring(st, "Fr", [128, 64], F32, 16)
                Hs = self.ring(st, "Hs", [128, 64], BF16, 3)
                Yr = self.ring(st, "Yr", [128, 2, 64], F32, 4)
                gn = self.ring(st, "gn", [128, 2, 64], F32, 4)
                gs_ = self.ring(st, "gs_", [128, 2], F32, 6)
                yT = self.ring(st, "yT", [128, 512], F32, 2)
                yst = self.ring(st, "yst", [128, 512], BF16, 2)
                ych = [S.chan(), S.chan()]
                pi2 = [0]

                def pb2():
                    b_ = banks[pi2[0] % 7]
                    pi2[0] += 1
                    return b_

                e2 = lambda ap: ap.rearrange("p (e s) -> p e s", e=2)
                idb3 = identb[:].unsqueeze(1).broadcast_to([128, 2, 128])
                NBK = 4
                for m in range(4):
                    ms = slice(m * 128, (m + 1) * 128)
                    H, RH = Hs.next()
                    self.memset("pool", H[:], 0.0, [RH])
                    for t in range(NT):
                        own = t >= NTP
                        o0 = (t - NTP) * 512
                        c0 = t * 512
                        ar, Rar = arl.next(); bt, Rbt = btl.next(); kt, Rkt = ktl.next(); vt, Rvt = vtl.next(); pc, Rpc = pcl.next()
                        lk = (arl.i - 1) % 2
                        pairs = [(ar[:], ARt[ms, t * 4:(t + 1) * 4, :]), (bt[:], Bt[ms, c0:c0 + 512]), (kt[:], Kt[ms, c0:c0 + 512]),
                                 (vt[:], Vt[ms, c0:c0 + 512]), (pc[:], PCt[ms, t * 8:(t + 1) * 8])]
                        wr = [Rar, Rbt, Rkt, Rvt, Rpc]
                        if own:
                            bo, Rbo = bol.next(); gr, Rgr = grl.next()
                            pairs += [(bo[:], BoT[ms, o0:o0 + 512]), (gr[:], GrT[ms, o0:o0 + 512])]
                            wr += [Rbo, Rgr]
                            yt_, Ryt = yT.next()
                        S.dma(ldch[lk], pairs, reads=[R_rw], writes=wr)
                        X = [dict() for _ in range(NBK)]
                        for b in range(NBK):
                            c_ = X[b]
                            bs = slice(b * 128, (b + 1) * 128)
                            c_["bs"] = bs
                            mt, Rmt = MT.next()
                            L0, RL0 = Lr.next()
                            for e in range(2):
                                hs = slice(e * 64, (e + 1) * 64)
                                ps, Rps = pb2()
                                self.mm(ps[:, 0:256], bt[hs, bs], ar[hs, b, :], [Rbt, Rar], [Rps])
                                self.mm(ps[:, 256:512], kt[hs, bs], ar[hs, b, :], [Rkt, Rar], [Rps])
                                self.tt("dve", mt[:, e, :], ps[:, :], mask4[:], ALU.mult, [Rps, Rm4], [Rmt])
                                psL, RpsL = pb2()
                                self.mm(psL[:, 0:128], ar[hs, b, 0:128], bt[hs, bs], [Rar, Rbt], [RpsL])
                                self.tt("dve", L0[:, e, :], psL[:, 0:128], maskL[:], ALU.mult, [RpsL, RmL], [RL0])
                            c_["mt"], c_["Rmt"], c_["L"], c_["RL"] = mt, Rmt, L0, RL0
                        for b in range(NBK):
                            c_ = X[b]
                            bs = c_["bs"]
                            pst, Rpst = bankb
                            o_ = (b % 2) * 512
                            self.tr(pst[:, o_ + 0:o_ + 128], ar[:, b, 0:128], identb[:], [Rar, Ridb], [Rpst])
                            self.tr(pst[:, o_ + 128:o_ + 256], bt[:, bs], identb[:], [Rbt, Ridb], [Rpst])
                            self.tr(pst[:, o_ + 256:o_ + 384], kt[:, bs], identb[:], [Rkt, Ridb], [Rpst])
                            self.tr(pst[:, o_ + 384:o_ + 512], vt[:, bs], identb[:], [Rvt, Ridb], [Rpst])
                            tm, Rtm = TM.next()
                            self.cp("act", tm[:], pst[:, o_:o_ + 512].rearrange("p (q f) -> p q f", q=4), [Rpst], [Rtm])
                            c_["tm"], c_["Rtm"] = tm, Rtm
                        for b in range(NBK):
                            c_ = X[b]
                            mt, Rmt = c_["mt"], c_["Rmt"]
                            AS, RAS = ASr.next()
                            self.cp("pool", AS[:, :, 0:128], mt[:, :, 0:128], [Rmt], [RAS])
                            self.tt("pool", AS[:, :, 128:256], mt[:, :, 0:128], idb3, ALU.add, [Rmt, Ridb], [RAS])
                            c_["AS"], c_["RAS"] = AS, RAS
                        for b in range(NBK):
                            c_ = X[b]
                            AS, RAS, Lk, RLk = c_["AS"], c_["RAS"], c_["L"], c_["RL"]
                            psa, Rpsa = pb2()
                            psl, Rpsl = pb2()
                            for e in range(2):
                                self.mm(psa[:, e * 128:(e + 1) * 128], Lk[:, e, :], AS[:, e, 0:128], [RLk, RAS], [Rpsa])
                                self.mm(psl[:, e * 128:(e + 1) * 128], AS[:, e, 0:128], Lk[:, e, :], [RAS, RLk], [Rpsl])
                            ASn, RASn = ASr.next()
                            self.cp("dve", ASn[:, :, 0:128], e2(psa[:, 0:256]), [Rpsa], [RASn])
                            self.cp("pool", ASn[:, :, 128:256], AS[:, :, 128:256], [RAS], [RASn])
                            Ln, RLn = Lr.next()
                            self.cp("act", Ln[:], e2(psl[:, 0:256]), [Rpsl], [RLn])
                            c_["AS"], c_["RAS"], c_["L"], c_["RL"] = ASn, RASn, Ln, RLn
                        for it in range(5):
                            last = it == 4
                            w_ = 128 if last else 0
                            for b in range(NBK):
                                c_ = X[b]
                                AS, RAS, Lk, RLk = c_["AS"], c_["RAS"], c_["L"], c_["RL"]
                                psm, Rpsm = pb2()
                                for e in range(2):
                                    self.mm(psm[:, e * 256 + w_:(e + 1) * 256], Lk[:, e, :], AS[:, e, w_:256], [RLk, RAS], [Rpsm])
                                ASn, RASn = ASr.next()
                                pm3 = e2(psm[:, 0:512])
                                self.tt("dve", ASn[:, :, 128:256], pm3[:, :, 128:256], AS[:, :, 128:256], ALU.add, [Rpsm, RAS], [RASn])
                                if not last:
                                    self.cp("act", ASn[:, :, 0:128], pm3[:, :, 0:128], [Rpsm], [RASn])
                                    psl, Rpsl = pb2()
                                    for e in range(2):
                                        self.mm(psl[:, e * 128:(e + 1) * 128], AS[:, e, 0:128], Lk[:, e, :], [RAS, RLk], [Rpsl])
                                    Ln, RLn = Lr.next()
                                    self.cp("act", Ln[:], e2(psl[:, 0:256]), [Rpsl], [RLn])
                                    c_["L"], c_["RL"] = Ln, RLn
                                c_["AS"], c_["RAS"] = ASn, RASn
                        for b in range(NBK):
                            c_ = X[b]
                            mt, Rmt, tm, Rtm = c_["mt"], c_["Rmt"], c_["tm"], c_["Rtm"]
                            xb, Rxb = Xb.next()
                            psv, Rpsv = pb2()
                            for e in range(2):
                                self.mm(psv[:, e * 64:(e + 1) * 64], mt[:, e, 256:384], tm[:, 3, e * 64:(e + 1) * 64], [Rmt, Rtm], [Rpsv])
                            self.cp("act", xb[:, :, 64:128], psv[:, 0:128].rearrange("p (e v) -> p e v", e=2), [Rpsv], [Rxb])
                            self.cp("pool", xb[:, :, 0:64], tm[:, 0, :].rearrange("p (e v) -> p e v", e=2), [Rtm], [Rxb])
                            c_["xb"], c_["Rxb"] = xb, Rxb
                        for b in range(NBK):
                            c_ = X[b]
                            AS, RAS, xb, Rxb = c_["AS"], c_["RAS"], c_["xb"], c_["Rxb"]
                            pstx, Rpstx = pb2()
                            for e in range(2):
                                self.mm(pstx[:, e * 128:(e + 1) * 128], AS[:, e, 128:256], xb[:, e, :], [RAS, Rxb], [Rpstx])
                            tx, Rtx = TXr.next()
                            self.cp("act", tx[:], e2(pstx[:, 0:256]), [Rpstx], [Rtx])
                            c_["tx"], c_["Rtx"] = tx, Rtx
                        if own:
                            for b in range(NBK):
                                c_ = X[b]
                                mt, Rmt, tm, Rtm, tx, Rtx = c_["mt"], c_["Rmt"], c_["tm"], c_["Rtm"], c_["tx"], c_["Rtx"]
                                psr, Rpsr = pb2()
                                for e in range(2):
                                    self.mm(psr[e * 64:(e + 1) * 64, 0:128], tx[:, e, 0:64], mt[:, e, 128:256], [Rtx, Rmt], [Rpsr])
                                rq, Rrq = RqT.next()
                                self.tt("dve", rq[:], psr[:, 0:128], ar[:, b, 128:256], ALU.add, [Rpsr, Rar], [Rrq])
                                psy, Rpsy = pb2()
                                for e in range(2):
                                    self.mm(psy[:, e * 64:(e + 1) * 64], mt[:, e, 128:256], tx[:, e, 64:128], [Rmt, Rtx], [Rpsy], start=True, stop=False)
                                    self.mm(psy[:, e * 64:(e + 1) * 64], mt[:, e, 384:512], tm[:, 3, e * 64:(e + 1) * 64], [Rmt, Rtm], [Rpsy], start=False, stop=True)
                                y0, Ry0 = Y0.next()
                                self.cp("act", y0[:], psy[:, 0:128].rearrange("p (e v) -> p e v", e=2), [Rpsy], [Ry0])
                                c_["rq"], c_["Rrq"], c_["y0"], c_["Ry0"] = rq, Rrq, y0, Ry0
                        for b in range(NBK):
                            c_ = X[b]
                            tm, Rtm, tx, Rtx = c_["tm"], c_["Rtm"], c_["tx"], c_["Rtx"]
                            c_["gt"], c_["ff"] = [], []
                            for c in range(2):
                                cs = slice(c * 64, (c + 1) * 64)
                                psg, Rpsg = pb2()
                                for e in range(2):
                                    self.mm(psg[e * 64:(e + 1) * 64, 0:64], tx[cs, e, 0:64], tm[cs, 1, e * 64:(e + 1) * 64], [Rtx, Rtm], [Rpsg])
                                gt, Rgt = GTr.next()
                                self.tt("dve", gt[0:64, :], psg[0:64, 0:64], ident[0:64, 0:64], ALU.add, [Rpsg, Rid], [Rgt])
                                self.tt("dve", gt[64:128, :], psg[64:128, 0:64], ident[64:128, 64:128], ALU.add, [Rpsg, Rid], [Rgt])
                                psf, Rpsf = pb2()
                                for e in range(2):
                                    es = slice(e * 64, (e + 1) * 64)
                                    self.mm(psf[es, 0:64], tm[cs, 1, es], tx[cs, e, 64:128], [Rtm, Rtx], [Rpsf], start=True, stop=False)
                                    self.mm(psf[es, 0:64], tm[cs, 2, es], tm[cs, 3, es], [Rtm], [Rpsf], start=False, stop=True)
                                ff, Rff = Fr.next()
                                pcc = pc[:, b * 2 + c:b * 2 + c + 1]
                                self.act(ff[:], psf[:, 0:64], AF.Copy, [Rpsf, Rpc], [Rff], scale=pcc)
                                c_["gt"].append((gt, Rgt))
                                c_["ff"].append((ff, Rff))
                        for b in range(NBK):
                            c_ = X[b]
                            bs = c_["bs"]
                            if own:
                                yy, Ryy = Yr.next()
                                rq, Rrq, y0, Ry0 = c_["rq"], c_["Rrq"], c_["y0"], c_["Ry0"]
                            for c in range(2):
                                cs = slice(c * 64, (c + 1) * 64)
                                gt, Rgt = c_["gt"][c]
                                ff, Rff = c_["ff"][c]
                                if own:
                                    for e in range(2):
                                        es = slice(e * 64, (e + 1) * 64)
                                        psq, Rpsq = pb2()
                                        self.mm(psq[cs, 0:64], rq[es, cs], H[es, :], [Rrq, RH], [Rpsq])
                                        self.tt("dve", yy[cs, e, :], psq[cs, 0:64], y0[cs, e, :], ALU.add, [Rpsq, Ry0], [Ryy])
                                Hn, RHn = Hs.next()
                                for e in range(2):
                                    es = slice(e * 64, (e + 1) * 64)
                                    psh, Rpsh = pb2()
                                    self.mm(psh[es, 0:64], gt[es, :], H[es, :], [Rgt, RH], [Rpsh])
                                    self.stt(Hn[es, :], psh[es, 0:64], pc[es, b * 2 + c:b * 2 + c + 1], ff[es, :], ALU.mult, ALU.add, [Rpsh, Rpc, Rff], [RHn])
                                H, RH = Hn, RHn
                            if own:
                                s1, Rs1 = gs_.next()
                                S.op("dve", lambda e, o=s1, i=yy: e.tensor_reduce(out=o[:], in_=i[:], axis=AX.X, op=ALU.add), [Ryy], [Rs1])
                                self.ts("dve", s1[:], s1[:], -1.0 / 64, ALU.mult, [Rs1], [Rs1])
                                yc, Ryc = gn.next()
                                self.tt("dve", yc[:], yy[:], s1[:].unsqueeze(2).broadcast_to([128, 2, 64]), ALU.add, [Ryy, Rs1], [Ryc])
                                y2, Ry2 = gn.next()
                                self.tt("pool", y2[:], yc[:], yc[:], ALU.mult, [Ryc], [Ry2])
                                s2, Rs2 = gs_.next()
                                S.op("dve", lambda e, o=s2, i=y2: e.tensor_reduce(out=o[:], in_=i[:], axis=AX.X, op=ALU.add), [Ry2], [Rs2])
                                self.act(s2[:], s2[:], AF.Sqrt, [Rs2], [Rs2], bias=GN_EPS, scale=1.0 / 64)
                                self.recip(s2[:], s2[:], [Rs2], [Rs2])
                                self.tt("dve", yc[:], yc[:], s2[:].unsqueeze(2).broadcast_to([128, 2, 64]), ALU.mult, [Ryc, Rs2], [Ryc])
                                pstt, Rpstt = pb2()
                                self.tr(pstt[:, 0:128], yc[:].rearrange("p e v -> p (e v)"), ident[:], [Ryc, Rid], [Rpstt])
                                self.act(yt_[:, bs], pstt[:, 0:128], AF.Identity, [Rpstt, Rvec], [Ryt], bias=vcol("ln_b", m), scale=vcol("ln_w", m))
                        if own:
                            self.tt("pool", yt_[:], yt_[:], bo[:], ALU.add, [Ryt, Rbo], [Ryt])
                            ys, Rys = yst.next()
                            yk = (yst.i - 1) % 2
                            self.tt("pool", ys[:], yt_[:], gr[:], ALU.mult, [Ryt, Rgr], [Rys])
                            S.dma(ych[yk], [(YbT[ms, o0:o0 + 512], ys[:])], reads=[Rys], writes=[R_YbT])
                S.barrier()
                S.emit()
                for n_ in S.ENGS:
                    S.e[n_].ops = []

            with ExitStack() as st:
                if self.upto < 4:
                    raise _Stop()
                Vall = self.sb(st, "Vall", [128, NB, 520], BF16); RVall = Res()
                nv = max(1, NB // 16)
                for i in range(0, NB, 16):
                    j = min(NB, i + 16)
                    S.dma(S.misc(), [(Vall[:, i:j, :], Vs[:, i:j, :])], reads=[R_Vs], writes=[RVall])
                Kh = self.ring(st, "Kh", [128, TT], BF16, 2)
                Qh = self.ring(st, "Qh", [128, TO], BF16, 2)
                kqch = [S.chan(), S.chan()]
                PT = self.ring(st, "PT", [128, 512], BF16, 6)
                osb = self.ring(st, "osb", [128, 512], F32, 2)
                rl = self.ring(st, "rl", [128, 512], F32, 2)
                ost = self.ring(st, "ost", [128, 512], BF16, 2)
                och = [S.chan(), S.chan()]
                sbanks = Ring(banks[0:4])
                obanks = Ring(banks[4:6])
                bbanks = Ring(banks[6:7])
                V5 = Vall[:].rearrange("p b (h d) -> p b h d", d=65)
                LOOK = 2
                heads = []

                def load_head(hd):
                    K_, RK_ = Kh.next(); Q_, RQ_ = Qh.next()
                    hk = (Kh.i - 1) % 2
                    S.dma(kqch[hk], [(K_[0:64, :], KnT[hd * 64:(hd + 1) * 64, :]), (K_[64:97, :], KpeT[:, :]), (Q_[0:97, :], QT[hd, :, :])],
                          reads=[R_KnT, R_KpeT, R_QT], writes=[RK_, RQ_])
                    return (K_, RK_, Q_, RQ_)

                pend = []

                def do_pv(item):
                    (hd, qt, kb, nkb, cst, pt, Rpt, po, Rpo) = item
                    self.mm(po[0:65, cst:512], V5[:, kb, hd, :], pt[:, cst:512], [RVall, Rpt], [Rpo], start=(kb == 0), stop=(kb == nkb - 1), inc=True)
                    if kb == nkb - 1:
                        q0 = qt * 512
                        o_, Ro_ = osb.next()
                        self.cp("act", o_[0:65, :], po[0:65, :], [Rpo], [Ro_])
                        r_, Rr_ = rl.next()
                        self.recip(r_[64:65, :], o_[64:65, :], [Ro_], [Rr_])
                        pb_, Rpb_ = bbanks.next()
                        self.mm(pb_[0:64, :], onesf[64:65, 0:64], r_[64:65, :], [Ronesf, Rr_], [Rpb_])
                        os_, Ros_ = ost.next()
                        ok = (ost.i - 1) % 2
                        self.tt("dve", os_[0:64, :], pb_[0:64, :], o_[0:64, :], ALU.mult, [Rpb_, Ro_], [Ros_])
                        S.dma(och[ok], [(OaT[hd * 64:(hd + 1) * 64, q0:q0 + 512], os_[0:64, :])], reads=[Ros_], writes=[R_OaT])

                nxt = load_head(0)
                for hd in range(NH):
                    K_, RK_, Q_, RQ_ = nxt
                    if hd + 1 < NH:
                        nxt = load_head(hd + 1)
                    for qt in range(NTO):
                        q0 = qt * 512
                        nkb = (TP + q0 + 512) // 128
                        po, Rpo = obanks.next()
                        for kb in range(nkb):
                            jd = kb - (nkb - 4)
                            cst = 0 if jd < 0 else jd * 128
                            ps, Rps = sbanks.next()
                            self.mm(ps[:, cst:512], K_[0:97, kb * 128:(kb + 1) * 128], Q_[0:97, q0 + cst:q0 + 512], [RK_, RQ_], [Rps])
                            pt, Rpt = PT.next()
                            self.act(pt[:, cst:512], ps[:, cst:512], AF.Exp, [Rps], [Rpt])
                            if jd >= 0:
                                self.tt("pool", pt[:, cst:cst + 128], pt[:, cst:cst + 128], trim[:], ALU.mult, [Rpt, Rtri], [Rpt])
                            pend.append((hd, qt, kb, nkb, cst, pt, Rpt, po, Rpo))
                            if len(pend) > LOOK:
                                do_pv(pend.pop(0))
                while pend:
                    do_pv(pend.pop(0))
                S.barrier()
                S.emit()
                for n_ in S.ENGS:
                    S.e[n_].ops = []

            with ExitStack() as st:
                if self.upto < 5:
                    raise _Stop()
                womla = self.sb(st, "womla", [128, 4, D], BF16)
                worw = self.sb(st, "worw", [128, 4, D], BF16)
                wout = self.sb(st, "wout", [128, 8, D], BF16)
                wpg = self.sb(st, "wpg", [128, 8, D], BF16)
                wpp = self.sb(st, "wpp", [128, 2, D], BF16)
                Rw4 = Res("w4")
                S.dma(S.misc("pool"), [(womla[:], womla_d.rearrange("(c p) n -> p c n", p=128)),
                                 (worw[:], worw_d.rearrange("(c p) n -> p c n", p=128)),
                                 (wpp[:], wpp_d.rearrange("(c p) n -> p c n", p=128))], writes=[Rw4], q="pool")
                wout3 = wout_d.rearrange("(c p) n -> p c n", p=128)
                wpg3 = wpg_d.rearrange("(c p) n -> p c n", p=128)
                for c in range(8):
                    S.dma(S.misc("pool"), [(wout[:, c, :], wout3[:, c, :]), (wpg[:, c, :], wpg3[:, c, :])], writes=[Rw4], q="pool")
                wupr = self.ring(st, "wupr", [128, 8, 256], BF16, 3)
                wupc = [S.chan() for _ in range(3)]
                wdnr = self.ring(st, "wdnr", [128, 2, 512], BF16, 4)
                wdnc = [S.chan() for _ in range(4)]
                wup3 = wupb.rearrange("(c p) n -> p c n", p=128)
                wdn3 = wdnb.rearrange("(c p) n -> p c n", p=128)
                x4 = self.ring(st, "x4_", [128, 8, 512], F32, 1)
                xc4 = S.chan()
                inb = self.ring(st, "inb", [128, 4, 512], BF16, 2)
                inc_ = [S.chan(), S.chan()]
                gtl = self.ring(st, "gtl", [128, 8, 512], BF16, 2)
                gtc = [S.chan(), S.chan()]
                pin = self.sb(st, "pin", [128, 2, 512], BF16); Rpin = Res()
                pinc = S.chan()
                mix = self.sb(st, "mix", [128, 8, 512], BF16); Rmix = Res()
                mtmp = self.ring(st, "mtmp", [128, 512], F32, 3)
                sq4 = self.ring(st, "sq4", [128, 512], BF16, 4)
                h4 = self.sb(st, "h4", [128, 8, 512], BF16); Rh4 = Res()
                hid = self.sb(st, "hid", [128, 16, 512], BF16)
                Rhid = [Res() for _ in range(16)]
                rel = self.ring(st, "rel", [128, 512], F32, 3)
                rs4 = self.ring(st, "rs4", [128, 512], F32, 2)
                t4 = self.ring(st, "t4", [128, 512], F32, 2)
                gsg = self.ring(st, "gsg", [128, 512], F32, 2)
                osb4 = self.ring(st, "osb4", [128, 512], F32, 2)
                och4 = [S.chan(), S.chan()]
                pi4 = [0]

                def pb4():
                    b_ = banks[pi4[0] % 7]
                    pi4[0] += 1
                    return b_

                def rms4(xt, Rxt):
                    ps, Rps = pb4()
                    for c in range(8):
                        sq, Rsq = sq4.next()
                        self.act(sq[:], xt[:, c, :], AF.Square, [Rxt], [Rsq])
                        self.mm(ps[:, :], onesb[:, :], sq[:], [Rones, Rsq], [Rps], start=(c == 0), stop=(c == 7), inc=True)
                    t1, Rt1 = t4.next()
                    self.act(t1[:], ps[:, :], AF.Ln, [Rps, Reps], [Rt1], bias=epsc[:, 0:1], scale=1.0 / D)
                    rs, Rrs = rs4.next()
                    self.act(rs[:], t1[:], AF.Exp, [Rt1], [Rrs], scale=-0.5)
                    return rs, Rrs

                for to in range(NTO):
                    o0 = to * 512
                    xt, Rxt = x4.next()
                    S.dma(xc4, [(xt[:], xT3[:, :, TP + o0:TP + o0 + 512])], writes=[Rxt])
                    oa, Roa = inb.next()
                    S.dma(inc_[0], [(oa[:], OaT[:, o0:o0 + 512].rearrange("(c p) t -> p c t", p=128))], reads=[R_OaT], writes=[Roa])
                    yb, Ryb = inb.next()
                    S.dma(inc_[1], [(yb[:], YbT[:, o0:o0 + 512].rearrange("(c p) t -> p c t", p=128))], reads=[R_YbT], writes=[Ryb])
                    ga, Rga = gtl.next()
                    S.dma(gtc[0], [(ga[:], GT[0:1024, o0:o0 + 512].rearrange("(c p) t -> p c t", p=128))], reads=[R_GT], writes=[Rga])
                    gb, Rgb = gtl.next()
                    S.dma(gtc[1], [(gb[:], GT[1024:2048, o0:o0 + 512].rearrange("(c p) t -> p c t", p=128))], reads=[R_GT], writes=[Rgb])
                    S.dma(pinc, [(pin[:], pT[:, o0:o0 + 512].rearrange("(c p) t -> p c t", p=128))], writes=[Rpin], q="pool")
                    for mt_ in range(8):
                        msl = slice(mt_ * 128, (mt_ + 1) * 128)
                        psa, Rpsa = pb4()
                        for c in range(4):
                            self.mm(psa[:, :], womla[:, c, msl], oa[:, c, :], [Rw4, Roa], [Rpsa], start=(c == 0), stop=(c == 3))
                        psb, Rpsb = pb4()
                        for c in range(4):
                            self.mm(psb[:, :], worw[:, c, msl], yb[:, c, :], [Rw4, Ryb], [Rpsb], start=(c == 0), stop=(c == 3))
                        m1, Rm1 = mtmp.next()
                        self.tt("dve", m1[:], psa[:, :], ga[:, mt_, :], ALU.mult, [Rpsa, Rga], [Rm1])
                        m2, Rm2 = mtmp.next()
                        self.tt("dve", m2[:], psb[:, :], gb[:, mt_, :], ALU.mult, [Rpsb, Rgb], [Rm2])
                        self.tt("pool", mix[:, mt_, :], m1[:], m2[:], ALU.add, [Rm1, Rm2], [Rmix])
                    for mt_ in range(8):
                        msl = slice(mt_ * 128, (mt_ + 1) * 128)
                        ps, Rps = pb4()
                        for c in range(8):
                            self.mm(ps[:, :], wout[:, c, msl], mix[:, c, :], [Rw4, Rmix], [Rps], start=(c == 0), stop=(c == 7))
                        self.tt("dve", xt[:, mt_, :], ps[:, :], xt[:, mt_, :], ALU.add, [Rps, Rxt], [Rxt])
                    rs, Rrs = rms4(xt, Rxt)
                    for c in range(8):
                        self.stt(h4[:, c, :], xt[:, c, :], vcol("g_ffn", c), rs[:], ALU.mult, ALU.mult, [Rxt, Rvec, Rrs], [Rh4])
                    for hh in range(2):
                        for uc in range(8):
                            wu, Rwu = wupr.next()
                            uk = (wupr.i - 1) % 3
                            col = hh * 2048 + uc * 256
                            S.dma(wupc[uk], [(wu[:], wup3[:, :, col:col + 256])], reads=[R_wconv], writes=[Rwu])
                            for j in range(2):
                                mi = uc * 2 + j
                                ps, Rps = pb4()
                                for c in range(8):
                                    self.mm(ps[:, :], wu[:, c, j * 128:(j + 1) * 128], h4[:, c, :], [Rwu, Rh4], [Rps], start=(c == 0), stop=(c == 7))
                                rl_, Rrl = rel.next()
                                self.act(rl_[:], ps[:, :], AF.Relu, [Rps], [Rrl])
                                self.tt("pool", hid[:, mi, :], rl_[:], rl_[:], ALU.mult, [Rrl], [Rhid[mi]])
                        for oh in range(2):
                            pss = [pb4() for _ in range(4)]
                            for kc in range(8):
                                wd, Rwd = wdnr.next()
                                dk = (wdnr.i - 1) % 4
                                kr = hh * 16 + kc * 2
                                S.dma(wdnc[dk], [(wd[:], wdn3[:, kr:kr + 2, oh * 512:(oh + 1) * 512])], reads=[R_wconv], writes=[Rwd])
                                for j in range(2):
                                    kci = kc * 2 + j
                                    for mq in range(4):
                                        ps, Rps = pss[mq]
                                        self.mm(ps[:, :], wd[:, j, mq * 128:(mq + 1) * 128], hid[:, kci, :], [Rwd, Rhid[kci]], [Rps],
                                                start=(kci == 0), stop=(kci == 15), inc=(kci == 15 or (j == 1 and mq == 3)))
                            for mq in range(4):
                                mt_ = oh * 4 + mq
                                ps, Rps = pss[mq]
                                self.tt("dve", xt[:, mt_, :], ps[:, :], xt[:, mt_, :], ALU.add, [Rps, Rxt], [Rxt])
                    rs, Rrs = rms4(xt, Rxt)
                    for c in range(8):
                        self.stt(h4[:, c, :], xt[:, c, :], vcol("g_ple", c), rs[:], ALU.mult, ALU.mult, [Rxt, Rvec, Rrs], [Rh4])
                    for mt_ in range(8):
                        msl = slice(mt_ * 128, (mt_ + 1) * 128)
                        ps, Rps = pb4()
                        for c in range(8):
                            self.mm(ps[:, :], wpg[:, c, msl], h4[:, c, :], [Rw4, Rh4], [Rps], start=(c == 0), stop=(c == 7))
                        gg, Rgg = gsg.next()
                        self.act(gg[:], ps[:, :], AF.Sigmoid, [Rps], [Rgg])
                        ps2, Rps2 = pb4()
                        for c in range(2):
                            self.mm(ps2[:, :], wpp[:, c, msl], pin[:, c, :], [Rw4, Rpin], [Rps2], start=(c == 0), stop=(c == 1))
                        self.tt("dve", gg[:], ps2[:, :], gg[:], ALU.mult, [Rps2, Rgg], [Rgg])
                        self.tt("pool", xt[:, mt_, :], xt[:, mt_, :], gg[:], ALU.add, [Rxt, Rgg], [Rxt])
                    rs, Rrs = rms4(xt, Rxt)
                    for c in range(8):
                        ob, Rob = osb4.next()
                        okk = (osb4.i - 1) % 2
                        self.stt(ob[:], xt[:, c, :], vcol("g_final", c), rs[:], ALU.mult, ALU.mult, [Rxt, Rvec, Rrs], [Rob])
                        S.dma(och4[okk], [(outT[c * 128:(c + 1) * 128, o0:o0 + 512], ob[:])], reads=[Rob])
                S.barrier()
                S.emit()
        return nc


def const_inputs():
    s = np.arange(128)[:, None]
    t = np.arange(128)[None, :]
    same = (s // 64) == (t // 64)
    m_lt = ((s < t) & same).astype(np.float32)
    m_le = ((s <= t) & same).astype(np.float32)
    mask4 = np.concatenate([m_lt, m_le, m_lt, m_le], axis=1)
    maskL = ((t < s) & same).astype(np.float32)
    tri = (s <= t).astype(np.float32)
    bd = same.astype(np.float32)
    scan = np.ones((128, 512), np.float32)
    scan[:, ::64] = 0.0
    inv_freq = (np.float32(10000.0) ** (-np.arange(16, dtype=np.float32) / np.float32(16))).astype(np.float32)
    ropec = np.zeros((128, 2), np.float32)
    ropec[64:80, 0] = inv_freq
    ropec[80:96, 0] = inv_freq
    ropec[64:80, 1] = -1.0
    ropec[80:96, 1] = 1.0
    return dict(c_ident=np.eye(128, dtype=np.float32), c_mask4=mask4, c_maskL=maskL, c_tri=tri, c_bd=bd, c_scan=scan, ropec=ropec)


def shared_inputs(inp):
    f = lambda a: np.ascontiguousarray(np.asarray(a, dtype=np.float32))
    w_in = f(inp["w_in"][0])
    z64 = np.zeros((D, 64), np.float32)
    kpe = w_in[:, 640:672]
    w_kpe = np.concatenate([z64, kpe, z64, kpe[:, 16:32], kpe[:, 0:16]], axis=1)
    w_uq = f(inp["w_uq"][0]).reshape(384, NH, 96)
    wq_sw = np.zeros_like(w_uq)
    wq_sw[:, :, 64:80] = w_uq[:, :, 80:96]
    wq_sw[:, :, 80:96] = w_uq[:, :, 64:80]
    w_ukv = f(inp["w_ukv"][0]).reshape(256, NH, 128)
    vec = {"g_mix": inp["g_mix"][0], "g_q_a": inp["g_q_a"][0], "g_kv_a": inp["g_kv_a"][0], "mu": inp["mu_rwkv"][0],
           "w0": inp["w0"][0], "a0": inp["a0"][0], "k_k": inp["k_k"][0], "k_a": inp["k_a"][0],
           "r_k": np.asarray(inp["r_k"][0]).reshape(-1), "ln_w": inp["ln_x_w"][0], "ln_b": inp["ln_x_b"][0],
           "g_ffn": inp["g_ffn"][0], "g_ple": inp["g_ple"][0], "g_final": inp["g_final"]}
    vecs = np.zeros((128, NVEC), np.float32)
    for name, n in VEC_LAYOUT:
        v = f(vec[name]).reshape(n, 128)
        vecs[:, VEC_OFF[name]:VEC_OFF[name] + n] = v.T
    d = dict(vecs=vecs, w_in=w_in, w_kpe=np.ascontiguousarray(w_kpe), wq=np.ascontiguousarray(w_uq.reshape(384, 768)),
             wq_sw=np.ascontiguousarray(wq_sw.reshape(384, 768)),
             wukv_k=np.ascontiguousarray(w_ukv[:, :, 0:64].reshape(256, 512)),
             wukv_v=np.ascontiguousarray(w_ukv[:, :, 64:128].reshape(256, 512)),
             w_o_mla=f(inp["w_o_mla"][0]), w2=f(inp["w2"][0]), a2=f(inp["a2"][0]), g2=f(inp["g2"][0]),
             w_o_rwkv=f(inp["w_o_rwkv"][0]), w_out=f(inp["w_out"][0]), w_up=f(inp["w_ffn_up"][0]), w_down=f(inp["w_ffn_down"][0]),
             w_pg=f(inp["w_ple_gate"][0]), w_pp=f(inp["w_ple_proj"][0]))
    d.update(const_inputs())
    return d


def core_inputs(x_b, p_b, pos_b, half, TP, TO):
    TT = TP + TO
    xT = np.zeros((D, TT), np.float32)
    posr = np.zeros((1, TT), np.int32)
    mrow = np.zeros((1, TT), np.float32)
    o0 = half * TP
    if half == 1:
        xT[:, 0:TP] = x_b[0:TP].T
        posr[0, 0:TP] = pos_b[0:TP]
    else:
        mrow[0, 0:TP] = -30000.0
    xT[:, TP:] = x_b[o0:o0 + TO].T
    posr[0, TP:] = pos_b[o0:o0 + TO]
    pT = np.ascontiguousarray(p_b[o0:o0 + TO].T.astype(np.float32))
    return dict(xT=xT, pT=pT, pos=posr, maskrow=mrow.astype(ml_dtypes.bfloat16))


_NC_CACHE = {}


def get_nc(TP, TO):
    if (TP, TO) not in _NC_CACHE:
        _NC_CACHE[(TP, TO)] = B(TP, TO).build()
    return _NC_CACHE[(TP, TO)]


def kernel(**inputs):
    x = np.asarray(inputs["x"], dtype=np.float32)
    p = np.asarray(inputs["p"], dtype=np.float32)[0]
    pos = np.asarray(inputs["positions"]).astype(np.int32)
    Bn, Sq, _ = x.shape
    TP = TO = Sq // 2
    nc = get_nc(TP, TO)
    sh = shared_inputs(inputs)
    in_maps = []
    for c in range(8):
        b, half = c // 2, c % 2
        m = dict(sh)
        m.update(core_inputs(x[b], p[b], pos[b], half, TP, TO))
        in_maps.append(m)
    res = run_bass_kernel_spmd(nc, in_maps, core_ids=list(range(8)))
    out = np.zeros((Bn, Sq, D), np.float32)
    for c in range(8):
        b, half = c // 2, c % 2
        out[b, half * TO:(half + 1) * TO, :] = res.results[c]["outT"].T
    return out
```

```python
from contextlib import ExitStack
import numpy as np
import ml_dtypes
import concourse.bass as bass
import concourse.mybir as mybir
from concourse.bass_utils import run_bass_kernel_spmd

F32 = mybir.dt.float32
BF16 = mybir.dt.bfloat16
I32 = mybir.dt.int32
AF = mybir.ActivationFunctionType
ALU = mybir.AluOpType
AX = mybir.AxisListType

D = 1024
NH = 8
RMS_EPS = 1e-6
GN_EPS = 64 * 1e-5
SCALE = 96 ** -0.5
EXPH = float(np.exp(-0.5))
TWO_PI = 2.0 * np.pi
C1 = 6.28125
C2 = float(TWO_PI - 6.28125)


class Res:
    __slots__ = ("name", "w", "rd")

    def __init__(self, name=""):
        self.name = name
        self.w = None
        self.rd = []


class Chan:
    def __init__(self, sem, name):
        self.sem = sem
        self.count = 0
        self.name = name


class _Eng:
    def __init__(self, name, sem):
        self.name = name
        self.sem = sem
        self.count = 0
        self.ops = []
        self.waited = {}


class Sched:
    ENGS = ("pe", "act", "dve", "pool", "sp")
    HMAP = {"pe": "tensor", "act": "scalar", "dve": "vector", "pool": "gpsimd", "sp": "sync"}

    def __init__(self, nc, stack, n_chan=90):
        self.nc = nc
        self.e = {}
        for n in self.ENGS:
            self.e[n] = _Eng(n, stack.enter_context(nc.semaphore("s_" + n)))
        self.chans = [Chan(stack.enter_context(nc.semaphore("c%d" % i)), "c%d" % i) for i in range(n_chan)]
        self.chan_i = 4
        self.nops = 0
        self.misc_i = 0

    def misc(self, q="sp"):
        base = 0 if q == "sp" else 2
        c = self.chans[base + self.misc_i % 2]
        self.misc_i += 1
        return c

    def chan(self):
        c = self.chans[self.chan_i]
        self.chan_i += 1
        return c

    def _need(self, eng, reads, writes):
        E = self.e[eng]
        need = {}

        def add(t):
            if t is None:
                return
            key, val = t
            if key is E and eng == "pe":
                return
            if need.get(key, 0) < val:
                need[key] = val

        for r in reads:
            add(r.w)
        for w in writes:
            add(w.w)
            for t in w.rd:
                add(t)
        for key, val in need.items():
            if E.waited.get(key, 0) < val:
                E.waited[key] = val
                E.ops.append(("wait", key.sem, val))

    def op(self, eng, fn, reads=(), writes=(), inc=True):
        E = self.e[eng]
        self._need(eng, reads, writes)
        if inc:
            E.count += 1
            t = (E, E.count)
        else:
            t = (E, E.count + 1)
        E.ops.append(("op", fn, inc))
        for r in reads:
            r.rd.append(t)
            if len(r.rd) > 64:
                r.rd = _compress(r.rd)
        for w in writes:
            w.w = t
            w.rd = []
        self.nops += 1
        return t

    def dma(self, chan, pairs, reads=(), writes=(), q="sp"):
        E = self.e[q]
        if chan.count > 0 and E.waited.get(chan, 0) < chan.count:
            E.waited[chan] = chan.count
            E.ops.append(("wait", chan.sem, chan.count))
        self._need(q, reads, writes)
        for (o, i) in pairs:
            chan.count += 16
            E.ops.append(("dma", o, i, chan.sem))
        t = (chan, chan.count)
        for r in reads:
            r.rd.append(t)
        for w in writes:
            w.w = t
            w.rd = []
        return t

    def barrier(self):
        for n in self.ENGS:
            E = self.e[n]
            for m in self.ENGS:
                O = self.e[m]
                if O is E or O.count == 0:
                    continue
                if E.waited.get(O, 0) < O.count:
                    E.waited[O] = O.count
                    E.ops.append(("wait", O.sem, O.count))
            for c in self.chans:
                if c.count and E.waited.get(c, 0) < c.count:
                    E.waited[c] = c.count
                    E.ops.append(("wait", c.sem, c.count))

    def emit(self):
        nc = self.nc
        with nc.Block() as block:
            for n in self.ENGS:
                E = self.e[n]

                def body(h, E=E):
                    for o in E.ops:
                        if o[0] == "wait":
                            h.wait_ge(o[1], o[2])
                        elif o[0] == "op":
                            ins = o[1](h)
                            if o[2]:
                                ins.then_inc(E.sem, 1)
                        else:
                            h.dma_start(out=o[1], in_=o[2]).then_inc(o[3], 16)

                getattr(block, self.HMAP[n])(body)


def _compress(tickets):
    best = {}
    for key, val in tickets:
        if best.get(key, 0) < val:
            best[key] = val
    return list(best.items())


class Ring:
    def __init__(self, items):
        self.items = items
        self.i = 0

    def next(self):
        it = self.items[self.i % len(self.items)]
        self.i += 1
        return it


VEC_LAYOUT = [("g_mix", 8), ("g_q_a", 3), ("g_kv_a", 2), ("mu", 14), ("w0", 4), ("a0", 4), ("k_k", 4),
              ("k_a", 4), ("r_k", 4), ("ln_w", 4), ("ln_b", 4), ("g_ffn", 8), ("g_ple", 8), ("g_final", 8)]
VEC_OFF = {}
_o = 0
for _n, _c in VEC_LAYOUT:
    VEC_OFF[_n] = _o
    _o += _c
NVEC = _o
OM_OFF = NVEC
OMKA_OFF = NVEC + 14
NVEC_TOT = NVEC + 18


class _Stop(Exception):
    pass


class B:
    def __init__(self, TP, TO, debug=False, upto=5, p2_stop=0):
        self.p2_stop = p2_stop
        self.debug = debug
        self.upto = upto
        self.TP, self.TO = TP, TO
        self.TT = TP + TO
        self.nc = bass.Bass("TRN2", target_bir_lowering=False)
        self.st = ExitStack()
        self.S = None

    def din(self, name, shape, dt=F32):
        return self.nc.dram_tensor(name, list(shape), dt, kind="ExternalInput").ap()

    def dscr(self, name, shape, dt=BF16):
        kind = "ExternalOutput" if self.debug else "Internal"
        return self.nc.dram_tensor(name, list(shape), dt, kind=kind).ap()

    def sb(self, st, name, shape, dt=F32):
        self._uid = getattr(self, "_uid", 0) + 1
        return st.enter_context(self.nc.sbuf_tensor("s%d_%s" % (self._uid, name), list(shape), dt))

    def ring(self, st, name, shape, dt, n):
        return Ring([(self.sb(st, "%s%d" % (name, i), shape, dt), Res("%s%d" % (name, i))) for i in range(n)])

    def mm(self, out, lhsT, rhs, reads, writes, start=True, stop=True, inc=None):
        self.S.op("pe", lambda e: e.matmul(out, lhsT, rhs, start=start, stop=stop), reads, writes, inc=(stop if inc is None else inc))

    def tr(self, out, in_, ident, reads, writes, inc=True):
        self.S.op("pe", lambda e: e.transpose(out, in_, ident), reads, writes, inc=inc)

    def act(self, out, in_, func, reads, writes, bias=0.0, scale=1.0):
        if func == AF.Copy and not (isinstance(bias, float) and isinstance(scale, float)):
            func = AF.Identity
        self.S.op("act", lambda e: e.activation(out=out, in_=in_, func=func, bias=bias, scale=scale), reads, writes)

    def tt(self, eng, out, in0, in1, op, reads, writes):
        self.S.op(eng, lambda e: e.tensor_tensor(out=out, in0=in0, in1=in1, op=op), reads, writes)

    def ts(self, eng, out, in0, s1, op0, reads, writes, s2=None, op1=None):
        if op1 is None:
            self.S.op(eng, lambda e: e.tensor_scalar(out=out, in0=in0, scalar1=s1, scalar2=None, op0=op0), reads, writes)
        else:
            self.S.op(eng, lambda e: e.tensor_scalar(out=out, in0=in0, scalar1=s1, scalar2=s2, op0=op0, op1=op1), reads, writes)

    def stt(self, out, in0, scalar, in1, op0, op1, reads, writes):
        self.S.op("dve", lambda e: e.scalar_tensor_tensor(out=out, in0=in0, scalar=scalar, in1=in1, op0=op0, op1=op1), reads, writes)

    def cp(self, eng, out, in_, reads, writes):
        if eng == "act":
            self.act(out, in_, AF.Copy, reads, writes)
        else:
            self.S.op(eng, lambda e: e.tensor_copy(out=out, in_=in_), reads, writes)

    def ckpt(self, n):
        if self.p2_stop == n:
            self.S.barrier()
            self.S.emit()
            raise _Stop()

    def memset(self, eng, ap, val, writes):
        self.S.op(eng, lambda e: e.memset(ap, val), (), writes)

    def recip(self, out, in_, reads, writes):
        self.S.op("dve", lambda e: e.reciprocal(out=out, in_=in_), reads, writes)

    def build(self):
        try:
            self._build()
        except _Stop:
            pass
        return self.nc

    def _build(self):
        nc, TP, TO, TT = self.nc, self.TP, self.TO, self.TT
        NT, NTP, NTO = TT // 512, TP // 512, TO // 512
        NB = TT // 128
        xT = self.din("xT", [D, TT])
        pT = self.din("pT", [256, TO])
        pos = self.din("pos", [1, TT], I32)
        maskrow = self.din("maskrow", [1, TT], BF16)
        vecs_d = self.din("vecs", [128, NVEC])
        rc_d = self.din("ropec", [128, 2])
        w_in = self.din("w_in", [D, 4512])
        w_kpe = self.din("w_kpe", [D, 192])
        wq_d = self.din("wq", [384, 768])
        wqs_d = self.din("wq_sw", [384, 768])
        wkk_d = self.din("wukv_k", [256, 512])
        wkv_d = self.din("wukv_v", [256, 512])
        womla_d = self.din("w_o_mla", [512, D])
        w2_d = self.din("w2", [64, 512])
        a2_d = self.din("a2", [64, 512])
        g2_d = self.din("g2", [128, 512])
        worw_d = self.din("w_o_rwkv", [512, D])
        wout_d = self.din("w_out", [D, D])
        wup_d = self.din("w_up", [D, 4096])
        wdn_d = self.din("w_down", [4096, D])
        wpg_d = self.din("w_pg", [D, D])
        wpp_d = self.din("w_pp", [256, D])
        cm_ident_d = self.din("c_ident", [128, 128])
        cm_mask4_d = self.din("c_mask4", [128, 512])
        cm_maskL_d = self.din("c_maskL", [128, 128])
        cm_tri_d = self.din("c_tri", [128, 128])
        cm_bd_d = self.din("c_bd", [128, 128])
        cm_scan_d = self.din("c_scan", [128, 512])
        outT = nc.dram_tensor("outT", [D, TO], F32, kind="ExternalOutput").ap()
        QT = self.dscr("QT", [NH, 97, TO])
        KnT = self.dscr("KnT", [512, TT])
        KpeT = self.dscr("KpeT", [33, TT])
        Vs = self.dscr("Vs", [128, NB, 520])
        GT = self.dscr("GT", [2048, TO])
        ARt = self.dscr("ARt", [512, NB, 256])
        Bt = self.dscr("Bt", [512, TT])
        Kt = self.dscr("Kt", [512, TT])
        Vt = self.dscr("Vt", [512, TT])
        PCt = self.dscr("PCt", [512, TT // 64], F32)
        GrT = self.dscr("GrT", [512, TO])
        BoT = self.dscr("BoT", [512, TO])
        OaT = self.dscr("OaT", [512, TO])
        YbT = self.dscr("YbT", [512, TO])
        wupb = self.dscr("wupb", [D, 4096])
        wdnb = self.dscr("wdnb", [4096, D])
        R_wconv = Res("wconv")
        R_QT, R_KnT, R_KpeT, R_Vs, R_GT = Res("QT"), Res("KnT"), Res("KpeT"), Res("Vs"), Res("GT")
        R_rw, R_OaT, R_YbT = Res("rwscr"), Res("OaT"), Res("YbT")

        with self.st as st0:
            S = self.S = Sched(nc, st0)
            vecs = self.sb(st0, "vecs", [128, NVEC_TOT]); Rvec = Res("vecs")
            ropec = self.sb(st0, "ropec", [128, 2]); Rrc = Res()
            ident = self.sb(st0, "ident", [128, 128]); Rid = Res()
            identb = self.sb(st0, "identb", [128, 128], BF16); Ridb = Res()
            mask4 = self.sb(st0, "mask4", [128, 512]); Rm4 = Res()
            maskL = self.sb(st0, "maskL", [128, 128]); RmL = Res()
            trim = self.sb(st0, "trim", [128, 128], BF16); Rtri = Res()
            bdb = self.sb(st0, "bdb", [128, 128], BF16); Rbd = Res()
            onesb = self.sb(st0, "onesb", [128, 128], BF16); Rones = Res()
            onesf = self.sb(st0, "onesf", [128, 128]); Ronesf = Res()
            scanm = self.sb(st0, "scanm", [128, 512]); Rscan = Res()
            S.dma(S.misc(), [(vecs[:, 0:NVEC], vecs_d[:, :])], writes=[Rvec])
            S.dma(S.misc(), [(ropec[:], rc_d[:, :])], writes=[Rrc])
            S.dma(S.misc(), [(ident[:], cm_ident_d[:, :])], writes=[Rid])
            S.dma(S.misc("pool"), [(identb[:], cm_ident_d[:, :])], writes=[Ridb], q="pool")
            S.dma(S.misc(), [(mask4[:], cm_mask4_d[:, :])], writes=[Rm4])
            S.dma(S.misc(), [(maskL[:], cm_maskL_d[:, :])], writes=[RmL])
            S.dma(S.misc("pool"), [(trim[:], cm_tri_d[:, :])], writes=[Rtri], q="pool")
            S.dma(S.misc("pool"), [(bdb[:], cm_bd_d[:, :])], writes=[Rbd], q="pool")
            S.dma(S.misc(), [(scanm[:], cm_scan_d[:, :])], writes=[Rscan])
            epsc = self.sb(st0, "epsc", [128, 2]); Reps = Res()
            self.memset("pool", epsc[:, 0:1], RMS_EPS, [Reps])
            self.memset("pool", epsc[:, 1:2], 1e-24, [Reps])
            self.memset("pool", onesb[:], 1.0, [Rones])
            self.memset("pool", onesf[:], 1.0, [Ronesf])
            self.ts("dve", vecs[:, OM_OFF:OM_OFF + 14], vecs[:, VEC_OFF["mu"]:VEC_OFF["mu"] + 14], -1.0, ALU.mult,
                    [Rvec], [Rvec], s2=1.0, op1=ALU.add)
            self.ts("dve", vecs[:, OMKA_OFF:OMKA_OFF + 4], vecs[:, VEC_OFF["k_a"]:VEC_OFF["k_a"] + 4], -1.0, ALU.mult,
                    [Rvec], [Rvec], s2=1.0, op1=ALU.add)
            S.dma(S.misc(), [(KpeT[32:33, :], maskrow[0:1, :])], writes=[R_KpeT])

            def vcol(name, j, p0=0, p1=128):
                o = VEC_OFF[name] + j
                return vecs[p0:p1, o:o + 1]

            banks = [(st0.enter_context(nc.psum_tensor("bank%d" % i, [128, 512], F32)), Res("bank%d" % i)) for i in range(7)]
            bankb = (st0.enter_context(nc.psum_tensor("bankb", [128, 1024], BF16)), Res("bankb"))
            consts = [Rvec, Rrc, Rid, Ridb, Rm4, RmL, Rtri, Rbd, Rones, Ronesf, Rscan]

            xT3 = xT.rearrange("(c p) t -> p c t", p=128)
            w_in3 = w_in.rearrange("(c p) n -> p c n", p=128)
            w_kpe3 = w_kpe.rearrange("(c p) n -> p c n", p=128)
            pi = [0]

            def pbank():
                b_ = banks[pi[0] % 7]
                pi[0] += 1
                return b_

            def make_common(st, ncol):
                cm = {}
                cm["win"] = self.sb(st, "win", [128, 8, ncol], BF16)
                cm["Rwin"] = Res()
                cm["xr"] = self.ring(st, "x1_", [128, 8, 512], F32, 1)
                cm["xch"] = S.chan()
                cm["sqr"] = self.ring(st, "sq1_", [128, 512], BF16, 4)
                cm["hr"] = self.ring(st, "h1_", [128, 8, 512], BF16, 2)
                cm["rstdr"] = self.ring(st, "rstd1_", [128, 512], F32, 2)
                cm["sqt"] = self.ring(st, "sqt1_", [128, 512], F32, 2)
                return cm

            def rms_stats(cm, src3, nchunk, scale, Rsrc):
                ps, Rps = pbank()
                for c in range(nchunk):
                    sq, Rsq = cm["sqr"].next()
                    self.act(sq[:], src3[:, c, :], AF.Square, [Rsrc], [Rsq])
                    self.mm(ps[:, :], onesb[:, :], sq[:], [Rones, Rsq], [Rps], start=(c == 0), stop=(c == nchunk - 1), inc=True)
                t1, Rt1 = cm["sqt"].next()
                self.act(t1[:], ps[:, :], AF.Ln, [Rps, Reps], [Rt1], bias=epsc[:, 0:1], scale=scale)
                rs, Rrs = cm["rstdr"].next()
                self.act(rs[:], t1[:], AF.Exp, [Rt1], [Rrs], scale=-0.5)
                return rs, Rrs

            def load_h(cm, t):
                c0 = t * 512
                xt, Rxt = cm["xr"].next()
                S.dma(cm["xch"], [(xt[:], xT3[:, :, c0:c0 + 512])], writes=[Rxt])
                rs, Rrs = rms_stats(cm, xt, 8, 1.0 / D, Rxt)
                h, Rh = cm["hr"].next()
                for c in range(8):
                    self.stt(h[:, c, :], xt[:, c, :], vcol("g_mix", c), rs[:], ALU.mult, ALU.mult, [Rxt, Rvec, Rrs], [Rh])
                return h, Rh

            def zmm(cm, h, Rh, col0, M, rw=None):
                ps, Rps = pbank()
                Rw_ = cm["Rwin"] if rw is None else rw
                for c in range(8):
                    self.mm(ps[0:M, :], cm["win"][:, c, col0:col0 + M], h[:, c, :], [Rw_, Rh], [Rps], start=(c == 0), stop=(c == 7))
                return ps, Rps

            with ExitStack() as st:
                if self.upto < 1:
                    raise _Stop()
                NCOL = 384 + 256 + 192 + 2048
                OFF_CQ, OFF_CKV, OFF_KPE, OFF_G = 0, 384, 640, 832
                cm = make_common(st, NCOL)
                win, Rwin = cm["win"], cm["Rwin"]
                Rwing = Res("wing")
                for c in range(8):
                    S.dma(S.misc("pool"), [(win[:, c, 0:640], w_in3[:, c, 0:640]),
                                     (win[:, c, 640:832], w_kpe3[:, c, :])], writes=[Rwin], q="pool")
                for c in range(8):
                    S.dma(S.misc("pool"), [(win[:, c, 832:NCOL], w_in3[:, c, 2464:4512])], writes=[Rwing], q="pool")
                cm["Rwing"] = Rwing
                wq = self.sb(st, "wq", [128, 3, 768], BF16)
                wqs = self.sb(st, "wqs", [128, 3, 768], BF16)
                wkk = self.sb(st, "wkk", [128, 2, 512], BF16)
                wkv = self.sb(st, "wkv", [128, 2, 512], BF16)
                Rw1 = Res("w1")
                S.dma(S.misc("pool"), [(wq[:], wq_d.rearrange("(c p) n -> p c n", p=128)),
                                 (wqs[:], wqs_d.rearrange("(c p) n -> p c n", p=128)),
                                 (wkk[:], wkk_d.rearrange("(c p) n -> p c n", p=128)),
                                 (wkv[:], wkv_d.rearrange("(c p) n -> p c n", p=128))],
                      writes=[Rw1], q="pool")
                tmpr = self.ring(st, "tmp1_", [128, 512], F32, 4)
                cq = self.sb(st, "cq", [128, 3, 512]); Rcq = Res()
                cqn = self.sb(st, "cqn", [128, 3, 512], BF16); Rcqn = Res()
                ckv = self.sb(st, "ckv", [128, 2, 512]); Rckv = Res()
                ckvn = self.sb(st, "ckvn", [128, 2, 512], BF16); Rckvn = Res()
                qst = self.ring(st, "qst", [128, 512], BF16, 3)
                qch = [S.chan() for _ in range(3)]
                for (t_, r_) in qst.items:
                    self.memset("pool", t_[64:97, :], 1.0, [r_])
                knst = self.ring(st, "knst", [128, 512], BF16, 2)
                knch = [S.chan() for _ in range(2)]
                vst = self.ring(st, "vst", [128, 4, 520], BF16, 2)
                vch = [S.chan() for _ in range(2)]
                for (t_, r_) in vst.items:
                    self.memset("pool", t_[:], 1.0, [r_])
                kpst = self.ring(st, "kpst", [128, 512], BF16, 2)
                kpch = [S.chan() for _ in range(2)]
                gst = self.ring(st, "gst", [128, 4, 512], BF16, 2)
                gch = [S.chan() for _ in range(2)]
                posi = self.sb(st, "posi", [128, 512], I32); Rposi = Res()
                posch = S.chan()
                rp = [self.sb(st, "rp%d" % i, [128, 512]) for i in range(6)]
                Rrp = [Res() for _ in range(6)]
                ki = self.sb(st, "ki", [128, 512], I32); Rki = Res()
                sl = slice(64, 96)
                for t in range(NT):
                    own = t >= NTP
                    to = t - NTP
                    c0 = t * 512
                    if t == 0:
                        hnext = load_h(cm, 0)
                    h, Rh = hnext
                    S.dma(posch, [(posi[64:96, :], pos[0:1, c0:c0 + 512].partition_broadcast(32))], writes=[Rposi])
                    ang, sinT, cosT, sinQ, cosQ, rr = rp
                    Rang, RsinT, RcosT, RsinQ, RcosQ, Rrr = Rrp
                    self.cp("dve", ang[sl, :], posi[sl, :], [Rposi], [Rang])
                    self.ts("dve", ang[sl, :], ang[sl, :], ropec[sl, 0:1], ALU.mult, [Rang, Rrc], [Rang])
                    self.ts("dve", rr[sl, :], ang[sl, :], float(1.0 / TWO_PI), ALU.mult, [Rang], [Rrr])
                    self.cp("dve", ki[sl, :], rr[sl, :], [Rrr], [Rki])
                    self.cp("dve", rr[sl, :], ki[sl, :], [Rki], [Rrr])
                    self.stt(ang[sl, :], rr[sl, :], -C1, ang[sl, :], ALU.mult, ALU.add, [Rrr, Rang], [Rang])
                    self.stt(ang[sl, :], rr[sl, :], -C2, ang[sl, :], ALU.mult, ALU.add, [Rrr, Rang], [Rang])
                    self.ts("dve", ang[sl, :], ang[sl, :], float(np.pi), ALU.min, [Rang], [Rang], s2=float(-np.pi), op1=ALU.max)
                    self.act(sinT[sl, :], ang[sl, :], AF.Sin, [Rang, Rrc], [RsinT], scale=ropec[sl, 1:2])
                    self.act(rr[sl, :], ang[sl, :], AF.Abs, [Rang], [Rrr])
                    self.ts("dve", rr[sl, :], rr[sl, :], -1.0, ALU.mult, [Rrr], [Rrr], s2=float(np.pi / 2), op1=ALU.add)
                    self.act(cosT[sl, :], rr[sl, :], AF.Sin, [Rrr], [RcosT])
                    if own:
                        self.act(sinQ[sl, :], sinT[sl, :], AF.Copy, [RsinT], [RsinQ], scale=SCALE)
                        self.act(cosQ[sl, :], cosT[sl, :], AF.Copy, [RcosT], [RcosQ], scale=SCALE)
                    psA, RpsA = zmm(cm, h, Rh, OFF_KPE, 96)
                    psB, RpsB = zmm(cm, h, Rh, OFF_KPE + 96, 96)
                    ta, Rta = tmpr.next()
                    tb, Rtb = tmpr.next()
                    self.tt("dve", ta[sl, :], psA[sl, :], cosT[sl, :], ALU.mult, [RpsA, RcosT], [Rta])
                    self.tt("dve", tb[sl, :], psB[sl, :], sinT[sl, :], ALU.mult, [RpsB, RsinT], [Rtb])
                    kp, Rkp = kpst.next()
                    self.tt("pool", kp[sl, :], ta[sl, :], tb[sl, :], ALU.add, [Rta, Rtb], [Rkp])
                    S.dma(kpch[t % 2], [(KpeT[0:32, c0:c0 + 512], kp[sl, :])], reads=[Rkp], writes=[R_KpeT])
                    for m in range(2):
                        ps, Rps = zmm(cm, h, Rh, OFF_CKV + m * 128, 128)
                        self.cp("act", ckv[:, m, :], ps[:, :], [Rps], [Rckv])
                    if t + 1 < NT:
                        hnext = load_h(cm, t + 1)
                    rs2, Rrs2 = rms_stats(cm, ckv, 2, 1.0 / 256, Rckv)
                    for m in range(2):
                        self.stt(ckvn[:, m, :], ckv[:, m, :], vcol("g_kv_a", m), rs2[:], ALU.mult, ALU.mult, [Rckv, Rvec, Rrs2], [Rckvn])
                    for m in range(4):
                        ps, Rps = pbank()
                        for c in range(2):
                            self.mm(ps[:, :], wkk[:, c, m * 128:(m + 1) * 128], ckvn[:, c, :], [Rw1, Rckvn], [Rps], start=(c == 0), stop=(c == 1))
                        kn, Rkn = knst.next()
                        kk_ = (knst.i - 1) % 2
                        self.cp("act", kn[:], ps[:, :], [Rps], [Rkn])
                        S.dma(knch[kk_], [(KnT[m * 128:(m + 1) * 128, c0:c0 + 512], kn[:])], reads=[Rkn], writes=[R_KnT])
                    vt_, Rvt = vst.next()
                    vk = (vst.i - 1) % 2
                    for s_ in range(4):
                        ps, Rps = pbank()
                        for c in range(2):
                            self.mm(ps[:, :], ckvn[:, c, s_ * 128:(s_ + 1) * 128], wkv[:, c, :], [Rckvn, Rw1], [Rps], start=(c == 0), stop=(c == 1))
                        v4 = vt_[:, s_, :].rearrange("p (h d) -> p h d", d=65)
                        self.cp("act", v4[:, :, 0:64], ps[:, :].rearrange("p (h d) -> p h d", d=64), [Rps], [Rvt])
                    S.dma(vch[vk], [(Vs[:, t * 4:(t + 1) * 4, :], vt_[:])], reads=[Rvt], writes=[R_Vs])
                    if own:
                        o0 = to * 512
                        for m in range(3):
                            ps, Rps = zmm(cm, h, Rh, OFF_CQ + m * 128, 128)
                            self.cp("act", cq[:, m, :], ps[:, :], [Rps], [Rcq])
                        rs3, Rrs3 = rms_stats(cm, cq, 3, 1.0 / 384, Rcq)
                        for m in range(3):
                            self.stt(cqn[:, m, :], cq[:, m, :], vcol("g_q_a", m), rs3[:], ALU.mult, ALU.mult, [Rcq, Rvec, Rrs3], [Rcqn])
                        for hd in range(NH):
                            psA, RpsA = pbank()
                            psB, RpsB = pbank()
                            for c in range(3):
                                self.mm(psA[0:96, :], wq[:, c, hd * 96:(hd + 1) * 96], cqn[:, c, :], [Rw1, Rcqn], [RpsA], start=(c == 0), stop=(c == 2))
                            for c in range(3):
                                self.mm(psB[0:96, :], wqs[:, c, hd * 96:(hd + 1) * 96], cqn[:, c, :], [Rw1, Rcqn], [RpsB], start=(c == 0), stop=(c == 2))
                            q_, Rq_ = qst.next()
                            qk = (qst.i - 1) % 3
                            self.act(q_[0:64, :], psA[0:64, :], AF.Copy, [RpsA], [Rq_], scale=SCALE)
                            ta, Rta = tmpr.next()
                            tb, Rtb = tmpr.next()
                            self.tt("dve", ta[sl, :], psA[sl, :], cosQ[sl, :], ALU.mult, [RpsA, RcosQ], [Rta])
                            self.tt("dve", tb[sl, :], psB[sl, :], sinQ[sl, :], ALU.mult, [RpsB, RsinQ], [Rtb])
                            self.tt("pool", q_[sl, :], ta[sl, :], tb[sl, :], ALU.add, [Rta, Rtb], [Rq_])
                            S.dma(qch[qk], [(QT[hd, :, o0:o0 + 512], q_[0:97, :])], reads=[Rq_], writes=[R_QT])
                        for gq in range(4):
                            g_, Rg_ = gst.next()
                            gk = (gst.i - 1) % 2
                            for j in range(4):
                                ps, Rps = zmm(cm, h, Rh, OFF_G + (gq * 4 + j) * 128, 128, rw=cm["Rwing"])
                                self.act(g_[:, j, :], ps[:, :], AF.Sigmoid, [Rps], [Rg_])
                            S.dma(gch[gk], [(GT[gq * 512:(gq + 1) * 512, o0:o0 + 512].rearrange("(j p) t -> p j t", p=128), g_[:])],
                                  reads=[Rg_], writes=[R_GT])
                S.barrier()
                S.emit()
                for n_ in S.ENGS:
                    S.e[n_].ops = []

            with ExitStack() as st:
                if self.upto < 2:
                    raise _Stop()
                cm = make_common(st, 1792)
                win, Rwin = cm["win"], cm["Rwin"]
                for c in range(8):
                    S.dma(S.misc("pool"), [(win[:, c, :], w_in3[:, c, 672:2464])], writes=[Rwin], q="pool")
                w2s = self.sb(st, "w2s", [128, 512], BF16)
                a2s = self.sb(st, "a2s", [128, 512], BF16)
                g2s = self.sb(st, "g2s", [128, 512], BF16)
                Rw1 = Res("w1b")
                S.dma(S.misc("pool"), [(w2s[0:64, :], w2_d[:, :]), (a2s[64:128, :], a2_d[:, :]), (g2s[:], g2_d[:, :])], writes=[Rw1], q="pool")
                tmpr = self.ring(st, "tmp1b_", [128, 512], F32, 3)
                tmpb = self.ring(st, "tmpb1_", [128, 512], BF16, 4)
                zcw = self.ring(st, "zcw", [128, 513], F32, 4)
                carry = self.sb(st, "carry", [128, 16]); Rcar = Res()
                self.memset("pool", carry[:], 0.0, [Rcar])
                zsr = self.ring(st, "zsr", [128, 512], F32, 2)
                zsk = self.ring(st, "zsk", [128, 512], F32, 2)
                zsv = self.ring(st, "zsv", [128, 512], F32, 2)
                zs12 = self.sb(st, "zs12", [128, 512]); Rzs12 = Res()
                zs13 = self.sb(st, "zs13", [128, 512]); Rzs13 = Res()
                names = ["sig", "av", "Lc", "Lx", "kk", "nr", "tk", "EP", "EN"]
                nb2 = [{n_: (self.sb(st, "rb%d_" % k_ + n_, [128, 512]), Res(n_)) for n_ in names} for k_ in range(2)]
                twb = self.sb(st, "twb", [128, 512], BF16); Rtwb = Res()
                gsb = self.sb(st, "gsb", [128, 512], BF16); Rgsb = Res()
                rwst = self.ring(st, "rwst", [128, 512], BF16, 12)
                rwch = [S.chan() for _ in range(12)]
                rwi = [0]
                pcst = self.ring(st, "pcst", [128, 8], F32, 4)
                pcch = [S.chan() for _ in range(4)]

                def rw_store(dst_ap, src_fn, eng_fn):
                    k = rwi[0] % 12
                    rwi[0] += 1
                    t_, r_ = rwst.items[k]
                    eng_fn(t_, r_)
                    S.dma(rwch[k], [(dst_ap, src_fn(t_))], reads=[r_], writes=[R_rw])

                MU, OM = VEC_OFF["mu"], OM_OFF

                def shift(cm, h, Rh, m, dst, Rdst):
                    ps, Rps = zmm(cm, h, Rh, m * 128, 128)
                    zc, Rzc = zcw.next()
                    self.act(zc[:, 1:513], ps[:, :], AF.Copy, [Rps, Rvec], [Rzc], scale=vecs[:, MU + m:MU + m + 1])
                    self.cp("pool", zc[:, 0:1], carry[:, m:m + 1], [Rcar], [Rzc])
                    self.stt(dst[:], ps[:, :], vecs[:, OM + m:OM + m + 1], zc[:, 0:512], ALU.mult, ALU.add, [Rps, Rvec, Rzc], [Rdst])
                    self.cp("pool", carry[:, m:m + 1], zc[:, 512:513], [Rzc], [Rcar])

                for t in range(NT):
                    own = t >= NTP
                    o0 = (t - NTP) * 512
                    c0 = t * 512
                    if t == 0:
                        hnext = load_h(cm, 0)
                    h, Rh = hnext
                    shift(cm, h, Rh, 12, zs12, Rzs12)
                    shift(cm, h, Rh, 13, zs13, Rzs13)
                    if t + 1 < NT:
                        hnext = load_h(cm, t + 1)
                    self.act(twb[0:64, :], zs12[0:64, :], AF.Tanh, [Rzs12], [Rtwb])
                    self.cp("pool", twb[64:128, :], zs12[64:128, :], [Rzs12], [Rtwb])
                    self.act(gsb[:], zs13[:], AF.Sigmoid, [Rzs13], [Rgsb])
                    def st1(c_):
                        m = c_["m"]
                        c_["r"] = zsr.next(); c_["k"] = zsk.next(); c_["v"] = zsv.next()
                        shift(cm, h, Rh, m, *c_["r"])
                        shift(cm, h, Rh, 4 + m, *c_["k"])
                        shift(cm, h, Rh, 8 + m, *c_["v"])
                        c_["ms"] = slice(m * 128, (m + 1) * 128)
                        c_["nb"] = nb2[m % 2]

                    def st2(c_):
                        m, ms, nb = c_["m"], c_["ms"], c_["nb"]
                        (sig, Rsig), (av, Rav) = nb["sig"], nb["av"]
                        ps, Rps = pbank()
                        self.mm(ps[:, :], w2s[0:64, ms], twb[0:64, :], [Rw1, Rtwb], [Rps])
                        self.act(sig[:], ps[:, :], AF.Sigmoid, [Rps, Rvec], [Rsig], bias=vcol("w0", m))
                        ps, Rps = pbank()
                        self.mm(ps[:, :], a2s[64:128, ms], twb[64:128, :], [Rw1, Rtwb], [Rps])
                        self.act(av[:], ps[:, :], AF.Sigmoid, [Rps, Rvec], [Rav], bias=vcol("a0", m))

                    def st3(c_):
                        m, nb = c_["m"], c_["nb"]
                        (sig, Rsig), (Lc, RLc), (kk, Rkk) = nb["sig"], nb["Lc"], nb["kk"]
                        k_m, Rk = c_["k"]
                        S.op("dve", lambda e, o=Lc, d=sig: e.tensor_tensor_scan(out=o[:], data0=scanm[:], data1=d[:], initial=0.0,
                                                                              op0=ALU.mult, op1=ALU.add), [Rscan, Rsig], [RLc])
                        self.act(kk[:], k_m[:], AF.Copy, [Rk, Rvec], [Rkk], scale=vcol("k_k", m))
                        kk2, Rkk2 = tmpb.next()
                        self.act(kk2[:], k_m[:], AF.Square, [Rk, Rvec], [Rkk2], scale=vcol("k_k", m))
                        ps, Rps = pbank()
                        self.mm(ps[:, :], bdb[:, :], kk2[:], [Rbd, Rkk2], [Rps])
                        c_["psn"] = (ps, Rps)

                    def st4(c_):
                        m, nb = c_["m"], c_["nb"]
                        (av, Rav), (kk, Rkk), (nr, Rnr), (tk, Rtk) = nb["av"], nb["kk"], nb["nr"], nb["tk"]
                        k_m, Rk = c_["k"]
                        ps, Rps = c_["psn"]
                        self.ts("dve", nr[:], ps[:, :], 1e-24, ALU.max, [Rps], [Rnr])
                        self.act(nr[:], nr[:], AF.Ln, [Rnr], [Rnr])
                        self.act(nr[:], nr[:], AF.Exp, [Rnr], [Rnr], scale=-0.5)
                        self.tt("dve", kk[:], kk[:], nr[:], ALU.mult, [Rkk, Rnr], [Rkk])
                        self.ts("dve", tk[:], av[:], vcol("k_a", m), ALU.mult, [Rav, Rvec], [Rtk],
                                s2=vecs[:, OMKA_OFF + m:OMKA_OFF + m + 1], op1=ALU.add)
                        self.tt("pool", tk[:], tk[:], k_m[:], ALU.mult, [Rtk, Rk], [Rtk])

                    def st5(c_):
                        nb = c_["nb"]
                        (Lc, RLc), (EP, REP), (EN, REN) = nb["Lc"], nb["EP"], nb["EN"]
                        self.act(EP[:], Lc[:], AF.Exp, [RLc], [REP], scale=-EXPH)
                        self.act(EN[:], Lc[:], AF.Exp, [RLc], [REN], scale=EXPH)

                    def st6(c_):
                        m, ms, nb = c_["m"], c_["ms"], c_["nb"]
                        (tk, Rtk) = nb["tk"]
                        r_m, Rr = c_["r"]; v_m, Rv = c_["v"]
                        rk, Rrk = tmpb.next()
                        self.stt(rk[:], r_m[:], vcol("r_k", m), tk[:], ALU.mult, ALU.mult, [Rr, Rvec, Rtk], [Rrk])
                        psb, Rpsb = pbank()
                        self.mm(psb[:, :], bdb[:, :], rk[:], [Rbd, Rrk], [Rpsb])
                        rw_store(BoT[ms, o0:o0 + 512], lambda t_: t_[:],
                                 lambda t_, r_: self.tt("dve", t_[:], psb[:, :], v_m[:], ALU.mult, [Rpsb, Rv], [r_]))
                        psg, Rpsg = pbank()
                        self.mm(psg[:, :], g2s[:, ms], gsb[:], [Rw1, Rgsb], [Rpsg])
                        rw_store(GrT[ms, o0:o0 + 512], lambda t_: t_[:],
                                 lambda t_, r_: self.cp("act", t_[:], psg[:, :], [Rpsg], [r_]))

                    def st7(c_):
                        m, ms, nb = c_["m"], c_["ms"], c_["nb"]
                        (av, Rav), (Lx, RLx), (kk, Rkk), (tk, Rtk), (EP, REP), (EN, REN) = nb["av"], nb["Lx"], nb["kk"], nb["tk"], nb["EP"], nb["EN"]
                        r_m, Rr = c_["r"]; v_m, Rv = c_["v"]
                        ARv = ARt[ms, t * 4:(t + 1) * 4, :]

                        def a_tilde(t_, r_):
                            self.stt(t_[:, 1:512], kk[:, 1:512], -1.0, EP[:, 0:511], ALU.mult, ALU.mult, [Rkk, REP], [r_])
                            self.ts("pool", t_[:].rearrange("p (c t) -> p c t", t=64)[:, :, 0:1],
                                    kk[:].rearrange("p (c t) -> p c t", t=64)[:, :, 0:1], -1.0, ALU.mult, [Rkk], [r_])

                        rw_store(ARv[:, :, 0:128], lambda t_: t_[:].rearrange("p (b t) -> p b t", t=128), a_tilde)
                        rw_store(ARv[:, :, 128:256], lambda t_: t_[:].rearrange("p (b t) -> p b t", t=128),
                                 lambda t_, r_: self.tt("pool", t_[:], r_m[:], EP[:], ALU.mult, [Rr, REP], [r_]))
                        self.tt("pool", Lx[:], kk[:], av[:], ALU.mult, [Rkk, Rav], [RLx])
                        rw_store(Bt[ms, c0:c0 + 512], lambda t_: t_[:],
                                 lambda t_, r_: self.tt("dve", t_[:], Lx[:], EN[:], ALU.mult, [RLx, REN], [r_]))
                        rw_store(Kt[ms, c0:c0 + 512], lambda t_: t_[:],
                                 lambda t_, r_: self.tt("pool", t_[:], tk[:], EN[:], ALU.mult, [Rtk, REN], [r_]))
                        rw_store(Vt[ms, c0:c0 + 512], lambda t_: t_[:],
                                 lambda t_, r_: self.cp("act", t_[:], v_m[:], [Rv], [r_]))
                        pc, Rpc = pcst.next()
                        pk = (pcst.i - 1) % 4
                        self.cp("pool", pc[:], EP[:].rearrange("p (c t) -> p c t", t=64)[:, :, 63], [REP], [Rpc])
                        S.dma(pcch[pk], [(PCt[ms, t * 8:(t + 1) * 8], pc[:])], reads=[Rpc], writes=[R_rw])

                    for pr in range(2):
                        cs_ = [{"m": pr * 2}, {"m": pr * 2 + 1}]
                        for stg in (st1, st2, st3, st4, st5):
                            for c_ in cs_:
                                stg(c_)
                        if own:
                            for c_ in cs_:
                                st6(c_)
                        for c_ in cs_:
                            st7(c_)
                S.barrier()
                S.emit()
                for n_ in S.ENGS:
                    S.e[n_].ops = []

            with ExitStack() as st:
                if self.upto < 3:
                    raise _Stop()
                for i in range(8):
                    S.dma(S.misc("pool"), [(wupb[i * 128:(i + 1) * 128, :], wup_d[i * 128:(i + 1) * 128, :])], writes=[R_wconv], q="pool")
                for i in range(8):
                    S.dma(S.misc("pool"), [(wdnb[i * 512:(i + 1) * 512, :], wdn_d[i * 512:(i + 1) * 512, :])], writes=[R_wconv], q="pool")
                arl = self.ring(st, "arl", [128, 4, 256], BF16, 2)
                btl = self.ring(st, "btl", [128, 512], BF16, 2)
                ktl = self.ring(st, "ktl", [128, 512], BF16, 2)
                vtl = self.ring(st, "vtl", [128, 512], BF16, 2)
                pcl = self.ring(st, "pcl", [128, 8], F32, 2)
                bol = self.ring(st, "bol", [128, 512], BF16, 2)
                grl = self.ring(st, "grl", [128, 512], BF16, 2)
                ldch = [S.chan(), S.chan()]
                MT = self.ring(st, "MT", [128, 2, 512], BF16, 8)
                Lr = self.ring(st, "Lr", [128, 2, 128], BF16, 12)
                ASr = self.ring(st, "ASr", [128, 2, 256], BF16, 12)
                TM = self.ring(st, "TM", [128, 4, 128], BF16, 8)
                Xb = self.ring(st, "Xb", [128, 2, 128], BF16, 8)
                TXr = self.ring(st, "TXr", [128, 2, 128], BF16, 8)
                RqT = self.ring(st, "RqT", [128, 128], BF16, 8)
                Y0 = self.ring(st, "Y0", [128, 2, 64], F32, 8)
                GTr = self.ring(st, "GTr", [128, 64], BF16, 16)
                Fr = self.ring(st, "Fr", [128, 64], F32, 16)
                Hs = self.ring(st, "Hs", [128, 64], BF16, 3)
                Yr = self.ring(st, "Yr", [128, 2, 64], F32, 4)
                gn = self.ring(st, "gn", [128, 2, 64], F32, 4)
                gs_ = self.ring(st, "gs_", [128, 2], F32, 6)
                yT = self.ring(st, "yT", [128, 512], F32, 2)
                yst = self.ring(st, "yst", [128, 512], BF16, 2)
                ych = [S.chan(), S.chan()]
                pi2 = [0]

                def pb2():
                    b_ = banks[pi2[0] % 7]
                    pi2[0] += 1
                    return b_

                e2 = lambda ap: ap.rearrange("p (e s) -> p e s", e=2)
                idb3 = identb[:].unsqueeze(1).broadcast_to([128, 2, 128])
                NBK = 4
                for m in range(4):
                    ms = slice(m * 128, (m + 1) * 128)
                    H, RH = Hs.next()
                    self.memset("pool", H[:], 0.0, [RH])
                    for t in range(NT):
                        own = t >= NTP
                        o0 = (t - NTP) * 512
                        c0 = t * 512
                        ar, Rar = arl.next(); bt, Rbt = btl.next(); kt, Rkt = ktl.next(); vt, Rvt = vtl.next(); pc, Rpc = pcl.next()
                        lk = (arl.i - 1) % 2
                        pairs = [(ar[:], ARt[ms, t * 4:(t + 1) * 4, :]), (bt[:], Bt[ms, c0:c0 + 512]), (kt[:], Kt[ms, c0:c0 + 512]),
                                 (vt[:], Vt[ms, c0:c0 + 512]), (pc[:], PCt[ms, t * 8:(t + 1) * 8])]
                        wr = [Rar, Rbt, Rkt, Rvt, Rpc]
                        if own:
                            bo, Rbo = bol.next(); gr, Rgr = grl.next()
                            pairs += [(bo[:], BoT[ms, o0:o0 + 512]), (gr[:], GrT[ms, o0:o0 + 512])]
                            wr += [Rbo, Rgr]
                            yt_, Ryt = yT.next()
                        S.dma(ldch[lk], pairs, reads=[R_rw], writes=wr)
                        X = [dict() for _ in range(NBK)]
                        for b in range(NBK):
                            c_ = X[b]
                            bs = slice(b * 128, (b + 1) * 128)
                            c_["bs"] = bs
                            mt, Rmt = MT.next()
                            L0, RL0 = Lr.next()
                            for e in range(2):
                                hs = slice(e * 64, (e + 1) * 64)
                                ps, Rps = pb2()
                                self.mm(ps[:, 0:256], bt[hs, bs], ar[hs, b, :], [Rbt, Rar], [Rps])
                                self.mm(ps[:, 256:512], kt[hs, bs], ar[hs, b, :], [Rkt, Rar], [Rps])
                                self.tt("dve", mt[:, e, :], ps[:, :], mask4[:], ALU.mult, [Rps, Rm4], [Rmt])
                                psL, RpsL = pb2()
                                self.mm(psL[:, 0:128], ar[hs, b, 0:128], bt[hs, bs], [Rar, Rbt], [RpsL])
                                self.tt("dve", L0[:, e, :], psL[:, 0:128], maskL[:], ALU.mult, [RpsL, RmL], [RL0])
                            c_["mt"], c_["Rmt"], c_["L"], c_["RL"] = mt, Rmt, L0, RL0
                        for b in range(NBK):
                            c_ = X[b]
                            bs = c_["bs"]
                            pst, Rpst = bankb
                            o_ = (b % 2) * 512
                            self.tr(pst[:, o_ + 0:o_ + 128], ar[:, b, 0:128], identb[:], [Rar, Ridb], [Rpst])
                            self.tr(pst[:, o_ + 128:o_ + 256], bt[:, bs], identb[:], [Rbt, Ridb], [Rpst])
                            self.tr(pst[:, o_ + 256:o_ + 384], kt[:, bs], identb[:], [Rkt, Ridb], [Rpst])
                            self.tr(pst[:, o_ + 384:o_ + 512], vt[:, bs], identb[:], [Rvt, Ridb], [Rpst])
                            tm, Rtm = TM.next()
                            self.cp("act", tm[:], pst[:, o_:o_ + 512].rearrange("p (q f) -> p q f", q=4), [Rpst], [Rtm])
                            c_["tm"], c_["Rtm"] = tm, Rtm
                        for b in range(NBK):
                            c_ = X[b]
                            mt, Rmt = c_["mt"], c_["Rmt"]
                            AS, RAS = ASr.next()
                            self.cp("pool", AS[:, :, 0:128], mt[:, :, 0:128], [Rmt], [RAS])
                            self.tt("pool", AS[:, :, 128:256], mt[:, :, 0:128], idb3, ALU.add, [Rmt, Ridb], [RAS])
                            c_["AS"], c_["RAS"] = AS, RAS
                        for b in range(NBK):
                            c_ = X[b]
                            AS, RAS, Lk, RLk = c_["AS"], c_["RAS"], c_["L"], c_["RL"]
                            psa, Rpsa = pb2()
                            psl, Rpsl = pb2()
                            for e in range(2):
                                self.mm(psa[:, e * 128:(e + 1) * 128], Lk[:, e, :], AS[:, e, 0:128], [RLk, RAS], [Rpsa])
                                self.mm(psl[:, e * 128:(e + 1) * 128], AS[:, e, 0:128], Lk[:, e, :], [RAS, RLk], [Rpsl])
                            ASn, RASn = ASr.next()
                            self.cp("dve", ASn[:, :, 0:128], e2(psa[:, 0:256]), [Rpsa], [RASn])
                            self.cp("pool", ASn[:, :, 128:256], AS[:, :, 128:256], [RAS], [RASn])
                            Ln, RLn = Lr.next()
                            self.cp("act", Ln[:], e2(psl[:, 0:256]), [Rpsl], [RLn])
                            c_["AS"], c_["RAS"], c_["L"], c_["RL"] = ASn, RASn, Ln, RLn
                        for it in range(5):
                            last = it == 4
                            w_ = 128 if last else 0
                            for b in range(NBK):
                                c_ = X[b]
                                AS, RAS, Lk, RLk = c_["AS"], c_["RAS"], c_["L"], c_["RL"]
                                psm, Rpsm = pb2()
                                for e in range(2):
                                    self.mm(psm[:, e * 256 + w_:(e + 1) * 256], Lk[:, e, :], AS[:, e, w_:256], [RLk, RAS], [Rpsm])
                                ASn, RASn = ASr.next()
                                pm3 = e2(psm[:, 0:512])
                                self.tt("dve", ASn[:, :, 128:256], pm3[:, :, 128:256], AS[:, :, 128:256], ALU.add, [Rpsm, RAS], [RASn])
                                if not last:
                                    self.cp("act", ASn[:, :, 0:128], pm3[:, :, 0:128], [Rpsm], [RASn])
                                    psl, Rpsl = pb2()
                                    for e in range(2):
                                        self.mm(psl[:, e * 128:(e + 1) * 128], AS[:, e, 0:128], Lk[:, e, :], [RAS, RLk], [Rpsl])
                                    Ln, RLn = Lr.next()
                                    self.cp("act", Ln[:], e2(psl[:, 0:256]), [Rpsl], [RLn])
                                    c_["L"], c_["RL"] = Ln, RLn
                                c_["AS"], c_["RAS"] = ASn, RASn
                        for b in range(NBK):
                            c_ = X[b]
                            mt, Rmt, tm, Rtm = c_["mt"], c_["Rmt"], c_["tm"], c_["Rtm"]
                            xb, Rxb = Xb.next()
                            psv, Rpsv = pb2()
                            for e in range(2):
                                self.mm(psv[:, e * 64:(e + 1) * 64], mt[:, e, 256:384], tm[:, 3, e * 64:(e + 1) * 64], [Rmt, Rtm], [Rpsv])
                            self.cp("act", xb[:, :, 64:128], psv[:, 0:128].rearrange("p (e v) -> p e v", e=2), [Rpsv], [Rxb])
                            self.cp("pool", xb[:, :, 0:64], tm[:, 0, :].rearrange("p (e v) -> p e v", e=2), [Rtm], [Rxb])
                            c_["xb"], c_["Rxb"] = xb, Rxb
                        for b in range(NBK):
                            c_ = X[b]
                            AS, RAS, xb, Rxb = c_["AS"], c_["RAS"], c_["xb"], c_["Rxb"]
                            pstx, Rpstx = pb2()
                            for e in range(2):
                                self.mm(pstx[:, e * 128:(e + 1) * 128], AS[:, e, 128:256], xb[:, e, :], [RAS, Rxb], [Rpstx])
                            tx, Rtx = TXr.next()
                            self.cp("act", tx[:], e2(pstx[:, 0:256]), [Rpstx], [Rtx])
                            c_["tx"], c_["Rtx"] = tx, Rtx
                        if own:
                            for b in range(NBK):
                                c_ = X[b]
                                mt, Rmt, tm, Rtm, tx, Rtx = c_["mt"], c_["Rmt"], c_["tm"], c_["Rtm"], c_["tx"], c_["Rtx"]
                                psr, Rpsr = pb2()
                                for e in range(2):
                                    self.mm(psr[e * 64:(e + 1) * 64, 0:128], tx[:, e, 0:64], mt[:, e, 128:256], [Rtx, Rmt], [Rpsr])
                                rq, Rrq = RqT.next()
                                self.tt("dve", rq[:], psr[:, 0:128], ar[:, b, 128:256], ALU.add, [Rpsr, Rar], [Rrq])
                                psy, Rpsy = pb2()
                                for e in range(2):
                                    self.mm(psy[:, e * 64:(e + 1) * 64], mt[:, e, 128:256], tx[:, e, 64:128], [Rmt, Rtx], [Rpsy], start=True, stop=False)
                                    self.mm(psy[:, e * 64:(e + 1) * 64], mt[:, e, 384:512], tm[:, 3, e * 64:(e + 1) * 64], [Rmt, Rtm], [Rpsy], start=False, stop=True)
                                y0, Ry0 = Y0.next()
                                self.cp("act", y0[:], psy[:, 0:128].rearrange("p (e v) -> p e v", e=2), [Rpsy], [Ry0])
                                c_["rq"], c_["Rrq"], c_["y0"], c_["Ry0"] = rq, Rrq, y0, Ry0
                        for b in range(NBK):
                            c_ = X[b]
                            tm, Rtm, tx, Rtx = c_["tm"], c_["Rtm"], c_["tx"], c_["Rtx"]
                            c_["gt"], c_["ff"] = [], []
                            for c in range(2):
                                cs = slice(c * 64, (c + 1) * 64)
                                psg, Rpsg = pb2()
                                for e in range(2):
                                    self.mm(psg[e * 64:(e + 1) * 64, 0:64], tx[cs, e, 0:64], tm[cs, 1, e * 64:(e + 1) * 64], [Rtx, Rtm], [Rpsg])
                                gt, Rgt = GTr.next()
                                self.tt("dve", gt[0:64, :], psg[0:64, 0:64], ident[0:64, 0:64], ALU.add, [Rpsg, Rid], [Rgt])
                                self.tt("dve", gt[64:128, :], psg[64:128, 0:64], ident[64:128, 64:128], ALU.add, [Rpsg, Rid], [Rgt])
                                psf, Rpsf = pb2()
                                for e in range(2):
                                    es = slice(e * 64, (e + 1) * 64)
                                    self.mm(psf[es, 0:64], tm[cs, 1, es], tx[cs, e, 64:128], [Rtm, Rtx], [Rpsf], start=True, stop=False)
                                    self.mm(psf[es, 0:64], tm[cs, 2, es], tm[cs, 3, es], [Rtm], [Rpsf], start=False, stop=True)
                                ff, Rff = Fr.next()
                                pcc = pc[:, b * 2 + c:b * 2 + c + 1]
                                self.act(ff[:], psf[:, 0:64], AF.Copy, [Rpsf, Rpc], [Rff], scale=pcc)
                                c_["gt"].append((gt, Rgt))
                                c_["ff"].append((ff, Rff))
                        for b in range(NBK):
                            c_ = X[b]
                            bs = c_["bs"]
                            if own:
                                yy, Ryy = Yr.next()
                                rq, Rrq, y0, Ry0 = c_["rq"], c_["Rrq"], c_["y0"], c_["Ry0"]
                            for c in range(2):
                                cs = slice(c * 64, (c + 1) * 64)
                                gt, Rgt = c_["gt"][c]
                                ff, Rff = c_["ff"][c]
                                if own:
                                    for e in range(2):
                                        es = slice(e * 64, (e + 1) * 64)
                                        psq, Rpsq = pb2()
                                        self.mm(psq[cs, 0:64], rq[es, cs], H[es, :], [Rrq, RH], [Rpsq])
                                        self.tt("dve", yy[cs, e, :], psq[cs, 0:64], y0[cs, e, :], ALU.add, [Rpsq, Ry0], [Ryy])
                                Hn, RHn = Hs.next()
                                for e in range(2):
                                    es = slice(e * 64, (e + 1) * 64)
                                    psh, Rpsh = pb2()
                                    self.mm(psh[es, 0:64], gt[es, :], H[es, :], [Rgt, RH], [Rpsh])
                                    self.stt(Hn[es, :], psh[es, 0:64], pc[es, b * 2 + c:b * 2 + c + 1], ff[es, :], ALU.mult, ALU.add, [Rpsh, Rpc, Rff], [RHn])
                                H, RH = Hn, RHn
                            if own:
                                s1, Rs1 = gs_.next()
                                S.op("dve", lambda e, o=s1, i=yy: e.tensor_reduce(out=o[:], in_=i[:], axis=AX.X, op=ALU.add), [Ryy], [Rs1])
                                self.ts("dve", s1[:], s1[:], -1.0 / 64, ALU.mult, [Rs1], [Rs1])
                                yc, Ryc = gn.next()
                                self.tt("dve", yc[:], yy[:], s1[:].unsqueeze(2).broadcast_to([128, 2, 64]), ALU.add, [Ryy, Rs1], [Ryc])
                                y2, Ry2 = gn.next()
                                self.tt("pool", y2[:], yc[:], yc[:], ALU.mult, [Ryc], [Ry2])
                                s2, Rs2 = gs_.next()
                                S.op("dve", lambda e, o=s2, i=y2: e.tensor_reduce(out=o[:], in_=i[:], axis=AX.X, op=ALU.add), [Ry2], [Rs2])
                                self.act(s2[:], s2[:], AF.Sqrt, [Rs2], [Rs2], bias=GN_EPS, scale=1.0 / 64)
                                self.recip(s2[:], s2[:], [Rs2], [Rs2])
                                self.tt("dve", yc[:], yc[:], s2[:].unsqueeze(2).broadcast_to([128, 2, 64]), ALU.mult, [Ryc, Rs2], [Ryc])
                                pstt, Rpstt = pb2()
                                self.tr(pstt[:, 0:128], yc[:].rearrange("p e v -> p (e v)"), ident[:], [Ryc, Rid], [Rpstt])
                                self.act(yt_[:, bs], pstt[:, 0:128], AF.Identity, [Rpstt, Rvec], [Ryt], bias=vcol("ln_b", m), scale=vcol("ln_w", m))
                        if own:
                            self.tt("pool", yt_[:], yt_[:], bo[:], ALU.add, [Ryt, Rbo], [Ryt])
                            ys, Rys = yst.next()
                            yk = (yst.i - 1) % 2
                            self.tt("pool", ys[:], yt_[:], gr[:], ALU.mult, [Ryt, Rgr], [Rys])
                            S.dma(ych[yk], [(YbT[ms, o0:o0 + 512], ys[:])], reads=[Rys], writes=[R_YbT])
                S.barrier()
                S.emit()
                for n_ in S.ENGS:
                    S.e[n_].ops = []

            with ExitStack() as st:
                if self.upto < 4:
                    raise _Stop()
                Vall = self.sb(st, "Vall", [128, NB, 520], BF16); RVall = Res()
                nv = max(1, NB // 16)
                for i in range(0, NB, 16):
                    j = min(NB, i + 16)
                    S.dma(S.misc(), [(Vall[:, i:j, :], Vs[:, i:j, :])], reads=[R_Vs], writes=[RVall])
                Kh = self.ring(st, "Kh", [128, TT], BF16, 2)
                Qh = self.ring(st, "Qh", [128, TO], BF16, 2)
                kqch = [S.chan(), S.chan()]
                PT = self.ring(st, "PT", [128, 512], BF16, 6)
                osb = self.ring(st, "osb", [128, 512], F32, 2)
                rl = self.ring(st, "rl", [128, 512], F32, 2)
                ost = self.ring(st, "ost", [128, 512], BF16, 2)
                och = [S.chan(), S.chan()]
                sbanks = Ring(banks[0:4])
                obanks = Ring(banks[4:6])
                bbanks = Ring(banks[6:7])
                V5 = Vall[:].rearrange("p b (h d) -> p b h d", d=65)
                LOOK = 2
                heads = []

                def load_head(hd):
                    K_, RK_ = Kh.next(); Q_, RQ_ = Qh.next()
                    hk = (Kh.i - 1) % 2
                    S.dma(kqch[hk], [(K_[0:64, :], KnT[hd * 64:(hd + 1) * 64, :]), (K_[64:97, :], KpeT[:, :]), (Q_[0:97, :], QT[hd, :, :])],
                          reads=[R_KnT, R_KpeT, R_QT], writes=[RK_, RQ_])
                    return (K_, RK_, Q_, RQ_)

                pend = []

                def do_pv(item):
                    (hd, qt, kb, nkb, cst, pt, Rpt, po, Rpo) = item
                    self.mm(po[0:65, cst:512], V5[:, kb, hd, :], pt[:, cst:512], [RVall, Rpt], [Rpo], start=(kb == 0), stop=(kb == nkb - 1), inc=True)
                    if kb == nkb - 1:
                        q0 = qt * 512
                        o_, Ro_ = osb.next()
                        self.cp("act", o_[0:65, :], po[0:65, :], [Rpo], [Ro_])
                        r_, Rr_ = rl.next()
                        self.recip(r_[64:65, :], o_[64:65, :], [Ro_], [Rr_])
                        pb_, Rpb_ = bbanks.next()
                        self.mm(pb_[0:64, :], onesf[64:65, 0:64], r_[64:65, :], [Ronesf, Rr_], [Rpb_])
                        os_, Ros_ = ost.next()
                        ok = (ost.i - 1) % 2
                        self.tt("dve", os_[0:64, :], pb_[0:64, :], o_[0:64, :], ALU.mult, [Rpb_, Ro_], [Ros_])
                        S.dma(och[ok], [(OaT[hd * 64:(hd + 1) * 64, q0:q0 + 512], os_[0:64, :])], reads=[Ros_], writes=[R_OaT])

                nxt = load_head(0)
                for hd in range(NH):
                    K_, RK_, Q_, RQ_ = nxt
                    if hd + 1 < NH:
                        nxt = load_head(hd + 1)
                    for qt in range(NTO):
                        q0 = qt * 512
                        nkb = (TP + q0 + 512) // 128
                        po, Rpo = obanks.next()
                        for kb in range(nkb):
                            jd = kb - (nkb - 4)
                            cst = 0 if jd < 0 else jd * 128
                            ps, Rps = sbanks.next()
                            self.mm(ps[:, cst:512], K_[0:97, kb * 128:(kb + 1) * 128], Q_[0:97, q0 + cst:q0 + 512], [RK_, RQ_], [Rps])
                            pt, Rpt = PT.next()
                            self.act(pt[:, cst:512], ps[:, cst:512], AF.Exp, [Rps], [Rpt])
                            if jd >= 0:
                                self.tt("pool", pt[:, cst:cst + 128], pt[:, cst:cst + 128], trim[:], ALU.mult, [Rpt, Rtri], [Rpt])
                            pend.append((hd, qt, kb, nkb, cst, pt, Rpt, po, Rpo))
                            if len(pend) > LOOK:
                                do_pv(pend.pop(0))
                while pend:
                    do_pv(pend.pop(0))
                S.barrier()
                S.emit()
                for n_ in S.ENGS:
                    S.e[n_].ops = []

            with ExitStack() as st:
                if self.upto < 5:
                    raise _Stop()
                womla = self.sb(st, "womla", [128, 4, D], BF16)
                worw = self.sb(st, "worw", [128, 4, D], BF16)
                wout = self.sb(st, "wout", [128, 8, D], BF16)
                wpg = self.sb(st, "wpg", [128, 8, D], BF16)
                wpp = self.sb(st, "wpp", [128, 2, D], BF16)
                Rw4 = Res("w4")
                S.dma(S.misc("pool"), [(womla[:], womla_d.rearrange("(c p) n -> p c n", p=128)),
                                 (worw[:], worw_d.rearrange("(c p) n -> p c n", p=128)),
                                 (wpp[:], wpp_d.rearrange("(c p) n -> p c n", p=128))], writes=[Rw4], q="pool")
                wout3 = wout_d.rearrange("(c p) n -> p c n", p=128)
                wpg3 = wpg_d.rearrange("(c p) n -> p c n", p=128)
                for c in range(8):
                    S.dma(S.misc("pool"), [(wout[:, c, :], wout3[:, c, :]), (wpg[:, c, :], wpg3[:, c, :])], writes=[Rw4], q="pool")
                wupr = self.ring(st, "wupr", [128, 8, 256], BF16, 3)
                wupc = [S.chan() for _ in range(3)]
                wdnr = self.ring(st, "wdnr", [128, 2, 512], BF16, 4)
                wdnc = [S.chan() for _ in range(4)]
                wup3 = wupb.rearrange("(c p) n -> p c n", p=128)
                wdn3 = wdnb.rearrange("(c p) n -> p c n", p=128)
                x4 = self.ring(st, "x4_", [128, 8, 512], F32, 1)
                xc4 = S.chan()
                inb = self.ring(st, "inb", [128, 4, 512], BF16, 2)
                inc_ = [S.chan(), S.chan()]
                gtl = self.ring(st, "gtl", [128, 8, 512], BF16, 2)
                gtc = [S.chan(), S.chan()]
                pin = self.sb(st, "pin", [128, 2, 512], BF16); Rpin = Res()
                pinc = S.chan()
                mix = self.sb(st, "mix", [128, 8, 512], BF16); Rmix = Res()
                mtmp = self.ring(st, "mtmp", [128, 512], F32, 3)
                sq4 = self.ring(st, "sq4", [128, 512], BF16, 4)
                h4 = self.sb(st, "h4", [128, 8, 512], BF16); Rh4 = Res()
                hid = self.sb(st, "hid", [128, 16, 512], BF16)
                Rhid = [Res() for _ in range(16)]
                rel = self.ring(st, "rel", [128, 512], F32, 3)
                rs4 = self.ring(st, "rs4", [128, 512], F32, 2)
                t4 = self.ring(st, "t4", [128, 512], F32, 2)
                gsg = self.ring(st, "gsg", [128, 512], F32, 2)
                osb4 = self.ring(st, "osb4", [128, 512], F32, 2)
                och4 = [S.chan(), S.chan()]
                pi4 = [0]

                def pb4():
                    b_ = banks[pi4[0] % 7]
                    pi4[0] += 1
                    return b_

                def rms4(xt, Rxt):
                    ps, Rps = pb4()
                    for c in range(8):
                        sq, Rsq = sq4.next()
                        self.act(sq[:], xt[:, c, :], AF.Square, [Rxt], [Rsq])
                        self.mm(ps[:, :], onesb[:, :], sq[:], [Rones, Rsq], [Rps], start=(c == 0), stop=(c == 7), inc=True)
                    t1, Rt1 = t4.next()
                    self.act(t1[:], ps[:, :], AF.Ln, [Rps, Reps], [Rt1], bias=epsc[:, 0:1], scale=1.0 / D)
                    rs, Rrs = rs4.next()
                    self.act(rs[:], t1[:], AF.Exp, [Rt1], [Rrs], scale=-0.5)
                    return rs, Rrs

                for to in range(NTO):
                    o0 = to * 512
                    xt, Rxt = x4.next()
                    S.dma(xc4, [(xt[:], xT3[:, :, TP + o0:TP + o0 + 512])], writes=[Rxt])
                    oa, Roa = inb.next()
                    S.dma(inc_[0], [(oa[:], OaT[:, o0:o0 + 512].rearrange("(c p) t -> p c t", p=128))], reads=[R_OaT], writes=[Roa])
                    yb, Ryb = inb.next()
                    S.dma(inc_[1], [(yb[:], YbT[:, o0:o0 + 512].rearrange("(c p) t -> p c t", p=128))], reads=[R_YbT], writes=[Ryb])
                    ga, Rga = gtl.next()
                    S.dma(gtc[0], [(ga[:], GT[0:1024, o0:o0 + 512].rearrange("(c p) t -> p c t", p=128))], reads=[R_GT], writes=[Rga])
                    gb, Rgb = gtl.next()
                    S.dma(gtc[1], [(gb[:], GT[1024:2048, o0:o0 + 512].rearrange("(c p) t -> p c t", p=128))], reads=[R_GT], writes=[Rgb])
                    S.dma(pinc, [(pin[:], pT[:, o0:o0 + 512].rearrange("(c p) t -> p c t", p=128))], writes=[Rpin], q="pool")
                    for mt_ in range(8):
                        msl = slice(mt_ * 128, (mt_ + 1) * 128)
                        psa, Rpsa = pb4()
                        for c in range(4):
                            self.mm(psa[:, :], womla[:, c, msl], oa[:, c, :], [Rw4, Roa], [Rpsa], start=(c == 0), stop=(c == 3))
                        psb, Rpsb = pb4()
                        for c in range(4):
                            self.mm(psb[:, :], worw[:, c, msl], yb[:, c, :], [Rw4, Ryb], [Rpsb], start=(c == 0), stop=(c == 3))
                        m1, Rm1 = mtmp.next()
                        self.tt("dve", m1[:], psa[:, :], ga[:, mt_, :], ALU.mult, [Rpsa, Rga], [Rm1])
                        m2, Rm2 = mtmp.next()
                        self.tt("dve", m2[:], psb[:, :], gb[:, mt_, :], ALU.mult, [Rpsb, Rgb], [Rm2])
                        self.tt("pool", mix[:, mt_, :], m1[:], m2[:], ALU.add, [Rm1, Rm2], [Rmix])
                    for mt_ in range(8):
                        msl = slice(mt_ * 128, (mt_ + 1) * 128)
                        ps, Rps = pb4()
                        for c in range(8):
                            self.mm(ps[:, :], wout[:, c, msl], mix[:, c, :], [Rw4, Rmix], [Rps], start=(c == 0), stop=(c == 7))
                        self.tt("dve", xt[:, mt_, :], ps[:, :], xt[:, mt_, :], ALU.add, [Rps, Rxt], [Rxt])
                    rs, Rrs = rms4(xt, Rxt)
                    for c in range(8):
                        self.stt(h4[:, c, :], xt[:, c, :], vcol("g_ffn", c), rs[:], ALU.mult, ALU.mult, [Rxt, Rvec, Rrs], [Rh4])
                    for hh in range(2):
                        for uc in range(8):
                            wu, Rwu = wupr.next()
                            uk = (wupr.i - 1) % 3
                            col = hh * 2048 + uc * 256
                            S.dma(wupc[uk], [(wu[:], wup3[:, :, col:col + 256])], reads=[R_wconv], writes=[Rwu])
                            for j in range(2):
                                mi = uc * 2 + j
                                ps, Rps = pb4()
                                for c in range(8):
                                    self.mm(ps[:, :], wu[:, c, j * 128:(j + 1) * 128], h4[:, c, :], [Rwu, Rh4], [Rps], start=(c == 0), stop=(c == 7))
                                rl_, Rrl = rel.next()
                                self.act(rl_[:], ps[:, :], AF.Relu, [Rps], [Rrl])
                                self.tt("pool", hid[:, mi, :], rl_[:], rl_[:], ALU.mult, [Rrl], [Rhid[mi]])
                        for oh in range(2):
                            pss = [pb4() for _ in range(4)]
                            for kc in range(8):
                                wd, Rwd = wdnr.next()
                                dk = (wdnr.i - 1) % 4
                                kr = hh * 16 + kc * 2
                                S.dma(wdnc[dk], [(wd[:], wdn3[:, kr:kr + 2, oh * 512:(oh + 1) * 512])], reads=[R_wconv], writes=[Rwd])
                                for j in range(2):
                                    kci = kc * 2 + j
                                    for mq in range(4):
                                        ps, Rps = pss[mq]
                                        self.mm(ps[:, :], wd[:, j, mq * 128:(mq + 1) * 128], hid[:, kci, :], [Rwd, Rhid[kci]], [Rps],
                                                start=(kci == 0), stop=(kci == 15), inc=(kci == 15 or (j == 1 and mq == 3)))
                            for mq in range(4):
                                mt_ = oh * 4 + mq
                                ps, Rps = pss[mq]
                                self.tt("dve", xt[:, mt_, :], ps[:, :], xt[:, mt_, :], ALU.add, [Rps, Rxt], [Rxt])
                    rs, Rrs = rms4(xt, Rxt)
                    for c in range(8):
                        self.stt(h4[:, c, :], xt[:, c, :], vcol("g_ple", c), rs[:], ALU.mult, ALU.mult, [Rxt, Rvec, Rrs], [Rh4])
                    for mt_ in range(8):
                        msl = slice(mt_ * 128, (mt_ + 1) * 128)
                        ps, Rps = pb4()
                        for c in range(8):
                            self.mm(ps[:, :], wpg[:, c, msl], h4[:, c, :], [Rw4, Rh4], [Rps], start=(c == 0), stop=(c == 7))
                        gg, Rgg = gsg.next()
                        self.act(gg[:], ps[:, :], AF.Sigmoid, [Rps], [Rgg])
                        ps2, Rps2 = pb4()
                        for c in range(2):
                            self.mm(ps2[:, :], wpp[:, c, msl], pin[:, c, :], [Rw4, Rpin], [Rps2], start=(c == 0), stop=(c == 1))
                        self.tt("dve", gg[:], ps2[:, :], gg[:], ALU.mult, [Rps2, Rgg], [Rgg])
                        self.tt("pool", xt[:, mt_, :], xt[:, mt_, :], gg[:], ALU.add, [Rxt, Rgg], [Rxt])
                    rs, Rrs = rms4(xt, Rxt)
                    for c in range(8):
                        ob, Rob = osb4.next()
                        okk = (osb4.i - 1) % 2
                        self.stt(ob[:], xt[:, c, :], vcol("g_final", c), rs[:], ALU.mult, ALU.mult, [Rxt, Rvec, Rrs], [Rob])
                        S.dma(och4[okk], [(outT[c * 128:(c + 1) * 128, o0:o0 + 512], ob[:])], reads=[Rob])
                S.barrier()
                S.emit()
        return nc


def const_inputs():
    s = np.arange(128)[:, None]
    t = np.arange(128)[None, :]
    same = (s // 64) == (t // 64)
    m_lt = ((s < t) & same).astype(np.float32)
    m_le = ((s <= t) & same).astype(np.float32)
    mask4 = np.concatenate([m_lt, m_le, m_lt, m_le], axis=1)
    maskL = ((t < s) & same).astype(np.float32)
    tri = (s <= t).astype(np.float32)
    bd = same.astype(np.float32)
    scan = np.ones((128, 512), np.float32)
    scan[:, ::64] = 0.0
    inv_freq = (np.float32(10000.0) ** (-np.arange(16, dtype=np.float32) / np.float32(16))).astype(np.float32)
    ropec = np.zeros((128, 2), np.float32)
    ropec[64:80, 0] = inv_freq
    ropec[80:96, 0] = inv_freq
    ropec[64:80, 1] = -1.0
    ropec[80:96, 1] = 1.0
    return dict(c_ident=np.eye(128, dtype=np.float32), c_mask4=mask4, c_maskL=maskL, c_tri=tri, c_bd=bd, c_scan=scan, ropec=ropec)


def shared_inputs(inp):
    f = lambda a: np.ascontiguousarray(np.asarray(a, dtype=np.float32))
    w_in = f(inp["w_in"][0])
    z64 = np.zeros((D, 64), np.float32)
    kpe = w_in[:, 640:672]
    w_kpe = np.concatenate([z64, kpe, z64, kpe[:, 16:32], kpe[:, 0:16]], axis=1)
    w_uq = f(inp["w_uq"][0]).reshape(384, NH, 96)
    wq_sw = np.zeros_like(w_uq)
    wq_sw[:, :, 64:80] = w_uq[:, :, 80:96]
    wq_sw[:, :, 80:96] = w_uq[:, :, 64:80]
    w_ukv = f(inp["w_ukv"][0]).reshape(256, NH, 128)
    vec = {"g_mix": inp["g_mix"][0], "g_q_a": inp["g_q_a"][0], "g_kv_a": inp["g_kv_a"][0], "mu": inp["mu_rwkv"][0],
           "w0": inp["w0"][0], "a0": inp["a0"][0], "k_k": inp["k_k"][0], "k_a": inp["k_a"][0],
           "r_k": np.asarray(inp["r_k"][0]).reshape(-1), "ln_w": inp["ln_x_w"][0], "ln_b": inp["ln_x_b"][0],
           "g_ffn": inp["g_ffn"][0], "g_ple": inp["g_ple"][0], "g_final": inp["g_final"]}
    vecs = np.zeros((128, NVEC), np.float32)
    for name, n in VEC_LAYOUT:
        v = f(vec[name]).reshape(n, 128)
        vecs[:, VEC_OFF[name]:VEC_OFF[name] + n] = v.T
    d = dict(vecs=vecs, w_in=w_in, w_kpe=np.ascontiguousarray(w_kpe), wq=np.ascontiguousarray(w_uq.reshape(384, 768)),
             wq_sw=np.ascontiguousarray(wq_sw.reshape(384, 768)),
             wukv_k=np.ascontiguousarray(w_ukv[:, :, 0:64].reshape(256, 512)),
             wukv_v=np.ascontiguousarray(w_ukv[:, :, 64:128].reshape(256, 512)),
             w_o_mla=f(inp["w_o_mla"][0]), w2=f(inp["w2"][0]), a2=f(inp["a2"][0]), g2=f(inp["g2"][0]),
             w_o_rwkv=f(inp["w_o_rwkv"][0]), w_out=f(inp["w_out"][0]), w_up=f(inp["w_ffn_up"][0]), w_down=f(inp["w_ffn_down"][0]),
             w_pg=f(inp["w_ple_gate"][0]), w_pp=f(inp["w_ple_proj"][0]))
    d.update(const_inputs())
    return d


def core_inputs(x_b, p_b, pos_b, half, TP, TO):
    TT = TP + TO
    xT = np.zeros((D, TT), np.float32)
    posr = np.zeros((1, TT), np.int32)
    mrow = np.zeros((1, TT), np.float32)
    o0 = half * TP
    if half == 1:
        xT[:, 0:TP] = x_b[0:TP].T
        posr[0, 0:TP] = pos_b[0:TP]
    else:
        mrow[0, 0:TP] = -30000.0
    xT[:, TP:] = x_b[o0:o0 + TO].T
    posr[0, TP:] = pos_b[o0:o0 + TO]
    pT = np.ascontiguousarray(p_b[o0:o0 + TO].T.astype(np.float32))
    return dict(xT=xT, pT=pT, pos=posr, maskrow=mrow.astype(ml_dtypes.bfloat16))


_NC_CACHE = {}


def get_nc(TP, TO):
    if (TP, TO) not in _NC_CACHE:
        _NC_CACHE[(TP, TO)] = B(TP, TO).build()
    return _NC_CACHE[(TP, TO)]


def kernel(**inputs):
    x = np.asarray(inputs["x"], dtype=np.float32)
    p = np.asarray(inputs["p"], dtype=np.float32)[0]
    pos = np.asarray(inputs["positions"]).astype(np.int32)
    Bn, Sq, _ = x.shape
    TP = TO = Sq // 2
    nc = get_nc(TP, TO)
    sh = shared_inputs(inputs)
    in_maps = []
    for c in range(8):
        b, half = c // 2, c % 2
        m = dict(sh)
        m.update(core_inputs(x[b], p[b], pos[b], half, TP, TO))
        in_maps.append(m)
    res = run_bass_kernel_spmd(nc, in_maps, core_ids=list(range(8)))
    out = np.zeros((Bn, Sq, D), np.float32)
    for c in range(8):
        b, half = c // 2, c % 2
        out[b, half * TO:(half + 1) * TO, :] = res.results[c]["outT"].T
    return out
```

```python
from contextlib import ExitStack
import numpy as np
import ml_dtypes
import concourse.bass as bass
import concourse.mybir as mybir
from concourse.bass_utils import run_bass_kernel_spmd

F32 = mybir.dt.float32
BF16 = mybir.dt.bfloat16
I32 = mybir.dt.int32
AF = mybir.ActivationFunctionType
ALU = mybir.AluOpType
AX = mybir.AxisListType

D = 1024
NH = 8
RMS_EPS = 1e-6
GN_EPS = 64 * 1e-5
SCALE = 96 ** -0.5
EXPH = float(np.exp(-0.5))
TWO_PI = 2.0 * np.pi
C1 = 6.28125
C2 = float(TWO_PI - 6.28125)


class Res:
    __slots__ = ("name", "w", "rd")

    def __init__(self, name=""):
        self.name = name
        self.w = None
        self.rd = []


class Chan:
    def __init__(self, sem, name):
        self.sem = sem
        self.count = 0
        self.name = name


class _Eng:
    def __init__(self, name, sem):
        self.name = name
        self.sem = sem
        self.count = 0
        self.ops = []
        self.waited = {}


class Sched:
    ENGS = ("pe", "act", "dve", "pool", "sp")
    HMAP = {"pe": "tensor", "act": "scalar", "dve": "vector", "pool": "gpsimd", "sp": "sync"}

    def __init__(self, nc, stack, n_chan=90):
        self.nc = nc
        self.e = {}
        for n in self.ENGS:
            self.e[n] = _Eng(n, stack.enter_context(nc.semaphore("s_" + n)))
        self.chans = [Chan(stack.enter_context(nc.semaphore("c%d" % i)), "c%d" % i) for i in range(n_chan)]
        self.chan_i = 4
        self.nops = 0
        self.misc_i = 0

    def misc(self, q="sp"):
        base = 0 if q == "sp" else 2
        c = self.chans[base + self.misc_i % 2]
        self.misc_i += 1
        return c

    def chan(self):
        c = self.chans[self.chan_i]
        self.chan_i += 1
        return c

    def _need(self, eng, reads, writes):
        E = self.e[eng]
        need = {}

        def add(t):
            if t is None:
                return
            key, val = t
            if key is E and eng == "pe":
                return
            if need.get(key, 0) < val:
                need[key] = val

        for r in reads:
            add(r.w)
        for w in writes:
            add(w.w)
            for t in w.rd:
                add(t)
        for key, val in need.items():
            if E.waited.get(key, 0) < val:
                E.waited[key] = val
                E.ops.append(("wait", key.sem, val))

    def op(self, eng, fn, reads=(), writes=(), inc=True):
        E = self.e[eng]
        self._need(eng, reads, writes)
        if inc:
            E.count += 1
            t = (E, E.count)
        else:
            t = (E, E.count + 1)
        E.ops.append(("op", fn, inc))
        for r in reads:
            r.rd.append(t)
            if len(r.rd) > 64:
                r.rd = _compress(r.rd)
        for w in writes:
            w.w = t
            w.rd = []
        self.nops += 1
        return t

    def dma(self, chan, pairs, reads=(), writes=(), q="sp"):
        E = self.e[q]
        if chan.count > 0 and E.waited.get(chan, 0) < chan.count:
            E.waited[chan] = chan.count
            E.ops.append(("wait", chan.sem, chan.count))
        self._need(q, reads, writes)
        for (o, i) in pairs:
            chan.count += 16
            E.ops.append(("dma", o, i, chan.sem))
        t = (chan, chan.count)
        for r in reads:
            r.rd.append(t)
        for w in writes:
            w.w = t
            w.rd = []
        return t

    def barrier(self):
        for n in self.ENGS:
            E = self.e[n]
            for m in self.ENGS:
                O = self.e[m]
                if O is E or O.count == 0:
                    continue
                if E.waited.get(O, 0) < O.count:
                    E.waited[O] = O.count
                    E.ops.append(("wait", O.sem, O.count))
            for c in self.chans:
                if c.count and E.waited.get(c, 0) < c.count:
                    E.waited[c] = c.count
                    E.ops.append(("wait", c.sem, c.count))

    def emit(self):
        nc = self.nc
        with nc.Block() as block:
            for n in self.ENGS:
                E = self.e[n]

                def body(h, E=E):
                    for o in E.ops:
                        if o[0] == "wait":
                            h.wait_ge(o[1], o[2])
                        elif o[0] == "op":
                            ins = o[1](h)
                            if o[2]:
                                ins.then_inc(E.sem, 1)
                        else:
                            h.dma_start(out=o[1], in_=o[2]).then_inc(o[3], 16)

                getattr(block, self.HMAP[n])(body)


def _compress(tickets):
    best = {}
    for key, val in tickets:
        if best.get(key, 0) < val:
            best[key] = val
    return list(best.items())


class Ring:
    def __init__(self, items):
        self.items = items
        self.i = 0

    def next(self):
        it = self.items[self.i % len(self.items)]
        self.i += 1
        return it


VEC_LAYOUT = [("g_mix", 8), ("g_q_a", 3), ("g_kv_a", 2), ("mu", 14), ("w0", 4), ("a0", 4), ("k_k", 4),
              ("k_a", 4), ("r_k", 4), ("ln_w", 4), ("ln_b", 4), ("g_ffn", 8), ("g_ple", 8), ("g_final", 8)]
VEC_OFF = {}
_o = 0
for _n, _c in VEC_LAYOUT:
    VEC_OFF[_n] = _o
    _o += _c
NVEC = _o
OM_OFF = NVEC
OMKA_OFF = NVEC + 14
NVEC_TOT = NVEC + 18


class _Stop(Exception):
    pass


class B:
    def __init__(self, TP, TO, debug=False, upto=5, p2_stop=0):
        self.p2_stop = p2_stop
        self.debug = debug
        self.upto = upto
        self.TP, self.TO = TP, TO
        self.TT = TP + TO
        self.nc = bass.Bass("TRN2", target_bir_lowering=False)
        self.st = ExitStack()
        self.S = None

    def din(self, name, shape, dt=F32):
        return self.nc.dram_tensor(name, list(shape), dt, kind="ExternalInput").ap()

    def dscr(self, name, shape, dt=BF16):
        kind = "ExternalOutput" if self.debug else "Internal"
        return self.nc.dram_tensor(name, list(shape), dt, kind=kind).ap()

    def sb(self, st, name, shape, dt=F32):
        self._uid = getattr(self, "_uid", 0) + 1
        return st.enter_context(self.nc.sbuf_tensor("s%d_%s" % (self._uid, name), list(shape), dt))

    def ring(self, st, name, shape, dt, n):
        return Ring([(self.sb(st, "%s%d" % (name, i), shape, dt), Res("%s%d" % (name, i))) for i in range(n)])

    def mm(self, out, lhsT, rhs, reads, writes, start=True, stop=True, inc=None):
        self.S.op("pe", lambda e: e.matmul(out, lhsT, rhs, start=start, stop=stop), reads, writes, inc=(stop if inc is None else inc))

    def tr(self, out, in_, ident, reads, writes, inc=True):
        self.S.op("pe", lambda e: e.transpose(out, in_, ident), reads, writes, inc=inc)

    def act(self, out, in_, func, reads, writes, bias=0.0, scale=1.0):
        if func == AF.Copy and not (isinstance(bias, float) and isinstance(scale, float)):
            func = AF.Identity
        self.S.op("act", lambda e: e.activation(out=out, in_=in_, func=func, bias=bias, scale=scale), reads, writes)

    def tt(self, eng, out, in0, in1, op, reads, writes):
        self.S.op(eng, lambda e: e.tensor_tensor(out=out, in0=in0, in1=in1, op=op), reads, writes)

    def ts(self, eng, out, in0, s1, op0, reads, writes, s2=None, op1=None):
        if op1 is None:
            self.S.op(eng, lambda e: e.tensor_scalar(out=out, in0=in0, scalar1=s1, scalar2=None, op0=op0), reads, writes)
        else:
            self.S.op(eng, lambda e: e.tensor_scalar(out=out, in0=in0, scalar1=s1, scalar2=s2, op0=op0, op1=op1), reads, writes)

    def stt(self, out, in0, scalar, in1, op0, op1, reads, writes):
        self.S.op("dve", lambda e: e.scalar_tensor_tensor(out=out, in0=in0, scalar=scalar, in1=in1, op0=op0, op1=op1), reads, writes)

    def cp(self, eng, out, in_, reads, writes):
        if eng == "act":
            self.act(out, in_, AF.Copy, reads, writes)
        else:
            self.S.op(eng, lambda e: e.tensor_copy(out=out, in_=in_), reads, writes)

    def ckpt(self, n):
        if self.p2_stop == n:
            self.S.barrier()
            self.S.emit()
            raise _Stop()

    def memset(self, eng, ap, val, writes):
        self.S.op(eng, lambda e: e.memset(ap, val), (), writes)

    def recip(self, out, in_, reads, writes):
        self.S.op("dve", lambda e: e.reciprocal(out=out, in_=in_), reads, writes)

    def build(self):
        try:
            self._build()
        except _Stop:
            pass
        return self.nc

    def _build(self):
        nc, TP, TO, TT = self.nc, self.TP, self.TO, self.TT
        NT, NTP, NTO = TT // 512, TP // 512, TO // 512
        NB = TT // 128
        xT = self.din("xT", [D, TT])
        pT = self.din("pT", [256, TO])
        pos = self.din("pos", [1, TT], I32)
        maskrow = self.din("maskrow", [1, TT], BF16)
        vecs_d = self.din("vecs", [128, NVEC])
        rc_d = self.din("ropec", [128, 2])
        w_in = self.din("w_in", [D, 4512])
        w_kpe = self.din("w_kpe", [D, 192])
        wq_d = self.din("wq", [384, 768])
        wqs_d = self.din("wq_sw", [384, 768])
        wkk_d = self.din("wukv_k", [256, 512])
        wkv_d = self.din("wukv_v", [256, 512])
        womla_d = self.din("w_o_mla", [512, D])
        w2_d = self.din("w2", [64, 512])
        a2_d = self.din("a2", [64, 512])
        g2_d = self.din("g2", [128, 512])
        worw_d = self.din("w_o_rwkv", [512, D])
        wout_d = self.din("w_out", [D, D])
        wup_d = self.din("w_up", [D, 4096])
        wdn_d = self.din("w_down", [4096, D])
        wpg_d = self.din("w_pg", [D, D])
        wpp_d = self.din("w_pp", [256, D])
        cm_ident_d = self.din("c_ident", [128, 128])
        cm_mask4_d = self.din("c_mask4", [128, 512])
        cm_maskL_d = self.din("c_maskL", [128, 128])
        cm_tri_d = self.din("c_tri", [128, 128])
        cm_bd_d = self.din("c_bd", [128, 128])
        cm_scan_d = self.din("c_scan", [128, 512])
        outT = nc.dram_tensor("outT", [D, TO], F32, kind="ExternalOutput").ap()
        QT = self.dscr("QT", [NH, 97, TO])
        KnT = self.dscr("KnT", [512, TT])
        KpeT = self.dscr("KpeT", [33, TT])
        Vs = self.dscr("Vs", [128, NB, 520])
        GT = self.dscr("GT", [2048, TO])
        ARt = self.dscr("ARt", [512, NB, 256])
        Bt = self.dscr("Bt", [512, TT])
        Kt = self.dscr("Kt", [512, TT])
        Vt = self.dscr("Vt", [512, TT])
        PCt = self.dscr("PCt", [512, TT // 64], F32)
        GrT = self.dscr("GrT", [512, TO])
        BoT = self.dscr("BoT", [512, TO])
        OaT = self.dscr("OaT", [512, TO])
        YbT = self.dscr("YbT", [512, TO])
        wupb = self.dscr("wupb", [D, 4096])
        wdnb = self.dscr("wdnb", [4096, D])
        R_wconv = Res("wconv")
        R_QT, R_KnT, R_KpeT, R_Vs, R_GT = Res("QT"), Res("KnT"), Res("KpeT"), Res("Vs"), Res("GT")
        R_rw, R_OaT, R_YbT = Res("rwscr"), Res("OaT"), Res("YbT")

        with self.st as st0:
            S = self.S = Sched(nc, st0)
            vecs = self.sb(st0, "vecs", [128, NVEC_TOT]); Rvec = Res("vecs")
            ropec = self.sb(st0, "ropec", [128, 2]); Rrc = Res()
            ident = self.sb(st0, "ident", [128, 128]); Rid = Res()
            identb = self.sb(st0, "identb", [128, 128], BF16); Ridb = Res()
            mask4 = self.sb(st0, "mask4", [128, 512]); Rm4 = Res()
            maskL = self.sb(st0, "maskL", [128, 128]); RmL = Res()
            trim = self.sb(st0, "trim", [128, 128], BF16); Rtri = Res()
            bdb = self.sb(st0, "bdb", [128, 128], BF16); Rbd = Res()
            onesb = self.sb(st0, "onesb", [128, 128], BF16); Rones = Res()
            onesf = self.sb(st0, "onesf", [128, 128]); Ronesf = Res()
            scanm = self.sb(st0, "scanm", [128, 512]); Rscan = Res()
            S.dma(S.misc(), [(vecs[:, 0:NVEC], vecs_d[:, :])], writes=[Rvec])
            S.dma(S.misc(), [(ropec[:], rc_d[:, :])], writes=[Rrc])
            S.dma(S.misc(), [(ident[:], cm_ident_d[:, :])], writes=[Rid])
            S.dma(S.misc("pool"), [(identb[:], cm_ident_d[:, :])], writes=[Ridb], q="pool")
            S.dma(S.misc(), [(mask4[:], cm_mask4_d[:, :])], writes=[Rm4])
            S.dma(S.misc(), [(maskL[:], cm_maskL_d[:, :])], writes=[RmL])
            S.dma(S.misc("pool"), [(trim[:], cm_tri_d[:, :])], writes=[Rtri], q="pool")
            S.dma(S.misc("pool"), [(bdb[:], cm_bd_d[:, :])], writes=[Rbd], q="pool")
            S.dma(S.misc(), [(scanm[:], cm_scan_d[:, :])], writes=[Rscan])
            epsc = self.sb(st0, "epsc", [128, 2]); Reps = Res()
            self.memset("pool", epsc[:, 0:1], RMS_EPS, [Reps])
            self.memset("pool", epsc[:, 1:2], 1e-24, [Reps])
            self.memset("pool", onesb[:], 1.0, [Rones])
            self.memset("pool", onesf[:], 1.0, [Ronesf])
            self.ts("dve", vecs[:, OM_OFF:OM_OFF + 14], vecs[:, VEC_OFF["mu"]:VEC_OFF["mu"] + 14], -1.0, ALU.mult,
                    [Rvec], [Rvec], s2=1.0, op1=ALU.add)
            self.ts("dve", vecs[:, OMKA_OFF:OMKA_OFF + 4], vecs[:, VEC_OFF["k_a"]:VEC_OFF["k_a"] + 4], -1.0, ALU.mult,
                    [Rvec], [Rvec], s2=1.0, op1=ALU.add)
            S.dma(S.misc(), [(KpeT[32:33, :], maskrow[0:1, :])], writes=[R_KpeT])

            def vcol(name, j, p0=0, p1=128):
                o = VEC_OFF[name] + j
                return vecs[p0:p1, o:o + 1]

            banks = [(st0.enter_context(nc.psum_tensor("bank%d" % i, [128, 512], F32)), Res("bank%d" % i)) for i in range(7)]
            bankb = (st0.enter_context(nc.psum_tensor("bankb", [128, 1024], BF16)), Res("bankb"))
            consts = [Rvec, Rrc, Rid, Ridb, Rm4, RmL, Rtri, Rbd, Rones, Ronesf, Rscan]

            xT3 = xT.rearrange("(c p) t -> p c t", p=128)
            w_in3 = w_in.rearrange("(c p) n -> p c n", p=128)
            w_kpe3 = w_kpe.rearrange("(c p) n -> p c n", p=128)
            pi = [0]

            def pbank():
                b_ = banks[pi[0] % 7]
                pi[0] += 1
                return b_

            def make_common(st, ncol):
                cm = {}
                cm["win"] = self.sb(st, "win", [128, 8, ncol], BF16)
                cm["Rwin"] = Res()
                cm["xr"] = self.ring(st, "x1_", [128, 8, 512], F32, 1)
                cm["xch"] = S.chan()
                cm["sqr"] = self.ring(st, "sq1_", [128, 512], BF16, 4)
                cm["hr"] = self.ring(st, "h1_", [128, 8, 512], BF16, 2)
                cm["rstdr"] = self.ring(st, "rstd1_", [128, 512], F32, 2)
                cm["sqt"] = self.ring(st, "sqt1_", [128, 512], F32, 2)
                return cm

            def rms_stats(cm, src3, nchunk, scale, Rsrc):
                ps, Rps = pbank()
                for c in range(nchunk):
                    sq, Rsq = cm["sqr"].next()
                    self.act(sq[:], src3[:, c, :], AF.Square, [Rsrc], [Rsq])
                    self.mm(ps[:, :], onesb[:, :], sq[:], [Rones, Rsq], [Rps], start=(c == 0), stop=(c == nchunk - 1), inc=True)
                t1, Rt1 = cm["sqt"].next()
                self.act(t1[:], ps[:, :], AF.Ln, [Rps, Reps], [Rt1], bias=epsc[:, 0:1], scale=scale)
                rs, Rrs = cm["rstdr"].next()
                self.act(rs[:], t1[:], AF.Exp, [Rt1], [Rrs], scale=-0.5)
                return rs, Rrs

            def load_h(cm, t):
                c0 = t * 512
                xt, Rxt = cm["xr"].next()
                S.dma(cm["xch"], [(xt[:], xT3[:, :, c0:c0 + 512])], writes=[Rxt])
                rs, Rrs = rms_stats(cm, xt, 8, 1.0 / D, Rxt)
                h, Rh = cm["hr"].next()
                for c in range(8):
                    self.stt(h[:, c, :], xt[:, c, :], vcol("g_mix", c), rs[:], ALU.mult, ALU.mult, [Rxt, Rvec, Rrs], [Rh])
                return h, Rh

            def zmm(cm, h, Rh, col0, M, rw=None):
                ps, Rps = pbank()
                Rw_ = cm["Rwin"] if rw is None else rw
                for c in range(8):
                    self.mm(ps[0:M, :], cm["win"][:, c, col0:col0 + M], h[:, c, :], [Rw_, Rh], [Rps], start=(c == 0), stop=(c == 7))
                return ps, Rps

            with ExitStack() as st:
                if self.upto < 1:
                    raise _Stop()
                NCOL = 384 + 256 + 192 + 2048
                OFF_CQ, OFF_CKV, OFF_KPE, OFF_G = 0, 384, 640, 832
                cm = make_common(st, NCOL)
                win, Rwin = cm["win"], cm["Rwin"]
                Rwing = Res("wing")
                for c in range(8):
                    S.dma(S.misc("pool"), [(win[:, c, 0:640], w_in3[:, c, 0:640]),
                                     (win[:, c, 640:832], w_kpe3[:, c, :])], writes=[Rwin], q="pool")
                for c in range(8):
                    S.dma(S.misc("pool"), [(win[:, c, 832:NCOL], w_in3[:, c, 2464:4512])], writes=[Rwing], q="pool")
                cm["Rwing"] = Rwing
                wq = self.sb(st, "wq", [128, 3, 768], BF16)
                wqs = self.sb(st, "wqs", [128, 3, 768], BF16)
                wkk = self.sb(st, "wkk", [128, 2, 512], BF16)
                wkv = self.sb(st, "wkv", [128, 2, 512], BF16)
                Rw1 = Res("w1")
                S.dma(S.misc("pool"), [(wq[:], wq_d.rearrange("(c p) n -> p c n", p=128)),
                                 (wqs[:], wqs_d.rearrange("(c p) n -> p c n", p=128)),
                                 (wkk[:], wkk_d.rearrange("(c p) n -> p c n", p=128)),
                                 (wkv[:], wkv_d.rearrange("(c p) n -> p c n", p=128))],
                      writes=[Rw1], q="pool")
                tmpr = self.ring(st, "tmp1_", [128, 512], F32, 4)
                cq = self.sb(st, "cq", [128, 3, 512]); Rcq = Res()
                cqn = self.sb(st, "cqn", [128, 3, 512], BF16); Rcqn = Res()
                ckv = self.sb(st, "ckv", [128, 2, 512]); Rckv = Res()
                ckvn = self.sb(st, "ckvn", [128, 2, 512], BF16); Rckvn = Res()
                qst = self.ring(st, "qst", [128, 512], BF16, 3)
                qch = [S.chan() for _ in range(3)]
                for (t_, r_) in qst.items:
                    self.memset("pool", t_[64:97, :], 1.0, [r_])
                knst = self.ring(st, "knst", [128, 512], BF16, 2)
                knch = [S.chan() for _ in range(2)]
                vst = self.ring(st, "vst", [128, 4, 520], BF16, 2)
                vch = [S.chan() for _ in range(2)]
                for (t_, r_) in vst.items:
                    self.memset("pool", t_[:], 1.0, [r_])
                kpst = self.ring(st, "kpst", [128, 512], BF16, 2)
                kpch = [S.chan() for _ in range(2)]
                gst = self.ring(st, "gst", [128, 4, 512], BF16, 2)
                gch = [S.chan() for _ in range(2)]
                posi = self.sb(st, "posi", [128, 512], I32); Rposi = Res()
                posch = S.chan()
                rp = [self.sb(st, "rp%d" % i, [128, 512]) for i in range(6)]
                Rrp = [Res() for _ in range(6)]
                ki = self.sb(st, "ki", [128, 512], I32); Rki = Res()
                sl = slice(64, 96)
                for t in range(NT):
                    own = t >= NTP
                    to = t - NTP
                    c0 = t * 512
                    if t == 0:
                        hnext = load_h(cm, 0)
                    h, Rh = hnext
                    S.dma(posch, [(posi[64:96, :], pos[0:1, c0:c0 + 512].partition_broadcast(32))], writes=[Rposi])
                    ang, sinT, cosT, sinQ, cosQ, rr = rp
                    Rang, RsinT, RcosT, RsinQ, RcosQ, Rrr = Rrp
                    self.cp("dve", ang[sl, :], posi[sl, :], [Rposi], [Rang])
                    self.ts("dve", ang[sl, :], ang[sl, :], ropec[sl, 0:1], ALU.mult, [Rang, Rrc], [Rang])
                    self.ts("dve", rr[sl, :], ang[sl, :], float(1.0 / TWO_PI), ALU.mult, [Rang], [Rrr])
                    self.cp("dve", ki[sl, :], rr[sl, :], [Rrr], [Rki])
                    self.cp("dve", rr[sl, :], ki[sl, :], [Rki], [Rrr])
                    self.stt(ang[sl, :], rr[sl, :], -C1, ang[sl, :], ALU.mult, ALU.add, [Rrr, Rang], [Rang])
                    self.stt(ang[sl, :], rr[sl, :], -C2, ang[sl, :], ALU.mult, ALU.add, [Rrr, Rang], [Rang])
                    self.ts("dve", ang[sl, :], ang[sl, :], float(np.pi), ALU.min, [Rang], [Rang], s2=float(-np.pi), op1=ALU.max)
                    self.act(sinT[sl, :], ang[sl, :], AF.Sin, [Rang, Rrc], [RsinT], scale=ropec[sl, 1:2])
                    self.act(rr[sl, :], ang[sl, :], AF.Abs, [Rang], [Rrr])
                    self.ts("dve", rr[sl, :], rr[sl, :], -1.0, ALU.mult, [Rrr], [Rrr], s2=float(np.pi / 2), op1=ALU.add)
                    self.act(cosT[sl, :], rr[sl, :], AF.Sin, [Rrr], [RcosT])
                    if own:
                        self.act(sinQ[sl, :], sinT[sl, :], AF.Copy, [RsinT], [RsinQ], scale=SCALE)
                        self.act(cosQ[sl, :], cosT[sl, :], AF.Copy, [RcosT], [RcosQ], scale=SCALE)
                    psA, RpsA = zmm(cm, h, Rh, OFF_KPE, 96)
                    psB, RpsB = zmm(cm, h, Rh, OFF_KPE + 96, 96)
                    ta, Rta = tmpr.next()
                    tb, Rtb = tmpr.next()
                    self.tt("dve", ta[sl, :], psA[sl, :], cosT[sl, :], ALU.mult, [RpsA, RcosT], [Rta])
                    self.tt("dve", tb[sl, :], psB[sl, :], sinT[sl, :], ALU.mult, [RpsB, RsinT], [Rtb])
                    kp, Rkp = kpst.next()
                    self.tt("pool", kp[sl, :], ta[sl, :], tb[sl, :], ALU.add, [Rta, Rtb], [Rkp])
                    S.dma(kpch[t % 2], [(KpeT[0:32, c0:c0 + 512], kp[sl, :])], reads=[Rkp], writes=[R_KpeT])
                    for m in range(2):
                        ps, Rps = zmm(cm, h, Rh, OFF_CKV + m * 128, 128)
                        self.cp("act", ckv[:, m, :], ps[:, :], [Rps], [Rckv])
                    if t + 1 < NT:
                        hnext = load_h(cm, t + 1)
                    rs2, Rrs2 = rms_stats(cm, ckv, 2, 1.0 / 256, Rckv)
                    for m in range(2):
                        self.stt(ckvn[:, m, :], ckv[:, m, :], vcol("g_kv_a", m), rs2[:], ALU.mult, ALU.mult, [Rckv, Rvec, Rrs2], [Rckvn])
                    for m in range(4):
                        ps, Rps = pbank()
                        for c in range(2):
                            self.mm(ps[:, :], wkk[:, c, m * 128:(m + 1) * 128], ckvn[:, c, :], [Rw1, Rckvn], [Rps], start=(c == 0), stop=(c == 1))
                        kn, Rkn = knst.next()
                        kk_ = (knst.i - 1) % 2
                        self.cp("act", kn[:], ps[:, :], [Rps], [Rkn])
                        S.dma(knch[kk_], [(KnT[m * 128:(m + 1) * 128, c0:c0 + 512], kn[:])], reads=[Rkn], writes=[R_KnT])
                    vt_, Rvt = vst.next()
                    vk = (vst.i - 1) % 2
                    for s_ in range(4):
                        ps, Rps = pbank()
                        for c in range(2):
                            self.mm(ps[:, :], ckvn[:, c, s_ * 128:(s_ + 1) * 128], wkv[:, c, :], [Rckvn, Rw1], [Rps], start=(c == 0), stop=(c == 1))
                        v4 = vt_[:, s_, :].rearrange("p (h d) -> p h d", d=65)
                        self.cp("act", v4[:, :, 0:64], ps[:, :].rearrange("p (h d) -> p h d", d=64), [Rps], [Rvt])
                    S.dma(vch[vk], [(Vs[:, t * 4:(t + 1) * 4, :], vt_[:])], reads=[Rvt], writes=[R_Vs])
                    if own:
                        o0 = to * 512
                        for m in range(3):
                            ps, Rps = zmm(cm, h, Rh, OFF_CQ + m * 128, 128)
                            self.cp("act", cq[:, m, :], ps[:, :], [Rps], [Rcq])
                        rs3, Rrs3 = rms_stats(cm, cq, 3, 1.0 / 384, Rcq)
                        for m in range(3):
                            self.stt(cqn[:, m, :], cq[:, m, :], vcol("g_q_a", m), rs3[:], ALU.mult, ALU.mult, [Rcq, Rvec, Rrs3], [Rcqn])
                        for hd in range(NH):
                            psA, RpsA = pbank()
                            psB, RpsB = pbank()
                            for c in range(3):
                                self.mm(psA[0:96, :], wq[:, c, hd * 96:(hd + 1) * 96], cqn[:, c, :], [Rw1, Rcqn], [RpsA], start=(c == 0), stop=(c == 2))
                            for c in range(3):
                                self.mm(psB[0:96, :], wqs[:, c, hd * 96:(hd + 1) * 96], cqn[:, c, :], [Rw1, Rcqn], [RpsB], start=(c == 0), stop=(c == 2))
                            q_, Rq_ = qst.next()
                            qk = (qst.i - 1) % 3
                            self.act(q_[0:64, :], psA[0:64, :], AF.Copy, [RpsA], [Rq_], scale=SCALE)
                            ta, Rta = tmpr.next()
                            tb, Rtb = tmpr.next()
                            self.tt("dve", ta[sl, :], psA[sl, :], cosQ[sl, :], ALU.mult, [RpsA, RcosQ], [Rta])
                            self.tt("dve", tb[sl, :], psB[sl, :], sinQ[sl, :], ALU.mult, [RpsB, RsinQ], [Rtb])
                            self.tt("pool", q_[sl, :], ta[sl, :], tb[sl, :], ALU.add, [Rta, Rtb], [Rq_])
                            S.dma(qch[qk], [(QT[hd, :, o0:o0 + 512], q_[0:97, :])], reads=[Rq_], writes=[R_QT])
                        for gq in range(4):
                            g_, Rg_ = gst.next()
                            gk = (gst.i - 1) % 2
                            for j in range(4):
                                ps, Rps = zmm(cm, h, Rh, OFF_G + (gq * 4 + j) * 128, 128, rw=cm["Rwing"])
                                self.act(g_[:, j, :], ps[:, :], AF.Sigmoid, [Rps], [Rg_])
                            S.dma(gch[gk], [(GT[gq * 512:(gq + 1) * 512, o0:o0 + 512].rearrange("(j p) t -> p j t", p=128), g_[:])],
                                  reads=[Rg_], writes=[R_GT])
                S.barrier()
                S.emit()
                for n_ in S.ENGS:
                    S.e[n_].ops = []

            with ExitStack() as st:
                if self.upto < 2:
                    raise _Stop()
                cm = make_common(st, 1792)
                win, Rwin = cm["win"], cm["Rwin"]
                for c in range(8):
                    S.dma(S.misc("pool"), [(win[:, c, :], w_in3[:, c, 672:2464])], writes=[Rwin], q="pool")
                w2s = self.sb(st, "w2s", [128, 512], BF16)
                a2s = self.sb(st, "a2s", [128, 512], BF16)
                g2s = self.sb(st, "g2s", [128, 512], BF16)
                Rw1 = Res("w1b")
                S.dma(S.misc("pool"), [(w2s[0:64, :], w2_d[:, :]), (a2s[64:128, :], a2_d[:, :]), (g2s[:], g2_d[:, :])], writes=[Rw1], q="pool")
                tmpr = self.ring(st, "tmp1b_", [128, 512], F32, 3)
                tmpb = self.ring(st, "tmpb1_", [128, 512], BF16, 4)
                zcw = self.ring(st, "zcw", [128, 513], F32, 4)
                carry = self.sb(st, "carry", [128, 16]); Rcar = Res()
                self.memset("pool", carry[:], 0.0, [Rcar])
                zsr = self.ring(st, "zsr", [128, 512], F32, 2)
                zsk = self.ring(st, "zsk", [128, 512], F32, 2)
                zsv = self.ring(st, "zsv", [128, 512], F32, 2)
                zs12 = self.sb(st, "zs12", [128, 512]); Rzs12 = Res()
                zs13 = self.sb(st, "zs13", [128, 512]); Rzs13 = Res()
                names = ["sig", "av", "Lc", "Lx", "kk", "nr", "tk", "EP", "EN"]
                nb2 = [{n_: (self.sb(st, "rb%d_" % k_ + n_, [128, 512]), Res(n_)) for n_ in names} for k_ in range(2)]
                twb = self.sb(st, "twb", [128, 512], BF16); Rtwb = Res()
                gsb = self.sb(st, "gsb", [128, 512], BF16); Rgsb = Res()
                rwst = self.ring(st, "rwst", [128, 512], BF16, 12)
                rwch = [S.chan() for _ in range(12)]
                rwi = [0]
                pcst = self.ring(st, "pcst", [128, 8], F32, 4)
                pcch = [S.chan() for _ in range(4)]

                def rw_store(dst_ap, src_fn, eng_fn):
                    k = rwi[0] % 12
                    rwi[0] += 1
                    t_, r_ = rwst.items[k]
                    eng_fn(t_, r_)
                    S.dma(rwch[k], [(dst_ap, src_fn(t_))], reads=[r_], writes=[R_rw])

                MU, OM = VEC_OFF["mu"], OM_OFF

                def shift(cm, h, Rh, m, dst, Rdst):
                    ps, Rps = zmm(cm, h, Rh, m * 128, 128)
                    zc, Rzc = zcw.next()
                    self.act(zc[:, 1:513], ps[:, :], AF.Copy, [Rps, Rvec], [Rzc], scale=vecs[:, MU + m:MU + m + 1])
                    self.cp("pool", zc[:, 0:1], carry[:, m:m + 1], [Rcar], [Rzc])
                    self.stt(dst[:], ps[:, :], vecs[:, OM + m:OM + m + 1], zc[:, 0:512], ALU.mult, ALU.add, [Rps, Rvec, Rzc], [Rdst])
                    self.cp("pool", carry[:, m:m + 1], zc[:, 512:513], [Rzc], [Rcar])

                for t in range(NT):
                    own = t >= NTP
                    o0 = (t - NTP) * 512
                    c0 = t * 512
                    if t == 0:
                        hnext = load_h(cm, 0)
                    h, Rh = hnext
                    shift(cm, h, Rh, 12, zs12, Rzs12)
                    shift(cm, h, Rh, 13, zs13, Rzs13)
                    if t + 1 < NT:
                        hnext = load_h(cm, t + 1)
                    self.act(twb[0:64, :], zs12[0:64, :], AF.Tanh, [Rzs12], [Rtwb])
                    self.cp("pool", twb[64:128, :], zs12[64:128, :], [Rzs12], [Rtwb])
                    self.act(gsb[:], zs13[:], AF.Sigmoid, [Rzs13], [Rgsb])
                    def st1(c_):
                        m = c_["m"]
                        c_["r"] = zsr.next(); c_["k"] = zsk.next(); c_["v"] = zsv.next()
                        shift(cm, h, Rh, m, *c_["r"])
                        shift(cm, h, Rh, 4 + m, *c_["k"])
                        shift(cm, h, Rh, 8 + m, *c_["v"])
                        c_["ms"] = slice(m * 128, (m + 1) * 128)
                        c_["nb"] = nb2[m % 2]

                    def st2(c_):
                        m, ms, nb = c_["m"], c_["ms"], c_["nb"]
                        (sig, Rsig), (av, Rav) = nb["sig"], nb["av"]
                        ps, Rps = pbank()
                        self.mm(ps[:, :], w2s[0:64, ms], twb[0:64, :], [Rw1, Rtwb], [Rps])
                        self.act(sig[:], ps[:, :], AF.Sigmoid, [Rps, Rvec], [Rsig], bias=vcol("w0", m))
                        ps, Rps = pbank()
                        self.mm(ps[:, :], a2s[64:128, ms], twb[64:128, :], [Rw1, Rtwb], [Rps])
                        self.act(av[:], ps[:, :], AF.Sigmoid, [Rps, Rvec], [Rav], bias=vcol("a0", m))

                    def st3(c_):
                        m, nb = c_["m"], c_["nb"]
                        (sig, Rsig), (Lc, RLc), (kk, Rkk) = nb["sig"], nb["Lc"], nb["kk"]
                        k_m, Rk = c_["k"]
                        S.op("dve", lambda e, o=Lc, d=sig: e.tensor_tensor_scan(out=o[:], data0=scanm[:], data1=d[:], initial=0.0,
                                                                              op0=ALU.mult, op1=ALU.add), [Rscan, Rsig], [RLc])
                        self.act(kk[:], k_m[:], AF.Copy, [Rk, Rvec], [Rkk], scale=vcol("k_k", m))
                        kk2, Rkk2 = tmpb.next()
                        self.act(kk2[:], k_m[:], AF.Square, [Rk, Rvec], [Rkk2], scale=vcol("k_k", m))
                        ps, Rps = pbank()
                        self.mm(ps[:, :], bdb[:, :], kk2[:], [Rbd, Rkk2], [Rps])
                        c_["psn"] = (ps, Rps)

                    def st4(c_):
                        m, nb = c_["m"], c_["nb"]
                        (av, Rav), (kk, Rkk), (nr, Rnr), (tk, Rtk) = nb["av"], nb["kk"], nb["nr"], nb["tk"]
                        k_m, Rk = c_["k"]
                        ps, Rps = c_["psn"]
                        self.ts("dve", nr[:], ps[:, :], 1e-24, ALU.max, [Rps], [Rnr])
                        self.act(nr[:], nr[:], AF.Ln, [Rnr], [Rnr])
                        self.act(nr[:], nr[:], AF.Exp, [Rnr], [Rnr], scale=-0.5)
                        self.tt("dve", kk[:], kk[:], nr[:], ALU.mult, [Rkk, Rnr], [Rkk])
                        self.ts("dve", tk[:], av[:], vcol("k_a", m), ALU.mult, [Rav, Rvec], [Rtk],
                                s2=vecs[:, OMKA_OFF + m:OMKA_OFF + m + 1], op1=ALU.add)
                        self.tt("pool", tk[:], tk[:], k_m[:], ALU.mult, [Rtk, Rk], [Rtk])

                    def st5(c_):
                        nb = c_["nb"]
                        (Lc, RLc), (EP, REP), (EN, REN) = nb["Lc"], nb["EP"], nb["EN"]
                        self.act(EP[:], Lc[:], AF.Exp, [RLc], [REP], scale=-EXPH)
                        self.act(EN[:], Lc[:], AF.Exp, [RLc], [REN], scale=EXPH)

                    def st6(c_):
                        m, ms, nb = c_["m"], c_["ms"], c_["nb"]
                        (tk, Rtk) = nb["tk"]
                        r_m, Rr = c_["r"]; v_m, Rv = c_["v"]
                        rk, Rrk = tmpb.next()
                        self.stt(rk[:], r_m[:], vcol("r_k", m), tk[:], ALU.mult, ALU.mult, [Rr, Rvec, Rtk], [Rrk])
                        psb, Rpsb = pbank()
                        self.mm(psb[:, :], bdb[:, :], rk[:], [Rbd, Rrk], [Rpsb])
                        rw_store(BoT[ms, o0:o0 + 512], lambda t_: t_[:],
                                 lambda t_, r_: self.tt("dve", t_[:], psb[:, :], v_m[:], ALU.mult, [Rpsb, Rv], [r_]))
                        psg, Rpsg = pbank()
                        self.mm(psg[:, :], g2s[:, ms], gsb[:], [Rw1, Rgsb], [Rpsg])
                        rw_store(GrT[ms, o0:o0 + 512], lambda t_: t_[:],
                                 lambda t_, r_: self.cp("act", t_[:], psg[:, :], [Rpsg], [r_]))

                    def st7(c_):
                        m, ms, nb = c_["m"], c_["ms"], c_["nb"]
                        (av, Rav), (Lx, RLx), (kk, Rkk), (tk, Rtk), (EP, REP), (EN, REN) = nb["av"], nb["Lx"], nb["kk"], nb["tk"], nb["EP"], nb["EN"]
                        r_m, Rr = c_["r"]; v_m, Rv = c_["v"]
                        ARv = ARt[ms, t * 4:(t + 1) * 4, :]

                        def a_tilde(t_, r_):
                            self.stt(t_[:, 1:512], kk[:, 1:512], -1.0, EP[:, 0:511], ALU.mult, ALU.mult, [Rkk, REP], [r_])
                            self.ts("pool", t_[:].rearrange("p (c t) -> p c t", t=64)[:, :, 0:1],
                                    kk[:].rearrange("p (c t) -> p c t", t=64)[:, :, 0:1], -1.0, ALU.mult, [Rkk], [r_])

                        rw_store(ARv[:, :, 0:128], lambda t_: t_[:].rearrange("p (b t) -> p b t", t=128), a_tilde)
                        rw_store(ARv[:, :, 128:256], lambda t_: t_[:].rearrange("p (b t) -> p b t", t=128),
                                 lambda t_, r_: self.tt("pool", t_[:], r_m[:], EP[:], ALU.mult, [Rr, REP], [r_]))
                        self.tt("pool", Lx[:], kk[:], av[:], ALU.mult, [Rkk, Rav], [RLx])
                        rw_store(Bt[ms, c0:c0 + 512], lambda t_: t_[:],
                                 lambda t_, r_: self.tt("dve", t_[:], Lx[:], EN[:], ALU.mult, [RLx, REN], [r_]))
                        rw_store(Kt[ms, c0:c0 + 512], lambda t_: t_[:],
                                 lambda t_, r_: self.tt("pool", t_[:], tk[:], EN[:], ALU.mult, [Rtk, REN], [r_]))
                        rw_store(Vt[ms, c0:c0 + 512], lambda t_: t_[:],
                                 lambda t_, r_: self.cp("act", t_[:], v_m[:], [Rv], [r_]))
                        pc, Rpc = pcst.next()
                        pk = (pcst.i - 1) % 4
                        self.cp("pool", pc[:], EP[:].rearrange("p (c t) -> p c t", t=64)[:, :, 63], [REP], [Rpc])
                        S.dma(pcch[pk], [(PCt[ms, t * 8:(t + 1) * 8], pc[:])], reads=[Rpc], writes=[R_rw])

                    for pr in range(2):
                        cs_ = [{"m": pr * 2}, {"m": pr * 2 + 1}]
                        for stg in (st1, st2, st3, st4, st5):
                            for c_ in cs_:
                                stg(c_)
                        if own:
                            for c_ in cs_:
                                st6(c_)
                        for c_ in cs_:
                            st7(c_)
                S.barrier()
                S.emit()
                for n_ in S.ENGS:
                    S.e[n_].ops = []

            with ExitStack() as st:
                if self.upto < 3:
                    raise _Stop()
                for i in range(8):
                    S.dma(S.misc("pool"), [(wupb[i * 128:(i + 1) * 128, :], wup_d[i * 128:(i + 1) * 128, :])], writes=[R_wconv], q="pool")
                for i in range(8):
                    S.dma(S.misc("pool"), [(wdnb[i * 512:(i + 1) * 512, :], wdn_d[i * 512:(i + 1) * 512, :])], writes=[R_wconv], q="pool")
                arl = self.ring(st, "arl", [128, 4, 256], BF16, 2)
                btl = self.ring(st, "btl", [128, 512], BF16, 2)
                ktl = self.ring(st, "ktl", [128, 512], BF16, 2)
                vtl = self.ring(st, "vtl", [128, 512], BF16, 2)
                pcl = self.ring(st, "pcl", [128, 8], F32, 2)
                bol = self.ring(st, "bol", [128, 512], BF16, 2)
                grl = self.ring(st, "grl", [128, 512], BF16, 2)
                ldch = [S.chan(), S.chan()]
                MT = self.ring(st, "MT", [128, 2, 512], BF16, 8)
                Lr = self.ring(st, "Lr", [128, 2, 128], BF16, 12)
                ASr = self.ring(st, "ASr", [128, 2, 256], BF16, 12)
                TM = self.ring(st, "TM", [128, 4, 128], BF16, 8)
                Xb = self.ring(st, "Xb", [128, 2, 128], BF16, 8)
                TXr = self.ring(st, "TXr", [128, 2, 128], BF16, 8)
                RqT = self.ring(st, "RqT", [128, 128], BF16, 8)
                Y0 = self.ring(st, "Y0", [128, 2, 64], F32, 8)
                GTr = self.ring(st, "GTr", [128, 64], BF16, 16)
                Fr = self.ring(st, "Fr", [128, 64], F32, 16)
                Hs = self.ring(st, "Hs", [128, 64], BF16, 3)
                Yr = self.ring(st, "Yr", [128, 2, 64], F32, 4)
                gn = self.ring(st, "gn", [128, 2, 64], F32, 4)
                gs_ = self.ring(st, "gs_", [128, 2], F32, 6)
                yT = self.ring(st, "yT", [128, 512], F32, 2)
                yst = self.ring(st, "yst", [128, 512], BF16, 2)
                ych = [S.chan(), S.chan()]
                pi2 = [0]

                def pb2():
                    b_ = banks[pi2[0] % 7]
                    pi2[0] += 1
                    return b_

                e2 = lambda ap: ap.rearrange("p (e s) -> p e s", e=2)
                idb3 = identb[:].unsqueeze(1).broadcast_to([128, 2, 128])
                NBK = 4
                for m in range(4):
                    ms = slice(m * 128, (m + 1) * 128)
                    H, RH = Hs.next()
                    self.memset("pool", H[:], 0.0, [RH])
                    for t in range(NT):
                        own = t >= NTP
                        o0 = (t - NTP) * 512
                        c0 = t * 512
                        ar, Rar = arl.next(); bt, Rbt = btl.next(); kt, Rkt = ktl.next(); vt, Rvt = vtl.next(); pc, Rpc = pcl.next()
                        lk = (arl.i - 1) % 2
                        pairs = [(ar[:], ARt[ms, t * 4:(t + 1) * 4, :]), (bt[:], Bt[ms, c0:c0 + 512]), (kt[:], Kt[ms, c0:c0 + 512]),
                                 (vt[:], Vt[ms, c0:c0 + 512]), (pc[:], PCt[ms, t * 8:(t + 1) * 8])]
                        wr = [Rar, Rbt, Rkt, Rvt, Rpc]
                        if own:
                            bo, Rbo = bol.next(); gr, Rgr = grl.next()
                            pairs += [(bo[:], BoT[ms, o0:o0 + 512]), (gr[:], GrT[ms, o0:o0 + 512])]
                            wr += [Rbo, Rgr]
                            yt_, Ryt = yT.next()
                        S.dma(ldch[lk], pairs, reads=[R_rw], writes=wr)
                        X = [dict() for _ in range(NBK)]
                        for b in range(NBK):
                            c_ = X[b]
                            bs = slice(b * 128, (b + 1) * 128)
                            c_["bs"] = bs
                            mt, Rmt = MT.next()
                            L0, RL0 = Lr.next()
                            for e in range(2):
                                hs = slice(e * 64, (e + 1) * 64)
                                ps, Rps = pb2()
                                self.mm(ps[:, 0:256], bt[hs, bs], ar[hs, b, :], [Rbt, Rar], [Rps])
                                self.mm(ps[:, 256:512], kt[hs, bs], ar[hs, b, :], [Rkt, Rar], [Rps])
                                self.tt("dve", mt[:, e, :], ps[:, :], mask4[:], ALU.mult, [Rps, Rm4], [Rmt])
                                psL, RpsL = pb2()
                                self.mm(psL[:, 0:128], ar[hs, b, 0:128], bt[hs, bs], [Rar, Rbt], [RpsL])
                                self.tt("dve", L0[:, e, :], psL[:, 0:128], maskL[:], ALU.mult, [RpsL, RmL], [RL0])
                            c_["mt"], c_["Rmt"], c_["L"], c_["RL"] = mt, Rmt, L0, RL0
                        for b in range(NBK):
                            c_ = X[b]
                            bs = c_["bs"]
                            pst, Rpst = bankb
                            o_ = (b % 2) * 512
                            self.tr(pst[:, o_ + 0:o_ + 128], ar[:, b, 0:128], identb[:], [Rar, Ridb], [Rpst])
                            self.tr(pst[:, o_ + 128:o_ + 256], bt[:, bs], identb[:], [Rbt, Ridb], [Rpst])
                            self.tr(pst[:, o_ + 256:o_ + 384], kt[:, bs], identb[:], [Rkt, Ridb], [Rpst])
                            self.tr(pst[:, o_ + 384:o_ + 512], vt[:, bs], identb[:], [Rvt, Ridb], [Rpst])
                            tm, Rtm = TM.next()
                            self.cp("act", tm[:], pst[:, o_:o_ + 512].rearrange("p (q f) -> p q f", q=4), [Rpst], [Rtm])
                            c_["tm"], c_["Rtm"] = tm, Rtm
                        for b in range(NBK):
                            c_ = X[b]
                            mt, Rmt = c_["mt"], c_["Rmt"]
                            AS, RAS = ASr.next()
                            self.cp("pool", AS[:, :, 0:128], mt[:, :, 0:128], [Rmt], [RAS])
                            self.tt("pool", AS[:, :, 128:256], mt[:, :, 0:128], idb3, ALU.add, [Rmt, Ridb], [RAS])
                            c_["AS"], c_["RAS"] = AS, RAS
                        for b in range(NBK):
                            c_ = X[b]
                            AS, RAS, Lk, RLk = c_["AS"], c_["RAS"], c_["L"], c_["RL"]
                            psa, Rpsa = pb2()
                            psl, Rpsl = pb2()
                            for e in range(2):
                                self.mm(psa[:, e * 128:(e + 1) * 128], Lk[:, e, :], AS[:, e, 0:128], [RLk, RAS], [Rpsa])
                                self.mm(psl[:, e * 128:(e + 1) * 128], AS[:, e, 0:128], Lk[:, e, :], [RAS, RLk], [Rpsl])
                            ASn, RASn = ASr.next()
                            self.cp("dve", ASn[:, :, 0:128], e2(psa[:, 0:256]), [Rpsa], [RASn])
                            self.cp("pool", ASn[:, :, 128:256], AS[:, :, 128:256], [RAS], [RASn])
                            Ln, RLn = Lr.next()
                            self.cp("act", Ln[:], e2(psl[:, 0:256]), [Rpsl], [RLn])
                            c_["AS"], c_["RAS"], c_["L"], c_["RL"] = ASn, RASn, Ln, RLn
                        for it in range(5):
                            last = it == 4
                            needA = it < 3
                            w_ = 0 if needA else 128
                            for b in range(NBK):
                                c_ = X[b]
                                AS, RAS, Lk, RLk = c_["AS"], c_["RAS"], c_["L"], c_["RL"]
                                psm, Rpsm = pb2()
                                for e in range(2):
                                    self.mm(psm[:, e * 256 + w_:(e + 1) * 256], Lk[:, e, :], AS[:, e, w_:256], [RLk, RAS], [Rpsm])
                                ASn, RASn = ASr.next()
                                pm3 = e2(psm[:, 0:512])
                                self.tt("dve", ASn[:, :, 128:256], pm3[:, :, 128:256], AS[:, :, 128:256], ALU.add, [Rpsm, RAS], [RASn])
                                if needA:
                                    self.cp("act", ASn[:, :, 0:128], pm3[:, :, 0:128], [Rpsm], [RASn])
                                if not last:
                                    psl, Rpsl = pb2()
                                    for e in range(2):
                                        self.mm(psl[:, e * 128:(e + 1) * 128], AS[:, e, 0:128], Lk[:, e, :], [RAS, RLk], [Rpsl])
                                    Ln, RLn = Lr.next()
                                    self.cp("act", Ln[:], e2(psl[:, 0:256]), [Rpsl], [RLn])
                                    c_["L"], c_["RL"] = Ln, RLn
                                c_["AS"], c_["RAS"] = ASn, RASn
                        for b in range(NBK):
                            c_ = X[b]
                            mt, Rmt, tm, Rtm = c_["mt"], c_["Rmt"], c_["tm"], c_["Rtm"]
                            xb, Rxb = Xb.next()
                            psv, Rpsv = pb2()
                            for e in range(2):
                                self.mm(psv[:, e * 64:(e + 1) * 64], mt[:, e, 256:384], tm[:, 3, e * 64:(e + 1) * 64], [Rmt, Rtm], [Rpsv])
                            self.cp("act", xb[:, :, 64:128], psv[:, 0:128].rearrange("p (e v) -> p e v", e=2), [Rpsv], [Rxb])
                            self.cp("pool", xb[:, :, 0:64], tm[:, 0, :].rearrange("p (e v) -> p e v", e=2), [Rtm], [Rxb])
                            c_["xb"], c_["Rxb"] = xb, Rxb
                        for b in range(NBK):
                            c_ = X[b]
                            AS, RAS, xb, Rxb = c_["AS"], c_["RAS"], c_["xb"], c_["Rxb"]
                            pstx, Rpstx = pb2()
                            for e in range(2):
                                self.mm(pstx[:, e * 128:(e + 1) * 128], AS[:, e, 128:256], xb[:, e, :], [RAS, Rxb], [Rpstx])
                            tx, Rtx = TXr.next()
                            self.cp("act", tx[:], e2(pstx[:, 0:256]), [Rpstx], [Rtx])
                            c_["tx"], c_["Rtx"] = tx, Rtx
                        if own:
                            for b in range(NBK):
                                c_ = X[b]
                                mt, Rmt, tm, Rtm, tx, Rtx = c_["mt"], c_["Rmt"], c_["tm"], c_["Rtm"], c_["tx"], c_["Rtx"]
                                psr, Rpsr = pb2()
                                for e in range(2):
                                    self.mm(psr[e * 64:(e + 1) * 64, 0:128], tx[:, e, 0:64], mt[:, e, 128:256], [Rtx, Rmt], [Rpsr])
                                rq, Rrq = RqT.next()
                                self.tt("dve", rq[:], psr[:, 0:128], ar[:, b, 128:256], ALU.add, [Rpsr, Rar], [Rrq])
                                psy, Rpsy = pb2()
                                for e in range(2):
                                    self.mm(psy[:, e * 64:(e + 1) * 64], mt[:, e, 128:256], tx[:, e, 64:128], [Rmt, Rtx], [Rpsy], start=True, stop=False)
                                    self.mm(psy[:, e * 64:(e + 1) * 64], mt[:, e, 384:512], tm[:, 3, e * 64:(e + 1) * 64], [Rmt, Rtm], [Rpsy], start=False, stop=True)
                                y0, Ry0 = Y0.next()
                                self.cp("act", y0[:], psy[:, 0:128].rearrange("p (e v) -> p e v", e=2), [Rpsy], [Ry0])
                                c_["rq"], c_["Rrq"], c_["y0"], c_["Ry0"] = rq, Rrq, y0, Ry0
                        for b in range(NBK):
                            c_ = X[b]
                            tm, Rtm, tx, Rtx = c_["tm"], c_["Rtm"], c_["tx"], c_["Rtx"]
                            c_["gt"], c_["ff"] = [], []
                            for c in range(2):
                                cs = slice(c * 64, (c + 1) * 64)
                                psg, Rpsg = pb2()
                                for e in range(2):
                                    self.mm(psg[e * 64:(e + 1) * 64, 0:64], tx[cs, e, 0:64], tm[cs, 1, e * 64:(e + 1) * 64], [Rtx, Rtm], [Rpsg])
                                gt, Rgt = GTr.next()
                                self.tt("dve", gt[0:64, :], psg[0:64, 0:64], ident[0:64, 0:64], ALU.add, [Rpsg, Rid], [Rgt])
                                self.tt("dve", gt[64:128, :], psg[64:128, 0:64], ident[64:128, 64:128], ALU.add, [Rpsg, Rid], [Rgt])
                                psf, Rpsf = pb2()
                                for e in range(2):
                                    es = slice(e * 64, (e + 1) * 64)
                                    self.mm(psf[es, 0:64], tm[cs, 1, es], tx[cs, e, 64:128], [Rtm, Rtx], [Rpsf], start=True, stop=False)
                                    self.mm(psf[es, 0:64], tm[cs, 2, es], tm[cs, 3, es], [Rtm], [Rpsf], start=False, stop=True)
                                ff, Rff = Fr.next()
                                pcc = pc[:, b * 2 + c:b * 2 + c + 1]
                                self.act(ff[:], psf[:, 0:64], AF.Copy, [Rpsf, Rpc], [Rff], scale=pcc)
                                c_["gt"].append((gt, Rgt))
                                c_["ff"].append((ff, Rff))
                        for b in range(NBK):
                            c_ = X[b]
                            bs = c_["bs"]
                            if own:
                                yy, Ryy = Yr.next()
                                rq, Rrq, y0, Ry0 = c_["rq"], c_["Rrq"], c_["y0"], c_["Ry0"]
                            for c in range(2):
                                cs = slice(c * 64, (c + 1) * 64)
                                gt, Rgt = c_["gt"][c]
                                ff, Rff = c_["ff"][c]
                                if own:
                                    for e in range(2):
                                        es = slice(e * 64, (e + 1) * 64)
                                        psq, Rpsq = pb2()
                                        self.mm(psq[cs, 0:64], rq[es, cs], H[es, :], [Rrq, RH], [Rpsq])
                                        self.tt("dve", yy[cs, e, :], psq[cs, 0:64], y0[cs, e, :], ALU.add, [Rpsq, Ry0], [Ryy])
                                Hn, RHn = Hs.next()
                                for e in range(2):
                                    es = slice(e * 64, (e + 1) * 64)
                                    psh, Rpsh = pb2()
                                    self.mm(psh[es, 0:64], gt[es, :], H[es, :], [Rgt, RH], [Rpsh])
                                    self.stt(Hn[es, :], psh[es, 0:64], pc[es, b * 2 + c:b * 2 + c + 1], ff[es, :], ALU.mult, ALU.add, [Rpsh, Rpc, Rff], [RHn])
                                H, RH = Hn, RHn
                            if own:
                                s1, Rs1 = gs_.next()
                                S.op("dve", lambda e, o=s1, i=yy: e.tensor_reduce(out=o[:], in_=i[:], axis=AX.X, op=ALU.add), [Ryy], [Rs1])
                                self.ts("dve", s1[:], s1[:], -1.0 / 64, ALU.mult, [Rs1], [Rs1])
                                yc, Ryc = gn.next()
                                self.tt("dve", yc[:], yy[:], s1[:].unsqueeze(2).broadcast_to([128, 2, 64]), ALU.add, [Ryy, Rs1], [Ryc])
                                y2, Ry2 = gn.next()
                                self.tt("pool", y2[:], yc[:], yc[:], ALU.mult, [Ryc], [Ry2])
                                s2, Rs2 = gs_.next()
                                S.op("dve", lambda e, o=s2, i=y2: e.tensor_reduce(out=o[:], in_=i[:], axis=AX.X, op=ALU.add), [Ry2], [Rs2])
                                self.act(s2[:], s2[:], AF.Sqrt, [Rs2], [Rs2], bias=GN_EPS, scale=1.0 / 64)
                                self.recip(s2[:], s2[:], [Rs2], [Rs2])
                                self.tt("dve", yc[:], yc[:], s2[:].unsqueeze(2).broadcast_to([128, 2, 64]), ALU.mult, [Ryc, Rs2], [Ryc])
                                pstt, Rpstt = pb2()
                                self.tr(pstt[:, 0:128], yc[:].rearrange("p e v -> p (e v)"), ident[:], [Ryc, Rid], [Rpstt])
                                self.act(yt_[:, bs], pstt[:, 0:128], AF.Identity, [Rpstt, Rvec], [Ryt], bias=vcol("ln_b", m), scale=vcol("ln_w", m))
                        if own:
                            self.tt("pool", yt_[:], yt_[:], bo[:], ALU.add, [Ryt, Rbo], [Ryt])
                            ys, Rys = yst.next()
                            yk = (yst.i - 1) % 2
                            self.tt("pool", ys[:], yt_[:], gr[:], ALU.mult, [Ryt, Rgr], [Rys])
                            S.dma(ych[yk], [(YbT[ms, o0:o0 + 512], ys[:])], reads=[Rys], writes=[R_YbT])
                S.barrier()
                S.emit()
                for n_ in S.ENGS:
                    S.e[n_].ops = []

            with ExitStack() as st:
                if self.upto < 4:
                    raise _Stop()
                Vall = self.sb(st, "Vall", [128, NB, 520], BF16); RVall = Res()
                nv = max(1, NB // 16)
                for i in range(0, NB, 16):
                    j = min(NB, i + 16)
                    S.dma(S.misc(), [(Vall[:, i:j, :], Vs[:, i:j, :])], reads=[R_Vs], writes=[RVall])
                Kh = self.ring(st, "Kh", [128, TT], BF16, 2)
                Qh = self.ring(st, "Qh", [128, TO], BF16, 2)
                kqch = [S.chan(), S.chan()]
                PT = self.ring(st, "PT", [128, 512], BF16, 6)
                osb = self.ring(st, "osb", [128, 512], F32, 2)
                rl = self.ring(st, "rl", [128, 512], F32, 2)
                ost = self.ring(st, "ost", [128, 512], BF16, 2)
                och = [S.chan(), S.chan()]
                sbanks = Ring(banks[0:4])
                obanks = Ring(banks[4:6])
                bbanks = Ring(banks[6:7])
                V5 = Vall[:].rearrange("p b (h d) -> p b h d", d=65)
                LOOK = 3
                heads = []

                def load_head(hd):
                    K_, RK_ = Kh.next(); Q_, RQ_ = Qh.next()
                    hk = (Kh.i - 1) % 2
                    S.dma(kqch[hk], [(K_[0:64, :], KnT[hd * 64:(hd + 1) * 64, :]), (K_[64:97, :], KpeT[:, :]), (Q_[0:97, :], QT[hd, :, :])],
                          reads=[R_KnT, R_KpeT, R_QT], writes=[RK_, RQ_])
                    return (K_, RK_, Q_, RQ_)

                pend = []

                def do_pv(item):
                    (hd, qt, kb, nkb, cst, pt, Rpt, po, Rpo) = item
                    self.mm(po[0:65, cst:512], V5[:, kb, hd, :], pt[:, cst:512], [RVall, Rpt], [Rpo], start=(kb == 0), stop=(kb == nkb - 1), inc=True)
                    if kb == nkb - 1:
                        q0 = qt * 512
                        o_, Ro_ = osb.next()
                        self.cp("act", o_[0:65, :], po[0:65, :], [Rpo], [Ro_])
                        r_, Rr_ = rl.next()
                        self.recip(r_[64:65, :], o_[64:65, :], [Ro_], [Rr_])
                        pb_, Rpb_ = bbanks.next()
                        self.mm(pb_[0:64, :], onesf[64:65, 0:64], r_[64:65, :], [Ronesf, Rr_], [Rpb_])
                        os_, Ros_ = ost.next()
                        ok = (ost.i - 1) % 2
                        self.tt("dve", os_[0:64, :], pb_[0:64, :], o_[0:64, :], ALU.mult, [Rpb_, Ro_], [Ros_])
                        S.dma(och[ok], [(OaT[hd * 64:(hd + 1) * 64, q0:q0 + 512], os_[0:64, :])], reads=[Ros_], writes=[R_OaT])

                nxt = load_head(0)
                for hd in range(NH):
                    K_, RK_, Q_, RQ_ = nxt
                    if hd + 1 < NH:
                        nxt = load_head(hd + 1)
                    for qt in range(NTO):
                        q0 = qt * 512
                        nkb = (TP + q0 + 512) // 128
                        po, Rpo = obanks.next()
                        for kb in range(nkb):
                            jd = kb - (nkb - 4)
                            cst = 0 if jd < 0 else jd * 128
                            ps, Rps = sbanks.next()
                            self.mm(ps[:, cst:512], K_[0:97, kb * 128:(kb + 1) * 128], Q_[0:97, q0 + cst:q0 + 512], [RK_, RQ_], [Rps])
                            pt, Rpt = PT.next()
                            self.act(pt[:, cst:512], ps[:, cst:512], AF.Exp, [Rps], [Rpt])
                            if jd >= 0:
                                self.tt("pool", pt[:, cst:cst + 128], pt[:, cst:cst + 128], trim[:], ALU.mult, [Rpt, Rtri], [Rpt])
                            pend.append((hd, qt, kb, nkb, cst, pt, Rpt, po, Rpo))
                            if len(pend) > LOOK:
                                do_pv(pend.pop(0))
                while pend:
                    do_pv(pend.pop(0))
                S.barrier()
                S.emit()
                for n_ in S.ENGS:
                    S.e[n_].ops = []

            with ExitStack() as st:
                if self.upto < 5:
                    raise _Stop()
                womla = self.sb(st, "womla", [128, 4, D], BF16)
                worw = self.sb(st, "worw", [128, 4, D], BF16)
                wout = self.sb(st, "wout", [128, 8, D], BF16)
                wpg = self.sb(st, "wpg", [128, 8, D], BF16)
                wpp = self.sb(st, "wpp", [128, 2, D], BF16)
                Rw4 = Res("w4")
                S.dma(S.misc("pool"), [(womla[:], womla_d.rearrange("(c p) n -> p c n", p=128)),
                                 (worw[:], worw_d.rearrange("(c p) n -> p c n", p=128)),
                                 (wpp[:], wpp_d.rearrange("(c p) n -> p c n", p=128))], writes=[Rw4], q="pool")
                wout3 = wout_d.rearrange("(c p) n -> p c n", p=128)
                wpg3 = wpg_d.rearrange("(c p) n -> p c n", p=128)
                for c in range(8):
                    S.dma(S.misc("pool"), [(wout[:, c, :], wout3[:, c, :]), (wpg[:, c, :], wpg3[:, c, :])], writes=[Rw4], q="pool")
                wupr = self.ring(st, "wupr", [128, 8, 256], BF16, 3)
                wupc = [S.chan() for _ in range(3)]
                wdnr = self.ring(st, "wdnr", [128, 2, 512], BF16, 4)
                wdnc = [S.chan() for _ in range(4)]
                wup3 = wupb.rearrange("(c p) n -> p c n", p=128)
                wdn3 = wdnb.rearrange("(c p) n -> p c n", p=128)
                x4 = self.ring(st, "x4_", [128, 8, 512], F32, 1)
                xc4 = S.chan()
                inb = self.ring(st, "inb", [128, 4, 512], BF16, 2)
                inc_ = [S.chan(), S.chan()]
                gtl = self.ring(st, "gtl", [128, 8, 512], BF16, 2)
                gtc = [S.chan(), S.chan()]
                pin = self.sb(st, "pin", [128, 2, 512], BF16); Rpin = Res()
                pinc = S.chan()
                mix = self.sb(st, "mix", [128, 8, 512], BF16); Rmix = Res()
                mtmp = self.ring(st, "mtmp", [128, 512], F32, 3)
                sq4 = self.ring(st, "sq4", [128, 512], BF16, 4)
                h4 = self.sb(st, "h4", [128, 8, 512], BF16); Rh4 = Res()
                hid = self.sb(st, "hid", [128, 16, 512], BF16)
                Rhid = [Res() for _ in range(16)]
                rel = self.ring(st, "rel", [128, 512], F32, 3)
                rs4 = self.ring(st, "rs4", [128, 512], F32, 2)
                t4 = self.ring(st, "t4", [128, 512], F32, 2)
                gsg = self.ring(st, "gsg", [128, 512], F32, 2)
                osb4 = self.ring(st, "osb4", [128, 512], F32, 2)
                och4 = [S.chan(), S.chan()]
                pi4 = [0]

                def pb4():
                    b_ = banks[pi4[0] % 7]
                    pi4[0] += 1
                    return b_

                def rms4(xt, Rxt):
                    ps, Rps = pb4()
                    for c in range(8):
                        sq, Rsq = sq4.next()
                        self.act(sq[:], xt[:, c, :], AF.Square, [Rxt], [Rsq])
                        self.mm(ps[:, :], onesb[:, :], sq[:], [Rones, Rsq], [Rps], start=(c == 0), stop=(c == 7), inc=True)
                    t1, Rt1 = t4.next()
                    self.act(t1[:], ps[:, :], AF.Ln, [Rps, Reps], [Rt1], bias=epsc[:, 0:1], scale=1.0 / D)
                    rs, Rrs = rs4.next()
                    self.act(rs[:], t1[:], AF.Exp, [Rt1], [Rrs], scale=-0.5)
                    return rs, Rrs

                for to in range(NTO):
                    o0 = to * 512
                    xt, Rxt = x4.next()
                    S.dma(xc4, [(xt[:], xT3[:, :, TP + o0:TP + o0 + 512])], writes=[Rxt])
                    oa, Roa = inb.next()
                    S.dma(inc_[0], [(oa[:], OaT[:, o0:o0 + 512].rearrange("(c p) t -> p c t", p=128))], reads=[R_OaT], writes=[Roa])
                    yb, Ryb = inb.next()
                    S.dma(inc_[1], [(yb[:], YbT[:, o0:o0 + 512].rearrange("(c p) t -> p c t", p=128))], reads=[R_YbT], writes=[Ryb])
                    ga, Rga = gtl.next()
                    S.dma(gtc[0], [(ga[:], GT[0:1024, o0:o0 + 512].rearrange("(c p) t -> p c t", p=128))], reads=[R_GT], writes=[Rga])
                    gb, Rgb = gtl.next()
                    S.dma(gtc[1], [(gb[:], GT[1024:2048, o0:o0 + 512].rearrange("(c p) t -> p c t", p=128))], reads=[R_GT], writes=[Rgb])
                    S.dma(pinc, [(pin[:], pT[:, o0:o0 + 512].rearrange("(c p) t -> p c t", p=128))], writes=[Rpin], q="pool")
                    for mt_ in range(8):
                        msl = slice(mt_ * 128, (mt_ + 1) * 128)
                        psa, Rpsa = pb4()
                        for c in range(4):
                            self.mm(psa[:, :], womla[:, c, msl], oa[:, c, :], [Rw4, Roa], [Rpsa], start=(c == 0), stop=(c == 3))
                        psb, Rpsb = pb4()
                        for c in range(4):
                            self.mm(psb[:, :], worw[:, c, msl], yb[:, c, :], [Rw4, Ryb], [Rpsb], start=(c == 0), stop=(c == 3))
                        m1, Rm1 = mtmp.next()
                        self.tt("dve", m1[:], psa[:, :], ga[:, mt_, :], ALU.mult, [Rpsa, Rga], [Rm1])
                        m2, Rm2 = mtmp.next()
                        self.tt("dve", m2[:], psb[:, :], gb[:, mt_, :], ALU.mult, [Rpsb, Rgb], [Rm2])
                        self.tt("pool", mix[:, mt_, :], m1[:], m2[:], ALU.add, [Rm1, Rm2], [Rmix])
                    for mt_ in range(8):
                        msl = slice(mt_ * 128, (mt_ + 1) * 128)
                        ps, Rps = pb4()
                        for c in range(8):
                            self.mm(ps[:, :], wout[:, c, msl], mix[:, c, :], [Rw4, Rmix], [Rps], start=(c == 0), stop=(c == 7))
                        self.tt("dve", xt[:, mt_, :], ps[:, :], xt[:, mt_, :], ALU.add, [Rps, Rxt], [Rxt])
                    rs, Rrs = rms4(xt, Rxt)
                    for c in range(8):
                        self.stt(h4[:, c, :], xt[:, c, :], vcol("g_ffn", c), rs[:], ALU.mult, ALU.mult, [Rxt, Rvec, Rrs], [Rh4])
                    for hh in range(2):
                        for uc in range(8):
                            wu, Rwu = wupr.next()
                            uk = (wupr.i - 1) % 3
                            col = hh * 2048 + uc * 256
                            S.dma(wupc[uk], [(wu[:], wup3[:, :, col:col + 256])], reads=[R_wconv], writes=[Rwu])
                            for j in range(2):
                                mi = uc * 2 + j
                                ps, Rps = pb4()
                                for c in range(8):
                                    self.mm(ps[:, :], wu[:, c, j * 128:(j + 1) * 128], h4[:, c, :], [Rwu, Rh4], [Rps], start=(c == 0), stop=(c == 7))
                                rl_, Rrl = rel.next()
                                self.act(rl_[:], ps[:, :], AF.Relu, [Rps], [Rrl])
                                self.tt("pool", hid[:, mi, :], rl_[:], rl_[:], ALU.mult, [Rrl], [Rhid[mi]])
                        for oh in range(2):
                            pss = [pb4() for _ in range(4)]
                            for kc in range(8):
                                wd, Rwd = wdnr.next()
                                dk = (wdnr.i - 1) % 4
                                kr = hh * 16 + kc * 2
                                S.dma(wdnc[dk], [(wd[:], wdn3[:, kr:kr + 2, oh * 512:(oh + 1) * 512])], reads=[R_wconv], writes=[Rwd])
                                for j in range(2):
                                    kci = kc * 2 + j
                                    for mq in range(4):
                                        ps, Rps = pss[mq]
                                        self.mm(ps[:, :], wd[:, j, mq * 128:(mq + 1) * 128], hid[:, kci, :], [Rwd, Rhid[kci]], [Rps],
                                                start=(kci == 0), stop=(kci == 15), inc=(kci == 15 or (j == 1 and mq == 3)))
                            for mq in range(4):
                                mt_ = oh * 4 + mq
                                ps, Rps = pss[mq]
                                self.tt("dve", xt[:, mt_, :], ps[:, :], xt[:, mt_, :], ALU.add, [Rps, Rxt], [Rxt])
                    rs, Rrs = rms4(xt, Rxt)
                    for c in range(8):
                        self.stt(h4[:, c, :], xt[:, c, :], vcol("g_ple", c), rs[:], ALU.mult, ALU.mult, [Rxt, Rvec, Rrs], [Rh4])
                    for mt_ in range(8):
                        msl = slice(mt_ * 128, (mt_ + 1) * 128)
                        ps, Rps = pb4()
                        for c in range(8):
                            self.mm(ps[:, :], wpg[:, c, msl], h4[:, c, :], [Rw4, Rh4], [Rps], start=(c == 0), stop=(c == 7))
                        gg, Rgg = gsg.next()
                        self.act(gg[:], ps[:, :], AF.Sigmoid, [Rps], [Rgg])
                        ps2, Rps2 = pb4()
                        for c in range(2):
                            self.mm(ps2[:, :], wpp[:, c, msl], pin[:, c, :], [Rw4, Rpin], [Rps2], start=(c == 0), stop=(c == 1))
                        self.tt("dve", gg[:], ps2[:, :], gg[:], ALU.mult, [Rps2, Rgg], [Rgg])
                        self.tt("pool", xt[:, mt_, :], xt[:, mt_, :], gg[:], ALU.add, [Rxt, Rgg], [Rxt])
                    rs, Rrs = rms4(xt, Rxt)
                    for c in range(8):
                        ob, Rob = osb4.next()
                        okk = (osb4.i - 1) % 2
                        self.stt(ob[:], xt[:, c, :], vcol("g_final", c), rs[:], ALU.mult, ALU.mult, [Rxt, Rvec, Rrs], [Rob])
                        S.dma(och4[okk], [(outT[c * 128:(c + 1) * 128, o0:o0 + 512], ob[:])], reads=[Rob])
                S.barrier()
                S.emit()
        return nc


def const_inputs():
    s = np.arange(128)[:, None]
    t = np.arange(128)[None, :]
    same = (s // 64) == (t // 64)
    m_lt = ((s < t) & same).astype(np.float32)
    m_le = ((s <= t) & same).astype(np.float32)
    mask4 = np.concatenate([m_lt, m_le, m_lt, m_le], axis=1)
    maskL = ((t < s) & same).astype(np.float32)
    tri = (s <= t).astype(np.float32)
    bd = same.astype(np.float32)
    scan = np.ones((128, 512), np.float32)
    scan[:, ::64] = 0.0
    inv_freq = (np.float32(10000.0) ** (-np.arange(16, dtype=np.float32) / np.float32(16))).astype(np.float32)
    ropec = np.zeros((128, 2), np.float32)
    ropec[64:80, 0] = inv_freq
    ropec[80:96, 0] = inv_freq
    ropec[64:80, 1] = -1.0
    ropec[80:96, 1] = 1.0
    return dict(c_ident=np.eye(128, dtype=np.float32), c_mask4=mask4, c_maskL=maskL, c_tri=tri, c_bd=bd, c_scan=scan, ropec=ropec)


def shared_inputs(inp):
    f = lambda a: np.ascontiguousarray(np.asarray(a, dtype=np.float32))
    w_in = f(inp["w_in"][0])
    z64 = np.zeros((D, 64), np.float32)
    kpe = w_in[:, 640:672]
    w_kpe = np.concatenate([z64, kpe, z64, kpe[:, 16:32], kpe[:, 0:16]], axis=1)
    w_uq = f(inp["w_uq"][0]).reshape(384, NH, 96)
    wq_sw = np.zeros_like(w_uq)
    wq_sw[:, :, 64:80] = w_uq[:, :, 80:96]
    wq_sw[:, :, 80:96] = w_uq[:, :, 64:80]
    w_ukv = f(inp["w_ukv"][0]).reshape(256, NH, 128)
    vec = {"g_mix": inp["g_mix"][0], "g_q_a": inp["g_q_a"][0], "g_kv_a": inp["g_kv_a"][0], "mu": inp["mu_rwkv"][0],
           "w0": inp["w0"][0], "a0": inp["a0"][0], "k_k": inp["k_k"][0], "k_a": inp["k_a"][0],
           "r_k": np.asarray(inp["r_k"][0]).reshape(-1), "ln_w": inp["ln_x_w"][0], "ln_b": inp["ln_x_b"][0],
           "g_ffn": inp["g_ffn"][0], "g_ple": inp["g_ple"][0], "g_final": inp["g_final"]}
    vecs = np.zeros((128, NVEC), np.float32)
    for name, n in VEC_LAYOUT:
        v = f(vec[name]).reshape(n, 128)
        vecs[:, VEC_OFF[name]:VEC_OFF[name] + n] = v.T
    d = dict(vecs=vecs, w_in=w_in, w_kpe=np.ascontiguousarray(w_kpe), wq=np.ascontiguousarray(w_uq.reshape(384, 768)),
             wq_sw=np.ascontiguousarray(wq_sw.reshape(384, 768)),
             wukv_k=np.ascontiguousarray(w_ukv[:, :, 0:64].reshape(256, 512)),
             wukv_v=np.ascontiguousarray(w_ukv[:, :, 64:128].reshape(256, 512)),
             w_o_mla=f(inp["w_o_mla"][0]), w2=f(inp["w2"][0]), a2=f(inp["a2"][0]), g2=f(inp["g2"][0]),
             w_o_rwkv=f(inp["w_o_rwkv"][0]), w_out=f(inp["w_out"][0]), w_up=f(inp["w_ffn_up"][0]), w_down=f(inp["w_ffn_down"][0]),
             w_pg=f(inp["w_ple_gate"][0]), w_pp=f(inp["w_ple_proj"][0]))
    d.update(const_inputs())
    return d


def core_inputs(x_b, p_b, pos_b, half, TP, TO):
    TT = TP + TO
    xT = np.zeros((D, TT), np.float32)
    posr = np.zeros((1, TT), np.int32)
    mrow = np.zeros((1, TT), np.float32)
    o0 = half * TP
    if half == 1:
        xT[:, 0:TP] = x_b[0:TP].T
        posr[0, 0:TP] = pos_b[0:TP]
    else:
        mrow[0, 0:TP] = -30000.0
    xT[:, TP:] = x_b[o0:o0 + TO].T
    posr[0, TP:] = pos_b[o0:o0 + TO]
    pT = np.ascontiguousarray(p_b[o0:o0 + TO].T.astype(np.float32))
    return dict(xT=xT, pT=pT, pos=posr, maskrow=mrow.astype(ml_dtypes.bfloat16))


_NC_CACHE = {}


def get_nc(TP, TO):
    if (TP, TO) not in _NC_CACHE:
        _NC_CACHE[(TP, TO)] = B(TP, TO).build()
    return _NC_CACHE[(TP, TO)]


def kernel(**inputs):
    x = np.asarray(inputs["x"], dtype=np.float32)
    p = np.asarray(inputs["p"], dtype=np.float32)[0]
    pos = np.asarray(inputs["positions"]).astype(np.int32)
    Bn, Sq, _ = x.shape
    TP = TO = Sq // 2
    nc = get_nc(TP, TO)
    sh = shared_inputs(inputs)
    in_maps = []
    for c in range(8):
        b, half = c // 2, c % 2
        m = dict(sh)
        m.update(core_inputs(x[b], p[b], pos[b], half, TP, TO))
        in_maps.append(m)
    res = run_bass_kernel_spmd(nc, in_maps, core_ids=list(range(8)))
    out = np.zeros((Bn, Sq, D), np.float32)
    for c in range(8):
        b, half = c // 2, c % 2
        out[b, half * TO:(half + 1) * TO, :] = res.results[c]["outT"].T
    return out
```
